# Optimizing a Trainium2 kernel written in Bass

```python
import math, functools
import jax, jax.numpy as jnp
from jax import lax
import numpy as np

D_MODEL = 1024
BATCH = 8
SEQ = 2048
DEPTH = 2
DEC_BATCH = 128
DEC_SEQ = 4
PAST_LEN = 16384
PAGE_SIZE = 128

HEAD_DIM = 64
N_HEADS = D_MODEL // 128
N_KV_HEADS = 2
Q_GROUP = N_HEADS // N_KV_HEADS
WINDOW = 128
ATTN_WIDTH = N_HEADS * HEAD_DIM
KV_WIDTH = N_KV_HEADS * HEAD_DIM
SSM_WIDTH = D_MODEL // 2
GROUP_CH = 16
SSM_GROUPS = SSM_WIDTH // GROUP_CH
SSM_STATE = 64
D_FF = 4 * D_MODEL
IN_COLS = ATTN_WIDTH + 2 * KV_WIDTH + SSM_WIDTH + 2 * D_MODEL
EPS = 1e-6
DT_MIN = 0.001
DT_MAX = 0.1
NEG_INF = -1e30

kernel_name = "gated_s5_swa_sink_hybrid_step"


def _rmsnorm(x, g):
    xf = x.astype(jnp.float32)
    y = xf * lax.rsqrt(jnp.mean(xf * xf, axis=-1, keepdims=True) + EPS) * g.astype(jnp.float32)
    return y.astype(x.dtype)


def _attend(q, k, v, q_pos, k_pos, sinks):
    b, n, nq = q.shape[:3]
    qg = q.reshape(b, n, nq, N_KV_HEADS, Q_GROUP, HEAD_DIM)
    s = jnp.einsum('bnqkgd,bnskd->bnkgqs', qg, k, preferred_element_type=jnp.float32) * (HEAD_DIM ** -0.5)
    dist = q_pos[:, :, None] - k_pos[:, None, :]
    valid = (dist >= 0) & (dist <= WINDOW) & (k_pos >= 0)[:, None, :]
    slopes = jnp.exp2(-8.0 * jnp.arange(1, N_HEADS + 1, dtype=jnp.float32) / N_HEADS).reshape(N_KV_HEADS, Q_GROUP)
    s = s - slopes[None, :, :, None, None] * dist.astype(jnp.float32)[:, None, None]
    s = jnp.where(valid[:, None, None], s, NEG_INF)
    sink = sinks.astype(jnp.float32).reshape(N_KV_HEADS, Q_GROUP)[None, None, :, :, None, None]
    m = jnp.maximum(s.max(axis=-1, keepdims=True), sink)
    e = jnp.exp(s - m)
    p = e / (e.sum(axis=-1, keepdims=True) + jnp.exp(sink - m))
    o = jnp.einsum('bnkgqs,bnskd->bnqkgd', p.astype(v.dtype), v)
    return o.reshape(b, n, nq, ATTN_WIDTH)


def _prompt_attention(q, k, v, sinks):
    b, t = q.shape[:2]
    nb = t // WINDOW
    qb = q.reshape(b, nb, WINDOW, N_HEADS, HEAD_DIM)
    pad = jnp.zeros((b, WINDOW, N_KV_HEADS, HEAD_DIM), k.dtype)

    def bands(a):
        ap = jnp.concatenate([pad, a], axis=1)
        prev = ap[:, :-WINDOW].reshape(b, nb, WINDOW, N_KV_HEADS, HEAD_DIM)
        cur = ap[:, WINDOW:].reshape(b, nb, WINDOW, N_KV_HEADS, HEAD_DIM)
        return jnp.concatenate([prev, cur], axis=2)

    q_pos = jnp.arange(t).reshape(nb, WINDOW)
    k_pos = jnp.arange(nb)[:, None] * WINDOW - WINDOW + jnp.arange(2 * WINDOW)[None, :]
    out = _attend(qb, bands(k), bands(v), q_pos, k_pos, sinks).reshape(b, t, ATTN_WIDTH)
    rows = min(WINDOW, t)
    return out, k[:, t - rows:], v[:, t - rows:]


def _sample_attention(cache_k, cache_v, q, k, v, sinks):
    past = cache_k.shape[1]
    t = q.shape[1]
    kc = jnp.concatenate([cache_k.astype(k.dtype), k], axis=1)
    vc = jnp.concatenate([cache_v.astype(v.dtype), v], axis=1)
    k_pos = PAST_LEN - past + jnp.arange(past + t)
    q_pos = PAST_LEN + jnp.arange(t)
    out = _attend(q[:, None], kc[:, None], vc[:, None], q_pos[None], k_pos[None], sinks)[:, 0]
    return out, kc[:, -past:], vc[:, -past:]


def _combine(e1, e2):
    ar1, ai1, br1, bi1 = e1
    ar2, ai2, br2, bi2 = e2
    return (ar1 * ar2 - ai1 * ai2,
            ar1 * ai2 + ai1 * ar2,
            ar2 * br1 - ai2 * bi1 + br2,
            ar2 * bi1 + ai2 * br1 + bi2)


def _s5_branch(u, h0_re, h0_im, p):
    f32 = jnp.float32
    b, t, _ = u.shape
    uf = u.astype(f32).reshape(b, t, SSM_GROUPS, GROUP_CH)
    lam_re = p['lam_re'].astype(f32)
    lam_im = p['lam_im'].astype(f32)
    step = jnp.exp(p['log_step'].astype(f32))[:, None]
    mag = jnp.exp(lam_re * step)
    ar = mag * jnp.cos(lam_im * step)
    ai = mag * jnp.sin(lam_im * step)
    den = lam_re * lam_re + lam_im * lam_im
    cr = ((ar - 1.0) * lam_re + ai * lam_im) / den
    ci = (ai * lam_re - (ar - 1.0) * lam_im) / den
    b_re = p['b_re'].astype(f32)
    b_im = p['b_im'].astype(f32)
    bb_re = cr[..., None] * b_re - ci[..., None] * b_im
    bb_im = cr[..., None] * b_im + ci[..., None] * b_re
    x_re = jnp.einsum('bsgc,gpc->sbgp', uf, bb_re)
    x_im = jnp.einsum('bsgc,gpc->sbgp', uf, bb_im)
    h0r = h0_re.astype(f32)
    h0i = h0_im.astype(f32)
    x_re = x_re.at[0].add(ar * h0r - ai * h0i)
    x_im = x_im.at[0].add(ar * h0i + ai * h0r)
    a_re = jnp.broadcast_to(ar, x_re.shape)
    a_im = jnp.broadcast_to(ai, x_im.shape)
    _, _, h_re, h_im = lax.associative_scan(_combine, (a_re, a_im, x_re, x_im), axis=0)
    y = (jnp.einsum('sbgp,gcp->bsgc', h_re, p['c_re'].astype(f32))
         - jnp.einsum('sbgp,gcp->bsgc', h_im, p['c_im'].astype(f32)))
    y = y.reshape(b, t, SSM_WIDTH) + p['d_skip'].astype(f32) * uf.reshape(b, t, SSM_WIDTH)
    y = jax.nn.gelu(y).astype(u.dtype)
    y = y * jax.nn.sigmoid(y @ p['w_glu'] + p['b_glu'])
    return y, h_re[-1], h_im[-1]


def _layer(x, attn_fn, h0_re, h0_im, p):
    b, t, _ = x.shape
    z = _rmsnorm(x, p['norm1_g']) @ p['w_in']
    o1 = ATTN_WIDTH
    o2 = o1 + KV_WIDTH
    o3 = o2 + KV_WIDTH
    o4 = o3 + SSM_WIDTH
    q, k, v, u, g = jnp.split(z, [o1, o2, o3, o4], axis=-1)
    q = _rmsnorm(q.reshape(b, t, N_HEADS, HEAD_DIM), p['q_norm_g'])
    k = _rmsnorm(k.reshape(b, t, N_KV_HEADS, HEAD_DIM), p['k_norm_g'])
    v = v.reshape(b, t, N_KV_HEADS, HEAD_DIM)
    attn, k_state, v_state = attn_fn(q, k, v, p['attn_sinks'])
    ssm, h_re, h_im = _s5_branch(u, h0_re, h0_im, p)
    gates = jax.nn.sigmoid((g + p['b_gate']).astype(jnp.float32)).astype(x.dtype)
    g_attn, g_ssm = jnp.split(gates, 2, axis=-1)
    mixed = g_attn * (attn @ p['w_attn_o']) + g_ssm * (ssm @ p['w_ssm_o'])
    x = x + mixed @ p['w_out']
    hdn = jax.nn.relu(_rmsnorm(x, p['norm2_g']) @ p['w_up'])
    x = x + (hdn * hdn) @ p['w_down']
    return x, k_state, v_state, h_re, h_im


def setup_inputs(seed: int = 0) -> dict:
    key = jax.random.key(seed)
    ks = jax.random.split(key, 32)
    f32 = jnp.float32
    nrm = lambda k, shape, s: jax.random.normal(k, shape, f32) * s
    rows = min(WINDOW, PAST_LEN)
    L, G, P, GC = DEPTH, SSM_GROUPS, SSM_STATE, GROUP_CH
    n_idx = jnp.arange(P, dtype=f32)
    return {
        'x_prompt': nrm(ks[0], (BATCH, SEQ, D_MODEL), 1.0),
        'x_sample': nrm(ks[1], (DEC_BATCH, DEC_SEQ, D_MODEL), 1.0),
        'cache_k': nrm(ks[2], (L, DEC_BATCH, rows, N_KV_HEADS, HEAD_DIM), 1.0),
        'cache_v': nrm(ks[3], (L, DEC_BATCH, rows, N_KV_HEADS, HEAD_DIM), 1.0),
        'state_ssm_re': nrm(ks[4], (L, DEC_BATCH, G, P), 0.5),
        'state_ssm_im': nrm(ks[5], (L, DEC_BATCH, G, P), 0.5),
        'norm1_g': 1.0 + nrm(ks[6], (L, D_MODEL), 0.02),
        'w_in': nrm(ks[7], (L, D_MODEL, IN_COLS), D_MODEL ** -0.5),
        'b_gate': nrm(ks[8], (L, 2 * D_MODEL), 0.01),
        'q_norm_g': 1.0 + nrm(ks[9], (L, HEAD_DIM), 0.02),
        'k_norm_g': 1.0 + nrm(ks[10], (L, HEAD_DIM), 0.02),
        'attn_sinks': nrm(ks[11], (L, N_HEADS), 0.5),
        'lam_re': -0.5 + nrm(ks[12], (L, G, P), 0.01),
        'lam_im': math.pi * n_idx + nrm(ks[13], (L, G, P), 0.01),
        'log_step': jax.random.uniform(ks[14], (L, G), f32, math.log(DT_MIN), math.log(DT_MAX)),
        'b_re': nrm(ks[15], (L, G, P, GC), (2 * GC) ** -0.5),
        'b_im': nrm(ks[16], (L, G, P, GC), (2 * GC) ** -0.5),
        'c_re': nrm(ks[17], (L, G, GC, P), P ** -0.5),
        'c_im': nrm(ks[18], (L, G, GC, P), P ** -0.5),
        'd_skip': nrm(ks[19], (L, SSM_WIDTH), 1.0),
        'w_glu': nrm(ks[20], (L, SSM_WIDTH, SSM_WIDTH), SSM_WIDTH ** -0.5),
        'b_glu': nrm(ks[21], (L, SSM_WIDTH), 0.01),
        'w_attn_o': nrm(ks[22], (L, ATTN_WIDTH, D_MODEL), ATTN_WIDTH ** -0.5),
        'w_ssm_o': nrm(ks[23], (L, SSM_WIDTH, D_MODEL), SSM_WIDTH ** -0.5),
        'w_out': nrm(ks[24], (L, D_MODEL, D_MODEL), D_MODEL ** -0.5),
        'norm2_g': 1.0 + nrm(ks[25], (L, D_MODEL), 0.02),
        'w_up': nrm(ks[26], (L, D_MODEL, D_FF), D_MODEL ** -0.5),
        'w_down': nrm(ks[27], (L, D_FF, D_MODEL), D_FF ** -0.5),
    }


def reference(x_prompt, x_sample, cache_k, cache_v, state_ssm_re, state_ssm_im,
              norm1_g, w_in, b_gate, q_norm_g, k_norm_g, attn_sinks,
              lam_re, lam_im, log_step, b_re, b_im, c_re, c_im, d_skip, w_glu, b_glu,
              w_attn_o, w_ssm_o, w_out, norm2_g, w_up, w_down):
    yp, ys = x_prompt, x_sample
    kp_l, vp_l, hrp_l, hip_l = [], [], [], []
    ks_l, vs_l, hrs_l, his_l = [], [], [], []
    sdt = state_ssm_re.dtype
    for l in range(DEPTH):
        p = {'norm1_g': norm1_g[l], 'w_in': w_in[l], 'b_gate': b_gate[l],
             'q_norm_g': q_norm_g[l], 'k_norm_g': k_norm_g[l], 'attn_sinks': attn_sinks[l],
             'lam_re': lam_re[l], 'lam_im': lam_im[l], 'log_step': log_step[l],
             'b_re': b_re[l], 'b_im': b_im[l], 'c_re': c_re[l], 'c_im': c_im[l],
             'd_skip': d_skip[l], 'w_glu': w_glu[l], 'b_glu': b_glu[l],
             'w_attn_o': w_attn_o[l], 'w_ssm_o': w_ssm_o[l], 'w_out': w_out[l],
             'norm2_g': norm2_g[l], 'w_up': w_up[l], 'w_down': w_down[l]}
        h0 = jnp.zeros((yp.shape[0], SSM_GROUPS, SSM_STATE), jnp.float32)
        yp, kp, vp, hrp, hip = _layer(yp, _prompt_attention, h0, h0, p)
        ys, kss, vss, hrs, his = _layer(
            ys, functools.partial(_sample_attention, cache_k[l], cache_v[l]),
            state_ssm_re[l], state_ssm_im[l], p)
        kp_l.append(kp); vp_l.append(vp); hrp_l.append(hrp.astype(sdt)); hip_l.append(hip.astype(sdt))
        ks_l.append(kss); vs_l.append(vss); hrs_l.append(hrs.astype(sdt)); his_l.append(his.astype(sdt))
    k_prompt = jnp.stack(kp_l)
    v_prompt = jnp.stack(vp_l)
    ssm_re_prompt = jnp.stack(hrp_l)
    ssm_im_prompt = jnp.stack(hip_l)
    k_sample = jnp.stack(ks_l)
    v_sample = jnp.stack(vs_l)
    ssm_re_sample = jnp.stack(hrs_l)
    ssm_im_sample = jnp.stack(his_l)
    return (yp, ys, k_prompt, v_prompt, ssm_re_prompt, ssm_im_prompt,
            k_sample, v_sample, ssm_re_sample, ssm_im_sample)
```

```python
import math
import numpy as np
import ml_dtypes
from contextlib import ExitStack
import concourse.bass as bass
import concourse.mybir as mybir
from concourse.bass_utils import run_bass_kernel_spmd

F32 = mybir.dt.float32
BF16 = mybir.dt.bfloat16
I32 = mybir.dt.int32
AF = mybir.ActivationFunctionType
ALU = mybir.AluOpType

NCORES = 8
D = 1024
T = 2048
NS = 64
NT = T + NS
L = 2
EPS = 1e-6
HEAD_PERM = [0, 4, 1, 5, 2, 6, 3, 7]
STAGE = 99
NLAYERS = 2
CHUNK = 256
NCHUNK = 512 // CHUNK
DEBUG_DUMP = False
DEBUG_CORES = 0
SKIP = set()


class Stream:
    def __init__(self, nc, stack, name):
        self.sem = stack.enter_context(nc.semaphore(name))
        self.name = name
        self.cnt = 0


class Ctx:
    def __init__(self, nc, stack):
        self.nc = nc
        self.engs = {'pe': nc.tensor, 'act': nc.scalar, 'dve': nc.vector, 'pool': nc.gpsimd, 'sp': nc.sync}
        self.st = {n: Stream(nc, stack, 's_' + n) for n in self.engs}
        self.waited = {n: {} for n in self.engs}
        self.lw = {}
        self.rd = {}
        self.stack = stack
        self.nstream = 0

    def stream(self, name):
        s = Stream(self.nc, self.stack, name)
        self.st[name] = s
        return name

    def _deps(self, reads, writes):
        deps = {}

        def add(tok):
            if tok is None:
                return
            s, v = tok
            if deps.get(s, 0) < v:
                deps[s] = v
        for k in reads:
            add(self.lw.get(k))
        for k in writes:
            add(self.lw.get(k))
            for s, v in self.rd.get(k, {}).items():
                add((s, v))
        return deps

    def _wait(self, en, deps):
        e = self.engs[en]
        w = self.waited[en]
        for s, v in deps.items():
            if s == en and en in ('pe', 'sp'):
                continue
            if w.get(s, 0) >= v:
                continue
            e.wait_ge(self.st[s].sem, v)
            w[s] = v

    def _record(self, tok, reads, writes):
        s, v = tok
        for k in reads:
            d = self.rd.setdefault(k, {})
            if d.get(s, 0) < v:
                d[s] = v
        for k in writes:
            self.lw[k] = tok
            self.rd[k] = {}

    def op(self, en, fn, reads=(), writes=(), signal=True):
        psr = [k for k in reads if isinstance(k, tuple) and k[0] == 'ps']
        if psr:
            writes = list(writes) + psr
        self._wait(en, self._deps(reads, writes))
        inst = fn(self.engs[en])
        st = self.st[en]
        if signal:
            st.cnt += 1
            inst.then_inc(st.sem, 1)
            tok = (en, st.cnt)
        else:
            tok = (en, st.cnt + 1)
        self._record(tok, reads, writes)
        return tok

    def dma(self, q, stream, out, in_, reads=(), writes=(), **kw):
        self._wait(q, self._deps(reads, writes))
        st = self.st[stream]
        st.cnt += 16
        self.engs[q].dma_start(out=out, in_=in_, **kw).then_inc(st.sem, 16)
        tok = (stream, st.cnt)
        self._record(tok, reads, writes)
        return tok

    def fence(self):
        for en, e in self.engs.items():
            w = self.waited[en]
            for s, st in self.st.items():
                if s == en or st.cnt == 0:
                    continue
                if w.get(s, 0) >= st.cnt:
                    continue
                e.wait_ge(st.sem, st.cnt)
                w[s] = st.cnt
        self.lw = {}
        self.rd = {}


def build_program():
    nc = bass.Bass("TRN2", target_bir_lowering=False)

    def din(name, shape, dt=F32):
        return nc.dram_tensor(name, list(shape), dt, kind="ExternalInput").ap()

    def dout(name, shape, dt=F32):
        return nc.dram_tensor(name, list(shape), dt, kind="ExternalOutput").ap()

    xp = din("xp", [T, D])
    xs = din("xs", [NS, D])
    cache_k = din("cache_k", [L, 16, 128, 128])
    cache_v = din("cache_v", [L, 16, 128, 128])
    st_re = din("st_re", [L, 16, 2048])
    st_im = din("st_im", [L, 16, 2048])
    w_in = din("w_in", [L, D, 3328])
    w_glu = din("w_glu", [L, 512, 512])
    w_ao = din("w_ao", [L, 512, D])
    w_so = din("w_so", [L, 512, D])
    w_out = din("w_out", [L, D, D])
    w_up = din("w_up", [L, D, 4096])
    w_down = din("w_down", [L, 4096, D])
    norm1_g = din("norm1_g", [L, D])
    norm2_g = din("norm2_g", [L, D])
    b_gate = din("b_gate", [L, 2048])
    qk_g = din("qk_g", [L, 2, 128])
    sinks = din("sinks", [L, 4, 128])
    lam_re = din("lam_re", [L, 32, 64])
    lam_im = din("lam_im", [L, 32, 64])
    log_step = din("log_step", [L, 32])
    b_re = din("b_re", [L, 2048, 16])
    b_im = din("b_im", [L, 2048, 16])
    c_re = din("c_re", [L, 512, 64])
    c_im = din("c_im", [L, 512, 64])
    d_skip = din("d_skip", [L, 512])
    b_glu = din("b_glu", [L, 512])
    c_ident = din("c_ident", [128, 128])
    c_blk64 = din("c_blk64", [128, 128])
    c_biasp = din("c_biasp", [128, 8, 256])
    c_biasc = din("c_biasc", [128, 32])
    c_biasn = din("c_biasn", [4, 32])
    c_biasnf = din("c_biasnf", [64, 512])
    c_cmask = din("c_cmask", [128, 128])
    c_rmask = din("c_rmask", [128, 4])

    yp = dout("yp", [T, D])
    ys = dout("ys", [NS, D])
    kp = dout("kp", [L, 128, 128])
    vp = dout("vp", [L, 128, 128])
    hrp = dout("hrp", [L, 16, 128])
    hip = dout("hip", [L, 16, 128])
    ksam = dout("ksam", [L, 16, 128, 128])
    vsam = dout("vsam", [L, 16, 128, 128])
    hrs = dout("hrs", [L, 16, 16, 128])
    his = dout("his", [L, 16, 16, 128])
    if DEBUG_DUMP:
        dbg_at = dout("dbg_at", [4, 128, 4, 576], BF16)
        dbg_st = dout("dbg_st", [4, 128, 4, 576], BF16)
        dbg_yg = dout("dbg_yg", [4, 128, 4, 576], BF16)
        dbg_mix = dout("dbg_mix", [4, 128, 8, 576], BF16)

    stack = ExitStack()
    with stack:
        C = Ctx(nc, stack)
        op, dma, fence = C.op, C.dma, C.fence

        def sb(name, shape, dt=F32):
            return stack.enter_context(nc.sbuf_tensor(name, list(shape), dt))

        X = sb("X", [128, 8, NT])
        PSUM = stack.enter_context(nc.psum_tensor("PS", [128, 8, 512], F32))
        ident = sb("ident", [128, 128])
        ones = sb("ones", [128, 128], BF16)
        blk64 = sb("blk64", [128, 128], BF16)
        EB = sb("EB", [128, 8, 256], BF16)
        biasc = sb("biasc", [128, 32])
        biasnf = sb("biasnf", [64, 512])
        cmask = sb("cmask", [128, 128])
        rmask = sb("rmask", [128, 4])
        g1 = sb("g1", [128, L, 8])
        g2 = sb("g2", [128, L, 8])
        bg = sb("bg", [128, L, 16])
        qkg = sb("qkg", [128, L, 2])
        esink = sb("esink", [128, L, 4])
        dsk = sb("dsk", [128, L, 4])
        bgl = sb("bgl", [128, L, 4])
        WR = [sb("wr%d" % i, [128, 4096], BF16) for i in range(3)]
        wr_stream = [C.stream("wrs%d" % i) for i in range(3)]
        wr_i = [0]
        PHc = sb("PHc", [128, 16, CHUNK + 1], BF16)
        PHs = sb("PHs", [128, 16, CHUNK + 1], BF16)
        MAG = sb("MAG", [128, 16])
        AR = sb("AR", [128, 16])
        AI = sb("AI", [128, 16])
        R128c = sb("R128c", [128, 16])
        R128s = sb("R128s", [128, 16])
        nR128s = sb("nR128s", [128, 16])
        P127c = sb("P127c", [128, 16])
        P127s = sb("P127s", [128, 16])
        BTr = sb("BTr", [128, 16, 128], BF16)
        BTi = sb("BTi", [128, 16, 128], BF16)
        CTr = sb("CTr", [128, 16, 128], BF16)
        nCTi = sb("nCTi", [128, 16, 128], BF16)
        nCTr = sb("nCTr", [128, 16, 128], BF16)
        GE2 = sb("GE2", [128, 16, 2])
        SS = sb("SS", [128, 16, 2])

        s_par = C.stream("par")
        s_out = C.stream("outs")
        ps_i = [0]

        def psb(n=1):
            i = ps_i[0]
            if i + n > 8:
                i = 0
            ps_i[0] = (i + n) % 8
            return i

        def pk(i, n=1):
            return [("ps", j) for j in range(i, i + n)]

        class Ring:
            def __init__(self, bufs, streams, keys):
                self.bufs, self.streams, self.keys = bufs, streams, keys
                self.i = 0

        G3 = Ring(WR, wr_stream, [("wr", i) for i in range(3)])

        def wload(src_ap, view_shape, key, ring=None):
            ring = ring or G3
            i = ring.i % len(ring.bufs)
            ring.i += 1
            buf = ring.bufs[i]
            n = 1
            for s_ in view_shape[1:]:
                n *= s_
            flat = buf[:, 0:n]
            if len(view_shape) == 3:
                view = flat.rearrange("p (k n) -> p k n", k=view_shape[1])
            else:
                view = flat
            dma('pool', ring.streams[i], view, src_ap, writes=[ring.keys[i]])
            return view, ring.keys[i]

        class WQ:
            def __init__(self, specs, ring=None, ahead=1):
                self.specs = specs
                self.loaded = []
                self.ring = ring
                self.ahead = ahead

            def get(self, i):
                upto = min(i + self.ahead, len(self.specs) - 1)
                while len(self.loaded) <= upto:
                    src, shape = self.specs[len(self.loaded)]
                    self.loaded.append(wload(src, shape, None, self.ring))
                return self.loaded[i]

        def bc_mid(ap2, n):
            return ap2.unsqueeze(1).to_broadcast([ap2.shape[0], n, ap2.shape[1]])

        def bc_last(ap2, n):
            return ap2.unsqueeze(2).to_broadcast([ap2.shape[0], ap2.shape[1], n])

        with nc.allow_non_contiguous_dma(reason="small param loads"):
            for (dst, src) in [
                (ident[:], c_ident[:, :]),
                (biasc[:], c_biasc[:, :]), (biasnf[:], c_biasnf[:, :]), (cmask[:], c_cmask[:, :]),
                (rmask[:], c_rmask[:, :]),
                (g1[:], norm1_g.rearrange("l (k p) -> p l k", p=128)),
                (g2[:], norm2_g.rearrange("l (k p) -> p l k", p=128)),
                (bg[:], b_gate.rearrange("l (k p) -> p l k", p=128)),
                (qkg[:], qk_g.rearrange("l t p -> p l t")),
                (esink[:], sinks.rearrange("l t p -> p l t")),
                (dsk[:], d_skip.rearrange("l (k p) -> p l k", p=128)),
                (bgl[:], b_glu.rearrange("l (k p) -> p l k", p=128)),
            ]:
                dma('sp', s_par, dst, src, writes=["par"])
        s_b64 = C.stream("b64")
        dma('pool', s_b64, blk64[:], c_blk64[:, :], writes=["blk64"])
        op('dve', lambda e: e.memset(ones[:], 1.0), writes=["ones"])
        with ExitStack() as ph0:
            biasp = ph0.enter_context(nc.sbuf_tensor("biasp", [128, 8, 256], F32))
            s_bp = C.stream("bp")
            dma('sp', s_bp, biasp[:], c_biasp[:, :, :], writes=["biasp"])
            op('act', lambda e: e.activation(out=EB[:], in_=biasp[:], func=AF.Exp), reads=["biasp"], writes=["EB"])
            fence()
        op('act', lambda e: e.activation(out=esink[:], in_=esink[:], func=AF.Exp), reads=["par"], writes=["esink"])
        op('dve', lambda e: e.tensor_scalar(out=qkg[:, :, 0:1], in0=qkg[:, :, 0:1], scalar1=0.125, scalar2=None,
                                            op0=ALU.mult), reads=["par"], writes=["qkg"])
        fence()

        def phase0():
          with ExitStack() as ph:
              XT = [ph.enter_context(nc.sbuf_tensor("xt%d" % i, [128, D], F32)) for i in range(2)]
              xts = [C.stream("xts%d" % i) for i in range(2)]
              for tt in range(17):
                  b = tt % 2
                  rows = 128 if tt < 16 else NS
                  src = xp[tt * 128:(tt + 1) * 128, :] if tt < 16 else xs[:, :]
                  dma('sp', xts[b], XT[b][0:rows, :], src, writes=[("xt", b)])
                  for half in range(2):
                      pi = psb()
                      for j in range(4):
                          k = half * 4 + j
                          op('pe', lambda e, k=k, j=j, pi=pi, b=b, rows=rows: e.transpose(
                              out=PSUM[:, pi, j * 128:j * 128 + rows], in_=XT[b][0:rows, k * 128:(k + 1) * 128],
                              identity=ident[0:rows, 0:rows]),
                              reads=[("xt", b)], writes=pk(pi), signal=(j == 3))
                      eng = 'act'
                      src_ps = PSUM[:, pi, :].rearrange("p (j t) -> p j t", j=4)[:, :, 0:rows]
                      dst = X[:, half * 4:half * 4 + 4, tt * 128:tt * 128 + rows]
                      if eng == 'act':
                          op('act', lambda e, s_=src_ps, d_=dst: e.activation(out=d_, in_=s_, func=AF.Copy),
                             reads=pk(pi), writes=[("X", tt)])
                      else:
                          op('dve', lambda e, s_=src_ps, d_=dst: e.tensor_copy(out=d_, in_=s_),
                             reads=pk(pi), writes=[("X", tt)])
              fence()

        TWO_PI = 2.0 * math.pi

        def tt(en, out, a, b, o, reads, writes):
            return op(en, lambda e: e.tensor_tensor(out=out, in0=a, in1=b, op=o), reads=reads, writes=writes)

        def ssm_tables(l, mid=None):
            with ExitStack() as ph:
                def t(name, shape, dt=F32):
                    return ph.enter_context(nc.sbuf_tensor("st%d_" % l + name, list(shape), dt))
                LR = t("LR", [128, 16]); LI = t("LI", [128, 16]); LS = t("LS", [128, 16])
                ANG = t("ANG", [128, 32]); KI = t("KI", [128, 32], I32); KF = t("KF", [128, 32])
                M1 = t("M1", [128, 32]); SC = t("SC", [128, 32])
                T1 = t("T1", [128, 16]); T2 = t("T2", [128, 16]); T3 = t("T3", [128, 16])
                CR = t("CR", [128, 16]); CI = t("CI", [128, 16]); RDEN = t("RDEN", [128, 16])
                Pc = t("Pc", [128, 16, CHUNK + 1]); Ps = t("Ps", [128, 16, CHUNK + 1])
                Q1 = t("Q1", [128, 16, CHUNK // 2]); Q2 = t("Q2", [128, 16, CHUNK // 2])
                BNr = t("BNr", [128, 16, 16]); BNi = t("BNi", [128, 16, 16])
                BBr = t("BBr", [128, 16, 16]); BBi = t("BBi", [128, 16, 16]); BT1 = t("BT1", [128, 16, 16])
                BEr = t("BEr", [128, 16, 32]); BEi = t("BEi", [128, 16, 32])
                CNr = t("CNr", [128, 4, 64]); CNi = t("CNi", [128, 4, 64])
                CE = t("CE", [128, 128])
                sp_ = C.stream("sst%d" % l)
                with nc.allow_non_contiguous_dma(reason="ssm params"):
                    dma('sp', sp_, LR[:], lam_re[l].rearrange("(s gl) p -> (gl p) s", gl=2), writes=["sp"])
                    dma('sp', sp_, LI[:], lam_im[l].rearrange("(s gl) p -> (gl p) s", gl=2), writes=["sp"])
                    lsv = log_step[l:l + 1, :].rearrange("o (s gl) -> o gl s", gl=2)
                    for gl in range(2):
                        dma('sp', sp_, LS[gl * 64:(gl + 1) * 64, :], lsv[:, gl, :].to_broadcast([64, 16]), writes=["sp"])
                    dma('sp', sp_, BNr[:], b_re[l].rearrange("(s gl p) c -> (gl p) s c", gl=2, p=64), writes=["sp"])
                    dma('sp', sp_, BNi[:], b_im[l].rearrange("(s gl p) c -> (gl p) s c", gl=2, p=64), writes=["sp"])
                    dma('sp', sp_, CNr[:], c_re[l].rearrange("(q r) p -> r q p", r=128), writes=["sp"])
                    dma('sp', sp_, CNi[:], c_im[l].rearrange("(q r) p -> r q p", r=128), writes=["sp"])
                R = ["sp"]
                op('act', lambda e: e.activation(out=LS[:], in_=LS[:], func=AF.Exp), reads=R, writes=["LS"])
                tt('dve', T1[:], LR[:], LS[:], ALU.mult, R + ["LS"], ["T1"])
                tt('dve', ANG[:, 0:16], LI[:], LS[:], ALU.mult, R + ["LS"], ["ANG"])
                op('act', lambda e: e.activation(out=MAG[:], in_=T1[:], func=AF.Exp), reads=["T1"], writes=["MAG"])
                op('dve', lambda e: e.tensor_scalar(out=ANG[:, 16:32], in0=ANG[:, 0:16], scalar1=math.pi / 2, scalar2=None,
                                                    op0=ALU.add), reads=["ANG"], writes=["ANG"])
                op('dve', lambda e: e.tensor_scalar(out=KF[:], in0=ANG[:], scalar1=1.0 / TWO_PI, scalar2=None, op0=ALU.mult),
                   reads=["ANG"], writes=["KF"])
                op('dve', lambda e: e.tensor_copy(out=KI[:], in_=KF[:]), reads=["KF"], writes=["KI"])
                op('dve', lambda e: e.tensor_copy(out=KF[:], in_=KI[:]), reads=["KI"], writes=["KF"])
                op('dve', lambda e: e.scalar_tensor_tensor(out=ANG[:], in0=KF[:], scalar=-TWO_PI, in1=ANG[:], op0=ALU.mult,
                                                           op1=ALU.add), reads=["KF", "ANG"], writes=["ANG"])
                op('dve', lambda e: e.tensor_single_scalar(out=M1[:], in_=ANG[:], scalar=math.pi, op=ALU.is_gt),
                   reads=["ANG"], writes=["M1"])
                op('dve', lambda e: e.scalar_tensor_tensor(out=ANG[:], in0=M1[:], scalar=-TWO_PI, in1=ANG[:], op0=ALU.mult,
                                                           op1=ALU.add), reads=["M1", "ANG"], writes=["ANG"])
                op('dve', lambda e: e.tensor_single_scalar(out=M1[:], in_=ANG[:], scalar=-math.pi, op=ALU.is_lt),
                   reads=["ANG"], writes=["M1"])
                op('dve', lambda e: e.scalar_tensor_tensor(out=ANG[:], in0=M1[:], scalar=TWO_PI, in1=ANG[:], op0=ALU.mult,
                                                           op1=ALU.add), reads=["M1", "ANG"], writes=["ANG"])
                op('act', lambda e: e.activation(out=SC[:], in_=ANG[:], func=AF.Sin), reads=["ANG"], writes=["SC"])
                SN = SC[:, 0:16]
                CS = SC[:, 16:32]
                tt('dve', AR[:], MAG[:], CS, ALU.mult, ["MAG", "SC"], ["AR"])
                tt('dve', AI[:], MAG[:], SN, ALU.mult, ["MAG", "SC"], ["AI"])
                tt('dve', T1[:], LR[:], LR[:], ALU.mult, R + ["MAG"], ["T1"])
                tt('dve', T2[:], LI[:], LI[:], ALU.mult, R, ["T2"])
                tt('dve', T1[:], T1[:], T2[:], ALU.add, ["T1", "T2"], ["T1"])
                op('dve', lambda e: e.reciprocal(out=RDEN[:], in_=T1[:]), reads=["T1"], writes=["RDEN"])
                op('dve', lambda e: e.tensor_scalar(out=T3[:], in0=AR[:], scalar1=-1.0, scalar2=None, op0=ALU.add),
                   reads=["AR"], writes=["T3"])
                tt('dve', T1[:], T3[:], LR[:], ALU.mult, ["T3", "RDEN"], ["T1"])
                tt('dve', T2[:], AI[:], LI[:], ALU.mult, ["AI", "T1"], ["T2"])
                tt('dve', T1[:], T1[:], T2[:], ALU.add, ["T1", "T2"], ["T1"])
                tt('dve', CR[:], T1[:], RDEN[:], ALU.mult, ["T1", "RDEN"], ["CR"])
                tt('dve', T1[:], AI[:], LR[:], ALU.mult, ["CR", "AI"], ["T1"])
                tt('dve', T2[:], T3[:], LI[:], ALU.mult, ["T3", "CR"], ["T2"])
                tt('dve', T1[:], T1[:], T2[:], ALU.subtract, ["T1", "T2"], ["T1"])
                tt('dve', CI[:], T1[:], RDEN[:], ALU.mult, ["T1", "RDEN"], ["CI"])
                op('dve', lambda e: e.memset(Pc[:, :, 0:1], 1.0), writes=["P"])
                op('dve', lambda e: e.memset(Ps[:, :, 0:1], 0.0), writes=["P"])
                op('dve', lambda e: e.tensor_copy(out=Pc[:, :, 1:2], in_=CS.unsqueeze(2)), reads=["SC", "P"], writes=["P"])
                op('dve', lambda e: e.tensor_copy(out=Ps[:, :, 1:2], in_=SN.unsqueeze(2)), reads=["SC", "P"], writes=["P"])
                m = 1
                while m < CHUNK:
                    Ac = Pc[:, :, 1:m + 1]
                    As = Ps[:, :, 1:m + 1]
                    Bc = Pc[:, :, m:m + 1].to_broadcast([128, 16, m])
                    Bs = Ps[:, :, m:m + 1].to_broadcast([128, 16, m])
                    q1 = Q1[:, :, 0:m]
                    q2 = Q2[:, :, 0:m]
                    tt('dve', q1, Ac, Bc, ALU.mult, ["P"], ["Q1"])
                    tt('dve', q2, As, Bs, ALU.mult, ["P"], ["Q2"])
                    tt('dve', Pc[:, :, m + 1:2 * m + 1], q1, q2, ALU.subtract, ["Q1", "Q2", "P"], ["P"])
                    tt('dve', q1, Ac, Bs, ALU.mult, ["P"], ["Q1"])
                    tt('dve', q2, As, Bc, ALU.mult, ["P"], ["Q2"])
                    tt('dve', Ps[:, :, m + 1:2 * m + 1], q1, q2, ALU.add, ["Q1", "Q2", "P"], ["P"])
                    m *= 2
                op('dve', lambda e: e.tensor_copy(out=PHc[:], in_=Pc[:]), reads=["P"], writes=["PH"])
                op('dve', lambda e: e.tensor_copy(out=PHs[:], in_=Ps[:]), reads=["P"], writes=["PH"])
                op('dve', lambda e: e.tensor_copy(out=R128c[:], in_=Pc[:, :, CHUNK]), reads=["P"], writes=["R128"])
                op('dve', lambda e: e.tensor_copy(out=R128s[:], in_=Ps[:, :, CHUNK]), reads=["P"], writes=["R128"])
                op('dve', lambda e: e.tensor_scalar(out=nR128s[:], in0=Ps[:, :, CHUNK], scalar1=-1.0, scalar2=None,
                                                    op0=ALU.mult), reads=["P"], writes=["R128"])
                op('dve', lambda e: e.tensor_copy(out=P127c[:], in_=Pc[:, :, CHUNK - 1]), reads=["P"], writes=["R128"])
                op('dve', lambda e: e.tensor_copy(out=P127s[:], in_=Ps[:, :, CHUNK - 1]), reads=["P"], writes=["R128"])
                CRb = bc_last(CR[:], 16)
                CIb = bc_last(CI[:], 16)
                tt('dve', BBr[:], BNr[:], CRb, ALU.mult, R + ["CR"], ["BBr"])
                tt('dve', BT1[:], BNi[:], CIb, ALU.mult, R + ["CI"], ["BT1"])
                tt('dve', BBr[:], BBr[:], BT1[:], ALU.subtract, ["BBr", "BT1"], ["BBr"])
                tt('dve', BBi[:], BNi[:], CRb, ALU.mult, R + ["CR"], ["BBi"])
                tt('dve', BT1[:], BNr[:], CIb, ALU.mult, R + ["CI", "BBr"], ["BT1"])
                tt('dve', BBi[:], BBi[:], BT1[:], ALU.add, ["BBi", "BT1"], ["BBi"])
                if mid is not None:
                    mid()
                for (BE, BB, BTd, nm) in ((BEr, BBr, BTr, "r"), (BEi, BBi, BTi, "i")):
                    op('dve', lambda e, BE=BE: e.memset(BE[:], 0.0), writes=["BE" + nm])
                    op('dve', lambda e, BE=BE, BB=BB: e.tensor_copy(out=BE[0:64, :, 0:16], in_=BB[0:64, :, :]),
                       reads=["BB" + nm, "BE" + nm], writes=["BE" + nm])
                    op('dve', lambda e, BE=BE, BB=BB: e.tensor_copy(out=BE[64:128, :, 16:32], in_=BB[64:128, :, :]),
                       reads=["BB" + nm, "BE" + nm], writes=["BE" + nm])
                    for q in range(4):
                        pi = psb()
                        op('pe', lambda e, BE=BE, q=q, pi=pi: e.transpose(
                            out=PSUM[:, pi, 0:128], in_=BE[:, 4 * q:4 * q + 4, :].rearrange("p a b -> p (a b)"),
                            identity=ident[:, :]), reads=["BE" + nm], writes=pk(pi))
                        for slot in range(4):
                            op('dve', lambda e, BTd=BTd, q=q, slot=slot, pi=pi: e.tensor_scalar(
                                out=BTd[:, 4 * q + slot, :], in0=PSUM[:, pi, 0:128], scalar1=rmask[:, slot:slot + 1],
                                scalar2=None, op0=ALU.mult), reads=pk(pi), writes=["BT" + nm])
                for (CN, CTd, sgn, nm) in ((CNr, CTr, 1.0, "r"), (CNi, nCTi, -1.0, "i"), (CNr, nCTr, -1.0, "r2")):
                    op('dve', lambda e, CTd=CTd: e.memset(CTd[:], 0.0), writes=["CT" + nm])
                    for q in range(4):
                        tt('dve', CE[:, 0:64], CN[:, q, :], cmask[:, 0:64], ALU.mult, R, ["CE"])
                        tt('dve', CE[:, 64:128], CN[:, q, :], cmask[:, 64:128], ALU.mult, R, ["CE"])
                        pi = psb()
                        op('pe', lambda e, pi=pi: e.transpose(out=PSUM[:, pi, 0:128], in_=CE[:, :], identity=ident[:, :]),
                           reads=["CE"], writes=pk(pi))
                        for slot in range(4):
                            op('dve', lambda e, CTd=CTd, q=q, slot=slot, pi=pi, sgn=sgn: e.tensor_scalar(
                                out=CTd[:, 4 * q + slot, 32 * slot:32 * slot + 32], in0=PSUM[:, pi, 32 * slot:32 * slot + 32],
                                scalar1=sgn, scalar2=None, op0=ALU.mult), reads=pk(pi), writes=["CT" + nm])
                op('dve', lambda e: e.memset(GE2[:], 0.0), writes=["GE"])
                op('dve', lambda e: e.tensor_copy(out=SS[:, :, 0], in_=nR128s[:, :]), reads=["R128"], writes=["SS"])
                op('dve', lambda e: e.tensor_copy(out=SS[:, :, 1], in_=R128s[:, :]), reads=["R128"], writes=["SS"])
                fence()

        def mm_group(pi, n, pairs, reads, extra_writes=()):
            last = len(pairs) - 1
            tok = None
            for i, (lh, rh) in enumerate(pairs):
                tok = op('pe', lambda e, lh=lh, rh=rh, i=i: e.matmul(PSUM[:, pi, 0:n], lhsT=lh, rhs=rh, start=(i == 0),
                                                                     stop=(i == last)),
                         reads=reads, writes=pk(pi) + list(extra_writes), signal=(i == last))
            return tok

        def layer_phase_a(l):
            with ExitStack() as ph:
                def t(name, shape, dt=BF16):
                    return ph.enter_context(nc.sbuf_tensor("a%d_" % l + name, list(shape), dt))
                XN = t("XN", [128, 8, 576])
                MIX = t("MIX", [128, 8, 576])
                R2 = t("R2", [128, 4, 576])
                KT = t("KT", [128, 128 + 576])
                VT = t("VT", [128, 5, 128])
                VS = t("VS", [64, 128])
                U = t("U", [128, 4, 576])
                AT = t("AT", [128, 4, 576])
                STt = t("ST", [128, 4, 576])
                YG = t("YG", [128, 4, 576])
                s_o = C.stream("ao%d" % l)
                s_o1 = C.stream("ao1_%d" % l)
                s_o2 = C.stream("ao2_%d" % l)

                for bi in range(4):
                  c0 = bi * 512
                  subs = [(0, 512)] + ([(512, 64)] if bi == 3 else [])
                  NB = 576 if bi == 3 else 512
                  with ExitStack() as ph2:
                    def t2(name, shape, dt=BF16):
                        return ph2.enter_context(nc.sbuf_tensor("a%d_%d_" % (l, bi) + name, list(shape), dt))
                    SQ = t2("SQ", [128, 8, 512])
                    RS = t2("RS", [128, 512], F32)
                    XG = t2("XG", [128, 2, 512], F32)
                    RSq = t2("RSq", [128, 512], F32)
                    SQh = t2("SQh", [128, 512])
                    KF = t2("KF", [128, 128], F32)
                    KSF = t2("KSF", [128, 64], F32)
                    OUTS = t2("OUTS", [128, 128], F32)
                    OUTS2 = t2("OUTS2", [128, 128], F32)
                    wq_a2 = WQ([(w_in[l][:, ch_ * 512:ch_ * 512 + (512 if ch_ < 2 else 256)].rearrange("(k p) n -> p k n", p=128),
                                 [128, 8, (512 if ch_ < 2 else 256)]) for ch_ in range(3)])
                    wq_a2.get(0)
                    for (o, n) in subs:
                        xs_ = X[:, :, c0 + o:c0 + o + n]
                        op('act', lambda e, xs_=xs_, n=n: e.activation(out=SQ[:, :, 0:n], in_=xs_, func=AF.Square),
                           reads=["X"], writes=["SQ"])
                        if 'a1a' in SKIP:
                            continue
                        pi = psb()
                        mm_group(pi, n, [(ones[:, :], SQ[:, k, 0:n]) for k in range(8)], ["SQ", "ones"])
                        op('act', lambda e, pi=pi, n=n: e.activation(out=RS[:, 0:n], in_=PSUM[:, pi, 0:n], func=AF.Sqrt,
                                                                     bias=EPS, scale=1.0 / D), reads=pk(pi), writes=["RS"])
                        if 'a1b' in SKIP:
                            continue
                        op('dve', lambda e, n=n: e.reciprocal(out=RS[:, 0:n], in_=RS[:, 0:n]), reads=["RS"], writes=["RS"])
                        if 'a1c' in SKIP:
                            continue
                        for k in range(8):
                            xg = XG[:, k % 2, 0:n]
                            op('act', lambda e, k=k, o=o, n=n, xg=xg: e.activation(
                                out=xg, in_=X[:, k, c0 + o:c0 + o + n], func=AF.Copy, scale=g1[:, l, k:k + 1]),
                                reads=["X"], writes=[("XG", k % 2)])
                            op('dve', lambda e, k=k, o=o, n=n, xg=xg: e.tensor_tensor(
                                out=XN[:, k, o:o + n], in0=xg, in1=RS[:, 0:n], op=ALU.mult),
                                reads=[("XG", k % 2), "RS"], writes=[("XN", k)])
                    if STAGE >= 8 and bi >= 1:
                        norm2_into(l, MIX, SQ, RS, XG, (bi - 1) * 512, 0, 512)
                    XNr = [("XN", k) for k in range(8)]
                    for ch in range(3):
                        wcols = 512 if ch < 2 else 256
                        Wv, wk = wq_a2.get(ch)
                        for j in range(wcols // 128):
                            col = ch * 512 + j * 128
                            if col == 640:
                                continue
                            for (o, n) in subs:
                                pi = psb()
                                mm_group(pi, n, [(Wv[:, k, j * 128:(j + 1) * 128], XN[:, k, o:o + n]) for k in range(8)],
                                         XNr + [wk])
                                if 'a2a' in SKIP:
                                    continue
                                if col < 640:
                                    isq = col < 512
                                    op('act', lambda e, pi=pi, n=n: e.activation(out=SQh[:, 0:n], in_=PSUM[:, pi, 0:n],
                                                                                 func=AF.Square), reads=pk(pi), writes=["SQh"])
                                    p2 = psb()
                                    mm_group(p2, n, [(blk64[:, :], SQh[:, 0:n])], ["SQh", "blk64"])
                                    op('act', lambda e, p2=p2, n=n: e.activation(out=RSq[:, 0:n], in_=PSUM[:, p2, 0:n],
                                                                                 func=AF.Sqrt, bias=EPS, scale=1.0 / 64),
                                       reads=pk(p2), writes=["RSq"])
                                    op('dve', lambda e, n=n: e.reciprocal(out=RSq[:, 0:n], in_=RSq[:, 0:n]),
                                       reads=["RSq"], writes=["RSq"])
                                    if 'a2b' in SKIP:
                                        continue
                                    if isq:
                                        dst = R2[:, col // 128, o:o + n]
                                        dk = ("R2", col // 128)
                                        gi_ = 0
                                    else:
                                        dst = KT[:, 128 + o:128 + o + n]
                                        dk = "KT"
                                        gi_ = 1
                                    op('dve', lambda e, pi=pi, n=n, dst=dst, gi_=gi_: e.scalar_tensor_tensor(
                                        out=dst, in0=PSUM[:, pi, 0:n], scalar=qkg[:, l, gi_:gi_ + 1], in1=RSq[:, 0:n],
                                        op0=ALU.mult, op1=ALU.mult), reads=pk(pi) + ["RSq", "qkg"], writes=[dk])
                                    if (not isq) and bi == 3 and 'a2c' not in SKIP:
                                        if o == 0:
                                            op('dve', lambda e, pi=pi: e.scalar_tensor_tensor(
                                                out=KF[:, :], in0=PSUM[:, pi, 384:512], scalar=qkg[:, l, 1:2],
                                                in1=RSq[:, 384:512], op0=ALU.mult, op1=ALU.mult),
                                                reads=pk(pi) + ["RSq"], writes=["KF"])
                                            p3 = psb()
                                            op('pe', lambda e, p3=p3: e.transpose(out=PSUM[:, p3, 0:128], in_=KF[:, :],
                                                                                  identity=ident[:, :]),
                                               reads=["KF"], writes=pk(p3))
                                            op('act', lambda e, p3=p3: e.activation(out=OUTS[:, :], in_=PSUM[:, p3, 0:128],
                                                                                    func=AF.Copy), reads=pk(p3), writes=["OUTS"])
                                            dma('sp', s_o1, kp[l], OUTS[:, :], reads=["OUTS"])
                                        else:
                                            op('dve', lambda e, pi=pi: e.scalar_tensor_tensor(
                                                out=KSF[:, :], in0=PSUM[:, pi, 0:64], scalar=qkg[:, l, 1:2],
                                                in1=RSq[:, 0:64], op0=ALU.mult, op1=ALU.mult),
                                                reads=pk(pi) + ["RSq"], writes=["KSF"])
                                            p3 = psb()
                                            op('pe', lambda e, p3=p3: e.transpose(out=PSUM[0:64, p3, 0:128], in_=KSF[:, :],
                                                                                  identity=ident[:, :]),
                                               reads=["KSF"], writes=pk(p3))
                                            op('act', lambda e, p3=p3: e.activation(out=OUTS2[0:64, :], in_=PSUM[0:64, p3, 0:128],
                                                                                    func=AF.Copy), reads=pk(p3), writes=["OUTS2"])
                                            for i_ in range(4):
                                                dma('sp', s_o2, ksam[l][:, 124 + i_, :], OUTS2[i_ * 16:(i_ + 1) * 16, :],
                                                    reads=["OUTS2"])
                                else:
                                    ut = (col - 768) // 128
                                    op('act', lambda e, pi=pi, n=n, ut=ut, o=o: e.activation(
                                        out=U[:, ut, o:o + n], in_=PSUM[:, pi, 0:n], func=AF.Copy),
                                        reads=pk(pi), writes=[("U", ut)])
                        if ch == 1 and 'a2d' not in SKIP:
                            pi = psb()
                            for tl in range(4):
                                for k in range(8):
                                    op('pe', lambda e, tl=tl, k=k, pi=pi: e.matmul(
                                        PSUM[:, pi, tl * 128:(tl + 1) * 128],
                                        lhsT=XN[:, k, tl * 128:(tl + 1) * 128], rhs=Wv[:, k, 128:256],
                                        start=(k == 0), stop=(k == 7)),
                                        reads=XNr + [wk], writes=pk(pi), signal=(k == 7))
                            if 'v1' in SKIP:
                                continue
                            op('act', lambda e, pi=pi: e.activation(out=VT[:, 1:5, :], in_=PSUM[:, pi, :].rearrange(
                                "p (t d) -> p t d", t=4), func=AF.Copy), reads=pk(pi), writes=["VT"])
                            if bi == 3 and 'v2' not in SKIP:
                                op('dve', lambda e, pi=pi: e.tensor_copy(out=OUTS[:, :], in_=PSUM[:, pi, 384:512]),
                                   reads=pk(pi), writes=["OUTS"])
                                dma('sp', s_o1, vp[l], OUTS[:, :], reads=["OUTS"])
                                pv5 = psb()
                                for k in range(8):
                                    op('pe', lambda e, k=k, pv5=pv5: e.matmul(
                                        PSUM[0:64, pv5, 0:128], lhsT=XN[:, k, 512:576], rhs=Wv[:, k, 128:256],
                                        start=(k == 0), stop=(k == 7)), reads=XNr + [wk], writes=pk(pv5), signal=(k == 7))
                                op('act', lambda e, pv5=pv5: e.activation(out=VS[:, :], in_=PSUM[0:64, pv5, 0:128], func=AF.Copy),
                                   reads=pk(pv5), writes=["VS"])
                                op('dve', lambda e, pv5=pv5: e.tensor_copy(out=OUTS2[0:64, :], in_=PSUM[0:64, pv5, 0:128]),
                                   reads=pk(pv5), writes=["OUTS2"])
                                for i_ in range(4):
                                    dma('sp', s_o2, vsam[l][:, 124 + i_, :], OUTS2[i_ * 16:(i_ + 1) * 16, :], reads=["OUTS2"])
                    fence()
                  E = dict(U=U, s_o=s_o, R2=R2, KT=KT, VT=VT, VS=VS, AT=AT, ST=STt, YG=YG, XN=XN, MIX=MIX, subs=subs, c0=c0)
                  if STAGE >= 3:
                      attn_block(l, bi, E)
                  if STAGE >= 2:
                      ssm_block(l, bi, E)
                  if STAGE >= 5:
                      mix_block(l, bi, E)
                  if DEBUG_DUMP and l == 0:
                      dma('sp', s_o, dbg_at[bi], AT[:, :, :], reads=[])
                      dma('sp', s_o, dbg_st[bi], STt[:, :, :], reads=[])
                      dma('sp', s_o, dbg_yg[bi], YG[:, :, :], reads=[])
                      dma('sp', s_o, dbg_mix[bi], MIX[:, :, :], reads=[])
                  op('pool', lambda e: e.tensor_copy(out=KT[:, 0:128], in_=KT[:, 128 + 384:128 + 512]), reads=["KT"], writes=["KT"])
                  op('pool', lambda e: e.tensor_copy(out=VT[:, 0, :], in_=VT[:, 4, :]), reads=["VT"], writes=["VT"])
                if "shift" not in SKIP:
                    with nc.allow_non_contiguous_dma(reason="cache shift"):
                        dma('sp', s_o, ksam[l][:, 0:124, :], cache_k[l][:, 4:128, :])
                        dma('sp', s_o, vsam[l][:, 0:124, :], cache_v[l][:, 4:128, :])
                if STAGE >= 8:
                    ffn_tail(l, MIX)
                fence()

        def ffn_specs(l):
            fs = []
            for c in range(8):
                fs.append((w_up[l][:, c * 512:(c + 1) * 512].rearrange("(k p) n -> p k n", p=128), [128, 8, 512]))
                fs.append((w_down[l][c * 512:(c + 1) * 512, :].rearrange("(k p) n -> p k n", p=128), [128, 4, 1024]))
            return fs

        def norm2_into(l, XN2, SQ, RS, XG, xcol, o, n):
            pi = psb()
            for hf in range(2):
                op('act', lambda e, hf=hf: e.activation(out=SQ[:, 0:4, 0:n], in_=X[:, 4 * hf:4 * hf + 4, xcol:xcol + n], func=AF.Square),
                   reads=["X"], writes=["SQ"])
                for k_ in range(4):
                    op('pe', lambda e, k_=k_, hf=hf: e.matmul(PSUM[:, pi, 0:n], lhsT=ones[:, :], rhs=SQ[:, k_, 0:n],
                                                             start=(hf == 0 and k_ == 0), stop=(hf == 1 and k_ == 3)),
                       reads=["SQ", "ones"], writes=pk(pi), signal=(k_ == 3))
            op('act', lambda e: e.activation(out=RS[:, 0:n], in_=PSUM[:, pi, 0:n], func=AF.Sqrt, bias=EPS, scale=1.0 / D),
               reads=pk(pi), writes=["RS"])
            op('dve', lambda e: e.reciprocal(out=RS[:, 0:n], in_=RS[:, 0:n]), reads=["RS"], writes=["RS"])
            for k_ in range(8):
                xg = XG[:, k_ % 2, 0:n]
                op('act', lambda e, k_=k_, xg=xg: e.activation(out=xg, in_=X[:, k_, xcol:xcol + n], func=AF.Copy,
                                                              scale=g2[:, l, k_:k_ + 1]), reads=["X"], writes=[("XG", k_ % 2)])
                op('dve', lambda e, k_=k_, xg=xg: e.tensor_tensor(out=XN2[:, k_, o:o + n], in0=xg, in1=RS[:, 0:n], op=ALU.mult),
                   reads=[("XG", k_ % 2), "RS"], writes=[("MIX", k_)])

        def ssm_block(l, bi, E):
            U = E['U']; s_o = E['s_o']; YG = E['YG']; ST = E['ST']; subs = E['subs']
            with ExitStack() as ph:
                def t(name, shape, dt=BF16):
                    return ph.enter_context(nc.sbuf_tensor("s%d_%d_" % (l, bi) + name, list(shape), dt))
                Y1 = t("Y1", [128, 512]); T1 = t("T1", [128, 512]); SG = T1
                php = ExitStack()
                def tp(name, shape, dt=BF16):
                    return php.enter_context(nc.sbuf_tensor("sp%d_%d_" % (l, bi) + name, list(shape), dt))
                Zt = [tp("Z%d" % i, [128, 2, 512]) for i in range(2)]
                PRM = [tp("PRM%d" % i, [128, 4, 512]) for i in range(2)]
                Ma = PRM[1][:, 0:2, :]
                Mb = PRM[1][:, 2:4, :]
                MK = ("PR", 1)
                Xc = [tp("Xc%d" % i, [128, 2, 512]) for i in range(2)]
                if l == 0 and bi == 0:
                    print("SBUF remaining in ssm prompt scope:", nc.sbuf_bytes_remaining)
                Gt = [tp("G%d" % i, [128, 2, 512]) for i in range(2)]
                PR = PRM
                INt = [tp("IN%d" % i, [128, 4, 2], F32) for i in range(2)]
                Ut = [tp("U%d" % i, [128, 2], F32) for i in range(2)]
                HO = tp("HO", [128, 2, 16], F32); HT = tp("HT", [128, 16], F32)
                HOUT = tp("HOUT", [16, 2, 128], F32)
                PTa = [tp("PTa%d" % i, [128, 256]) for i in range(2)]
                DRa = tp("DRa", [128, 256], F32)
                v3 = lambda ap_: ap_.rearrange("p (c j) -> p c j", c=NCHUNK)
                R2 = E['R2']; KT = E['KT']; VT = E['VT']; AT = E['AT']
                att_it = [0]

                def att_unit(i, hp, b):
                    hs = i * 2 + hp
                    rows = slice(hp * 64, (hp + 1) * 64)
                    has_prev = (bi * 4 + b) > 0
                    tb = att_it[0] % 2
                    att_it[0] += 1
                    qv = R2[rows, i, b * 128:(b + 1) * 128]
                    if has_prev:
                        op('pe', lambda e: e.matmul(PSUM[:, BS, 0:128], lhsT=KT[rows, b * 128:(b + 1) * 128], rhs=qv,
                                                    start=True, stop=True), reads=[("R2", i), "KT"], writes=pk(BS), signal=False)
                    op('pe', lambda e: e.matmul(PSUM[:, BS, 128:256], lhsT=KT[rows, 128 + b * 128:128 + (b + 1) * 128], rhs=qv,
                                                start=True, stop=True), reads=[("R2", i), "KT"], writes=pk(BS), signal=True)
                    c_lo = 0 if has_prev else 128
                    op('act', lambda e: e.activation(out=PTa[tb][:, c_lo:256], in_=PSUM[:, BS, c_lo:256], func=AF.Exp),
                       reads=pk(BS), writes=[("PTa", tb)])
                    tt('pool', PTa[tb][:, c_lo:256], PTa[tb][:, c_lo:256], EB[:, hs, c_lo:256], ALU.mult, [("PTa", tb), "EB"],
                       [("PTa", tb)])
                    return lambda: att_pv(rows, b, tb, has_prev)

                def att_pv(rows, b, tb, has_prev):
                    parts = ([(VT[:, b, rows], PTa[tb][:, 0:128])] if has_prev else []) + [(VT[:, b + 1, rows], PTa[tb][:, 128:256])]
                    bb = b % 2
                    for (coff, use_ones) in ((0, False), (256, True)):
                        for ii, (vv, pp) in enumerate(parts):
                            lh = ones[:, 0:64] if use_ones else vv
                            op('pe', lambda e, lh=lh, pp=pp, ii=ii, coff=coff: e.matmul(
                                PSUM[rows, BOD, coff + bb * 128:coff + (bb + 1) * 128], lhsT=lh, rhs=pp, start=(ii == 0),
                                stop=(ii == len(parts) - 1)), reads=[("PTa", tb), "VT", "ones"], writes=pk(BOD),
                                signal=(ii == len(parts) - 1))

                def att_norm(i, h):
                    op('act', lambda e: e.activation(out=DRa[:, :], in_=PSUM[:, BOD, 256:512], func=AF.Ln, bias=esink[:, l, i:i + 1],
                                                     scale=1.0), reads=pk(BOD) + ["esink"], writes=["DRa"])
                    op('act', lambda e: e.activation(out=DRa[:, :], in_=DRa[:, :], func=AF.Exp, scale=-1.0), reads=["DRa"], writes=["DRa"])
                    tt('dve', AT[:, i, h * 256:(h + 1) * 256], PSUM[:, BOD, 0:256], DRa[:, :], ALU.mult, pk(BOD) + ["DRa"], [("AT", i)])

                att_groups = []
                if STAGE >= 3:
                    for i in range(4):
                        for h in range(2):
                            att_groups.append((i, h))

                def epilogue(q, yb, o, n):
                    op('dve', lambda e: e.scalar_tensor_tensor(out=Y1[:, 0:n], in0=U[:, q, o:o + n], scalar=dsk[:, l, q:q + 1],
                                                               in1=PSUM[:, yb, 0:n], op0=ALU.mult, op1=ALU.add),
                       reads=pk(yb) + [("U", q)], writes=["Y1"])
                    tt('dve', T1[:, 0:n], Y1[:, 0:n], Y1[:, 0:n], ALU.mult, ["Y1"], ["T1"])
                    op('dve', lambda e: e.tensor_scalar(out=T1[:, 0:n], in0=T1[:, 0:n], scalar1=0.044715, scalar2=1.0,
                                                        op0=ALU.mult, op1=ALU.add), reads=["T1"], writes=["T1"])
                    tt('dve', T1[:, 0:n], T1[:, 0:n], Y1[:, 0:n], ALU.mult, ["T1", "Y1"], ["T1"])
                    op('act', lambda e: e.activation(out=T1[:, 0:n], in_=T1[:, 0:n], func=AF.Sigmoid, scale=1.5957691216057308),
                       reads=["T1"], writes=["T1"])
                    tt('dve', YG[:, q, o:o + n], Y1[:, 0:n], T1[:, 0:n], ALU.mult, ["Y1", "T1"], [("YG", q)])

                BX0, BY, BS, BOD, BUP = 0, 2, 3, 4, 5

                def x0_mm(s):
                    q = s // 4
                    mm_group(BX0, 512, [(BTr[:, s, :], U[:, q, 0:512])], [("U", q), "BTr"])
                    mm_group(BX0 + 1, 512, [(BTi[:, s, :], U[:, q, 0:512])], [("U", q), "BTi"])

                def evac(s):
                    b_ = s % 2
                    xk = ("Xc", b_)
                    op('act', lambda e: e.activation(out=Xc[b_][:, 0, :], in_=PSUM[:, BX0, :], func=AF.Copy), reads=pk(BX0), writes=[xk])
                    op('act', lambda e: e.activation(out=Xc[b_][:, 1, :], in_=PSUM[:, BX0 + 1, :], func=AF.Copy), reads=pk(BX0 + 1),
                       writes=[xk])

                def modops(s):
                    b_ = s % 2
                    xk = ("Xc", b_)
                    zk = ("Z", b_)
                    c4 = bc_mid(PHc[:, s, 0:CHUNK], 2 * NCHUNK)
                    s4 = bc_mid(PHs[:, s, 0:CHUNK], 2 * NCHUNK)
                    x4 = Xc[b_][:, :, :].rearrange("p c (h j) -> p (c h) j", j=CHUNK)
                    tt('dve', Ma.rearrange("p c (h j) -> p (c h) j", j=CHUNK), x4, c4, ALU.mult, [xk, "PH"], [MK])
                    tt('dve', Mb.rearrange("p c (h j) -> p (c h) j", j=CHUNK), x4, s4, ALU.mult, [xk, "PH"], [MK])
                    tt('dve', Zt[b_][:, 0, :], Ma[:, 0, :], Mb[:, 1, :], ALU.add, [MK], [zk])
                    tt('dve', Zt[b_][:, 1, :], Ma[:, 1, :], Mb[:, 0, :], ALU.subtract, [MK], [zk])

                def chain(tiles, hook=lambda: None):
                    for ch in range(NCHUNK):
                        for s in tiles:
                            b_ = s % 2
                            gk = ("G", b_)
                            if ch == 0:
                                ge = GE2[:, s, :]
                                ger = GE2[:, s, ::-1]
                                rk = ["GE"]
                            else:
                                ge = Gt[b_][:, :, ch * CHUNK - 1]
                                ger = Gt[b_][:, ::-1, ch * CHUNK - 1]
                                rk = [gk]
                            tt('dve', Ut[b_][:, :], ger, SS[:, s, :], ALU.mult, rk + ["SS"], [("UT", b_)])
                            op('dve', lambda e, ge=ge, s=s, b_=b_, ch=ch: e.scalar_tensor_tensor(
                                out=INt[b_][:, ch, :], in0=ge, scalar=R128c[:, s:s + 1], in1=Ut[b_][:, :], op0=ALU.mult, op1=ALU.add),
                                reads=rk + [("UT", b_)], writes=[("IN", b_)])
                        hook()
                        for s in tiles:
                            b_ = s % 2
                            gk = ("G", b_)
                            cs_ = slice(ch * CHUNK, (ch + 1) * CHUNK)
                            for c_ in range(2):
                                op('dve', lambda e, s=s, ch=ch, cs_=cs_, c_=c_, b_=b_: e.tensor_tensor_scan(
                                    out=Gt[b_][:, c_, cs_], data0=MAG[:, s:s + 1].to_broadcast([128, CHUNK]), data1=Zt[b_][:, c_, cs_],
                                    initial=INt[b_][:, ch, c_:c_ + 1], op0=ALU.mult, op1=ALU.add),
                                    reads=[("Z", b_), ("IN", b_)], writes=[gk])
                            hook()

                def finish(s):
                    q = s // 4
                    slot = s % 4
                    yb = BY
                    b_ = s % 2
                    gk = ("G", b_)
                    cb = bc_mid(PHc[:, s, 0:CHUNK], NCHUNK)
                    sb_ = bc_mid(PHs[:, s, 0:CHUNK], NCHUNK)
                    op('dve', lambda e: e.tensor_copy(out=GE2[:, s, :], in_=Gt[b_][:, :, 511]), reads=[gk], writes=["GE"])
                    P_ = PR[b_]
                    pkey = ("PR", b_)
                    gr_ = v3(Gt[b_][:, 0, :])
                    gi_ = v3(Gt[b_][:, 1, :])
                    c4 = bc_mid(PHc[:, s, 0:CHUNK], 2 * NCHUNK)
                    s4 = bc_mid(PHs[:, s, 0:CHUNK], 2 * NCHUNK)
                    g4 = Gt[b_][:, :, :].rearrange("p c (h j) -> p (c h) j", j=CHUNK)
                    tt('dve', P_[:, 0:2, :].rearrange("p c (h j) -> p (c h) j", j=CHUNK), g4, c4, ALU.mult, [gk, "PH"], [pkey])
                    tt('dve', P_[:, 2:4, :].rearrange("p c (h j) -> p (c h) j", j=CHUNK), g4, s4, ALU.mult, [gk, "PH"], [pkey])
                    pairs = [(CTr[:, s, :], P_[:, 0, :]), (nCTi[:, s, :], P_[:, 1, :]), (nCTi[:, s, :], P_[:, 2, :]),
                             (nCTr[:, s, :], P_[:, 3, :])]
                    for ii, (lh, rh) in enumerate(pairs):
                        first = (slot == 0 and ii == 0)
                        lastm = (slot == 3 and ii == 3)
                        op('pe', lambda e, lh=lh, rh=rh, first=first, lastm=lastm: e.matmul(
                            PSUM[:, yb, 0:512], lhsT=lh, rhs=rh, start=first, stop=lastm),
                            reads=[pkey, "CT"], writes=pk(yb), signal=(ii == 3))
                    if slot == 3:
                        pending.append(lambda: epilogue(q, yb, 0, 512))

                ffn_on = (bi >= 1) and STAGE >= 8 and 'ffni' not in SKIP
                if ffn_on:
                    XN2 = E['MIX']
                    xo = (bi - 1) * 512
                    Hf = tp("Hf", [128, 4, 512])
                    wq_f = WQ(ffn_specs(l))
                    BDN = (6, 7)

                    def ffn_up_unit(c, j):
                        Wu, wuk = wq_f.get(2 * c)
                        mm_group(BUP, 512, [(Wu[:, k_, j * 128:(j + 1) * 128], XN2[:, k_, 0:512]) for k_ in range(8)],
                                 [("MIX", k_) for k_ in range(8)] + [wuk])
                        op('act', lambda e: e.activation(out=Hf[:, j, :], in_=PSUM[:, BUP, :], func=AF.Relu),
                           reads=pk(BUP), writes=[("Hf", j)])
                        op('act', lambda e: e.activation(out=Hf[:, j, :], in_=Hf[:, j, :], func=AF.Square),
                           reads=[("Hf", j)], writes=[("Hf", j)])

                    def ffn_down(c, m):
                        Wd, wdk = wq_f.get(2 * c + 1)
                        mm_group(BDN[m % 2], 512, [(Wd[:, j, m * 128:(m + 1) * 128], Hf[:, j, :]) for j in range(4)],
                                 [("Hf", j) for j in range(4)] + [wdk])

                    def ffn_add(m):
                        bank = BDN[m % 2]
                        op('dve', lambda e: e.tensor_tensor(out=X[:, m, xo:xo + 512], in0=PSUM[:, bank, :], in1=X[:, m, xo:xo + 512],
                                                            op=ALU.add), reads=pk(bank) + ["X"], writes=["X"])
                else:
                    Wg, wgk = wload(w_glu[l].rearrange("(k p) n -> p k n", p=128), [128, 4, 512], None)

                pending = []
                x0_mm(0)
                evac(0)
                x0_mm(1)
                evac(1)
                for p_ in range(8):
                    s0, s1 = 2 * p_, 2 * p_ + 1
                    modops(s0)
                    modops(s1)
                    if p_ < 7:
                        x0_mm(s0 + 2)
                        evac(s0 + 2)
                        x0_mm(s1 + 2)
                        evac(s1 + 2)
                    aunits = []
                    if att_groups:
                        if p_ > 0:
                            att_norm(*att_groups[p_ - 1])
                        gi_, gh_ = att_groups[p_]
                        aunits = [(gi_, hp, b) for hp in range(2) for b in (2 * gh_, 2 * gh_ + 1)]
                    for j in range(4):
                        pv_ = att_unit(*aunits[j]) if j < len(aunits) else None
                        if ffn_on:
                            ffn_up_unit(p_, j)
                        if pv_:
                            pv_()
                    chain([s0, s1])
                    todo = pending
                    pending = []
                    for f_ in todo:
                        f_()
                    if ffn_on:
                        for m_ in range(8):
                            ffn_down(p_, m_)
                            if m_ >= 1:
                                ffn_add(m_ - 1)
                    finish(s0)
                    if ffn_on:
                        ffn_add(7)
                    finish(s1)
                if att_groups:
                    att_norm(*att_groups[7])
                for f_ in pending:
                    f_()
                if ffn_on:
                    Wg, wgk = wload(w_glu[l].rearrange("(k p) n -> p k n", p=128), [128, 4, 512], None)
                if bi == 3:
                    tt('dve', HO[:, 0, :], GE2[:, :, 0], P127c[:, :], ALU.mult, ["GE"], ["HO"])
                    tt('dve', HT[:, :], GE2[:, :, 1], P127s[:, :], ALU.mult, ["GE"], ["HT"])
                    tt('dve', HO[:, 0, :], HO[:, 0, :], HT[:, :], ALU.subtract, ["HO", "HT"], ["HO"])
                    tt('dve', HO[:, 1, :], GE2[:, :, 0], P127s[:, :], ALU.mult, ["GE", "HO"], ["HO"])
                    tt('dve', HT[:, :], GE2[:, :, 1], P127c[:, :], ALU.mult, ["GE", "HO"], ["HT"])
                    tt('dve', HO[:, 1, :], HO[:, 1, :], HT[:, :], ALU.add, ["HO", "HT"], ["HO"])
                    for c_ in range(2):
                        pi = 6 + c_
                        op('pe', lambda e, c_=c_, pi=pi: e.transpose(out=PSUM[0:16, pi, 0:128], in_=HO[:, c_, :], identity=ident[:, :]),
                           reads=["HO"], writes=pk(pi))
                        op('act', lambda e, c_=c_, pi=pi: e.activation(out=HOUT[:, c_, :], in_=PSUM[0:16, pi, 0:128], func=AF.Copy),
                           reads=pk(pi), writes=["HOUT"])
                    dma('sp', s_o, hrp[l], HOUT[:, 0, :], reads=["HOUT"])
                    dma('sp', s_o, hip[l], HOUT[:, 1, :], reads=["HOUT"])
                fence()
                php.close()
                if bi == 3 and STAGE >= 4 and 'ssms' not in SKIP:
                    ssm_sample(l, E, epilogue)
                    fence()
                for j in range(4):
                    for (o, n) in subs:
                        pi = psb()
                        mm_group(pi, n, [(Wg[:, q_, j * 128:(j + 1) * 128], YG[:, q_, o:o + n]) for q_ in range(4)],
                                 [("YG", q_) for q_ in range(4)] + [wgk])
                        op('act', lambda e, pi=pi, n=n, j=j: e.activation(out=SG[:, 0:n], in_=PSUM[:, pi, 0:n], func=AF.Sigmoid,
                                                                          bias=bgl[:, l, j:j + 1], scale=1.0),
                           reads=pk(pi), writes=["T1"])
                        tt('dve', ST[:, j, o:o + n], YG[:, j, o:o + n], SG[:, 0:n], ALU.mult, ["T1", ("YG", j)], [("ST", j)])
                fence()

        def ssm_sample(l, E, epilogue):
            U = E['U']; s_o = E['s_o']
            with ExitStack() as ph:
                def t(name, shape, dt=F32):
                    return ph.enter_context(nc.sbuf_tensor("ss%d_" % l + name, list(shape), dt))
                SN = t("SN", [16, 2048])
                H0 = [t("H0%d" % c_, [128, 16, 16]) for c_ in range(2)]
                HS = [t("HS%d" % c_, [128, 16, 64]) for c_ in range(2)]
                HSb = [t("HSb%d" % c_, [128, 16, 64], BF16) for c_ in range(2)]
                TA = t("TA", [128, 16, 16]); TB = t("TB", [128, 16, 16])
                s_s = C.stream("ssl%d" % l)
                for s in range(16):
                    q = s // 4
                    for c_, BT in ((0, BTr), (1, BTi)):
                        bank = 2 * c_ + s // 8
                        op('pe', lambda e, s=s, q=q, BT=BT, bank=bank: e.matmul(
                            PSUM[:, bank, (s % 8) * 64:(s % 8) * 64 + 64], lhsT=BT[:, s, :], rhs=U[:, q, 512:576],
                            start=True, stop=True), reads=[("U", q), "BTr", "BTi"], writes=pk(bank), signal=True)
                for c_, src in ((0, st_re), (1, st_im)):
                    dma('sp', s_s, SN[:, :], src[l], writes=["SN"])
                    pi = 6 + c_
                    for s in range(16):
                        op('pe', lambda e, s=s, pi=pi: e.transpose(out=PSUM[:, pi, s * 16:(s + 1) * 16],
                                                                   in_=SN[0:16, s * 128:(s + 1) * 128], identity=ident[0:16, 0:16]),
                           reads=["SN"], writes=pk(pi), signal=(s == 15))
                    op('act', lambda e, c_=c_, pi=pi: e.activation(out=H0[c_][:, :, :], in_=PSUM[:, pi, 0:256].rearrange(
                        "p (s b) -> p s b", s=16), func=AF.Copy), reads=pk(pi), writes=[("H0", c_)])
                ARb = bc_last(AR[:, :], 16)
                AIb = bc_last(AI[:, :], 16)
                xv = [PSUM[:, 2 * c_:2 * c_ + 2, :].rearrange("p b (s c) -> p (b s) c", c=64) for c_ in range(2)]
                for i_ in range(4):
                    cs_ = slice(i_ * 16, (i_ + 1) * 16)
                    if i_ == 0:
                        pr_, pi_ = H0[0][:, :, :], H0[1][:, :, :]
                        rk = [("H0", 0), ("H0", 1)]
                    else:
                        ps_ = slice((i_ - 1) * 16, i_ * 16)
                        pr_, pi_ = HS[0][:, :, ps_], HS[1][:, :, ps_]
                        rk = ["HS"]
                    tt('dve', TA[:, :, :], pr_, ARb, ALU.mult, rk + ["AR"], ["TA"])
                    tt('dve', TB[:, :, :], pi_, AIb, ALU.mult, rk + ["AR"], ["TB"])
                    tt('dve', TA[:, :, :], TA[:, :, :], TB[:, :, :], ALU.subtract, ["TA", "TB"], ["TA"])
                    tt('dve', HS[0][:, :, cs_], xv[0][:, :, cs_], TA[:, :, :], ALU.add, pk(0, 2) + ["TA"], ["HS"])
                    tt('dve', TA[:, :, :], pi_, ARb, ALU.mult, rk + ["AR", "HS"], ["TA"])
                    tt('dve', TB[:, :, :], pr_, AIb, ALU.mult, rk + ["AR"], ["TB"])
                    tt('dve', TA[:, :, :], TA[:, :, :], TB[:, :, :], ALU.add, ["TA", "TB"], ["TA"])
                    tt('dve', HS[1][:, :, cs_], xv[1][:, :, cs_], TA[:, :, :], ALU.add, pk(2, 2) + ["TA", "HS"], ["HS"])
                for c_ in range(2):
                    op('dve', lambda e, c_=c_: e.tensor_copy(out=HSb[c_][:, :, :], in_=HS[c_][:, :, :]), reads=["HS", "HS"],
                       writes=[("HSb", c_)])
                for q in range(4):
                    yb = 4 + (q % 2)
                    pairs = []
                    for slot in range(4):
                        s = 4 * q + slot
                        pairs.append((CTr[:, s, :], HSb[0][:, s, :]))
                        pairs.append((nCTi[:, s, :], HSb[1][:, s, :]))
                    mm_group(yb, 64, pairs, [("HSb", 0), ("HSb", 1), "CT"])
                    epilogue(q, yb, 512, 64)
                for c_, dst in ((0, hrs), (1, his)):
                    for g4 in range(4):
                        pi = g4
                        for j in range(4):
                            s = g4 * 4 + j
                            op('pe', lambda e, s=s, j=j, pi=pi, c_=c_: e.transpose(
                                out=PSUM[0:16, pi, j * 128:(j + 1) * 128], in_=HS[c_][:, s, 48:64], identity=ident[:, :]),
                                reads=["HS", "HS"], writes=pk(pi), signal=(j == 3))
                        op('act', lambda e, pi=pi, g4=g4: e.activation(out=SN[:, g4 * 512:(g4 + 1) * 512], in_=PSUM[0:16, pi, :],
                                                                      func=AF.Copy), reads=pk(pi), writes=["SN"])
                    dma('sp', s_s, dst[l].rearrange("b s r -> b (s r)"), SN[:, :], reads=["SN"])
                fence()

        def attn_block(l, bi, E):
            if bi == 3 and STAGE >= 4 and 'atts' not in SKIP:
                attn_sample(l, E)
                fence()

        def attn_sample(l, E):
            R2 = E['R2']; KT = E['KT']; VS = E['VS']; AT = E['AT']
            with ExitStack() as ph:
                def t(name, shape, dt=BF16):
                    return ph.enter_context(nc.sbuf_tensor("as%d_" % l + name, list(shape), dt))
                CK = t("CK", [128, 16, 128], F32)
                CKT = t("CKT", [128, 16, 128])
                CV = t("CV", [128, 16, 128])
                TMPc = t("TMPc", [128, 512], F32)
                Pc = t("Pc", [128, 512])
                TMPn = t("TMPn", [64, 512], F32)
                Pn = t("Pn", [64, 512])
                DR = t("DR", [128, 256], F32)
                s_c = C.stream("asl%d" % l)
                s_v = C.stream("asv%d" % l)
                with nc.allow_non_contiguous_dma(reason="cache load"):
                    dma('sp', s_c, CK[:, :, :], cache_k[l].rearrange("s j d -> j s d"), writes=["CK"])
                    dma('pool', s_v, CV[:, :, :], cache_v[l].rearrange("s j d -> j s d"), writes=["CV"])
                for sl in range(16):
                    pi = sl % 4
                    op('pe', lambda e, sl=sl, pi=pi: e.transpose(out=PSUM[:, pi, 0:128], in_=CK[:, sl, :], identity=ident[:, :]),
                       reads=["CK"], writes=pk(pi))
                    if sl % 2 == 0:
                        op('act', lambda e, sl=sl, pi=pi: e.activation(out=CKT[:, sl, :], in_=PSUM[:, pi, 0:128], func=AF.Copy),
                           reads=pk(pi), writes=["CKT"])
                    else:
                        op('dve', lambda e, sl=sl, pi=pi: e.tensor_copy(out=CKT[:, sl, :], in_=PSUM[:, pi, 0:128]),
                           reads=pk(pi), writes=["CKT"])
                po, pd = 6, 7
                for kv in range(2):
                    rows = slice(kv * 64, (kv + 1) * 64)
                    for sl in range(16):
                        op('pe', lambda e, sl=sl, kv=kv, rows=rows: e.matmul(
                            PSUM[:, 4 + kv, sl:256:16], lhsT=CKT[rows, sl, :],
                            rhs=R2[rows, :, 512 + sl:576:16], start=True, stop=True),
                            reads=["CKT"] + [("R2", i) for i in range(4)], writes=pk(4 + kv), signal=(sl == 15))
                for kv in range(2):
                    op('dve', lambda e, kv=kv: e.tensor_tensor(
                        out=TMPc[:, kv * 256:(kv + 1) * 256].rearrange("p (a s) -> p a s", s=16),
                        in0=PSUM[:, 4 + kv, 0:256].rearrange("p (a s) -> p a s", s=16),
                        in1=bc_last(biasc[:, kv * 16:(kv + 1) * 16], 16), op=ALU.add), reads=pk(4 + kv), writes=["TMPc"])
                op('act', lambda e: e.activation(out=Pc[:, :], in_=TMPc[:, :], func=AF.Exp), reads=["TMPc"], writes=["Pc"])
                for kv in range(2):
                    rows = slice(kv * 64, (kv + 1) * 64)
                    op('pe', lambda e, kv=kv, rows=rows: e.matmul(
                        PSUM[0:64, kv, 0:256], lhsT=KT[rows, 128 + 512:128 + 576],
                        rhs=R2[rows, :, 512:576], start=True, stop=True),
                        reads=["KT"] + [("R2", i) for i in range(4)], writes=pk(kv), signal=True)
                    tt('dve', TMPn[:, kv * 256:(kv + 1) * 256], PSUM[0:64, kv, 0:256], biasnf[:, kv * 256:(kv + 1) * 256], ALU.add,
                       pk(kv), ["TMPn"])
                op('act', lambda e: e.activation(out=Pn[:, :], in_=TMPn[:, :], func=AF.Exp), reads=["TMPn"], writes=["Pn"])
                for (bank, use_ones) in ((po, False), (pd, True)):
                    for kv in range(2):
                        rows = slice(kv * 64, (kv + 1) * 64)
                        lh = ones[0:64, 0:64] if use_ones else VS[0:64, rows]
                        op('pe', lambda e, bank=bank, kv=kv, rows=rows, lh=lh: e.matmul(
                            PSUM[rows, bank, 0:256], lhsT=lh, rhs=Pn[0:64, kv * 256:(kv + 1) * 256], start=True, stop=False),
                            reads=["Pn", "VS", "ones"], writes=pk(bank), signal=False)
                        for sl in range(16):
                            lh2 = ones[:, 0:64] if use_ones else CV[:, sl, rows]
                            op('pe', lambda e, bank=bank, kv=kv, rows=rows, sl=sl, lh2=lh2: e.matmul(
                                PSUM[rows, bank, sl:256:16], lhsT=lh2, rhs=Pc[:, kv * 256 + sl:kv * 256 + 256:16],
                                start=False, stop=(sl == 15)), reads=["Pc", "CV", "ones"], writes=pk(bank), signal=(sl == 15))
                op('dve', lambda e: e.tensor_tensor(out=DR[:, :].rearrange("p (h c) -> p h c", h=4),
                                                    in0=PSUM[:, pd, 0:256].rearrange("p (h c) -> p h c", h=4),
                                                    in1=bc_last(esink[:, l, :], 64), op=ALU.add), reads=pk(pd) + ["esink"], writes=["DRs"])
                op('dve', lambda e: e.reciprocal(out=DR[:, :], in_=DR[:, :]), reads=["DRs"], writes=["DRs"])
                op('dve', lambda e: e.tensor_tensor(out=AT[:, :, 512:576], in0=PSUM[:, po, 0:256].rearrange("p (h c) -> p h c", h=4),
                                                    in1=DR[:, :].rearrange("p (h c) -> p h c", h=4), op=ALU.mult),
                   reads=pk(po) + ["DRs"], writes=[("AT", i) for i in range(4)])
                fence()

        def mix_block(l, bi, E):
            XN = E['XN']; MIX = E['MIX']; AT = E['AT']; ST = E['ST']; subs = E['subs']; c0 = E['c0']
            with ExitStack() as ph:
                def t(name, shape, dt=BF16):
                    return ph.enter_context(nc.sbuf_tensor("mx%d_%d_" % (l, bi) + name, list(shape), dt))
                SGA = t("SGA", [128, 576], F32)
                TM = t("TM", [128, 576])
                WL = [t("WL%d" % i, [128, 4096]) for i in range(2)]
                wl_s = [C.stream("wl%d_%d_%d" % (l, bi, i)) for i in range(2)]
                R5 = Ring(WR + WL, wr_stream + wl_s, [("wr", i) for i in range(3)] + [("wl", i) for i in range(2)])
                XNr = [("XN", k_) for k_ in range(8)]
                mspecs = []
                for h in range(2):
                    for (Wsrc, gcol) in ((w_ao, 1280), (w_so, 2304)):
                        mspecs.append((Wsrc[l][:, h * 512:(h + 1) * 512].rearrange("(k p) n -> p k n", p=128), [128, 4, 512]))
                        mspecs.append((w_in[l][:, gcol + h * 512:gcol + (h + 1) * 512].rearrange("(k p) n -> p k n", p=128), [128, 8, 512]))
                for h in range(2):
                    mspecs.append((w_out[l][:, h * 512:(h + 1) * 512].rearrange("(k p) n -> p k n", p=128), [128, 8, 512]))
                wq_m = WQ(mspecs, ring=R5, ahead=2)
                mi = 0
                for h in range(2):
                    for (Wsrc, Act, akey, gcol, first) in ((w_ao, AT, "AT", 1280, True), (w_so, ST, "ST", 2304, False)):
                        Wo_, wok = wq_m.get(mi)
                        Wg_, wgk = wq_m.get(mi + 1)
                        mi += 2
                        for j in range(4):
                            m = h * 4 + j
                            bcol = (0 if first else 8) + m
                            for (o, n) in subs:
                                pg = psb()
                                mm_group(pg, n, [(Wg_[:, k_, j * 128:(j + 1) * 128], XN[:, k_, o:o + n]) for k_ in range(8)],
                                         XNr + [wgk])
                                op('act', lambda e, pg=pg, n=n, bcol=bcol: e.activation(
                                    out=SGA[:, 0:n], in_=PSUM[:, pg, 0:n], func=AF.Sigmoid, bias=bg[:, l, bcol:bcol + 1], scale=1.0),
                                    reads=pk(pg), writes=["SGA"])
                                pa = psb()
                                mm_group(pa, n, [(Wo_[:, k_, j * 128:(j + 1) * 128], Act[:, k_, o:o + n]) for k_ in range(4)],
                                         [(akey, k_) for k_ in range(4)] + [wok])
                                if first:
                                    tt('dve', MIX[:, m, o:o + n], PSUM[:, pa, 0:n], SGA[:, 0:n], ALU.mult, pk(pa) + ["SGA"], [("MIX", m)])
                                else:
                                    tt('dve', TM[:, 0:n], PSUM[:, pa, 0:n], SGA[:, 0:n], ALU.mult, pk(pa) + ["SGA"], ["TM"])
                                    tt('pool', MIX[:, m, o:o + n], MIX[:, m, o:o + n], TM[:, 0:n], ALU.add, ["TM", ("MIX", m)], [("MIX", m)])
                for h in range(2):
                    Wo_, wok = wq_m.get(8 + h)
                    for j in range(4):
                        m = h * 4 + j
                        for (o, n) in subs:
                            pi = psb()
                            mm_group(pi, n, [(Wo_[:, k_, j * 128:(j + 1) * 128], MIX[:, k_, o:o + n]) for k_ in range(8)],
                                     [("MIX", k_) for k_ in range(8)] + [wok])
                            xc = c0 + o
                            op('dve', lambda e, pi=pi, n=n, m=m, xc=xc: e.tensor_tensor(
                                out=X[:, m, xc:xc + n], in0=PSUM[:, pi, 0:n], in1=X[:, m, xc:xc + n], op=ALU.add),
                                reads=pk(pi) + ["X"], writes=["X"])
                if STAGE >= 8 and bi == 3:
                    SQn = t("SQn", [128, 4, 512]); RSn = t("RSn", [128, 512], F32); XGn = t("XGn", [128, 2, 512], F32)
                    for (o, n) in subs:
                        norm2_into(l, MIX, SQn, RSn, XGn, c0 + o, o, n)
                fence()

        def ffn_tail(l, XN2):
            with ExitStack() as ph:
                def t(name, shape, dt=BF16):
                    return ph.enter_context(nc.sbuf_tensor("b%d_" % l + name, list(shape), dt))
                Hh = [t("H%d" % i, [128, 4, 512]) for i in range(2)]
                Rr = [t("R%d" % i, [128, 512]) for i in range(2)]
                subs = [(0, 512), (512, 64)]
                wq_f = WQ(ffn_specs(l))
                it = 0
                for c in range(8):
                    Wu, wuk = wq_f.get(2 * c)
                    Wd, wdk = wq_f.get(2 * c + 1)
                    for (o, n) in subs:
                        hb = it % 2
                        it += 1
                        for j in range(4):
                            pi = psb()
                            mm_group(pi, n, [(Wu[:, k_, j * 128:(j + 1) * 128], XN2[:, k_, o:o + n]) for k_ in range(8)],
                                     [("MIX", k_) for k_ in range(8)] + [wuk])
                            rb = j % 2
                            op('act', lambda e, pi=pi, n=n, rb=rb: e.activation(out=Rr[rb][:, 0:n], in_=PSUM[:, pi, 0:n],
                                                                               func=AF.Relu), reads=pk(pi), writes=[("R", rb)])
                            op('act', lambda e, n=n, rb=rb, hb=hb, j=j: e.activation(out=Hh[hb][:, j, 0:n], in_=Rr[rb][:, 0:n],
                                                                                    func=AF.Square), reads=[("R", rb)], writes=[("H", hb, j)])
                        for m in range(8):
                            pi = psb()
                            mm_group(pi, n, [(Wd[:, j, m * 128:(m + 1) * 128], Hh[hb][:, j, 0:n]) for j in range(4)],
                                     [("H", hb, j) for j in range(4)] + [wdk])
                            xc = 1536 + o
                            op('dve', lambda e, pi=pi, n=n, m=m, xc=xc: e.tensor_tensor(
                                out=X[:, m, xc:xc + n], in0=PSUM[:, pi, 0:n], in1=X[:, m, xc:xc + n], op=ALU.add),
                                reads=pk(pi) + ["X"], writes=["X"])
                fence()

        for l in range(NLAYERS):
            if STAGE < 1:
                break
            if STAGE >= 2:
                ssm_tables(l, mid=(phase0 if l == 0 else None))
            elif l == 0:
                phase0()
            layer_phase_a(l)
        with ExitStack() as ph:
            YT = [ph.enter_context(nc.sbuf_tensor("yt%d" % i, [128, D], F32)) for i in range(2)]
            yts = [C.stream("yts%d" % i) for i in range(2)]
            for tt in range(17):
                b = tt % 2
                rows = 128 if tt < 16 else NS
                for half in range(2):
                    pi = psb()
                    for j in range(4):
                        k = half * 4 + j
                        op('pe', lambda e, k=k, j=j, pi=pi, rows=rows, tt=tt: e.transpose(
                            out=PSUM[0:rows, pi, j * 128:(j + 1) * 128], in_=X[:, k, tt * 128:tt * 128 + rows],
                            identity=ident[:, :]),
                            reads=[("X", tt)], writes=pk(pi), signal=(j == 3))
                    dst = YT[b][0:rows, half * 512:(half + 1) * 512]
                    src_ps = PSUM[0:rows, pi, :]
                    if half == 0:
                        op('act', lambda e, s_=src_ps, d_=dst: e.activation(out=d_, in_=s_, func=AF.Copy),
                           reads=pk(pi), writes=[("yt", b)])
                    else:
                        op('dve', lambda e, s_=src_ps, d_=dst: e.tensor_copy(out=d_, in_=s_),
                           reads=pk(pi), writes=[("yt", b)])
                dstd = yp[tt * 128:(tt + 1) * 128, :] if tt < 16 else ys[:, :]
                dma('sp', yts[b], dstd, YT[b][0:rows, :], reads=[("yt", b)])
            fence()
    return nc


_NC_CACHE = {}


def _consts():
    ident = np.eye(128, dtype=np.float32)
    blk = np.zeros((128, 128), np.float32)
    blk[:64, :64] = 1.0
    blk[64:, 64:] = 1.0
    slopes = np.exp2(-8.0 * np.arange(1, 9, dtype=np.float64) / 8.0)
    j = np.arange(128)[:, None]
    i = np.arange(128)[None, :]
    biasp = np.zeros((128, 8, 256), np.float32)
    for t in range(4):
        for hp in range(2):
            h = t + 4 * hp
            d_prev = 128 + i - j
            d_cur = i - j
            bp = np.where(d_prev <= 128, -slopes[h] * d_prev, -30000.0)
            bc = np.where(d_cur >= 0, -slopes[h] * d_cur, -30000.0)
            biasp[:, t * 2 + hp, 0:128] = bp
            biasp[:, t * 2 + hp, 128:256] = bc
    biasc = np.zeros((128, 32), np.float32)
    biasn = np.zeros((4, 32), np.float32)
    for kv in range(2):
        for hq in range(4):
            h = kv * 4 + hq
            for qi in range(4):
                col = kv * 16 + hq * 4 + qi
                jj = np.arange(128)
                dist = 128 + qi - jj
                biasc[:, col] = np.where(jj >= qi, -slopes[h] * dist, -30000.0)
                jn = np.arange(4)
                dn = qi - jn
                biasn[:, col] = np.where(dn >= 0, -slopes[h] * dn, -30000.0)
    biasnf = np.full((64, 512), -30000.0, np.float32)
    for ip in range(4):
        for slp in range(16):
            r = ip * 16 + slp
            for kv in range(2):
                for hq in range(4):
                    h = kv * 4 + hq
                    for qi in range(ip, 4):
                        biasnf[r, kv * 256 + hq * 64 + qi * 16 + slp] = -slopes[h] * (qi - ip)
    cmask = np.zeros((128, 128), np.float32)
    for r in range(128):
        glp = (r % 32) // 16
        cmask[r, glp * 64:(glp + 1) * 64] = 1.0
    rmask = np.zeros((128, 4), np.float32)
    for r in range(128):
        rmask[r, r // 32] = 1.0
    return dict(c_ident=ident, c_blk64=blk, c_biasp=biasp, c_biasc=biasc, c_biasn=biasn, c_biasnf=biasnf,
                c_cmask=cmask, c_rmask=rmask)


def kernel(**inp):
    f = lambda a: np.ascontiguousarray(np.asarray(a), dtype=np.float32)
    qperm = np.concatenate([np.arange(h * 64, (h + 1) * 64) for h in HEAD_PERM])
    w_in = f(inp['w_in']).copy()
    w_in[:, :, 0:512] = w_in[:, :, qperm]
    w_ao = f(inp['w_attn_o'])[:, qperm, :]
    qg = f(inp['q_norm_g'])
    kg = f(inp['k_norm_g'])
    qk_g = np.stack([np.concatenate([qg, qg], axis=1), np.concatenate([kg, kg], axis=1)], axis=1)
    sk = f(inp['attn_sinks'])
    sinks = np.zeros((L, 4, 128), np.float32)
    for t in range(4):
        sinks[:, t, 0:64] = sk[:, t][:, None]
        sinks[:, t, 64:128] = sk[:, t + 4][:, None]
    shared = dict(
        w_in=np.ascontiguousarray(w_in), w_glu=f(inp['w_glu']), w_ao=np.ascontiguousarray(w_ao), w_so=f(inp['w_ssm_o']),
        w_out=f(inp['w_out']), w_up=f(inp['w_up']), w_down=f(inp['w_down']),
        norm1_g=f(inp['norm1_g']), norm2_g=f(inp['norm2_g']), b_gate=f(inp['b_gate']),
        qk_g=np.ascontiguousarray(qk_g), sinks=sinks,
        lam_re=f(inp['lam_re']), lam_im=f(inp['lam_im']), log_step=f(inp['log_step']),
        b_re=f(inp['b_re']).reshape(L, 2048, 16), b_im=f(inp['b_im']).reshape(L, 2048, 16),
        c_re=f(inp['c_re']).reshape(L, 512, 64), c_im=f(inp['c_im']).reshape(L, 512, 64),
        d_skip=f(inp['d_skip']), b_glu=f(inp['b_glu']),
    )
    shared.update(_consts())
    x_prompt = f(inp['x_prompt'])
    x_sample = f(inp['x_sample'])
    ck = f(inp['cache_k']).reshape(L, 128, 128, 128)
    cv = f(inp['cache_v']).reshape(L, 128, 128, 128)
    sre = f(inp['state_ssm_re']).reshape(L, 128, 2048)
    sim = f(inp['state_ssm_im']).reshape(L, 128, 2048)
    in_maps = []
    for c in range(NCORES):
        m = dict(shared)
        m['xp'] = x_prompt[c]
        m['xs'] = np.ascontiguousarray(x_sample[c * 16:(c + 1) * 16].transpose(1, 0, 2).reshape(NS, D))
        m['cache_k'] = np.ascontiguousarray(ck[:, c * 16:(c + 1) * 16])
        m['cache_v'] = np.ascontiguousarray(cv[:, c * 16:(c + 1) * 16])
        m['st_re'] = np.ascontiguousarray(sre[:, c * 16:(c + 1) * 16])
        m['st_im'] = np.ascontiguousarray(sim[:, c * 16:(c + 1) * 16])
        in_maps.append(m)
    if 'nc' not in _NC_CACHE:
        _NC_CACHE['nc'] = build_program()
    ncr = DEBUG_CORES or NCORES
    res = run_bass_kernel_spmd(_NC_CACHE['nc'], in_maps[:ncr], core_ids=list(range(ncr)))
    R = list(res.results)
    _NC_CACHE['last'] = R
    while len(R) < NCORES:
        R.append(R[0])
    y_prompt = np.stack([R[c]['yp'] for c in range(NCORES)]).astype(np.float32)
    y_sample = np.concatenate([R[c]['ys'].reshape(4, 16, D).transpose(1, 0, 2) for c in range(NCORES)]).astype(np.float32)
    k_prompt = np.stack([R[c]['kp'] for c in range(NCORES)], axis=1).reshape(L, 8, 128, 2, 64)
    v_prompt = np.stack([R[c]['vp'] for c in range(NCORES)], axis=1).reshape(L, 8, 128, 2, 64)
    hr_p = np.stack([R[c]['hrp'] for c in range(NCORES)], axis=1).reshape(L, 8, 32, 64)
    hi_p = np.stack([R[c]['hip'] for c in range(NCORES)], axis=1).reshape(L, 8, 32, 64)
    k_s = np.concatenate([R[c]['ksam'] for c in range(NCORES)], axis=1).reshape(L, 128, 128, 2, 64)
    v_s = np.concatenate([R[c]['vsam'] for c in range(NCORES)], axis=1).reshape(L, 128, 128, 2, 64)
    hr_s = np.concatenate([R[c]['hrs'] for c in range(NCORES)], axis=1).reshape(L, 128, 32, 64)
    hi_s = np.concatenate([R[c]['his'] for c in range(NCORES)], axis=1).reshape(L, 128, 32, 64)
    return (y_prompt, y_sample, k_prompt.astype(np.float32), v_prompt.astype(np.float32),
            hr_p.astype(np.float32), hi_p.astype(np.float32), k_s.astype(np.float32), v_s.astype(np.float32),
            hr_s.astype(np.float32), hi_s.astype(np.float32))
```

```python
import math
import numpy as np
import ml_dtypes
from contextlib import ExitStack
import concourse.bass as bass
import concourse.mybir as mybir
from concourse.bass_utils import run_bass_kernel_spmd

F32 = mybir.dt.float32
BF16 = mybir.dt.bfloat16
I32 = mybir.dt.int32
AF = mybir.ActivationFunctionType
ALU = mybir.AluOpType

NCORES = 8
D = 1024
T = 2048
NS = 64
NT = T + NS
L = 2
EPS = 1e-6
HEAD_PERM = [0, 4, 1, 5, 2, 6, 3, 7]
STAGE = 99
NLAYERS = 2
CHUNK = 256
NCHUNK = 512 // CHUNK
DEBUG_DUMP = False
DEBUG_CORES = 0
SKIP = set()


class Stream:
    def __init__(self, nc, stack, name):
        self.sem = stack.enter_context(nc.semaphore(name))
        self.name = name
        self.cnt = 0


class Ctx:
    def __init__(self, nc, stack):
        self.nc = nc
        self.engs = {'pe': nc.tensor, 'act': nc.scalar, 'dve': nc.vector, 'pool': nc.gpsimd, 'sp': nc.sync}
        self.st = {n: Stream(nc, stack, 's_' + n) for n in self.engs}
        self.waited = {n: {} for n in self.engs}
        self.lw = {}
        self.rd = {}
        self.stack = stack
        self.nstream = 0

    def stream(self, name):
        s = Stream(self.nc, self.stack, name)
        self.st[name] = s
        return name

    def _deps(self, reads, writes):
        deps = {}

        def add(tok):
            if tok is None:
                return
            s, v = tok
            if deps.get(s, 0) < v:
                deps[s] = v
        for k in reads:
            add(self.lw.get(k))
        for k in writes:
            add(self.lw.get(k))
            for s, v in self.rd.get(k, {}).items():
                add((s, v))
        return deps

    def _wait(self, en, deps):
        e = self.engs[en]
        w = self.waited[en]
        for s, v in deps.items():
            if s == en and en in ('pe', 'sp'):
                continue
            if w.get(s, 0) >= v:
                continue
            e.wait_ge(self.st[s].sem, v)
            w[s] = v

    def _record(self, tok, reads, writes):
        s, v = tok
        for k in reads:
            d = self.rd.setdefault(k, {})
            if d.get(s, 0) < v:
                d[s] = v
        for k in writes:
            self.lw[k] = tok
            self.rd[k] = {}

    def op(self, en, fn, reads=(), writes=(), signal=True):
        psr = [k for k in reads if isinstance(k, tuple) and k[0] == 'ps']
        if psr:
            writes = list(writes) + psr
        self._wait(en, self._deps(reads, writes))
        inst = fn(self.engs[en])
        st = self.st[en]
        if signal:
            st.cnt += 1
            inst.then_inc(st.sem, 1)
            tok = (en, st.cnt)
        else:
            tok = (en, st.cnt + 1)
        self._record(tok, reads, writes)
        return tok

    def dma(self, q, stream, out, in_, reads=(), writes=(), **kw):
        self._wait(q, self._deps(reads, writes))
        st = self.st[stream]
        st.cnt += 16
        self.engs[q].dma_start(out=out, in_=in_, **kw).then_inc(st.sem, 16)
        tok = (stream, st.cnt)
        self._record(tok, reads, writes)
        return tok

    def fence(self):
        for en, e in self.engs.items():
            w = self.waited[en]
            for s, st in self.st.items():
                if s == en or st.cnt == 0:
                    continue
                if w.get(s, 0) >= st.cnt:
                    continue
                e.wait_ge(st.sem, st.cnt)
                w[s] = st.cnt
        self.lw = {}
        self.rd = {}


def build_program():
    nc = bass.Bass("TRN2", target_bir_lowering=False)

    def din(name, shape, dt=F32):
        return nc.dram_tensor(name, list(shape), dt, kind="ExternalInput").ap()

    def dout(name, shape, dt=F32):
        return nc.dram_tensor(name, list(shape), dt, kind="ExternalOutput").ap()

    xp = din("xp", [T, D])
    xs = din("xs", [NS, D])
    cache_k = din("cache_k", [L, 16, 128, 128])
    cache_v = din("cache_v", [L, 16, 128, 128])
    st_re = din("st_re", [L, 16, 2048])
    st_im = din("st_im", [L, 16, 2048])
    w_in = din("w_in", [L, D, 3328])
    w_glu = din("w_glu", [L, 512, 512])
    w_ao = din("w_ao", [L, 512, D])
    w_so = din("w_so", [L, 512, D])
    w_out = din("w_out", [L, D, D])
    w_up = din("w_up", [L, D, 4096])
    w_down = din("w_down", [L, 4096, D])
    norm1_g = din("norm1_g", [L, D])
    norm2_g = din("norm2_g", [L, D])
    b_gate = din("b_gate", [L, 2048])
    qk_g = din("qk_g", [L, 2, 128])
    sinks = din("sinks", [L, 4, 128])
    lam_re = din("lam_re", [L, 32, 64])
    lam_im = din("lam_im", [L, 32, 64])
    log_step = din("log_step", [L, 32])
    b_re = din("b_re", [L, 2048, 16])
    b_im = din("b_im", [L, 2048, 16])
    c_re = din("c_re", [L, 512, 64])
    c_im = din("c_im", [L, 512, 64])
    d_skip = din("d_skip", [L, 512])
    b_glu = din("b_glu", [L, 512])
    c_ident = din("c_ident", [128, 128])
    c_blk64 = din("c_blk64", [128, 128])
    c_biasp = din("c_biasp", [128, 8, 256])
    c_biasc = din("c_biasc", [128, 32])
    c_biasn = din("c_biasn", [4, 32])
    c_biasnf = din("c_biasnf", [64, 512])
    c_cmask = din("c_cmask", [128, 128])
    c_rmask = din("c_rmask", [128, 4])

    yp = dout("yp", [T, D])
    ys = dout("ys", [NS, D])
    kp = dout("kp", [L, 128, 128])
    vp = dout("vp", [L, 128, 128])
    hrp = dout("hrp", [L, 16, 128])
    hip = dout("hip", [L, 16, 128])
    ksam = dout("ksam", [L, 16, 128, 128])
    vsam = dout("vsam", [L, 16, 128, 128])
    hrs = dout("hrs", [L, 16, 16, 128])
    his = dout("his", [L, 16, 16, 128])
    if DEBUG_DUMP:
        dbg_at = dout("dbg_at", [4, 128, 4, 576], BF16)
        dbg_st = dout("dbg_st", [4, 128, 4, 576], BF16)
        dbg_yg = dout("dbg_yg", [4, 128, 4, 576], BF16)
        dbg_mix = dout("dbg_mix", [4, 128, 8, 576], BF16)

    stack = ExitStack()
    with stack:
        C = Ctx(nc, stack)
        op, dma, fence = C.op, C.dma, C.fence

        def sb(name, shape, dt=F32):
            return stack.enter_context(nc.sbuf_tensor(name, list(shape), dt))

        X = sb("X", [128, 8, NT])
        PSUM = stack.enter_context(nc.psum_tensor("PS", [128, 8, 512], F32))
        ident = sb("ident", [128, 128])
        ones = sb("ones", [128, 128], BF16)
        blk64 = sb("blk64", [128, 128], BF16)
        EB = sb("EB", [128, 8, 256], BF16)
        biasc = sb("biasc", [128, 32])
        biasnf = sb("biasnf", [64, 512])
        cmask = sb("cmask", [128, 128])
        rmask = sb("rmask", [128, 4])
        g1 = sb("g1", [128, L, 8])
        g2 = sb("g2", [128, L, 8])
        bg = sb("bg", [128, L, 16])
        qkg = sb("qkg", [128, L, 2])
        esink = sb("esink", [128, L, 4])
        dsk = sb("dsk", [128, L, 4])
        bgl = sb("bgl", [128, L, 4])
        WR = [sb("wr%d" % i, [128, 4096], BF16) for i in range(3)]
        wr_stream = [C.stream("wrs%d" % i) for i in range(3)]
        wr_i = [0]
        PHc = sb("PHc", [128, 16, CHUNK + 1], BF16)
        PHs = sb("PHs", [128, 16, CHUNK + 1], BF16)
        MAG = sb("MAG", [128, 16])
        AR = sb("AR", [128, 16])
        AI = sb("AI", [128, 16])
        R128c = sb("R128c", [128, 16])
        R128s = sb("R128s", [128, 16])
        nR128s = sb("nR128s", [128, 16])
        P127c = sb("P127c", [128, 16])
        P127s = sb("P127s", [128, 16])
        BTr = sb("BTr", [128, 16, 128], BF16)
        BTi = sb("BTi", [128, 16, 128], BF16)
        CTr = sb("CTr", [128, 16, 128], BF16)
        nCTi = sb("nCTi", [128, 16, 128], BF16)
        nCTr = sb("nCTr", [128, 16, 128], BF16)
        GE2 = sb("GE2", [128, 16, 2])
        SS = sb("SS", [128, 16, 2])

        s_par = C.stream("par")
        s_out = C.stream("outs")
        ps_i = [0]

        def psb(n=1):
            i = ps_i[0]
            if i + n > 8:
                i = 0
            ps_i[0] = (i + n) % 8
            return i

        def pk(i, n=1):
            return [("ps", j) for j in range(i, i + n)]

        class Ring:
            def __init__(self, bufs, streams, keys):
                self.bufs, self.streams, self.keys = bufs, streams, keys
                self.i = 0

        G3 = Ring(WR, wr_stream, [("wr", i) for i in range(3)])

        def wload(src_ap, view_shape, key, ring=None):
            ring = ring or G3
            i = ring.i % len(ring.bufs)
            ring.i += 1
            buf = ring.bufs[i]
            n = 1
            for s_ in view_shape[1:]:
                n *= s_
            flat = buf[:, 0:n]
            if len(view_shape) == 3:
                view = flat.rearrange("p (k n) -> p k n", k=view_shape[1])
            else:
                view = flat
            dma('pool', ring.streams[i], view, src_ap, writes=[ring.keys[i]])
            return view, ring.keys[i]

        class WQ:
            def __init__(self, specs, ring=None, ahead=1):
                self.specs = specs
                self.loaded = []
                self.ring = ring
                self.ahead = ahead

            def get(self, i):
                upto = min(i + self.ahead, len(self.specs) - 1)
                while len(self.loaded) <= upto:
                    src, shape = self.specs[len(self.loaded)]
                    self.loaded.append(wload(src, shape, None, self.ring))
                return self.loaded[i]

        def bc_mid(ap2, n):
            return ap2.unsqueeze(1).to_broadcast([ap2.shape[0], n, ap2.shape[1]])

        def bc_last(ap2, n):
            return ap2.unsqueeze(2).to_broadcast([ap2.shape[0], ap2.shape[1], n])

        with nc.allow_non_contiguous_dma(reason="small param loads"):
            for (dst, src) in [
                (ident[:], c_ident[:, :]),
                (biasc[:], c_biasc[:, :]), (biasnf[:], c_biasnf[:, :]), (cmask[:], c_cmask[:, :]),
                (rmask[:], c_rmask[:, :]),
                (g1[:], norm1_g.rearrange("l (k p) -> p l k", p=128)),
                (g2[:], norm2_g.rearrange("l (k p) -> p l k", p=128)),
                (bg[:], b_gate.rearrange("l (k p) -> p l k", p=128)),
                (qkg[:], qk_g.rearrange("l t p -> p l t")),
                (esink[:], sinks.rearrange("l t p -> p l t")),
                (dsk[:], d_skip.rearrange("l (k p) -> p l k", p=128)),
                (bgl[:], b_glu.rearrange("l (k p) -> p l k", p=128)),
            ]:
                dma('sp', s_par, dst, src, writes=["par"])
        s_b64 = C.stream("b64")
        dma('pool', s_b64, blk64[:], c_blk64[:, :], writes=["blk64"])
        op('dve', lambda e: e.memset(ones[:], 1.0), writes=["ones"])
        with ExitStack() as ph0:
            biasp = ph0.enter_context(nc.sbuf_tensor("biasp", [128, 8, 256], F32))
            s_bp = C.stream("bp")
            dma('sp', s_bp, biasp[:], c_biasp[:, :, :], writes=["biasp"])
            op('act', lambda e: e.activation(out=EB[:], in_=biasp[:], func=AF.Exp), reads=["biasp"], writes=["EB"])
            fence()
        op('act', lambda e: e.activation(out=esink[:], in_=esink[:], func=AF.Exp), reads=["par"], writes=["esink"])
        op('dve', lambda e: e.tensor_scalar(out=qkg[:, :, 0:1], in0=qkg[:, :, 0:1], scalar1=0.125, scalar2=None,
                                            op0=ALU.mult), reads=["par"], writes=["qkg"])
        fence()

        def phase0():
          with ExitStack() as ph:
              XT = [ph.enter_context(nc.sbuf_tensor("xt%d" % i, [128, D], F32)) for i in range(2)]
              xts = [C.stream("xts%d" % i) for i in range(2)]
              for tt in range(17):
                  b = tt % 2
                  rows = 128 if tt < 16 else NS
                  src = xp[tt * 128:(tt + 1) * 128, :] if tt < 16 else xs[:, :]
                  dma('sp', xts[b], XT[b][0:rows, :], src, writes=[("xt", b)])
                  for half in range(2):
                      pi = psb()
                      for j in range(4):
                          k = half * 4 + j
                          op('pe', lambda e, k=k, j=j, pi=pi, b=b, rows=rows: e.transpose(
                              out=PSUM[:, pi, j * 128:j * 128 + rows], in_=XT[b][0:rows, k * 128:(k + 1) * 128],
                              identity=ident[0:rows, 0:rows]),
                              reads=[("xt", b)], writes=pk(pi), signal=(j == 3))
                      eng = 'act'
                      src_ps = PSUM[:, pi, :].rearrange("p (j t) -> p j t", j=4)[:, :, 0:rows]
                      dst = X[:, half * 4:half * 4 + 4, tt * 128:tt * 128 + rows]
                      if eng == 'act':
                          op('act', lambda e, s_=src_ps, d_=dst: e.activation(out=d_, in_=s_, func=AF.Copy),
                             reads=pk(pi), writes=[("X", tt)])
                      else:
                          op('dve', lambda e, s_=src_ps, d_=dst: e.tensor_copy(out=d_, in_=s_),
                             reads=pk(pi), writes=[("X", tt)])
              fence()

        TWO_PI = 2.0 * math.pi

        def tt(en, out, a, b, o, reads, writes):
            return op(en, lambda e: e.tensor_tensor(out=out, in0=a, in1=b, op=o), reads=reads, writes=writes)

        def ssm_tables(l, mid=None):
            with ExitStack() as ph:
                def t(name, shape, dt=F32):
                    return ph.enter_context(nc.sbuf_tensor("st%d_" % l + name, list(shape), dt))
                LR = t("LR", [128, 16]); LI = t("LI", [128, 16]); LS = t("LS", [128, 16])
                ANG = t("ANG", [128, 32]); KI = t("KI", [128, 32], I32); KF = t("KF", [128, 32])
                M1 = t("M1", [128, 32]); SC = t("SC", [128, 32])
                T1 = t("T1", [128, 16]); T2 = t("T2", [128, 16]); T3 = t("T3", [128, 16])
                CR = t("CR", [128, 16]); CI = t("CI", [128, 16]); RDEN = t("RDEN", [128, 16])
                Pc = t("Pc", [128, 16, CHUNK + 1]); Ps = t("Ps", [128, 16, CHUNK + 1])
                Q1 = t("Q1", [128, 16, CHUNK // 2]); Q2 = t("Q2", [128, 16, CHUNK // 2])
                BNr = t("BNr", [128, 16, 16]); BNi = t("BNi", [128, 16, 16])
                BBr = t("BBr", [128, 16, 16]); BBi = t("BBi", [128, 16, 16]); BT1 = t("BT1", [128, 16, 16])
                BEr = t("BEr", [128, 16, 32]); BEi = t("BEi", [128, 16, 32])
                CNr = t("CNr", [128, 4, 64]); CNi = t("CNi", [128, 4, 64])
                CE = t("CE", [128, 128])
                sp_ = C.stream("sst%d" % l)
                with nc.allow_non_contiguous_dma(reason="ssm params"):
                    dma('sp', sp_, LR[:], lam_re[l].rearrange("(s gl) p -> (gl p) s", gl=2), writes=["sp"])
                    dma('sp', sp_, LI[:], lam_im[l].rearrange("(s gl) p -> (gl p) s", gl=2), writes=["sp"])
                    lsv = log_step[l:l + 1, :].rearrange("o (s gl) -> o gl s", gl=2)
                    for gl in range(2):
                        dma('sp', sp_, LS[gl * 64:(gl + 1) * 64, :], lsv[:, gl, :].to_broadcast([64, 16]), writes=["sp"])
                    dma('sp', sp_, BNr[:], b_re[l].rearrange("(s gl p) c -> (gl p) s c", gl=2, p=64), writes=["sp"])
                    dma('sp', sp_, BNi[:], b_im[l].rearrange("(s gl p) c -> (gl p) s c", gl=2, p=64), writes=["sp"])
                    dma('sp', sp_, CNr[:], c_re[l].rearrange("(q r) p -> r q p", r=128), writes=["sp"])
                    dma('sp', sp_, CNi[:], c_im[l].rearrange("(q r) p -> r q p", r=128), writes=["sp"])
                R = ["sp"]
                op('act', lambda e: e.activation(out=LS[:], in_=LS[:], func=AF.Exp), reads=R, writes=["LS"])
                tt('dve', T1[:], LR[:], LS[:], ALU.mult, R + ["LS"], ["T1"])
                tt('dve', ANG[:, 0:16], LI[:], LS[:], ALU.mult, R + ["LS"], ["ANG"])
                op('act', lambda e: e.activation(out=MAG[:], in_=T1[:], func=AF.Exp), reads=["T1"], writes=["MAG"])
                op('dve', lambda e: e.tensor_scalar(out=ANG[:, 16:32], in0=ANG[:, 0:16], scalar1=math.pi / 2, scalar2=None,
                                                    op0=ALU.add), reads=["ANG"], writes=["ANG"])
                op('dve', lambda e: e.tensor_scalar(out=KF[:], in0=ANG[:], scalar1=1.0 / TWO_PI, scalar2=None, op0=ALU.mult),
                   reads=["ANG"], writes=["KF"])
                op('dve', lambda e: e.tensor_copy(out=KI[:], in_=KF[:]), reads=["KF"], writes=["KI"])
                op('dve', lambda e: e.tensor_copy(out=KF[:], in_=KI[:]), reads=["KI"], writes=["KF"])
                op('dve', lambda e: e.scalar_tensor_tensor(out=ANG[:], in0=KF[:], scalar=-TWO_PI, in1=ANG[:], op0=ALU.mult,
                                                           op1=ALU.add), reads=["KF", "ANG"], writes=["ANG"])
                op('dve', lambda e: e.tensor_single_scalar(out=M1[:], in_=ANG[:], scalar=math.pi, op=ALU.is_gt),
                   reads=["ANG"], writes=["M1"])
                op('dve', lambda e: e.scalar_tensor_tensor(out=ANG[:], in0=M1[:], scalar=-TWO_PI, in1=ANG[:], op0=ALU.mult,
                                                           op1=ALU.add), reads=["M1", "ANG"], writes=["ANG"])
                op('dve', lambda e: e.tensor_single_scalar(out=M1[:], in_=ANG[:], scalar=-math.pi, op=ALU.is_lt),
                   reads=["ANG"], writes=["M1"])
                op('dve', lambda e: e.scalar_tensor_tensor(out=ANG[:], in0=M1[:], scalar=TWO_PI, in1=ANG[:], op0=ALU.mult,
                                                           op1=ALU.add), reads=["M1", "ANG"], writes=["ANG"])
                op('act', lambda e: e.activation(out=SC[:], in_=ANG[:], func=AF.Sin), reads=["ANG"], writes=["SC"])
                SN = SC[:, 0:16]
                CS = SC[:, 16:32]
                tt('dve', AR[:], MAG[:], CS, ALU.mult, ["MAG", "SC"], ["AR"])
                tt('dve', AI[:], MAG[:], SN, ALU.mult, ["MAG", "SC"], ["AI"])
                tt('dve', T1[:], LR[:], LR[:], ALU.mult, R + ["MAG"], ["T1"])
                tt('dve', T2[:], LI[:], LI[:], ALU.mult, R, ["T2"])
                tt('dve', T1[:], T1[:], T2[:], ALU.add, ["T1", "T2"], ["T1"])
                op('dve', lambda e: e.reciprocal(out=RDEN[:], in_=T1[:]), reads=["T1"], writes=["RDEN"])
                op('dve', lambda e: e.tensor_scalar(out=T3[:], in0=AR[:], scalar1=-1.0, scalar2=None, op0=ALU.add),
                   reads=["AR"], writes=["T3"])
                tt('dve', T1[:], T3[:], LR[:], ALU.mult, ["T3", "RDEN"], ["T1"])
                tt('dve', T2[:], AI[:], LI[:], ALU.mult, ["AI", "T1"], ["T2"])
                tt('dve', T1[:], T1[:], T2[:], ALU.add, ["T1", "T2"], ["T1"])
                tt('dve', CR[:], T1[:], RDEN[:], ALU.mult, ["T1", "RDEN"], ["CR"])
                tt('dve', T1[:], AI[:], LR[:], ALU.mult, ["CR", "AI"], ["T1"])
                tt('dve', T2[:], T3[:], LI[:], ALU.mult, ["T3", "CR"], ["T2"])
                tt('dve', T1[:], T1[:], T2[:], ALU.subtract, ["T1", "T2"], ["T1"])
                tt('dve', CI[:], T1[:], RDEN[:], ALU.mult, ["T1", "RDEN"], ["CI"])
                op('dve', lambda e: e.memset(Pc[:, :, 0:1], 1.0), writes=["P"])
                op('dve', lambda e: e.memset(Ps[:, :, 0:1], 0.0), writes=["P"])
                op('dve', lambda e: e.tensor_copy(out=Pc[:, :, 1:2], in_=CS.unsqueeze(2)), reads=["SC", "P"], writes=["P"])
                op('dve', lambda e: e.tensor_copy(out=Ps[:, :, 1:2], in_=SN.unsqueeze(2)), reads=["SC", "P"], writes=["P"])
                m = 1
                while m < CHUNK:
                    Ac = Pc[:, :, 1:m + 1]
                    As = Ps[:, :, 1:m + 1]
                    Bc = Pc[:, :, m:m + 1].to_broadcast([128, 16, m])
                    Bs = Ps[:, :, m:m + 1].to_broadcast([128, 16, m])
                    q1 = Q1[:, :, 0:m]
                    q2 = Q2[:, :, 0:m]
                    tt('dve', q1, Ac, Bc, ALU.mult, ["P"], ["Q1"])
                    tt('dve', q2, As, Bs, ALU.mult, ["P"], ["Q2"])
                    tt('dve', Pc[:, :, m + 1:2 * m + 1], q1, q2, ALU.subtract, ["Q1", "Q2", "P"], ["P"])
                    tt('dve', q1, Ac, Bs, ALU.mult, ["P"], ["Q1"])
                    tt('dve', q2, As, Bc, ALU.mult, ["P"], ["Q2"])
                    tt('dve', Ps[:, :, m + 1:2 * m + 1], q1, q2, ALU.add, ["Q1", "Q2", "P"], ["P"])
                    m *= 2
                op('dve', lambda e: e.tensor_copy(out=PHc[:], in_=Pc[:]), reads=["P"], writes=["PH"])
                op('dve', lambda e: e.tensor_copy(out=PHs[:], in_=Ps[:]), reads=["P"], writes=["PH"])
                op('dve', lambda e: e.tensor_copy(out=R128c[:], in_=Pc[:, :, CHUNK]), reads=["P"], writes=["R128"])
                op('dve', lambda e: e.tensor_copy(out=R128s[:], in_=Ps[:, :, CHUNK]), reads=["P"], writes=["R128"])
                op('dve', lambda e: e.tensor_scalar(out=nR128s[:], in0=Ps[:, :, CHUNK], scalar1=-1.0, scalar2=None,
                                                    op0=ALU.mult), reads=["P"], writes=["R128"])
                op('dve', lambda e: e.tensor_copy(out=P127c[:], in_=Pc[:, :, CHUNK - 1]), reads=["P"], writes=["R128"])
                op('dve', lambda e: e.tensor_copy(out=P127s[:], in_=Ps[:, :, CHUNK - 1]), reads=["P"], writes=["R128"])
                CRb = bc_last(CR[:], 16)
                CIb = bc_last(CI[:], 16)
                tt('dve', BBr[:], BNr[:], CRb, ALU.mult, R + ["CR"], ["BBr"])
                tt('dve', BT1[:], BNi[:], CIb, ALU.mult, R + ["CI"], ["BT1"])
                tt('dve', BBr[:], BBr[:], BT1[:], ALU.subtract, ["BBr", "BT1"], ["BBr"])
                tt('dve', BBi[:], BNi[:], CRb, ALU.mult, R + ["CR"], ["BBi"])
                tt('dve', BT1[:], BNr[:], CIb, ALU.mult, R + ["CI", "BBr"], ["BT1"])
                tt('dve', BBi[:], BBi[:], BT1[:], ALU.add, ["BBi", "BT1"], ["BBi"])
                if mid is not None:
                    mid()
                for (BE, BB, BTd, nm) in ((BEr, BBr, BTr, "r"), (BEi, BBi, BTi, "i")):
                    op('dve', lambda e, BE=BE: e.memset(BE[:], 0.0), writes=["BE" + nm])
                    op('dve', lambda e, BE=BE, BB=BB: e.tensor_copy(out=BE[0:64, :, 0:16], in_=BB[0:64, :, :]),
                       reads=["BB" + nm, "BE" + nm], writes=["BE" + nm])
                    op('dve', lambda e, BE=BE, BB=BB: e.tensor_copy(out=BE[64:128, :, 16:32], in_=BB[64:128, :, :]),
                       reads=["BB" + nm, "BE" + nm], writes=["BE" + nm])
                    for q in range(4):
                        pi = psb()
                        op('pe', lambda e, BE=BE, q=q, pi=pi: e.transpose(
                            out=PSUM[:, pi, 0:128], in_=BE[:, 4 * q:4 * q + 4, :].rearrange("p a b -> p (a b)"),
                            identity=ident[:, :]), reads=["BE" + nm], writes=pk(pi))
                        for slot in range(4):
                            op('dve', lambda e, BTd=BTd, q=q, slot=slot, pi=pi: e.tensor_scalar(
                                out=BTd[:, 4 * q + slot, :], in0=PSUM[:, pi, 0:128], scalar1=rmask[:, slot:slot + 1],
                                scalar2=None, op0=ALU.mult), reads=pk(pi), writes=["BT" + nm])
                for (CN, CTd, sgn, nm) in ((CNr, CTr, 1.0, "r"), (CNi, nCTi, -1.0, "i"), (CNr, nCTr, -1.0, "r2")):
                    op('dve', lambda e, CTd=CTd: e.memset(CTd[:], 0.0), writes=["CT" + nm])
                    for q in range(4):
                        tt('dve', CE[:, 0:64], CN[:, q, :], cmask[:, 0:64], ALU.mult, R, ["CE"])
                        tt('dve', CE[:, 64:128], CN[:, q, :], cmask[:, 64:128], ALU.mult, R, ["CE"])
                        pi = psb()
                        op('pe', lambda e, pi=pi: e.transpose(out=PSUM[:, pi, 0:128], in_=CE[:, :], identity=ident[:, :]),
                           reads=["CE"], writes=pk(pi))
                        for slot in range(4):
                            op('dve', lambda e, CTd=CTd, q=q, slot=slot, pi=pi, sgn=sgn: e.tensor_scalar(
                                out=CTd[:, 4 * q + slot, 32 * slot:32 * slot + 32], in0=PSUM[:, pi, 32 * slot:32 * slot + 32],
                                scalar1=sgn, scalar2=None, op0=ALU.mult), reads=pk(pi), writes=["CT" + nm])
                op('dve', lambda e: e.memset(GE2[:], 0.0), writes=["GE"])
                op('dve', lambda e: e.tensor_copy(out=SS[:, :, 0], in_=nR128s[:, :]), reads=["R128"], writes=["SS"])
                op('dve', lambda e: e.tensor_copy(out=SS[:, :, 1], in_=R128s[:, :]), reads=["R128"], writes=["SS"])
                fence()

        def mm_group(pi, n, pairs, reads, extra_writes=()):
            last = len(pairs) - 1
            tok = None
            for i, (lh, rh) in enumerate(pairs):
                tok = op('pe', lambda e, lh=lh, rh=rh, i=i: e.matmul(PSUM[:, pi, 0:n], lhsT=lh, rhs=rh, start=(i == 0),
                                                                     stop=(i == last)),
                         reads=reads, writes=pk(pi) + list(extra_writes), signal=(i == last))
            return tok

        def layer_phase_a(l):
            with ExitStack() as ph:
                def t(name, shape, dt=BF16):
                    return ph.enter_context(nc.sbuf_tensor("a%d_" % l + name, list(shape), dt))
                XN = t("XN", [128, 8, 576])
                MIX = t("MIX", [128, 8, 576])
                R2 = t("R2", [128, 4, 576])
                KT = t("KT", [128, 128 + 576])
                VT = t("VT", [128, 5, 128])
                VS = t("VS", [64, 128])
                U = t("U", [128, 4, 576])
                AT = t("AT", [128, 4, 576])
                STt = t("ST", [128, 4, 576])
                YG = t("YG", [128, 4, 576])
                s_o = C.stream("ao%d" % l)
                s_o1 = C.stream("ao1_%d" % l)
                s_o2 = C.stream("ao2_%d" % l)

                for bi in range(4):
                  c0 = bi * 512
                  subs = [(0, 512)] + ([(512, 64)] if bi == 3 else [])
                  NB = 576 if bi == 3 else 512
                  with ExitStack() as ph2:
                    def t2(name, shape, dt=BF16):
                        return ph2.enter_context(nc.sbuf_tensor("a%d_%d_" % (l, bi) + name, list(shape), dt))
                    SQ = t2("SQ", [128, 8, 512])
                    RS = t2("RS", [128, 512], F32)
                    XG = t2("XG", [128, 2, 512], F32)
                    RSq = t2("RSq", [128, 512], F32)
                    SQh = t2("SQh", [128, 512])
                    KF = t2("KF", [128, 128], F32)
                    KSF = t2("KSF", [128, 64], F32)
                    OUTS = t2("OUTS", [128, 128], F32)
                    OUTS2 = t2("OUTS2", [128, 128], F32)
                    wq_a2 = WQ([(w_in[l][:, ch_ * 512:ch_ * 512 + (512 if ch_ < 2 else 256)].rearrange("(k p) n -> p k n", p=128),
                                 [128, 8, (512 if ch_ < 2 else 256)]) for ch_ in range(3)])
                    wq_a2.get(0)
                    for (o, n) in subs:
                        xs_ = X[:, :, c0 + o:c0 + o + n]
                        op('act', lambda e, xs_=xs_, n=n: e.activation(out=SQ[:, :, 0:n], in_=xs_, func=AF.Square),
                           reads=["X"], writes=["SQ"])
                        if 'a1a' in SKIP:
                            continue
                        pi = psb()
                        mm_group(pi, n, [(ones[:, :], SQ[:, k, 0:n]) for k in range(8)], ["SQ", "ones"])
                        op('act', lambda e, pi=pi, n=n: e.activation(out=RS[:, 0:n], in_=PSUM[:, pi, 0:n], func=AF.Sqrt,
                                                                     bias=EPS, scale=1.0 / D), reads=pk(pi), writes=["RS"])
                        if 'a1b' in SKIP:
                            continue
                        op('dve', lambda e, n=n: e.reciprocal(out=RS[:, 0:n], in_=RS[:, 0:n]), reads=["RS"], writes=["RS"])
                        if 'a1c' in SKIP:
                            continue
                        for k in range(8):
                            xg = XG[:, k % 2, 0:n]
                            op('act', lambda e, k=k, o=o, n=n, xg=xg: e.activation(
                                out=xg, in_=X[:, k, c0 + o:c0 + o + n], func=AF.Copy, scale=g1[:, l, k:k + 1]),
                                reads=["X"], writes=[("XG", k % 2)])
                            op('dve', lambda e, k=k, o=o, n=n, xg=xg: e.tensor_tensor(
                                out=XN[:, k, o:o + n], in0=xg, in1=RS[:, 0:n], op=ALU.mult),
                                reads=[("XG", k % 2), "RS"], writes=[("XN", k)])
                    if STAGE >= 8 and bi >= 1:
                        norm2_into(l, MIX, SQ, RS, XG, (bi - 1) * 512, 0, 512)
                    XNr = [("XN", k) for k in range(8)]
                    for ch in range(3):
                        wcols = 512 if ch < 2 else 256
                        Wv, wk = wq_a2.get(ch)
                        for j in range(wcols // 128):
                            col = ch * 512 + j * 128
                            if col == 640:
                                continue
                            for (o, n) in subs:
                                pi = psb()
                                mm_group(pi, n, [(Wv[:, k, j * 128:(j + 1) * 128], XN[:, k, o:o + n]) for k in range(8)],
                                         XNr + [wk])
                                if 'a2a' in SKIP:
                                    continue
                                if col < 640:
                                    isq = col < 512
                                    op('act', lambda e, pi=pi, n=n: e.activation(out=SQh[:, 0:n], in_=PSUM[:, pi, 0:n],
                                                                                 func=AF.Square), reads=pk(pi), writes=["SQh"])
                                    p2 = psb()
                                    mm_group(p2, n, [(blk64[:, :], SQh[:, 0:n])], ["SQh", "blk64"])
                                    op('act', lambda e, p2=p2, n=n: e.activation(out=RSq[:, 0:n], in_=PSUM[:, p2, 0:n],
                                                                                 func=AF.Sqrt, bias=EPS, scale=1.0 / 64),
                                       reads=pk(p2), writes=["RSq"])
                                    op('dve', lambda e, n=n: e.reciprocal(out=RSq[:, 0:n], in_=RSq[:, 0:n]),
                                       reads=["RSq"], writes=["RSq"])
                                    if 'a2b' in SKIP:
                                        continue
                                    if isq:
                                        dst = R2[:, col // 128, o:o + n]
                                        dk = ("R2", col // 128)
                                        gi_ = 0
                                    else:
                                        dst = KT[:, 128 + o:128 + o + n]
                                        dk = "KT"
                                        gi_ = 1
                                    op('dve', lambda e, pi=pi, n=n, dst=dst, gi_=gi_: e.scalar_tensor_tensor(
                                        out=dst, in0=PSUM[:, pi, 0:n], scalar=qkg[:, l, gi_:gi_ + 1], in1=RSq[:, 0:n],
                                        op0=ALU.mult, op1=ALU.mult), reads=pk(pi) + ["RSq", "qkg"], writes=[dk])
                                    if (not isq) and bi == 3 and 'a2c' not in SKIP:
                                        if o == 0:
                                            op('dve', lambda e, pi=pi: e.scalar_tensor_tensor(
                                                out=KF[:, :], in0=PSUM[:, pi, 384:512], scalar=qkg[:, l, 1:2],
                                                in1=RSq[:, 384:512], op0=ALU.mult, op1=ALU.mult),
                                                reads=pk(pi) + ["RSq"], writes=["KF"])
                                            p3 = psb()
                                            op('pe', lambda e, p3=p3: e.transpose(out=PSUM[:, p3, 0:128], in_=KF[:, :],
                                                                                  identity=ident[:, :]),
                                               reads=["KF"], writes=pk(p3))
                                            op('act', lambda e, p3=p3: e.activation(out=OUTS[:, :], in_=PSUM[:, p3, 0:128],
                                                                                    func=AF.Copy), reads=pk(p3), writes=["OUTS"])
                                            dma('sp', s_o1, kp[l], OUTS[:, :], reads=["OUTS"])
                                        else:
                                            op('dve', lambda e, pi=pi: e.scalar_tensor_tensor(
                                                out=KSF[:, :], in0=PSUM[:, pi, 0:64], scalar=qkg[:, l, 1:2],
                                                in1=RSq[:, 0:64], op0=ALU.mult, op1=ALU.mult),
                                                reads=pk(pi) + ["RSq"], writes=["KSF"])
                                            p3 = psb()
                                            op('pe', lambda e, p3=p3: e.transpose(out=PSUM[0:64, p3, 0:128], in_=KSF[:, :],
                                                                                  identity=ident[:, :]),
                                               reads=["KSF"], writes=pk(p3))
                                            op('act', lambda e, p3=p3: e.activation(out=OUTS2[0:64, :], in_=PSUM[0:64, p3, 0:128],
                                                                                    func=AF.Copy), reads=pk(p3), writes=["OUTS2"])
                                            for i_ in range(4):
                                                dma('sp', s_o2, ksam[l][:, 124 + i_, :], OUTS2[i_ * 16:(i_ + 1) * 16, :],
                                                    reads=["OUTS2"])
                                else:
                                    ut = (col - 768) // 128
                                    op('act', lambda e, pi=pi, n=n, ut=ut, o=o: e.activation(
                                        out=U[:, ut, o:o + n], in_=PSUM[:, pi, 0:n], func=AF.Copy),
                                        reads=pk(pi), writes=[("U", ut)])
                        if ch == 1 and 'a2d' not in SKIP:
                            pi = psb()
                            for tl in range(4):
                                for k in range(8):
                                    op('pe', lambda e, tl=tl, k=k, pi=pi: e.matmul(
                                        PSUM[:, pi, tl * 128:(tl + 1) * 128],
                                        lhsT=XN[:, k, tl * 128:(tl + 1) * 128], rhs=Wv[:, k, 128:256],
                                        start=(k == 0), stop=(k == 7)),
                                        reads=XNr + [wk], writes=pk(pi), signal=(k == 7))
                            if 'v1' in SKIP:
                                continue
                            op('act', lambda e, pi=pi: e.activation(out=VT[:, 1:5, :], in_=PSUM[:, pi, :].rearrange(
                                "p (t d) -> p t d", t=4), func=AF.Copy), reads=pk(pi), writes=["VT"])
                            if bi == 3 and 'v2' not in SKIP:
                                op('dve', lambda e, pi=pi: e.tensor_copy(out=OUTS[:, :], in_=PSUM[:, pi, 384:512]),
                                   reads=pk(pi), writes=["OUTS"])
                                dma('sp', s_o1, vp[l], OUTS[:, :], reads=["OUTS"])
                                pv5 = psb()
                                for k in range(8):
                                    op('pe', lambda e, k=k, pv5=pv5: e.matmul(
                                        PSUM[0:64, pv5, 0:128], lhsT=XN[:, k, 512:576], rhs=Wv[:, k, 128:256],
                                        start=(k == 0), stop=(k == 7)), reads=XNr + [wk], writes=pk(pv5), signal=(k == 7))
                                op('act', lambda e, pv5=pv5: e.activation(out=VS[:, :], in_=PSUM[0:64, pv5, 0:128], func=AF.Copy),
                                   reads=pk(pv5), writes=["VS"])
                                op('dve', lambda e, pv5=pv5: e.tensor_copy(out=OUTS2[0:64, :], in_=PSUM[0:64, pv5, 0:128]),
                                   reads=pk(pv5), writes=["OUTS2"])
                                for i_ in range(4):
                                    dma('sp', s_o2, vsam[l][:, 124 + i_, :], OUTS2[i_ * 16:(i_ + 1) * 16, :], reads=["OUTS2"])
                    fence()
                  E = dict(U=U, s_o=s_o, R2=R2, KT=KT, VT=VT, VS=VS, AT=AT, ST=STt, YG=YG, XN=XN, MIX=MIX, subs=subs, c0=c0)
                  if STAGE >= 3:
                      attn_block(l, bi, E)
                  if STAGE >= 2:
                      ssm_block(l, bi, E)
                  if STAGE >= 5:
                      mix_block(l, bi, E)
                  if DEBUG_DUMP and l == 0:
                      dma('sp', s_o, dbg_at[bi], AT[:, :, :], reads=[])
                      dma('sp', s_o, dbg_st[bi], STt[:, :, :], reads=[])
                      dma('sp', s_o, dbg_yg[bi], YG[:, :, :], reads=[])
                      dma('sp', s_o, dbg_mix[bi], MIX[:, :, :], reads=[])
                  op('pool', lambda e: e.tensor_copy(out=KT[:, 0:128], in_=KT[:, 128 + 384:128 + 512]), reads=["KT"], writes=["KT"])
                  op('pool', lambda e: e.tensor_copy(out=VT[:, 0, :], in_=VT[:, 4, :]), reads=["VT"], writes=["VT"])
                if "shift" not in SKIP:
                    with nc.allow_non_contiguous_dma(reason="cache shift"):
                        dma('sp', s_o, ksam[l][:, 0:124, :], cache_k[l][:, 4:128, :])
                        dma('sp', s_o, vsam[l][:, 0:124, :], cache_v[l][:, 4:128, :])
                if STAGE >= 8:
                    ffn_tail(l, MIX)
                fence()

        def ffn_specs(l):
            fs = []
            for c in range(8):
                fs.append((w_up[l][:, c * 512:(c + 1) * 512].rearrange("(k p) n -> p k n", p=128), [128, 8, 512]))
                fs.append((w_down[l][c * 512:(c + 1) * 512, :].rearrange("(k p) n -> p k n", p=128), [128, 4, 1024]))
            return fs

        def norm2_into(l, XN2, SQ, RS, XG, xcol, o, n):
            pi = psb()
            for hf in range(2):
                op('act', lambda e, hf=hf: e.activation(out=SQ[:, 0:4, 0:n], in_=X[:, 4 * hf:4 * hf + 4, xcol:xcol + n], func=AF.Square),
                   reads=["X"], writes=["SQ"])
                for k_ in range(4):
                    op('pe', lambda e, k_=k_, hf=hf: e.matmul(PSUM[:, pi, 0:n], lhsT=ones[:, :], rhs=SQ[:, k_, 0:n],
                                                             start=(hf == 0 and k_ == 0), stop=(hf == 1 and k_ == 3)),
                       reads=["SQ", "ones"], writes=pk(pi), signal=(k_ == 3))
            op('act', lambda e: e.activation(out=RS[:, 0:n], in_=PSUM[:, pi, 0:n], func=AF.Sqrt, bias=EPS, scale=1.0 / D),
               reads=pk(pi), writes=["RS"])
            op('dve', lambda e: e.reciprocal(out=RS[:, 0:n], in_=RS[:, 0:n]), reads=["RS"], writes=["RS"])
            for k_ in range(8):
                xg = XG[:, k_ % 2, 0:n]
                op('act', lambda e, k_=k_, xg=xg: e.activation(out=xg, in_=X[:, k_, xcol:xcol + n], func=AF.Copy,
                                                              scale=g2[:, l, k_:k_ + 1]), reads=["X"], writes=[("XG", k_ % 2)])
                op('dve', lambda e, k_=k_, xg=xg: e.tensor_tensor(out=XN2[:, k_, o:o + n], in0=xg, in1=RS[:, 0:n], op=ALU.mult),
                   reads=[("XG", k_ % 2), "RS"], writes=[("MIX", k_)])

        def ssm_block(l, bi, E):
            U = E['U']; s_o = E['s_o']; YG = E['YG']; ST = E['ST']; subs = E['subs']
            with ExitStack() as ph:
                def t(name, shape, dt=BF16):
                    return ph.enter_context(nc.sbuf_tensor("s%d_%d_" % (l, bi) + name, list(shape), dt))
                Y1 = t("Y1", [128, 512]); T1 = t("T1", [128, 512]); SG = T1
                php = ExitStack()
                def tp(name, shape, dt=BF16):
                    return php.enter_context(nc.sbuf_tensor("sp%d_%d_" % (l, bi) + name, list(shape), dt))
                Zt = [tp("Z%d" % i, [128, 2, 512]) for i in range(2)]
                PRM = [tp("PRM%d" % i, [128, 4, 512]) for i in range(2)]
                Ma = PRM[1][:, 0:2, :]
                Mb = PRM[1][:, 2:4, :]
                MK = ("PR", 1)
                Xc = [tp("Xc%d" % i, [128, 2, 512]) for i in range(2)]
                if l == 0 and bi == 0:
                    print("SBUF remaining in ssm prompt scope:", nc.sbuf_bytes_remaining)
                Gt = [tp("G%d" % i, [128, 2, 512]) for i in range(2)]
                PR = PRM
                INt = [tp("IN%d" % i, [128, 4, 2], F32) for i in range(2)]
                Ut = [tp("U%d" % i, [128, 2], F32) for i in range(2)]
                HO = tp("HO", [128, 2, 16], F32); HT = tp("HT", [128, 16], F32)
                HOUT = tp("HOUT", [16, 2, 128], F32)
                PTa = [tp("PTa%d" % i, [128, 256]) for i in range(2)]
                DRa = tp("DRa", [128, 256], F32)
                v3 = lambda ap_: ap_.rearrange("p (c j) -> p c j", c=NCHUNK)
                R2 = E['R2']; KT = E['KT']; VT = E['VT']; AT = E['AT']
                att_it = [0]

                def att_unit(i, hp, b):
                    hs = i * 2 + hp
                    rows = slice(hp * 64, (hp + 1) * 64)
                    has_prev = (bi * 4 + b) > 0
                    tb = att_it[0] % 2
                    att_it[0] += 1
                    qv = R2[rows, i, b * 128:(b + 1) * 128]
                    if has_prev:
                        op('pe', lambda e: e.matmul(PSUM[:, BS, 0:128], lhsT=KT[rows, b * 128:(b + 1) * 128], rhs=qv,
                                                    start=True, stop=True), reads=[("R2", i), "KT"], writes=pk(BS), signal=False)
                    op('pe', lambda e: e.matmul(PSUM[:, BS, 128:256], lhsT=KT[rows, 128 + b * 128:128 + (b + 1) * 128], rhs=qv,
                                                start=True, stop=True), reads=[("R2", i), "KT"], writes=pk(BS), signal=True)
                    c_lo = 0 if has_prev else 128
                    op('act', lambda e: e.activation(out=PTa[tb][:, c_lo:256], in_=PSUM[:, BS, c_lo:256], func=AF.Exp),
                       reads=pk(BS), writes=[("PTa", tb)])
                    tt('pool', PTa[tb][:, c_lo:256], PTa[tb][:, c_lo:256], EB[:, hs, c_lo:256], ALU.mult, [("PTa", tb), "EB"],
                       [("PTa", tb)])
                    return lambda: att_pv(rows, b, tb, has_prev)

                def att_pv(rows, b, tb, has_prev):
                    parts = ([(VT[:, b, rows], PTa[tb][:, 0:128])] if has_prev else []) + [(VT[:, b + 1, rows], PTa[tb][:, 128:256])]
                    bb = b % 2
                    for (coff, use_ones) in ((0, False), (256, True)):
                        for ii, (vv, pp) in enumerate(parts):
                            lh = ones[:, 0:64] if use_ones else vv
                            op('pe', lambda e, lh=lh, pp=pp, ii=ii, coff=coff: e.matmul(
                                PSUM[rows, BOD, coff + bb * 128:coff + (bb + 1) * 128], lhsT=lh, rhs=pp, start=(ii == 0),
                                stop=(ii == len(parts) - 1)), reads=[("PTa", tb), "VT", "ones"], writes=pk(BOD),
                                signal=(ii == len(parts) - 1))

                def att_norm(i, h):
                    op('act', lambda e: e.activation(out=DRa[:, :], in_=PSUM[:, BOD, 256:512], func=AF.Ln, bias=esink[:, l, i:i + 1],
                                                     scale=1.0), reads=pk(BOD) + ["esink"], writes=["DRa"])
                    op('act', lambda e: e.activation(out=DRa[:, :], in_=DRa[:, :], func=AF.Exp, scale=-1.0), reads=["DRa"], writes=["DRa"])
                    tt('dve', AT[:, i, h * 256:(h + 1) * 256], PSUM[:, BOD, 0:256], DRa[:, :], ALU.mult, pk(BOD) + ["DRa"], [("AT", i)])

                att_groups = []
                if STAGE >= 3:
                    for i in range(4):
                        for h in range(2):
                            att_groups.append((i, h))

                def epilogue(q, yb, o, n):
                    op('dve', lambda e: e.scalar_tensor_tensor(out=Y1[:, 0:n], in0=U[:, q, o:o + n], scalar=dsk[:, l, q:q + 1],
                                                               in1=PSUM[:, yb, 0:n], op0=ALU.mult, op1=ALU.add),
                       reads=pk(yb) + [("U", q)], writes=["Y1"])
                    tt('dve', T1[:, 0:n], Y1[:, 0:n], Y1[:, 0:n], ALU.mult, ["Y1"], ["T1"])
                    op('dve', lambda e: e.tensor_scalar(out=T1[:, 0:n], in0=T1[:, 0:n], scalar1=0.044715, scalar2=1.0,
                                                        op0=ALU.mult, op1=ALU.add), reads=["T1"], writes=["T1"])
                    tt('dve', T1[:, 0:n], T1[:, 0:n], Y1[:, 0:n], ALU.mult, ["T1", "Y1"], ["T1"])
                    op('act', lambda e: e.activation(out=T1[:, 0:n], in_=T1[:, 0:n], func=AF.Sigmoid, scale=1.5957691216057308),
                       reads=["T1"], writes=["T1"])
                    tt('dve', YG[:, q, o:o + n], Y1[:, 0:n], T1[:, 0:n], ALU.mult, ["Y1", "T1"], [("YG", q)])

                BX0, BY, BS, BOD, BUP = 0, 2, 3, 4, 5

                def x0_mm(s):
                    q = s // 4
                    mm_group(BX0, 512, [(BTr[:, s, :], U[:, q, 0:512])], [("U", q), "BTr"])
                    mm_group(BX0 + 1, 512, [(BTi[:, s, :], U[:, q, 0:512])], [("U", q), "BTi"])

                def evac(s):
                    b_ = s % 2
                    xk = ("Xc", b_)
                    op('act', lambda e: e.activation(out=Xc[b_][:, 0, :], in_=PSUM[:, BX0, :], func=AF.Copy), reads=pk(BX0), writes=[xk])
                    op('act', lambda e: e.activation(out=Xc[b_][:, 1, :], in_=PSUM[:, BX0 + 1, :], func=AF.Copy), reads=pk(BX0 + 1),
                       writes=[xk])

                def modops(s):
                    b_ = s % 2
                    xk = ("Xc", b_)
                    zk = ("Z", b_)
                    c4 = bc_mid(PHc[:, s, 0:CHUNK], 2 * NCHUNK)
                    s4 = bc_mid(PHs[:, s, 0:CHUNK], 2 * NCHUNK)
                    x4 = Xc[b_][:, :, :].rearrange("p c (h j) -> p (c h) j", j=CHUNK)
                    tt('dve', Ma.rearrange("p c (h j) -> p (c h) j", j=CHUNK), x4, c4, ALU.mult, [xk, "PH"], [MK])
                    tt('dve', Mb.rearrange("p c (h j) -> p (c h) j", j=CHUNK), x4, s4, ALU.mult, [xk, "PH"], [MK])
                    tt('dve', Zt[b_][:, 0, :], Ma[:, 0, :], Mb[:, 1, :], ALU.add, [MK], [zk])
                    tt('dve', Zt[b_][:, 1, :], Ma[:, 1, :], Mb[:, 0, :], ALU.subtract, [MK], [zk])

                def chain(tiles, hook=lambda: None):
                    for ch in range(NCHUNK):
                        for s in tiles:
                            b_ = s % 2
                            gk = ("G", b_)
                            if ch == 0:
                                ge = GE2[:, s, :]
                                ger = GE2[:, s, ::-1]
                                rk = ["GE"]
                            else:
                                ge = Gt[b_][:, :, ch * CHUNK - 1]
                                ger = Gt[b_][:, ::-1, ch * CHUNK - 1]
                                rk = [gk]
                            tt('dve', Ut[b_][:, :], ger, SS[:, s, :], ALU.mult, rk + ["SS"], [("UT", b_)])
                            op('dve', lambda e, ge=ge, s=s, b_=b_, ch=ch: e.scalar_tensor_tensor(
                                out=INt[b_][:, ch, :], in0=ge, scalar=R128c[:, s:s + 1], in1=Ut[b_][:, :], op0=ALU.mult, op1=ALU.add),
                                reads=rk + [("UT", b_)], writes=[("IN", b_)])
                        hook()
                        for s in tiles:
                            b_ = s % 2
                            gk = ("G", b_)
                            cs_ = slice(ch * CHUNK, (ch + 1) * CHUNK)
                            for c_ in range(2):
                                op('dve', lambda e, s=s, ch=ch, cs_=cs_, c_=c_, b_=b_: e.tensor_tensor_scan(
                                    out=Gt[b_][:, c_, cs_], data0=MAG[:, s:s + 1].to_broadcast([128, CHUNK]), data1=Zt[b_][:, c_, cs_],
                                    initial=INt[b_][:, ch, c_:c_ + 1], op0=ALU.mult, op1=ALU.add),
                                    reads=[("Z", b_), ("IN", b_)], writes=[gk])
                            hook()

                def finish(s):
                    q = s // 4
                    slot = s % 4
                    yb = BY
                    b_ = s % 2
                    gk = ("G", b_)
                    cb = bc_mid(PHc[:, s, 0:CHUNK], NCHUNK)
                    sb_ = bc_mid(PHs[:, s, 0:CHUNK], NCHUNK)
                    op('dve', lambda e: e.tensor_copy(out=GE2[:, s, :], in_=Gt[b_][:, :, 511]), reads=[gk], writes=["GE"])
                    P_ = PR[b_]
                    pkey = ("PR", b_)
                    gr_ = v3(Gt[b_][:, 0, :])
                    gi_ = v3(Gt[b_][:, 1, :])
                    c4 = bc_mid(PHc[:, s, 0:CHUNK], 2 * NCHUNK)
                    s4 = bc_mid(PHs[:, s, 0:CHUNK], 2 * NCHUNK)
                    g4 = Gt[b_][:, :, :].rearrange("p c (h j) -> p (c h) j", j=CHUNK)
                    tt('dve', P_[:, 0:2, :].rearrange("p c (h j) -> p (c h) j", j=CHUNK), g4, c4, ALU.mult, [gk, "PH"], [pkey])
                    tt('dve', P_[:, 2:4, :].rearrange("p c (h j) -> p (c h) j", j=CHUNK), g4, s4, ALU.mult, [gk, "PH"], [pkey])
                    pairs = [(CTr[:, s, :], P_[:, 0, :]), (nCTi[:, s, :], P_[:, 1, :]), (nCTi[:, s, :], P_[:, 2, :]),
                             (nCTr[:, s, :], P_[:, 3, :])]
                    for ii, (lh, rh) in enumerate(pairs):
                        first = (slot == 0 and ii == 0)
                        lastm = (slot == 3 and ii == 3)
                        op('pe', lambda e, lh=lh, rh=rh, first=first, lastm=lastm: e.matmul(
                            PSUM[:, yb, 0:512], lhsT=lh, rhs=rh, start=first, stop=lastm),
                            reads=[pkey, "CT"], writes=pk(yb), signal=(ii == 3))
                    if slot == 3:
                        pending.append(lambda: epilogue(q, yb, 0, 512))

                ffn_on = (bi >= 1) and STAGE >= 8 and 'ffni' not in SKIP
                if ffn_on:
                    XN2 = E['MIX']
                    xo = (bi - 1) * 512
                    Hf = tp("Hf", [128, 4, 512])
                    wq_f = WQ(ffn_specs(l))
                    BDN = (6, 7)

                    def ffn_up_unit(c, j):
                        Wu, wuk = wq_f.get(2 * c)
                        mm_group(BUP, 512, [(Wu[:, k_, j * 128:(j + 1) * 128], XN2[:, k_, 0:512]) for k_ in range(8)],
                                 [("MIX", k_) for k_ in range(8)] + [wuk])
                        op('act', lambda e: e.activation(out=Hf[:, j, :], in_=PSUM[:, BUP, :], func=AF.Relu),
                           reads=pk(BUP), writes=[("Hf", j)])
                        op('act', lambda e: e.activation(out=Hf[:, j, :], in_=Hf[:, j, :], func=AF.Square),
                           reads=[("Hf", j)], writes=[("Hf", j)])

                    def ffn_down(c, m):
                        Wd, wdk = wq_f.get(2 * c + 1)
                        mm_group(BDN[m % 2], 512, [(Wd[:, j, m * 128:(m + 1) * 128], Hf[:, j, :]) for j in range(4)],
                                 [("Hf", j) for j in range(4)] + [wdk])

                    def ffn_add(m):
                        bank = BDN[m % 2]
                        op('dve', lambda e: e.tensor_tensor(out=X[:, m, xo:xo + 512], in0=PSUM[:, bank, :], in1=X[:, m, xo:xo + 512],
                                                            op=ALU.add), reads=pk(bank) + ["X"], writes=["X"])
                else:
                    Wg, wgk = wload(w_glu[l].rearrange("(k p) n -> p k n", p=128), [128, 4, 512], None)

                pending = []
                x0_mm(0)
                evac(0)
                x0_mm(1)
                evac(1)
                for p_ in range(8):
                    s0, s1 = 2 * p_, 2 * p_ + 1
                    modops(s0)
                    modops(s1)
                    if p_ < 7:
                        x0_mm(s0 + 2)
                        evac(s0 + 2)
                        x0_mm(s1 + 2)
                        evac(s1 + 2)
                    aunits = []
                    if att_groups:
                        if p_ > 0:
                            att_norm(*att_groups[p_ - 1])
                        gi_, gh_ = att_groups[p_]
                        aunits = [(gi_, hp, b) for hp in range(2) for b in (2 * gh_, 2 * gh_ + 1)]
                    pv_prev = None
                    for j in range(4):
                        pv_ = att_unit(*aunits[j]) if j < len(aunits) else None
                        if pv_prev:
                            pv_prev()
                        if ffn_on:
                            ffn_up_unit(p_, j)
                        pv_prev = pv_
                    if pv_prev:
                        pv_prev()
                    chain([s0, s1])
                    todo = pending
                    pending = []
                    for f_ in todo:
                        f_()
                    if ffn_on:
                        for m_ in range(8):
                            ffn_down(p_, m_)
                            if m_ >= 1:
                                ffn_add(m_ - 1)
                    finish(s0)
                    if ffn_on:
                        ffn_add(7)
                    finish(s1)
                if att_groups:
                    att_norm(*att_groups[7])
                for f_ in pending:
                    f_()
                if ffn_on:
                    Wg, wgk = wload(w_glu[l].rearrange("(k p) n -> p k n", p=128), [128, 4, 512], None)
                if bi == 3:
                    tt('dve', HO[:, 0, :], GE2[:, :, 0], P127c[:, :], ALU.mult, ["GE"], ["HO"])
                    tt('dve', HT[:, :], GE2[:, :, 1], P127s[:, :], ALU.mult, ["GE"], ["HT"])
                    tt('dve', HO[:, 0, :], HO[:, 0, :], HT[:, :], ALU.subtract, ["HO", "HT"], ["HO"])
                    tt('dve', HO[:, 1, :], GE2[:, :, 0], P127s[:, :], ALU.mult, ["GE", "HO"], ["HO"])
                    tt('dve', HT[:, :], GE2[:, :, 1], P127c[:, :], ALU.mult, ["GE", "HO"], ["HT"])
                    tt('dve', HO[:, 1, :], HO[:, 1, :], HT[:, :], ALU.add, ["HO", "HT"], ["HO"])
                    for c_ in range(2):
                        pi = 6 + c_
                        op('pe', lambda e, c_=c_, pi=pi: e.transpose(out=PSUM[0:16, pi, 0:128], in_=HO[:, c_, :], identity=ident[:, :]),
                           reads=["HO"], writes=pk(pi))
                        op('act', lambda e, c_=c_, pi=pi: e.activation(out=HOUT[:, c_, :], in_=PSUM[0:16, pi, 0:128], func=AF.Copy),
                           reads=pk(pi), writes=["HOUT"])
                    dma('sp', s_o, hrp[l], HOUT[:, 0, :], reads=["HOUT"])
                    dma('sp', s_o, hip[l], HOUT[:, 1, :], reads=["HOUT"])
                fence()
                php.close()
                if bi == 3 and STAGE >= 4 and 'ssms' not in SKIP:
                    ssm_sample(l, E, epilogue)
                    fence()
                for j in range(4):
                    for (o, n) in subs:
                        pi = psb()
                        mm_group(pi, n, [(Wg[:, q_, j * 128:(j + 1) * 128], YG[:, q_, o:o + n]) for q_ in range(4)],
                                 [("YG", q_) for q_ in range(4)] + [wgk])
                        op('act', lambda e, pi=pi, n=n, j=j: e.activation(out=SG[:, 0:n], in_=PSUM[:, pi, 0:n], func=AF.Sigmoid,
                                                                          bias=bgl[:, l, j:j + 1], scale=1.0),
                           reads=pk(pi), writes=["T1"])
                        tt('dve', ST[:, j, o:o + n], YG[:, j, o:o + n], SG[:, 0:n], ALU.mult, ["T1", ("YG", j)], [("ST", j)])
                fence()

        def ssm_sample(l, E, epilogue):
            U = E['U']; s_o = E['s_o']
            with ExitStack() as ph:
                def t(name, shape, dt=F32):
                    return ph.enter_context(nc.sbuf_tensor("ss%d_" % l + name, list(shape), dt))
                SN = t("SN", [16, 2048])
                H0 = [t("H0%d" % c_, [128, 16, 16]) for c_ in range(2)]
                HS = [t("HS%d" % c_, [128, 16, 64]) for c_ in range(2)]
                HSb = [t("HSb%d" % c_, [128, 16, 64], BF16) for c_ in range(2)]
                TA = t("TA", [128, 16, 16]); TB = t("TB", [128, 16, 16])
                s_s = C.stream("ssl%d" % l)
                for s in range(16):
                    q = s // 4
                    for c_, BT in ((0, BTr), (1, BTi)):
                        bank = 2 * c_ + s // 8
                        op('pe', lambda e, s=s, q=q, BT=BT, bank=bank: e.matmul(
                            PSUM[:, bank, (s % 8) * 64:(s % 8) * 64 + 64], lhsT=BT[:, s, :], rhs=U[:, q, 512:576],
                            start=True, stop=True), reads=[("U", q), "BTr", "BTi"], writes=pk(bank), signal=True)
                for c_, src in ((0, st_re), (1, st_im)):
                    dma('sp', s_s, SN[:, :], src[l], writes=["SN"])
                    pi = 6 + c_
                    for s in range(16):
                        op('pe', lambda e, s=s, pi=pi: e.transpose(out=PSUM[:, pi, s * 16:(s + 1) * 16],
                                                                   in_=SN[0:16, s * 128:(s + 1) * 128], identity=ident[0:16, 0:16]),
                           reads=["SN"], writes=pk(pi), signal=(s == 15))
                    op('act', lambda e, c_=c_, pi=pi: e.activation(out=H0[c_][:, :, :], in_=PSUM[:, pi, 0:256].rearrange(
                        "p (s b) -> p s b", s=16), func=AF.Copy), reads=pk(pi), writes=[("H0", c_)])
                ARb = bc_last(AR[:, :], 16)
                AIb = bc_last(AI[:, :], 16)
                xv = [PSUM[:, 2 * c_:2 * c_ + 2, :].rearrange("p b (s c) -> p (b s) c", c=64) for c_ in range(2)]
                for i_ in range(4):
                    cs_ = slice(i_ * 16, (i_ + 1) * 16)
                    if i_ == 0:
                        pr_, pi_ = H0[0][:, :, :], H0[1][:, :, :]
                        rk = [("H0", 0), ("H0", 1)]
                    else:
                        ps_ = slice((i_ - 1) * 16, i_ * 16)
                        pr_, pi_ = HS[0][:, :, ps_], HS[1][:, :, ps_]
                        rk = ["HS"]
                    tt('dve', TA[:, :, :], pr_, ARb, ALU.mult, rk + ["AR"], ["TA"])
                    tt('dve', TB[:, :, :], pi_, AIb, ALU.mult, rk + ["AR"], ["TB"])
                    tt('dve', TA[:, :, :], TA[:, :, :], TB[:, :, :], ALU.subtract, ["TA", "TB"], ["TA"])
                    tt('dve', HS[0][:, :, cs_], xv[0][:, :, cs_], TA[:, :, :], ALU.add, pk(0, 2) + ["TA"], ["HS"])
                    tt('dve', TA[:, :, :], pi_, ARb, ALU.mult, rk + ["AR", "HS"], ["TA"])
                    tt('dve', TB[:, :, :], pr_, AIb, ALU.mult, rk + ["AR"], ["TB"])
                    tt('dve', TA[:, :, :], TA[:, :, :], TB[:, :, :], ALU.add, ["TA", "TB"], ["TA"])
                    tt('dve', HS[1][:, :, cs_], xv[1][:, :, cs_], TA[:, :, :], ALU.add, pk(2, 2) + ["TA", "HS"], ["HS"])
                for c_ in range(2):
                    op('dve', lambda e, c_=c_: e.tensor_copy(out=HSb[c_][:, :, :], in_=HS[c_][:, :, :]), reads=["HS", "HS"],
                       writes=[("HSb", c_)])
                for q in range(4):
                    yb = 4 + (q % 2)
                    pairs = []
                    for slot in range(4):
                        s = 4 * q + slot
                        pairs.append((CTr[:, s, :], HSb[0][:, s, :]))
                        pairs.append((nCTi[:, s, :], HSb[1][:, s, :]))
                    mm_group(yb, 64, pairs, [("HSb", 0), ("HSb", 1), "CT"])
                    epilogue(q, yb, 512, 64)
                for c_, dst in ((0, hrs), (1, his)):
                    for g4 in range(4):
                        pi = g4
                        for j in range(4):
                            s = g4 * 4 + j
                            op('pe', lambda e, s=s, j=j, pi=pi, c_=c_: e.transpose(
                                out=PSUM[0:16, pi, j * 128:(j + 1) * 128], in_=HS[c_][:, s, 48:64], identity=ident[:, :]),
                                reads=["HS", "HS"], writes=pk(pi), signal=(j == 3))
                        op('act', lambda e, pi=pi, g4=g4: e.activation(out=SN[:, g4 * 512:(g4 + 1) * 512], in_=PSUM[0:16, pi, :],
                                                                      func=AF.Copy), reads=pk(pi), writes=["SN"])
                    dma('sp', s_s, dst[l].rearrange("b s r -> b (s r)"), SN[:, :], reads=["SN"])
                fence()

        def attn_block(l, bi, E):
            if bi == 3 and STAGE >= 4 and 'atts' not in SKIP:
                attn_sample(l, E)
                fence()

        def attn_sample(l, E):
            R2 = E['R2']; KT = E['KT']; VS = E['VS']; AT = E['AT']
            with ExitStack() as ph:
                def t(name, shape, dt=BF16):
                    return ph.enter_context(nc.sbuf_tensor("as%d_" % l + name, list(shape), dt))
                CK = t("CK", [128, 16, 128], F32)
                CKT = t("CKT", [128, 16, 128])
                CV = t("CV", [128, 16, 128])
                TMPc = t("TMPc", [128, 512], F32)
                Pc = t("Pc", [128, 512])
                TMPn = t("TMPn", [64, 512], F32)
                Pn = t("Pn", [64, 512])
                DR = t("DR", [128, 256], F32)
                s_c = C.stream("asl%d" % l)
                s_v = C.stream("asv%d" % l)
                with nc.allow_non_contiguous_dma(reason="cache load"):
                    dma('sp', s_c, CK[:, :, :], cache_k[l].rearrange("s j d -> j s d"), writes=["CK"])
                    dma('pool', s_v, CV[:, :, :], cache_v[l].rearrange("s j d -> j s d"), writes=["CV"])
                for sl in range(16):
                    pi = sl % 4
                    op('pe', lambda e, sl=sl, pi=pi: e.transpose(out=PSUM[:, pi, 0:128], in_=CK[:, sl, :], identity=ident[:, :]),
                       reads=["CK"], writes=pk(pi))
                    if sl % 2 == 0:
                        op('act', lambda e, sl=sl, pi=pi: e.activation(out=CKT[:, sl, :], in_=PSUM[:, pi, 0:128], func=AF.Copy),
                           reads=pk(pi), writes=["CKT"])
                    else:
                        op('dve', lambda e, sl=sl, pi=pi: e.tensor_copy(out=CKT[:, sl, :], in_=PSUM[:, pi, 0:128]),
                           reads=pk(pi), writes=["CKT"])
                po, pd = 6, 7
                for kv in range(2):
                    rows = slice(kv * 64, (kv + 1) * 64)
                    for sl in range(16):
                        op('pe', lambda e, sl=sl, kv=kv, rows=rows: e.matmul(
                            PSUM[:, 4 + kv, sl:256:16], lhsT=CKT[rows, sl, :],
                            rhs=R2[rows, :, 512 + sl:576:16], start=True, stop=True),
                            reads=["CKT"] + [("R2", i) for i in range(4)], writes=pk(4 + kv), signal=(sl == 15))
                for kv in range(2):
                    op('dve', lambda e, kv=kv: e.tensor_tensor(
                        out=TMPc[:, kv * 256:(kv + 1) * 256].rearrange("p (a s) -> p a s", s=16),
                        in0=PSUM[:, 4 + kv, 0:256].rearrange("p (a s) -> p a s", s=16),
                        in1=bc_last(biasc[:, kv * 16:(kv + 1) * 16], 16), op=ALU.add), reads=pk(4 + kv), writes=["TMPc"])
                op('act', lambda e: e.activation(out=Pc[:, :], in_=TMPc[:, :], func=AF.Exp), reads=["TMPc"], writes=["Pc"])
                for kv in range(2):
                    rows = slice(kv * 64, (kv + 1) * 64)
                    op('pe', lambda e, kv=kv, rows=rows: e.matmul(
                        PSUM[0:64, kv, 0:256], lhsT=KT[rows, 128 + 512:128 + 576],
                        rhs=R2[rows, :, 512:576], start=True, stop=True),
                        reads=["KT"] + [("R2", i) for i in range(4)], writes=pk(kv), signal=True)
                    tt('dve', TMPn[:, kv * 256:(kv + 1) * 256], PSUM[0:64, kv, 0:256], biasnf[:, kv * 256:(kv + 1) * 256], ALU.add,
                       pk(kv), ["TMPn"])
                op('act', lambda e: e.activation(out=Pn[:, :], in_=TMPn[:, :], func=AF.Exp), reads=["TMPn"], writes=["Pn"])
                for (bank, use_ones) in ((po, False), (pd, True)):
                    for kv in range(2):
                        rows = slice(kv * 64, (kv + 1) * 64)
                        lh = ones[0:64, 0:64] if use_ones else VS[0:64, rows]
                        op('pe', lambda e, bank=bank, kv=kv, rows=rows, lh=lh: e.matmul(
                            PSUM[rows, bank, 0:256], lhsT=lh, rhs=Pn[0:64, kv * 256:(kv + 1) * 256], start=True, stop=False),
                            reads=["Pn", "VS", "ones"], writes=pk(bank), signal=False)
                        for sl in range(16):
                            lh2 = ones[:, 0:64] if use_ones else CV[:, sl, rows]
                            op('pe', lambda e, bank=bank, kv=kv, rows=rows, sl=sl, lh2=lh2: e.matmul(
                                PSUM[rows, bank, sl:256:16], lhsT=lh2, rhs=Pc[:, kv * 256 + sl:kv * 256 + 256:16],
                                start=False, stop=(sl == 15)), reads=["Pc", "CV", "ones"], writes=pk(bank), signal=(sl == 15))
                op('dve', lambda e: e.tensor_tensor(out=DR[:, :].rearrange("p (h c) -> p h c", h=4),
                                                    in0=PSUM[:, pd, 0:256].rearrange("p (h c) -> p h c", h=4),
                                                    in1=bc_last(esink[:, l, :], 64), op=ALU.add), reads=pk(pd) + ["esink"], writes=["DRs"])
                op('dve', lambda e: e.reciprocal(out=DR[:, :], in_=DR[:, :]), reads=["DRs"], writes=["DRs"])
                op('dve', lambda e: e.tensor_tensor(out=AT[:, :, 512:576], in0=PSUM[:, po, 0:256].rearrange("p (h c) -> p h c", h=4),
                                                    in1=DR[:, :].rearrange("p (h c) -> p h c", h=4), op=ALU.mult),
                   reads=pk(po) + ["DRs"], writes=[("AT", i) for i in range(4)])
                fence()

        def mix_block(l, bi, E):
            XN = E['XN']; MIX = E['MIX']; AT = E['AT']; ST = E['ST']; subs = E['subs']; c0 = E['c0']
            with ExitStack() as ph:
                def t(name, shape, dt=BF16):
                    return ph.enter_context(nc.sbuf_tensor("mx%d_%d_" % (l, bi) + name, list(shape), dt))
                SGA = t("SGA", [128, 576], F32)
                TM = t("TM", [128, 576])
                WL = [t("WL%d" % i, [128, 4096]) for i in range(2)]
                wl_s = [C.stream("wl%d_%d_%d" % (l, bi, i)) for i in range(2)]
                R5 = Ring(WR + WL, wr_stream + wl_s, [("wr", i) for i in range(3)] + [("wl", i) for i in range(2)])
                XNr = [("XN", k_) for k_ in range(8)]
                mspecs = []
                for h in range(2):
                    for (Wsrc, gcol) in ((w_ao, 1280), (w_so, 2304)):
                        mspecs.append((Wsrc[l][:, h * 512:(h + 1) * 512].rearrange("(k p) n -> p k n", p=128), [128, 4, 512]))
                        mspecs.append((w_in[l][:, gcol + h * 512:gcol + (h + 1) * 512].rearrange("(k p) n -> p k n", p=128), [128, 8, 512]))
                for h in range(2):
                    mspecs.append((w_out[l][:, h * 512:(h + 1) * 512].rearrange("(k p) n -> p k n", p=128), [128, 8, 512]))
                wq_m = WQ(mspecs, ring=R5, ahead=2)
                mi = 0
                for h in range(2):
                    for (Wsrc, Act, akey, gcol, first) in ((w_ao, AT, "AT", 1280, True), (w_so, ST, "ST", 2304, False)):
                        Wo_, wok = wq_m.get(mi)
                        Wg_, wgk = wq_m.get(mi + 1)
                        mi += 2
                        for j in range(4):
                            m = h * 4 + j
                            bcol = (0 if first else 8) + m
                            for (o, n) in subs:
                                pg = psb()
                                mm_group(pg, n, [(Wg_[:, k_, j * 128:(j + 1) * 128], XN[:, k_, o:o + n]) for k_ in range(8)],
                                         XNr + [wgk])
                                op('act', lambda e, pg=pg, n=n, bcol=bcol: e.activation(
                                    out=SGA[:, 0:n], in_=PSUM[:, pg, 0:n], func=AF.Sigmoid, bias=bg[:, l, bcol:bcol + 1], scale=1.0),
                                    reads=pk(pg), writes=["SGA"])
                                pa = psb()
                                mm_group(pa, n, [(Wo_[:, k_, j * 128:(j + 1) * 128], Act[:, k_, o:o + n]) for k_ in range(4)],
                                         [(akey, k_) for k_ in range(4)] + [wok])
                                if first:
                                    tt('dve', MIX[:, m, o:o + n], PSUM[:, pa, 0:n], SGA[:, 0:n], ALU.mult, pk(pa) + ["SGA"], [("MIX", m)])
                                else:
                                    tt('dve', TM[:, 0:n], PSUM[:, pa, 0:n], SGA[:, 0:n], ALU.mult, pk(pa) + ["SGA"], ["TM"])
                                    tt('pool', MIX[:, m, o:o + n], MIX[:, m, o:o + n], TM[:, 0:n], ALU.add, ["TM", ("MIX", m)], [("MIX", m)])
                for h in range(2):
                    Wo_, wok = wq_m.get(8 + h)
                    for j in range(4):
                        m = h * 4 + j
                        for (o, n) in subs:
                            pi = psb()
                            mm_group(pi, n, [(Wo_[:, k_, j * 128:(j + 1) * 128], MIX[:, k_, o:o + n]) for k_ in range(8)],
                                     [("MIX", k_) for k_ in range(8)] + [wok])
                            xc = c0 + o
                            op('dve', lambda e, pi=pi, n=n, m=m, xc=xc: e.tensor_tensor(
                                out=X[:, m, xc:xc + n], in0=PSUM[:, pi, 0:n], in1=X[:, m, xc:xc + n], op=ALU.add),
                                reads=pk(pi) + ["X"], writes=["X"])
                if STAGE >= 8 and bi == 3:
                    SQn = t("SQn", [128, 4, 512]); RSn = t("RSn", [128, 512], F32); XGn = t("XGn", [128, 2, 512], F32)
                    for (o, n) in subs:
                        norm2_into(l, MIX, SQn, RSn, XGn, c0 + o, o, n)
                fence()

        def ffn_tail(l, XN2):
            with ExitStack() as ph:
                def t(name, shape, dt=BF16):
                    return ph.enter_context(nc.sbuf_tensor("b%d_" % l + name, list(shape), dt))
                Hh = [t("H%d" % i, [128, 4, 512]) for i in range(2)]
                Rr = [t("R%d" % i, [128, 512]) for i in range(2)]
                subs = [(0, 512), (512, 64)]
                wq_f = WQ(ffn_specs(l))
                it = 0
                for c in range(8):
                    Wu, wuk = wq_f.get(2 * c)
                    Wd, wdk = wq_f.get(2 * c + 1)
                    for (o, n) in subs:
                        hb = it % 2
                        it += 1
                        for j in range(4):
                            pi = psb()
                            mm_group(pi, n, [(Wu[:, k_, j * 128:(j + 1) * 128], XN2[:, k_, o:o + n]) for k_ in range(8)],
                                     [("MIX", k_) for k_ in range(8)] + [wuk])
                            rb = j % 2
                            op('act', lambda e, pi=pi, n=n, rb=rb: e.activation(out=Rr[rb][:, 0:n], in_=PSUM[:, pi, 0:n],
                                                                               func=AF.Relu), reads=pk(pi), writes=[("R", rb)])
                            op('act', lambda e, n=n, rb=rb, hb=hb, j=j: e.activation(out=Hh[hb][:, j, 0:n], in_=Rr[rb][:, 0:n],
                                                                                    func=AF.Square), reads=[("R", rb)], writes=[("H", hb, j)])
                        for m in range(8):
                            pi = psb()
                            mm_group(pi, n, [(Wd[:, j, m * 128:(m + 1) * 128], Hh[hb][:, j, 0:n]) for j in range(4)],
                                     [("H", hb, j) for j in range(4)] + [wdk])
                            xc = 1536 + o
                            op('dve', lambda e, pi=pi, n=n, m=m, xc=xc: e.tensor_tensor(
                                out=X[:, m, xc:xc + n], in0=PSUM[:, pi, 0:n], in1=X[:, m, xc:xc + n], op=ALU.add),
                                reads=pk(pi) + ["X"], writes=["X"])
                fence()

        for l in range(NLAYERS):
            if STAGE < 1:
                break
            if STAGE >= 2:
                ssm_tables(l, mid=(phase0 if l == 0 else None))
            elif l == 0:
                phase0()
            layer_phase_a(l)
        with ExitStack() as ph:
            YT = [ph.enter_context(nc.sbuf_tensor("yt%d" % i, [128, D], F32)) for i in range(2)]
            yts = [C.stream("yts%d" % i) for i in range(2)]
            for tt in range(17):
                b = tt % 2
                rows = 128 if tt < 16 else NS
                for half in range(2):
                    pi = psb()
                    for j in range(4):
                        k = half * 4 + j
                        op('pe', lambda e, k=k, j=j, pi=pi, rows=rows, tt=tt: e.transpose(
                            out=PSUM[0:rows, pi, j * 128:(j + 1) * 128], in_=X[:, k, tt * 128:tt * 128 + rows],
                            identity=ident[:, :]),
                            reads=[("X", tt)], writes=pk(pi), signal=(j == 3))
                    dst = YT[b][0:rows, half * 512:(half + 1) * 512]
                    src_ps = PSUM[0:rows, pi, :]
                    if half == 0:
                        op('act', lambda e, s_=src_ps, d_=dst: e.activation(out=d_, in_=s_, func=AF.Copy),
                           reads=pk(pi), writes=[("yt", b)])
                    else:
                        op('dve', lambda e, s_=src_ps, d_=dst: e.tensor_copy(out=d_, in_=s_),
                           reads=pk(pi), writes=[("yt", b)])
                dstd = yp[tt * 128:(tt + 1) * 128, :] if tt < 16 else ys[:, :]
                dma('sp', yts[b], dstd, YT[b][0:rows, :], reads=[("yt", b)])
            fence()
    return nc


_NC_CACHE = {}


def _consts():
    ident = np.eye(128, dtype=np.float32)
    blk = np.zeros((128, 128), np.float32)
    blk[:64, :64] = 1.0
    blk[64:, 64:] = 1.0
    slopes = np.exp2(-8.0 * np.arange(1, 9, dtype=np.float64) / 8.0)
    j = np.arange(128)[:, None]
    i = np.arange(128)[None, :]
    biasp = np.zeros((128, 8, 256), np.float32)
    for t in range(4):
        for hp in range(2):
            h = t + 4 * hp
            d_prev = 128 + i - j
            d_cur = i - j
            bp = np.where(d_prev <= 128, -slopes[h] * d_prev, -30000.0)
            bc = np.where(d_cur >= 0, -slopes[h] * d_cur, -30000.0)
            biasp[:, t * 2 + hp, 0:128] = bp
            biasp[:, t * 2 + hp, 128:256] = bc
    biasc = np.zeros((128, 32), np.float32)
    biasn = np.zeros((4, 32), np.float32)
    for kv in range(2):
        for hq in range(4):
            h = kv * 4 + hq
            for qi in range(4):
                col = kv * 16 + hq * 4 + qi
                jj = np.arange(128)
                dist = 128 + qi - jj
                biasc[:, col] = np.where(jj >= qi, -slopes[h] * dist, -30000.0)
                jn = np.arange(4)
                dn = qi - jn
                biasn[:, col] = np.where(dn >= 0, -slopes[h] * dn, -30000.0)
    biasnf = np.full((64, 512), -30000.0, np.float32)
    for ip in range(4):
        for slp in range(16):
            r = ip * 16 + slp
            for kv in range(2):
                for hq in range(4):
                    h = kv * 4 + hq
                    for qi in range(ip, 4):
                        biasnf[r, kv * 256 + hq * 64 + qi * 16 + slp] = -slopes[h] * (qi - ip)
    cmask = np.zeros((128, 128), np.float32)
    for r in range(128):
        glp = (r % 32) // 16
        cmask[r, glp * 64:(glp + 1) * 64] = 1.0
    rmask = np.zeros((128, 4), np.float32)
    for r in range(128):
        rmask[r, r // 32] = 1.0
    return dict(c_ident=ident, c_blk64=blk, c_biasp=biasp, c_biasc=biasc, c_biasn=biasn, c_biasnf=biasnf,
                c_cmask=cmask, c_rmask=rmask)


def kernel(**inp):
    f = lambda a: np.ascontiguousarray(np.asarray(a), dtype=np.float32)
    qperm = np.concatenate([np.arange(h * 64, (h + 1) * 64) for h in HEAD_PERM])
    w_in = f(inp['w_in']).copy()
    w_in[:, :, 0:512] = w_in[:, :, qperm]
    w_ao = f(inp['w_attn_o'])[:, qperm, :]
    qg = f(inp['q_norm_g'])
    kg = f(inp['k_norm_g'])
    qk_g = np.stack([np.concatenate([qg, qg], axis=1), np.concatenate([kg, kg], axis=1)], axis=1)
    sk = f(inp['attn_sinks'])
    sinks = np.zeros((L, 4, 128), np.float32)
    for t in range(4):
        sinks[:, t, 0:64] = sk[:, t][:, None]
        sinks[:, t, 64:128] = sk[:, t + 4][:, None]
    shared = dict(
        w_in=np.ascontiguousarray(w_in), w_glu=f(inp['w_glu']), w_ao=np.ascontiguousarray(w_ao), w_so=f(inp['w_ssm_o']),
        w_out=f(inp['w_out']), w_up=f(inp['w_up']), w_down=f(inp['w_down']),
        norm1_g=f(inp['norm1_g']), norm2_g=f(inp['norm2_g']), b_gate=f(inp['b_gate']),
        qk_g=np.ascontiguousarray(qk_g), sinks=sinks,
        lam_re=f(inp['lam_re']), lam_im=f(inp['lam_im']), log_step=f(inp['log_step']),
        b_re=f(inp['b_re']).reshape(L, 2048, 16), b_im=f(inp['b_im']).reshape(L, 2048, 16),
        c_re=f(inp['c_re']).reshape(L, 512, 64), c_im=f(inp['c_im']).reshape(L, 512, 64),
        d_skip=f(inp['d_skip']), b_glu=f(inp['b_glu']),
    )
    shared.update(_consts())
    x_prompt = f(inp['x_prompt'])
    x_sample = f(inp['x_sample'])
    ck = f(inp['cache_k']).reshape(L, 128, 128, 128)
    cv = f(inp['cache_v']).reshape(L, 128, 128, 128)
    sre = f(inp['state_ssm_re']).reshape(L, 128, 2048)
    sim = f(inp['state_ssm_im']).reshape(L, 128, 2048)
    in_maps = []
    for c in range(NCORES):
        m = dict(shared)
        m['xp'] = x_prompt[c]
        m['xs'] = np.ascontiguousarray(x_sample[c * 16:(c + 1) * 16].transpose(1, 0, 2).reshape(NS, D))
        m['cache_k'] = np.ascontiguousarray(ck[:, c * 16:(c + 1) * 16])
        m['cache_v'] = np.ascontiguousarray(cv[:, c * 16:(c + 1) * 16])
        m['st_re'] = np.ascontiguousarray(sre[:, c * 16:(c + 1) * 16])
        m['st_im'] = np.ascontiguousarray(sim[:, c * 16:(c + 1) * 16])
        in_maps.append(m)
    if 'nc' not in _NC_CACHE:
        _NC_CACHE['nc'] = build_program()
    ncr = DEBUG_CORES or NCORES
    res = run_bass_kernel_spmd(_NC_CACHE['nc'], in_maps[:ncr], core_ids=list(range(ncr)))
    R = list(res.results)
    _NC_CACHE['last'] = R
    while len(R) < NCORES:
        R.append(R[0])
    y_prompt = np.stack([R[c]['yp'] for c in range(NCORES)]).astype(np.float32)
    y_sample = np.concatenate([R[c]['ys'].reshape(4, 16, D).transpose(1, 0, 2) for c in range(NCORES)]).astype(np.float32)
    k_prompt = np.stack([R[c]['kp'] for c in range(NCORES)], axis=1).reshape(L, 8, 128, 2, 64)
    v_prompt = np.stack([R[c]['vp'] for c in range(NCORES)], axis=1).reshape(L, 8, 128, 2, 64)
    hr_p = np.stack([R[c]['hrp'] for c in range(NCORES)], axis=1).reshape(L, 8, 32, 64)
    hi_p = np.stack([R[c]['hip'] for c in range(NCORES)], axis=1).reshape(L, 8, 32, 64)
    k_s = np.concatenate([R[c]['ksam'] for c in range(NCORES)], axis=1).reshape(L, 128, 128, 2, 64)
    v_s = np.concatenate([R[c]['vsam'] for c in range(NCORES)], axis=1).reshape(L, 128, 128, 2, 64)
    hr_s = np.concatenate([R[c]['hrs'] for c in range(NCORES)], axis=1).reshape(L, 128, 32, 64)
    hi_s = np.concatenate([R[c]['his'] for c in range(NCORES)], axis=1).reshape(L, 128, 32, 64)
    return (y_prompt, y_sample, k_prompt.astype(np.float32), v_prompt.astype(np.float32),
            hr_p.astype(np.float32), hi_p.astype(np.float32), k_s.astype(np.float32), v_s.astype(np.float32),
            hr_s.astype(np.float32), hi_s.astype(np.float32))
```

```python
import math
import numpy as np
import ml_dtypes
from contextlib import ExitStack
import concourse.bass as bass
import concourse.mybir as mybir
from concourse.bass_utils import run_bass_kernel_spmd

F32 = mybir.dt.float32
BF16 = mybir.dt.bfloat16
I32 = mybir.dt.int32
AF = mybir.ActivationFunctionType
ALU = mybir.AluOpType

NCORES = 8
D = 1024
T = 2048
NS = 64
NT = T + NS
L = 2
EPS = 1e-6
HEAD_PERM = [0, 4, 1, 5, 2, 6, 3, 7]
STAGE = 99
NLAYERS = 2
CHUNK = 256
NCHUNK = 512 // CHUNK
DEBUG_DUMP = False
DEBUG_CORES = 0
SKIP = set()


class Stream:
    def __init__(self, nc, stack, name):
        self.sem = stack.enter_context(nc.semaphore(name))
        self.name = name
        self.cnt = 0


class Ctx:
    def __init__(self, nc, stack):
        self.nc = nc
        self.engs = {'pe': nc.tensor, 'act': nc.scalar, 'dve': nc.vector, 'pool': nc.gpsimd, 'sp': nc.sync}
        self.st = {n: Stream(nc, stack, 's_' + n) for n in self.engs}
        self.waited = {n: {} for n in self.engs}
        self.lw = {}
        self.rd = {}
        self.stack = stack
        self.nstream = 0

    def stream(self, name):
        s = Stream(self.nc, self.stack, name)
        self.st[name] = s
        return name

    def _deps(self, reads, writes):
        deps = {}

        def add(tok):
            if tok is None:
                return
            s, v = tok
            if deps.get(s, 0) < v:
                deps[s] = v
        for k in reads:
            add(self.lw.get(k))
        for k in writes:
            add(self.lw.get(k))
            for s, v in self.rd.get(k, {}).items():
                add((s, v))
        return deps

    def _wait(self, en, deps):
        e = self.engs[en]
        w = self.waited[en]
        for s, v in deps.items():
            if s == en and en in ('pe', 'sp'):
                continue
            if w.get(s, 0) >= v:
                continue
            e.wait_ge(self.st[s].sem, v)
            w[s] = v

    def _record(self, tok, reads, writes):
        s, v = tok
        for k in reads:
            d = self.rd.setdefault(k, {})
            if d.get(s, 0) < v:
                d[s] = v
        for k in writes:
            self.lw[k] = tok
            self.rd[k] = {}

    def op(self, en, fn, reads=(), writes=(), signal=True):
        psr = [k for k in reads if isinstance(k, tuple) and k[0] == 'ps']
        if psr:
            writes = list(writes) + psr
        self._wait(en, self._deps(reads, writes))
        inst = fn(self.engs[en])
        st = self.st[en]
        if signal:
            st.cnt += 1
            inst.then_inc(st.sem, 1)
            tok = (en, st.cnt)
        else:
            tok = (en, st.cnt + 1)
        self._record(tok, reads, writes)
        return tok

    def dma(self, q, stream, out, in_, reads=(), writes=(), **kw):
        self._wait(q, self._deps(reads, writes))
        st = self.st[stream]
        st.cnt += 16
        self.engs[q].dma_start(out=out, in_=in_, **kw).then_inc(st.sem, 16)
        tok = (stream, st.cnt)
        self._record(tok, reads, writes)
        return tok

    def fence(self):
        for en, e in self.engs.items():
            w = self.waited[en]
            for s, st in self.st.items():
                if s == en or st.cnt == 0:
                    continue
                if w.get(s, 0) >= st.cnt:
                    continue
                e.wait_ge(st.sem, st.cnt)
                w[s] = st.cnt
        self.lw = {}
        self.rd = {}


def build_program():
    nc = bass.Bass("TRN2", target_bir_lowering=False)

    def din(name, shape, dt=F32):
        return nc.dram_tensor(name, list(shape), dt, kind="ExternalInput").ap()

    def dout(name, shape, dt=F32):
        return nc.dram_tensor(name, list(shape), dt, kind="ExternalOutput").ap()

    xp = din("xp", [T, D])
    xs = din("xs", [NS, D])
    cache_k = din("cache_k", [L, 16, 128, 128])
    cache_v = din("cache_v", [L, 16, 128, 128])
    st_re = din("st_re", [L, 16, 2048])
    st_im = din("st_im", [L, 16, 2048])
    w_in = din("w_in", [L, D, 3328])
    w_glu = din("w_glu", [L, 512, 512])
    w_ao = din("w_ao", [L, 512, D])
    w_so = din("w_so", [L, 512, D])
    w_out = din("w_out", [L, D, D])
    w_up = din("w_up", [L, D, 4096])
    w_down = din("w_down", [L, 4096, D])
    pvec_a = din("pvec_a", [92, 128])
    pvec_b = din("pvec_b", [L, 48, 128])
    norm1_g = din("norm1_g", [L, D])
    norm2_g = din("norm2_g", [L, D])
    b_gate = din("b_gate", [L, 2048])
    qk_g = din("qk_g", [L, 2, 128])
    sinks = din("sinks", [L, 4, 128])
    lam_re = din("lam_re", [L, 32, 64])
    lam_im = din("lam_im", [L, 32, 64])
    log_step = din("log_step", [L, 32])
    b_re = din("b_re", [L, 2048, 16])
    b_im = din("b_im", [L, 2048, 16])
    c_re = din("c_re", [L, 512, 64])
    c_im = din("c_im", [L, 512, 64])
    d_skip = din("d_skip", [L, 512])
    b_glu = din("b_glu", [L, 512])
    c_ident = din("c_ident", [128, 128])
    c_blk64 = din("c_blk64", [128, 128])
    c_biasp = din("c_biasp", [128, 8, 256])
    c_biasc = din("c_biasc", [128, 32])
    c_biasn = din("c_biasn", [4, 32])
    c_biasnf = din("c_biasnf", [64, 512])
    c_cmask = din("c_cmask", [128, 128])
    c_rmask = din("c_rmask", [128, 4])

    yp = dout("yp", [T, D])
    ys = dout("ys", [NS, D])
    kp = dout("kp", [L, 128, 128])
    vp = dout("vp", [L, 128, 128])
    hrp = dout("hrp", [L, 16, 128])
    hip = dout("hip", [L, 16, 128])
    ksam = dout("ksam", [L, 16, 128, 128])
    vsam = dout("vsam", [L, 16, 128, 128])
    hrs = dout("hrs", [L, 16, 16, 128])
    his = dout("his", [L, 16, 16, 128])
    if DEBUG_DUMP:
        dbg_at = dout("dbg_at", [4, 128, 4, 576], BF16)
        dbg_st = dout("dbg_st", [4, 128, 4, 576], BF16)
        dbg_yg = dout("dbg_yg", [4, 128, 4, 576], BF16)
        dbg_mix = dout("dbg_mix", [4, 128, 8, 576], BF16)

    stack = ExitStack()
    with stack:
        C = Ctx(nc, stack)
        op, dma, fence = C.op, C.dma, C.fence

        def sb(name, shape, dt=F32):
            return stack.enter_context(nc.sbuf_tensor(name, list(shape), dt))

        X = sb("X", [128, 8, NT])
        PSUM = stack.enter_context(nc.psum_tensor("PS", [128, 8, 512], F32))
        ident = sb("ident", [128, 128])
        ones = sb("ones", [128, 128], BF16)
        blk64 = sb("blk64", [128, 128], BF16)
        EB = sb("EB", [128, 8, 256], BF16)
        biasc = sb("biasc", [128, 32])
        biasnf = sb("biasnf", [64, 512])
        cmask = sb("cmask", [128, 128])
        rmask = sb("rmask", [128, 4])
        PVA = sb("PVA", [128, 92])
        g1 = PVA[:, 0:16].rearrange("p (l k) -> p l k", l=L)
        g2 = PVA[:, 16:32].rearrange("p (l k) -> p l k", l=L)
        bg = PVA[:, 32:64].rearrange("p (l k) -> p l k", l=L)
        qkg = PVA[:, 64:68].rearrange("p (l k) -> p l k", l=L)
        esink = PVA[:, 68:76].rearrange("p (l k) -> p l k", l=L)
        dsk = PVA[:, 76:84].rearrange("p (l k) -> p l k", l=L)
        bgl = PVA[:, 84:92].rearrange("p (l k) -> p l k", l=L)
        WR = [sb("wr%d" % i, [128, 4096], BF16) for i in range(3)]
        wr_stream = [C.stream("wrs%d" % i) for i in range(3)]
        wr_i = [0]
        PHc = sb("PHc", [128, 16, CHUNK + 1], BF16)
        PHs = sb("PHs", [128, 16, CHUNK + 1], BF16)
        MAG = sb("MAG", [128, 16])
        AR = sb("AR", [128, 16])
        AI = sb("AI", [128, 16])
        R128c = sb("R128c", [128, 16])
        R128s = sb("R128s", [128, 16])
        nR128s = sb("nR128s", [128, 16])
        P127c = sb("P127c", [128, 16])
        P127s = sb("P127s", [128, 16])
        BTr = sb("BTr", [128, 16, 128], BF16)
        BTi = sb("BTi", [128, 16, 128], BF16)
        CTr = sb("CTr", [128, 16, 128], BF16)
        nCTi = sb("nCTi", [128, 16, 128], BF16)
        nCTr = sb("nCTr", [128, 16, 128], BF16)
        GE2 = sb("GE2", [128, 16, 2])
        SS = sb("SS", [128, 16, 2])

        s_par = C.stream("par")
        s_out = C.stream("outs")
        ps_i = [0]

        def psb(n=1):
            i = ps_i[0]
            if i + n > 8:
                i = 0
            ps_i[0] = (i + n) % 8
            return i

        def pk(i, n=1):
            return [("ps", j) for j in range(i, i + n)]

        class Ring:
            def __init__(self, bufs, streams, keys):
                self.bufs, self.streams, self.keys = bufs, streams, keys
                self.i = 0

        G3 = Ring(WR, wr_stream, [("wr", i) for i in range(3)])

        def wload(src_ap, view_shape, key, ring=None):
            ring = ring or G3
            i = ring.i % len(ring.bufs)
            ring.i += 1
            buf = ring.bufs[i]
            n = 1
            for s_ in view_shape[1:]:
                n *= s_
            flat = buf[:, 0:n]
            if len(view_shape) == 3:
                view = flat.rearrange("p (k n) -> p k n", k=view_shape[1])
            else:
                view = flat
            dma('pool', ring.streams[i], view, src_ap, writes=[ring.keys[i]])
            return view, ring.keys[i]

        class WQ:
            def __init__(self, specs, ring=None, ahead=1):
                self.specs = specs
                self.loaded = []
                self.ring = ring
                self.ahead = ahead

            def get(self, i):
                upto = min(i + self.ahead, len(self.specs) - 1)
                while len(self.loaded) <= upto:
                    src, shape = self.specs[len(self.loaded)]
                    self.loaded.append(wload(src, shape, None, self.ring))
                return self.loaded[i]

        def bc_mid(ap2, n):
            return ap2.unsqueeze(1).to_broadcast([ap2.shape[0], n, ap2.shape[1]])

        def bc_last(ap2, n):
            return ap2.unsqueeze(2).to_broadcast([ap2.shape[0], ap2.shape[1], n])

        with nc.allow_non_contiguous_dma(reason="small param loads"):
            for (dst, src) in [
                (ident[:], c_ident[:, :]),
                (biasc[:], c_biasc[:, :]), (biasnf[:], c_biasnf[:, :]), (cmask[:], c_cmask[:, :]),
                (rmask[:], c_rmask[:, :]),
            ]:
                dma('sp', s_par, dst, src, writes=["par"])
        with ExitStack() as phA:
            PST = phA.enter_context(nc.sbuf_tensor("pst", [92, 128], F32))
            s_pa = C.stream("pva")
            dma('sp', s_pa, PST[:, :], pvec_a[:, :], writes=["PST"])
            op('pe', lambda e: e.transpose(out=PSUM[:, 0, 0:92], in_=PST[0:92, :], identity=ident[0:92, 0:92]),
               reads=["PST", "par"], writes=pk(0))
            op('act', lambda e: e.activation(out=PVA[:, :], in_=PSUM[:, 0, 0:92], func=AF.Copy), reads=pk(0), writes=["par"])
            fence()
        s_b64 = C.stream("b64")
        dma('pool', s_b64, blk64[:], c_blk64[:, :], writes=["blk64"])
        op('dve', lambda e: e.memset(ones[:], 1.0), writes=["ones"])
        with ExitStack() as ph0:
            biasp = ph0.enter_context(nc.sbuf_tensor("biasp", [128, 8, 256], F32))
            s_bp = C.stream("bp")
            dma('sp', s_bp, biasp[:], c_biasp[:, :, :], writes=["biasp"])
            op('act', lambda e: e.activation(out=EB[:], in_=biasp[:], func=AF.Exp), reads=["biasp"], writes=["EB"])
            fence()
        op('act', lambda e: e.activation(out=esink, in_=esink, func=AF.Exp), reads=["par"], writes=["esink"])
        op('dve', lambda e: e.tensor_scalar(out=qkg[:, :, 0:1], in0=qkg[:, :, 0:1], scalar1=0.125, scalar2=None,
                                            op0=ALU.mult), reads=["par"], writes=["qkg"])
        fence()

        def phase0():
          with ExitStack() as ph:
              XT = [ph.enter_context(nc.sbuf_tensor("xt%d" % i, [128, D], F32)) for i in range(2)]
              xts = [C.stream("xts%d" % i) for i in range(2)]
              for tt in range(17):
                  b = tt % 2
                  rows = 128 if tt < 16 else NS
                  src = xp[tt * 128:(tt + 1) * 128, :] if tt < 16 else xs[:, :]
                  dma('sp', xts[b], XT[b][0:rows, :], src, writes=[("xt", b)])
                  for half in range(2):
                      pi = psb()
                      for j in range(4):
                          k = half * 4 + j
                          op('pe', lambda e, k=k, j=j, pi=pi, b=b, rows=rows: e.transpose(
                              out=PSUM[:, pi, j * 128:j * 128 + rows], in_=XT[b][0:rows, k * 128:(k + 1) * 128],
                              identity=ident[0:rows, 0:rows]),
                              reads=[("xt", b)], writes=pk(pi), signal=(j == 3))
                      eng = 'act'
                      src_ps = PSUM[:, pi, :].rearrange("p (j t) -> p j t", j=4)[:, :, 0:rows]
                      dst = X[:, half * 4:half * 4 + 4, tt * 128:tt * 128 + rows]
                      if eng == 'act':
                          op('act', lambda e, s_=src_ps, d_=dst: e.activation(out=d_, in_=s_, func=AF.Copy),
                             reads=pk(pi), writes=[("X", tt)])
                      else:
                          op('dve', lambda e, s_=src_ps, d_=dst: e.tensor_copy(out=d_, in_=s_),
                             reads=pk(pi), writes=[("X", tt)])
              fence()

        TWO_PI = 2.0 * math.pi

        def tt(en, out, a, b, o, reads, writes):
            return op(en, lambda e: e.tensor_tensor(out=out, in0=a, in1=b, op=o), reads=reads, writes=writes)

        def ssm_tables(l, mid=None):
            with ExitStack() as ph:
                def t(name, shape, dt=F32):
                    return ph.enter_context(nc.sbuf_tensor("st%d_" % l + name, list(shape), dt))
                LR = t("LR", [128, 16]); LI = t("LI", [128, 16]); LS = t("LS", [128, 16])
                ANG = t("ANG", [128, 32]); KI = t("KI", [128, 32], I32); KF = t("KF", [128, 32])
                M1 = t("M1", [128, 32]); SC = t("SC", [128, 32])
                T1 = t("T1", [128, 16]); T2 = t("T2", [128, 16]); T3 = t("T3", [128, 16])
                CR = t("CR", [128, 16]); CI = t("CI", [128, 16]); RDEN = t("RDEN", [128, 16])
                Pc = t("Pc", [128, 16, CHUNK + 1]); Ps = t("Ps", [128, 16, CHUNK + 1])
                Q1 = t("Q1", [128, 16, CHUNK // 2]); Q2 = t("Q2", [128, 16, CHUNK // 2])
                BNr = t("BNr", [128, 16, 16]); BNi = t("BNi", [128, 16, 16])
                BBr = t("BBr", [128, 16, 16]); BBi = t("BBi", [128, 16, 16]); BT1 = t("BT1", [128, 16, 16])
                BEr = t("BEr", [128, 16, 32]); BEi = t("BEi", [128, 16, 32])
                CNr = t("CNr", [128, 4, 64]); CNi = t("CNi", [128, 4, 64])
                CE = t("CE", [128, 128])
                sp_ = C.stream("sst%d" % l)
                with nc.allow_non_contiguous_dma(reason="ssm params"):
                    PBT = t("PBT", [48, 128])
                    dma('sp', sp_, PBT[:, :], pvec_b[l], writes=["PBT"])
                    op('pe', lambda e: e.transpose(out=PSUM[:, 0, 0:48], in_=PBT[0:48, :], identity=ident[0:48, 0:48]),
                       reads=["PBT"], writes=pk(0))
                    op('act', lambda e: e.activation(out=LR[:], in_=PSUM[:, 0, 0:16], func=AF.Copy), reads=pk(0), writes=["sp"])
                    op('act', lambda e: e.activation(out=LI[:], in_=PSUM[:, 0, 16:32], func=AF.Copy), reads=pk(0), writes=["sp"])
                    op('act', lambda e: e.activation(out=LS[:], in_=PSUM[:, 0, 32:48], func=AF.Copy), reads=pk(0), writes=["sp"])
                    dma('sp', sp_, BNr[:], b_re[l].rearrange("(s gl p) c -> (gl p) s c", gl=2, p=64), writes=["sp"])
                    dma('sp', sp_, BNi[:], b_im[l].rearrange("(s gl p) c -> (gl p) s c", gl=2, p=64), writes=["sp"])
                    dma('sp', sp_, CNr[:], c_re[l].rearrange("(q r) p -> r q p", r=128), writes=["sp"])
                    dma('sp', sp_, CNi[:], c_im[l].rearrange("(q r) p -> r q p", r=128), writes=["sp"])
                R = ["sp"]
                op('act', lambda e: e.activation(out=LS[:], in_=LS[:], func=AF.Exp), reads=R, writes=["LS"])
                tt('dve', T1[:], LR[:], LS[:], ALU.mult, R + ["LS"], ["T1"])
                tt('dve', ANG[:, 0:16], LI[:], LS[:], ALU.mult, R + ["LS"], ["ANG"])
                op('act', lambda e: e.activation(out=MAG[:], in_=T1[:], func=AF.Exp), reads=["T1"], writes=["MAG"])
                op('dve', lambda e: e.tensor_scalar(out=ANG[:, 16:32], in0=ANG[:, 0:16], scalar1=math.pi / 2, scalar2=None,
                                                    op0=ALU.add), reads=["ANG"], writes=["ANG"])
                op('dve', lambda e: e.tensor_scalar(out=KF[:], in0=ANG[:], scalar1=1.0 / TWO_PI, scalar2=None, op0=ALU.mult),
                   reads=["ANG"], writes=["KF"])
                op('dve', lambda e: e.tensor_copy(out=KI[:], in_=KF[:]), reads=["KF"], writes=["KI"])
                op('dve', lambda e: e.tensor_copy(out=KF[:], in_=KI[:]), reads=["KI"], writes=["KF"])
                op('dve', lambda e: e.scalar_tensor_tensor(out=ANG[:], in0=KF[:], scalar=-TWO_PI, in1=ANG[:], op0=ALU.mult,
                                                           op1=ALU.add), reads=["KF", "ANG"], writes=["ANG"])
                op('dve', lambda e: e.tensor_single_scalar(out=M1[:], in_=ANG[:], scalar=math.pi, op=ALU.is_gt),
                   reads=["ANG"], writes=["M1"])
                op('dve', lambda e: e.scalar_tensor_tensor(out=ANG[:], in0=M1[:], scalar=-TWO_PI, in1=ANG[:], op0=ALU.mult,
                                                           op1=ALU.add), reads=["M1", "ANG"], writes=["ANG"])
                op('dve', lambda e: e.tensor_single_scalar(out=M1[:], in_=ANG[:], scalar=-math.pi, op=ALU.is_lt),
                   reads=["ANG"], writes=["M1"])
                op('dve', lambda e: e.scalar_tensor_tensor(out=ANG[:], in0=M1[:], scalar=TWO_PI, in1=ANG[:], op0=ALU.mult,
                                                           op1=ALU.add), reads=["M1", "ANG"], writes=["ANG"])
                op('act', lambda e: e.activation(out=SC[:], in_=ANG[:], func=AF.Sin), reads=["ANG"], writes=["SC"])
                SN = SC[:, 0:16]
                CS = SC[:, 16:32]
                tt('dve', AR[:], MAG[:], CS, ALU.mult, ["MAG", "SC"], ["AR"])
                tt('dve', AI[:], MAG[:], SN, ALU.mult, ["MAG", "SC"], ["AI"])
                tt('dve', T1[:], LR[:], LR[:], ALU.mult, R + ["MAG"], ["T1"])
                tt('dve', T2[:], LI[:], LI[:], ALU.mult, R, ["T2"])
                tt('dve', T1[:], T1[:], T2[:], ALU.add, ["T1", "T2"], ["T1"])
                op('dve', lambda e: e.reciprocal(out=RDEN[:], in_=T1[:]), reads=["T1"], writes=["RDEN"])
                op('dve', lambda e: e.tensor_scalar(out=T3[:], in0=AR[:], scalar1=-1.0, scalar2=None, op0=ALU.add),
                   reads=["AR"], writes=["T3"])
                tt('dve', T1[:], T3[:], LR[:], ALU.mult, ["T3", "RDEN"], ["T1"])
                tt('dve', T2[:], AI[:], LI[:], ALU.mult, ["AI", "T1"], ["T2"])
                tt('dve', T1[:], T1[:], T2[:], ALU.add, ["T1", "T2"], ["T1"])
                tt('dve', CR[:], T1[:], RDEN[:], ALU.mult, ["T1", "RDEN"], ["CR"])
                tt('dve', T1[:], AI[:], LR[:], ALU.mult, ["CR", "AI"], ["T1"])
                tt('dve', T2[:], T3[:], LI[:], ALU.mult, ["T3", "CR"], ["T2"])
                tt('dve', T1[:], T1[:], T2[:], ALU.subtract, ["T1", "T2"], ["T1"])
                tt('dve', CI[:], T1[:], RDEN[:], ALU.mult, ["T1", "RDEN"], ["CI"])
                op('dve', lambda e: e.memset(Pc[:, :, 0:1], 1.0), writes=["P"])
                op('dve', lambda e: e.memset(Ps[:, :, 0:1], 0.0), writes=["P"])
                op('dve', lambda e: e.tensor_copy(out=Pc[:, :, 1:2], in_=CS.unsqueeze(2)), reads=["SC", "P"], writes=["P"])
                op('dve', lambda e: e.tensor_copy(out=Ps[:, :, 1:2], in_=SN.unsqueeze(2)), reads=["SC", "P"], writes=["P"])
                m = 1
                while m < CHUNK:
                    Ac = Pc[:, :, 1:m + 1]
                    As = Ps[:, :, 1:m + 1]
                    Bc = Pc[:, :, m:m + 1].to_broadcast([128, 16, m])
                    Bs = Ps[:, :, m:m + 1].to_broadcast([128, 16, m])
                    q1 = Q1[:, :, 0:m]
                    q2 = Q2[:, :, 0:m]
                    tt('dve', q1, Ac, Bc, ALU.mult, ["P"], ["Q1"])
                    tt('dve', q2, As, Bs, ALU.mult, ["P"], ["Q2"])
                    tt('dve', Pc[:, :, m + 1:2 * m + 1], q1, q2, ALU.subtract, ["Q1", "Q2", "P"], ["P"])
                    tt('dve', q1, Ac, Bs, ALU.mult, ["P"], ["Q1"])
                    tt('dve', q2, As, Bc, ALU.mult, ["P"], ["Q2"])
                    tt('dve', Ps[:, :, m + 1:2 * m + 1], q1, q2, ALU.add, ["Q1", "Q2", "P"], ["P"])
                    m *= 2
                op('dve', lambda e: e.tensor_copy(out=PHc[:], in_=Pc[:]), reads=["P"], writes=["PH"])
                op('dve', lambda e: e.tensor_copy(out=PHs[:], in_=Ps[:]), reads=["P"], writes=["PH"])
                op('dve', lambda e: e.tensor_copy(out=R128c[:], in_=Pc[:, :, CHUNK]), reads=["P"], writes=["R128"])
                op('dve', lambda e: e.tensor_copy(out=R128s[:], in_=Ps[:, :, CHUNK]), reads=["P"], writes=["R128"])
                op('dve', lambda e: e.tensor_scalar(out=nR128s[:], in0=Ps[:, :, CHUNK], scalar1=-1.0, scalar2=None,
                                                    op0=ALU.mult), reads=["P"], writes=["R128"])
                op('dve', lambda e: e.tensor_copy(out=P127c[:], in_=Pc[:, :, CHUNK - 1]), reads=["P"], writes=["R128"])
                op('dve', lambda e: e.tensor_copy(out=P127s[:], in_=Ps[:, :, CHUNK - 1]), reads=["P"], writes=["R128"])
                CRb = bc_last(CR[:], 16)
                CIb = bc_last(CI[:], 16)
                tt('dve', BBr[:], BNr[:], CRb, ALU.mult, R + ["CR"], ["BBr"])
                tt('dve', BT1[:], BNi[:], CIb, ALU.mult, R + ["CI"], ["BT1"])
                tt('dve', BBr[:], BBr[:], BT1[:], ALU.subtract, ["BBr", "BT1"], ["BBr"])
                tt('dve', BBi[:], BNi[:], CRb, ALU.mult, R + ["CR"], ["BBi"])
                tt('dve', BT1[:], BNr[:], CIb, ALU.mult, R + ["CI", "BBr"], ["BT1"])
                tt('dve', BBi[:], BBi[:], BT1[:], ALU.add, ["BBi", "BT1"], ["BBi"])
                if mid is not None:
                    mid()
                for (BE, BB, BTd, nm) in ((BEr, BBr, BTr, "r"), (BEi, BBi, BTi, "i")):
                    op('dve', lambda e, BE=BE: e.memset(BE[:], 0.0), writes=["BE" + nm])
                    op('dve', lambda e, BE=BE, BB=BB: e.tensor_copy(out=BE[0:64, :, 0:16], in_=BB[0:64, :, :]),
                       reads=["BB" + nm, "BE" + nm], writes=["BE" + nm])
                    op('dve', lambda e, BE=BE, BB=BB: e.tensor_copy(out=BE[64:128, :, 16:32], in_=BB[64:128, :, :]),
                       reads=["BB" + nm, "BE" + nm], writes=["BE" + nm])
                    for q in range(4):
                        pi = psb()
                        op('pe', lambda e, BE=BE, q=q, pi=pi: e.transpose(
                            out=PSUM[:, pi, 0:128], in_=BE[:, 4 * q:4 * q + 4, :].rearrange("p a b -> p (a b)"),
                            identity=ident[:, :]), reads=["BE" + nm], writes=pk(pi))
                        for slot in range(4):
                            op('dve', lambda e, BTd=BTd, q=q, slot=slot, pi=pi: e.tensor_scalar(
                                out=BTd[:, 4 * q + slot, :], in0=PSUM[:, pi, 0:128], scalar1=rmask[:, slot:slot + 1],
                                scalar2=None, op0=ALU.mult), reads=pk(pi), writes=["BT" + nm])
                for (CN, CTd, sgn, nm) in ((CNr, CTr, 1.0, "r"), (CNi, nCTi, -1.0, "i"), (CNr, nCTr, -1.0, "r2")):
                    op('dve', lambda e, CTd=CTd: e.memset(CTd[:], 0.0), writes=["CT" + nm])
                    for q in range(4):
                        tt('dve', CE[:, 0:64], CN[:, q, :], cmask[:, 0:64], ALU.mult, R, ["CE"])
                        tt('dve', CE[:, 64:128], CN[:, q, :], cmask[:, 64:128], ALU.mult, R, ["CE"])
                        pi = psb()
                        op('pe', lambda e, pi=pi: e.transpose(out=PSUM[:, pi, 0:128], in_=CE[:, :], identity=ident[:, :]),
                           reads=["CE"], writes=pk(pi))
                        for slot in range(4):
                            op('dve', lambda e, CTd=CTd, q=q, slot=slot, pi=pi, sgn=sgn: e.tensor_scalar(
                                out=CTd[:, 4 * q + slot, 32 * slot:32 * slot + 32], in0=PSUM[:, pi, 32 * slot:32 * slot + 32],
                                scalar1=sgn, scalar2=None, op0=ALU.mult), reads=pk(pi), writes=["CT" + nm])
                op('dve', lambda e: e.memset(GE2[:], 0.0), writes=["GE"])
                op('dve', lambda e: e.tensor_copy(out=SS[:, :, 0], in_=nR128s[:, :]), reads=["R128"], writes=["SS"])
                op('dve', lambda e: e.tensor_copy(out=SS[:, :, 1], in_=R128s[:, :]), reads=["R128"], writes=["SS"])
                fence()

        def mm_group(pi, n, pairs, reads, extra_writes=()):
            last = len(pairs) - 1
            tok = None
            for i, (lh, rh) in enumerate(pairs):
                tok = op('pe', lambda e, lh=lh, rh=rh, i=i: e.matmul(PSUM[:, pi, 0:n], lhsT=lh, rhs=rh, start=(i == 0),
                                                                     stop=(i == last)),
                         reads=reads, writes=pk(pi) + list(extra_writes), signal=(i == last))
            return tok

        def layer_phase_a(l):
            with ExitStack() as ph:
                def t(name, shape, dt=BF16):
                    return ph.enter_context(nc.sbuf_tensor("a%d_" % l + name, list(shape), dt))
                XN = t("XN", [128, 8, 576])
                MIX = t("MIX", [128, 8, 576])
                R2 = t("R2", [128, 4, 576])
                KT = t("KT", [128, 128 + 576])
                VT = t("VT", [128, 5, 128])
                VS = t("VS", [64, 128])
                U = t("U", [128, 4, 576])
                AT = t("AT", [128, 4, 576])
                STt = t("ST", [128, 4, 576])
                YG = t("YG", [128, 4, 576])
                s_o = C.stream("ao%d" % l)
                s_o1 = C.stream("ao1_%d" % l)
                s_o2 = C.stream("ao2_%d" % l)

                for bi in range(4):
                  c0 = bi * 512
                  subs = [(0, 512)] + ([(512, 64)] if bi == 3 else [])
                  NB = 576 if bi == 3 else 512
                  with ExitStack() as ph2:
                    def t2(name, shape, dt=BF16):
                        return ph2.enter_context(nc.sbuf_tensor("a%d_%d_" % (l, bi) + name, list(shape), dt))
                    SQ = t2("SQ", [128, 8, 512])
                    RS = t2("RS", [128, 512], F32)
                    XG = t2("XG", [128, 2, 512], F32)
                    RSq = t2("RSq", [128, 512], F32)
                    SQh = t2("SQh", [128, 512])
                    KF = t2("KF", [128, 128], F32)
                    KSF = t2("KSF", [128, 64], F32)
                    OUTS = t2("OUTS", [128, 128], F32)
                    OUTS2 = t2("OUTS2", [128, 128], F32)
                    wq_a2 = WQ([(w_in[l][:, ch_ * 512:ch_ * 512 + (512 if ch_ < 2 else 256)].rearrange("(k p) n -> p k n", p=128),
                                 [128, 8, (512 if ch_ < 2 else 256)]) for ch_ in range(3)])
                    wq_a2.get(0)
                    for (o, n) in subs:
                        xs_ = X[:, :, c0 + o:c0 + o + n]
                        op('act', lambda e, xs_=xs_, n=n: e.activation(out=SQ[:, :, 0:n], in_=xs_, func=AF.Square),
                           reads=["X"], writes=["SQ"])
                        if 'a1a' in SKIP:
                            continue
                        pi = psb()
                        mm_group(pi, n, [(ones[:, :], SQ[:, k, 0:n]) for k in range(8)], ["SQ", "ones"])
                        op('act', lambda e, pi=pi, n=n: e.activation(out=RS[:, 0:n], in_=PSUM[:, pi, 0:n], func=AF.Sqrt,
                                                                     bias=EPS, scale=1.0 / D), reads=pk(pi), writes=["RS"])
                        if 'a1b' in SKIP:
                            continue
                        op('dve', lambda e, n=n: e.reciprocal(out=RS[:, 0:n], in_=RS[:, 0:n]), reads=["RS"], writes=["RS"])
                        if 'a1c' in SKIP:
                            continue
                        for k in range(8):
                            xg = XG[:, k % 2, 0:n]
                            op('act', lambda e, k=k, o=o, n=n, xg=xg: e.activation(
                                out=xg, in_=X[:, k, c0 + o:c0 + o + n], func=AF.Copy, scale=g1[:, l, k:k + 1]),
                                reads=["X"], writes=[("XG", k % 2)])
                            op('dve', lambda e, k=k, o=o, n=n, xg=xg: e.tensor_tensor(
                                out=XN[:, k, o:o + n], in0=xg, in1=RS[:, 0:n], op=ALU.mult),
                                reads=[("XG", k % 2), "RS"], writes=[("XN", k)])
                    if STAGE >= 8 and bi >= 1:
                        norm2_into(l, MIX, SQ, RS, XG, (bi - 1) * 512, 0, 512)
                    XNr = [("XN", k) for k in range(8)]
                    for ch in range(3):
                        wcols = 512 if ch < 2 else 256
                        Wv, wk = wq_a2.get(ch)
                        for j in range(wcols // 128):
                            col = ch * 512 + j * 128
                            if col == 640:
                                continue
                            for (o, n) in subs:
                                pi = psb()
                                mm_group(pi, n, [(Wv[:, k, j * 128:(j + 1) * 128], XN[:, k, o:o + n]) for k in range(8)],
                                         XNr + [wk])
                                if 'a2a' in SKIP:
                                    continue
                                if col < 640:
                                    isq = col < 512
                                    op('act', lambda e, pi=pi, n=n: e.activation(out=SQh[:, 0:n], in_=PSUM[:, pi, 0:n],
                                                                                 func=AF.Square), reads=pk(pi), writes=["SQh"])
                                    p2 = psb()
                                    mm_group(p2, n, [(blk64[:, :], SQh[:, 0:n])], ["SQh", "blk64"])
                                    op('act', lambda e, p2=p2, n=n: e.activation(out=RSq[:, 0:n], in_=PSUM[:, p2, 0:n],
                                                                                 func=AF.Sqrt, bias=EPS, scale=1.0 / 64),
                                       reads=pk(p2), writes=["RSq"])
                                    op('dve', lambda e, n=n: e.reciprocal(out=RSq[:, 0:n], in_=RSq[:, 0:n]),
                                       reads=["RSq"], writes=["RSq"])
                                    if 'a2b' in SKIP:
                                        continue
                                    if isq:
                                        dst = R2[:, col // 128, o:o + n]
                                        dk = ("R2", col // 128)
                                        gi_ = 0
                                    else:
                                        dst = KT[:, 128 + o:128 + o + n]
                                        dk = "KT"
                                        gi_ = 1
                                    op('dve', lambda e, pi=pi, n=n, dst=dst, gi_=gi_: e.scalar_tensor_tensor(
                                        out=dst, in0=PSUM[:, pi, 0:n], scalar=qkg[:, l, gi_:gi_ + 1], in1=RSq[:, 0:n],
                                        op0=ALU.mult, op1=ALU.mult), reads=pk(pi) + ["RSq", "qkg"], writes=[dk])
                                    if (not isq) and bi == 3 and 'a2c' not in SKIP:
                                        if o == 0:
                                            op('dve', lambda e, pi=pi: e.scalar_tensor_tensor(
                                                out=KF[:, :], in0=PSUM[:, pi, 384:512], scalar=qkg[:, l, 1:2],
                                                in1=RSq[:, 384:512], op0=ALU.mult, op1=ALU.mult),
                                                reads=pk(pi) + ["RSq"], writes=["KF"])
                                            p3 = psb()
                                            op('pe', lambda e, p3=p3: e.transpose(out=PSUM[:, p3, 0:128], in_=KF[:, :],
                                                                                  identity=ident[:, :]),
                                               reads=["KF"], writes=pk(p3))
                                            op('act', lambda e, p3=p3: e.activation(out=OUTS[:, :], in_=PSUM[:, p3, 0:128],
                                                                                    func=AF.Copy), reads=pk(p3), writes=["OUTS"])
                                            dma('sp', s_o1, kp[l], OUTS[:, :], reads=["OUTS"])
                                        else:
                                            op('dve', lambda e, pi=pi: e.scalar_tensor_tensor(
                                                out=KSF[:, :], in0=PSUM[:, pi, 0:64], scalar=qkg[:, l, 1:2],
                                                in1=RSq[:, 0:64], op0=ALU.mult, op1=ALU.mult),
                                                reads=pk(pi) + ["RSq"], writes=["KSF"])
                                            p3 = psb()
                                            op('pe', lambda e, p3=p3: e.transpose(out=PSUM[0:64, p3, 0:128], in_=KSF[:, :],
                                                                                  identity=ident[:, :]),
                                               reads=["KSF"], writes=pk(p3))
                                            op('act', lambda e, p3=p3: e.activation(out=OUTS2[0:64, :], in_=PSUM[0:64, p3, 0:128],
                                                                                    func=AF.Copy), reads=pk(p3), writes=["OUTS2"])
                                            for i_ in range(4):
                                                dma('sp', s_o2, ksam[l][:, 124 + i_, :], OUTS2[i_ * 16:(i_ + 1) * 16, :],
                                                    reads=["OUTS2"])
                                else:
                                    ut = (col - 768) // 128
                                    op('act', lambda e, pi=pi, n=n, ut=ut, o=o: e.activation(
                                        out=U[:, ut, o:o + n], in_=PSUM[:, pi, 0:n], func=AF.Copy),
                                        reads=pk(pi), writes=[("U", ut)])
                        if ch == 1 and 'a2d' not in SKIP:
                            pi = psb()
                            for tl in range(4):
                                for k in range(8):
                                    op('pe', lambda e, tl=tl, k=k, pi=pi: e.matmul(
                                        PSUM[:, pi, tl * 128:(tl + 1) * 128],
                                        lhsT=XN[:, k, tl * 128:(tl + 1) * 128], rhs=Wv[:, k, 128:256],
                                        start=(k == 0), stop=(k == 7)),
                                        reads=XNr + [wk], writes=pk(pi), signal=(k == 7))
                            if 'v1' in SKIP:
                                continue
                            op('act', lambda e, pi=pi: e.activation(out=VT[:, 1:5, :], in_=PSUM[:, pi, :].rearrange(
                                "p (t d) -> p t d", t=4), func=AF.Copy), reads=pk(pi), writes=["VT"])
                            if bi == 3 and 'v2' not in SKIP:
                                op('dve', lambda e, pi=pi: e.tensor_copy(out=OUTS[:, :], in_=PSUM[:, pi, 384:512]),
                                   reads=pk(pi), writes=["OUTS"])
                                dma('sp', s_o1, vp[l], OUTS[:, :], reads=["OUTS"])
                                pv5 = psb()
                                for k in range(8):
                                    op('pe', lambda e, k=k, pv5=pv5: e.matmul(
                                        PSUM[0:64, pv5, 0:128], lhsT=XN[:, k, 512:576], rhs=Wv[:, k, 128:256],
                                        start=(k == 0), stop=(k == 7)), reads=XNr + [wk], writes=pk(pv5), signal=(k == 7))
                                op('act', lambda e, pv5=pv5: e.activation(out=VS[:, :], in_=PSUM[0:64, pv5, 0:128], func=AF.Copy),
                                   reads=pk(pv5), writes=["VS"])
                                op('dve', lambda e, pv5=pv5: e.tensor_copy(out=OUTS2[0:64, :], in_=PSUM[0:64, pv5, 0:128]),
                                   reads=pk(pv5), writes=["OUTS2"])
                                for i_ in range(4):
                                    dma('sp', s_o2, vsam[l][:, 124 + i_, :], OUTS2[i_ * 16:(i_ + 1) * 16, :], reads=["OUTS2"])
                    fence()
                  E = dict(U=U, s_o=s_o, R2=R2, KT=KT, VT=VT, VS=VS, AT=AT, ST=STt, YG=YG, XN=XN, MIX=MIX, subs=subs, c0=c0)
                  if STAGE >= 3:
                      attn_block(l, bi, E)
                  if STAGE >= 2:
                      ssm_block(l, bi, E)
                  if STAGE >= 5:
                      mix_block(l, bi, E)
                  if DEBUG_DUMP and l == 0:
                      dma('sp', s_o, dbg_at[bi], AT[:, :, :], reads=[])
                      dma('sp', s_o, dbg_st[bi], STt[:, :, :], reads=[])
                      dma('sp', s_o, dbg_yg[bi], YG[:, :, :], reads=[])
                      dma('sp', s_o, dbg_mix[bi], MIX[:, :, :], reads=[])
                  op('pool', lambda e: e.tensor_copy(out=KT[:, 0:128], in_=KT[:, 128 + 384:128 + 512]), reads=["KT"], writes=["KT"])
                  op('pool', lambda e: e.tensor_copy(out=VT[:, 0, :], in_=VT[:, 4, :]), reads=["VT"], writes=["VT"])
                if "shift" not in SKIP:
                    with nc.allow_non_contiguous_dma(reason="cache shift"):
                        dma('sp', s_o, ksam[l][:, 0:124, :], cache_k[l][:, 4:128, :])
                        dma('sp', s_o, vsam[l][:, 0:124, :], cache_v[l][:, 4:128, :])
                if STAGE >= 8:
                    ffn_tail(l, MIX)
                fence()

        def ffn_specs(l):
            fs = []
            for c in range(8):
                fs.append((w_up[l][:, c * 512:(c + 1) * 512].rearrange("(k p) n -> p k n", p=128), [128, 8, 512]))
                fs.append((w_down[l][c * 512:(c + 1) * 512, :].rearrange("(k p) n -> p k n", p=128), [128, 4, 1024]))
            return fs

        def norm2_into(l, XN2, SQ, RS, XG, xcol, o, n):
            pi = psb()
            for hf in range(2):
                op('act', lambda e, hf=hf: e.activation(out=SQ[:, 0:4, 0:n], in_=X[:, 4 * hf:4 * hf + 4, xcol:xcol + n], func=AF.Square),
                   reads=["X"], writes=["SQ"])
                for k_ in range(4):
                    op('pe', lambda e, k_=k_, hf=hf: e.matmul(PSUM[:, pi, 0:n], lhsT=ones[:, :], rhs=SQ[:, k_, 0:n],
                                                             start=(hf == 0 and k_ == 0), stop=(hf == 1 and k_ == 3)),
                       reads=["SQ", "ones"], writes=pk(pi), signal=(k_ == 3))
            op('act', lambda e: e.activation(out=RS[:, 0:n], in_=PSUM[:, pi, 0:n], func=AF.Sqrt, bias=EPS, scale=1.0 / D),
               reads=pk(pi), writes=["RS"])
            op('dve', lambda e: e.reciprocal(out=RS[:, 0:n], in_=RS[:, 0:n]), reads=["RS"], writes=["RS"])
            for k_ in range(8):
                xg = XG[:, k_ % 2, 0:n]
                op('act', lambda e, k_=k_, xg=xg: e.activation(out=xg, in_=X[:, k_, xcol:xcol + n], func=AF.Copy,
                                                              scale=g2[:, l, k_:k_ + 1]), reads=["X"], writes=[("XG", k_ % 2)])
                op('dve', lambda e, k_=k_, xg=xg: e.tensor_tensor(out=XN2[:, k_, o:o + n], in0=xg, in1=RS[:, 0:n], op=ALU.mult),
                   reads=[("XG", k_ % 2), "RS"], writes=[("MIX", k_)])

        def ssm_block(l, bi, E):
            U = E['U']; s_o = E['s_o']; YG = E['YG']; ST = E['ST']; subs = E['subs']
            with ExitStack() as ph:
                def t(name, shape, dt=BF16):
                    return ph.enter_context(nc.sbuf_tensor("s%d_%d_" % (l, bi) + name, list(shape), dt))
                Y1 = t("Y1", [128, 512]); T1 = t("T1", [128, 512]); SG = T1
                php = ExitStack()
                def tp(name, shape, dt=BF16):
                    return php.enter_context(nc.sbuf_tensor("sp%d_%d_" % (l, bi) + name, list(shape), dt))
                Zt = [tp("Z%d" % i, [128, 2, 512]) for i in range(2)]
                PRM = [tp("PRM%d" % i, [128, 4, 512]) for i in range(2)]
                Ma = PRM[1][:, 0:2, :]
                Mb = PRM[1][:, 2:4, :]
                MK = ("PR", 1)
                Xc = [tp("Xc%d" % i, [128, 2, 512]) for i in range(2)]
                if l == 0 and bi == 0:
                    print("SBUF remaining in ssm prompt scope:", nc.sbuf_bytes_remaining)
                Gt = [tp("G%d" % i, [128, 2, 512]) for i in range(2)]
                PR = PRM
                INt = [tp("IN%d" % i, [128, 4, 2], F32) for i in range(2)]
                Ut = [tp("U%d" % i, [128, 2], F32) for i in range(2)]
                HO = tp("HO", [128, 2, 16], F32); HT = tp("HT", [128, 16], F32)
                HOUT = tp("HOUT", [16, 2, 128], F32)
                PTa = [tp("PTa%d" % i, [128, 256]) for i in range(2)]
                DRa = tp("DRa", [128, 256], F32)
                v3 = lambda ap_: ap_.rearrange("p (c j) -> p c j", c=NCHUNK)
                R2 = E['R2']; KT = E['KT']; VT = E['VT']; AT = E['AT']
                att_it = [0]

                def att_unit(i, hp, b):
                    hs = i * 2 + hp
                    rows = slice(hp * 64, (hp + 1) * 64)
                    has_prev = (bi * 4 + b) > 0
                    tb = att_it[0] % 2
                    att_it[0] += 1
                    qv = R2[rows, i, b * 128:(b + 1) * 128]
                    if has_prev:
                        op('pe', lambda e: e.matmul(PSUM[:, BS, 0:128], lhsT=KT[rows, b * 128:(b + 1) * 128], rhs=qv,
                                                    start=True, stop=True), reads=[("R2", i), "KT"], writes=pk(BS), signal=False)
                    op('pe', lambda e: e.matmul(PSUM[:, BS, 128:256], lhsT=KT[rows, 128 + b * 128:128 + (b + 1) * 128], rhs=qv,
                                                start=True, stop=True), reads=[("R2", i), "KT"], writes=pk(BS), signal=True)
                    c_lo = 0 if has_prev else 128
                    op('act', lambda e: e.activation(out=PTa[tb][:, c_lo:256], in_=PSUM[:, BS, c_lo:256], func=AF.Exp),
                       reads=pk(BS), writes=[("PTa", tb)])
                    tt('pool', PTa[tb][:, c_lo:256], PTa[tb][:, c_lo:256], EB[:, hs, c_lo:256], ALU.mult, [("PTa", tb), "EB"],
                       [("PTa", tb)])
                    return lambda: att_pv(rows, b, tb, has_prev)

                def att_pv(rows, b, tb, has_prev):
                    parts = ([(VT[:, b, rows], PTa[tb][:, 0:128])] if has_prev else []) + [(VT[:, b + 1, rows], PTa[tb][:, 128:256])]
                    bb = b % 2
                    for (coff, use_ones) in ((0, False), (256, True)):
                        for ii, (vv, pp) in enumerate(parts):
                            lh = ones[:, 0:64] if use_ones else vv
                            op('pe', lambda e, lh=lh, pp=pp, ii=ii, coff=coff: e.matmul(
                                PSUM[rows, BOD, coff + bb * 128:coff + (bb + 1) * 128], lhsT=lh, rhs=pp, start=(ii == 0),
                                stop=(ii == len(parts) - 1)), reads=[("PTa", tb), "VT", "ones"], writes=pk(BOD),
                                signal=(ii == len(parts) - 1))

                def att_norm(i, h):
                    op('act', lambda e: e.activation(out=DRa[:, :], in_=PSUM[:, BOD, 256:512], func=AF.Ln, bias=esink[:, l, i:i + 1],
                                                     scale=1.0), reads=pk(BOD) + ["esink"], writes=["DRa"])
                    op('act', lambda e: e.activation(out=DRa[:, :], in_=DRa[:, :], func=AF.Exp, scale=-1.0), reads=["DRa"], writes=["DRa"])
                    tt('dve', AT[:, i, h * 256:(h + 1) * 256], PSUM[:, BOD, 0:256], DRa[:, :], ALU.mult, pk(BOD) + ["DRa"], [("AT", i)])

                att_groups = []
                if STAGE >= 3:
                    for i in range(4):
                        for h in range(2):
                            att_groups.append((i, h))

                def epilogue(q, yb, o, n):
                    op('dve', lambda e: e.scalar_tensor_tensor(out=Y1[:, 0:n], in0=U[:, q, o:o + n], scalar=dsk[:, l, q:q + 1],
                                                               in1=PSUM[:, yb, 0:n], op0=ALU.mult, op1=ALU.add),
                       reads=pk(yb) + [("U", q)], writes=["Y1"])
                    tt('dve', T1[:, 0:n], Y1[:, 0:n], Y1[:, 0:n], ALU.mult, ["Y1"], ["T1"])
                    op('dve', lambda e: e.tensor_scalar(out=T1[:, 0:n], in0=T1[:, 0:n], scalar1=0.044715, scalar2=1.0,
                                                        op0=ALU.mult, op1=ALU.add), reads=["T1"], writes=["T1"])
                    tt('dve', T1[:, 0:n], T1[:, 0:n], Y1[:, 0:n], ALU.mult, ["T1", "Y1"], ["T1"])
                    op('act', lambda e: e.activation(out=T1[:, 0:n], in_=T1[:, 0:n], func=AF.Sigmoid, scale=1.5957691216057308),
                       reads=["T1"], writes=["T1"])
                    tt('dve', YG[:, q, o:o + n], Y1[:, 0:n], T1[:, 0:n], ALU.mult, ["Y1", "T1"], [("YG", q)])

                BX0, BY, BS, BOD, BUP = 0, 2, 3, 4, 5

                def x0_mm(s):
                    q = s // 4
                    mm_group(BX0, 512, [(BTr[:, s, :], U[:, q, 0:512])], [("U", q), "BTr"])
                    mm_group(BX0 + 1, 512, [(BTi[:, s, :], U[:, q, 0:512])], [("U", q), "BTi"])

                def evac(s):
                    b_ = s % 2
                    xk = ("Xc", b_)
                    op('act', lambda e: e.activation(out=Xc[b_][:, 0, :], in_=PSUM[:, BX0, :], func=AF.Copy), reads=pk(BX0), writes=[xk])
                    op('act', lambda e: e.activation(out=Xc[b_][:, 1, :], in_=PSUM[:, BX0 + 1, :], func=AF.Copy), reads=pk(BX0 + 1),
                       writes=[xk])

                def modops(s):
                    b_ = s % 2
                    xk = ("Xc", b_)
                    zk = ("Z", b_)
                    c4 = bc_mid(PHc[:, s, 0:CHUNK], 2 * NCHUNK)
                    s4 = bc_mid(PHs[:, s, 0:CHUNK], 2 * NCHUNK)
                    x4 = Xc[b_][:, :, :].rearrange("p c (h j) -> p (c h) j", j=CHUNK)
                    tt('dve', Ma.rearrange("p c (h j) -> p (c h) j", j=CHUNK), x4, c4, ALU.mult, [xk, "PH"], [MK])
                    tt('dve', Mb.rearrange("p c (h j) -> p (c h) j", j=CHUNK), x4, s4, ALU.mult, [xk, "PH"], [MK])
                    tt('dve', Zt[b_][:, 0, :], Ma[:, 0, :], Mb[:, 1, :], ALU.add, [MK], [zk])
                    tt('dve', Zt[b_][:, 1, :], Ma[:, 1, :], Mb[:, 0, :], ALU.subtract, [MK], [zk])

                def chain(tiles, hook=lambda: None):
                    for ch in range(NCHUNK):
                        for s in tiles:
                            b_ = s % 2
                            gk = ("G", b_)
                            if ch == 0:
                                ge = GE2[:, s, :]
                                ger = GE2[:, s, ::-1]
                                rk = ["GE"]
                            else:
                                ge = Gt[b_][:, :, ch * CHUNK - 1]
                                ger = Gt[b_][:, ::-1, ch * CHUNK - 1]
                                rk = [gk]
                            tt('dve', Ut[b_][:, :], ger, SS[:, s, :], ALU.mult, rk + ["SS"], [("UT", b_)])
                            op('dve', lambda e, ge=ge, s=s, b_=b_, ch=ch: e.scalar_tensor_tensor(
                                out=INt[b_][:, ch, :], in0=ge, scalar=R128c[:, s:s + 1], in1=Ut[b_][:, :], op0=ALU.mult, op1=ALU.add),
                                reads=rk + [("UT", b_)], writes=[("IN", b_)])
                        hook()
                        for s in tiles:
                            b_ = s % 2
                            gk = ("G", b_)
                            cs_ = slice(ch * CHUNK, (ch + 1) * CHUNK)
                            for c_ in range(2):
                                op('dve', lambda e, s=s, ch=ch, cs_=cs_, c_=c_, b_=b_: e.tensor_tensor_scan(
                                    out=Gt[b_][:, c_, cs_], data0=MAG[:, s:s + 1].to_broadcast([128, CHUNK]), data1=Zt[b_][:, c_, cs_],
                                    initial=INt[b_][:, ch, c_:c_ + 1], op0=ALU.mult, op1=ALU.add),
                                    reads=[("Z", b_), ("IN", b_)], writes=[gk])
                            hook()

                def finish(s):
                    q = s // 4
                    slot = s % 4
                    yb = BY
                    b_ = s % 2
                    gk = ("G", b_)
                    cb = bc_mid(PHc[:, s, 0:CHUNK], NCHUNK)
                    sb_ = bc_mid(PHs[:, s, 0:CHUNK], NCHUNK)
                    op('dve', lambda e: e.tensor_copy(out=GE2[:, s, :], in_=Gt[b_][:, :, 511]), reads=[gk], writes=["GE"])
                    P_ = PR[b_]
                    pkey = ("PR", b_)
                    gr_ = v3(Gt[b_][:, 0, :])
                    gi_ = v3(Gt[b_][:, 1, :])
                    c4 = bc_mid(PHc[:, s, 0:CHUNK], 2 * NCHUNK)
                    s4 = bc_mid(PHs[:, s, 0:CHUNK], 2 * NCHUNK)
                    g4 = Gt[b_][:, :, :].rearrange("p c (h j) -> p (c h) j", j=CHUNK)
                    tt('dve', P_[:, 0:2, :].rearrange("p c (h j) -> p (c h) j", j=CHUNK), g4, c4, ALU.mult, [gk, "PH"], [pkey])
                    tt('dve', P_[:, 2:4, :].rearrange("p c (h j) -> p (c h) j", j=CHUNK), g4, s4, ALU.mult, [gk, "PH"], [pkey])
                    pairs = [(CTr[:, s, :], P_[:, 0, :]), (nCTi[:, s, :], P_[:, 1, :]), (nCTi[:, s, :], P_[:, 2, :]),
                             (nCTr[:, s, :], P_[:, 3, :])]
                    for ii, (lh, rh) in enumerate(pairs):
                        first = (slot == 0 and ii == 0)
                        lastm = (slot == 3 and ii == 3)
                        op('pe', lambda e, lh=lh, rh=rh, first=first, lastm=lastm: e.matmul(
                            PSUM[:, yb, 0:512], lhsT=lh, rhs=rh, start=first, stop=lastm),
                            reads=[pkey, "CT"], writes=pk(yb), signal=(ii == 3))
                    if slot == 3:
                        pending.append(lambda: epilogue(q, yb, 0, 512))

                ffn_on = (bi >= 1) and STAGE >= 8 and 'ffni' not in SKIP
                if ffn_on:
                    XN2 = E['MIX']
                    xo = (bi - 1) * 512
                    Hf = tp("Hf", [128, 4, 512])
                    wq_f = WQ(ffn_specs(l))
                    BDN = (6, 7)

                    def ffn_up_unit(c, j):
                        Wu, wuk = wq_f.get(2 * c)
                        mm_group(BUP, 512, [(Wu[:, k_, j * 128:(j + 1) * 128], XN2[:, k_, 0:512]) for k_ in range(8)],
                                 [("MIX", k_) for k_ in range(8)] + [wuk])
                        op('act', lambda e: e.activation(out=Hf[:, j, :], in_=PSUM[:, BUP, :], func=AF.Relu),
                           reads=pk(BUP), writes=[("Hf", j)])
                        op('act', lambda e: e.activation(out=Hf[:, j, :], in_=Hf[:, j, :], func=AF.Square),
                           reads=[("Hf", j)], writes=[("Hf", j)])

                    def ffn_down(c, m):
                        Wd, wdk = wq_f.get(2 * c + 1)
                        mm_group(BDN[m % 2], 512, [(Wd[:, j, m * 128:(m + 1) * 128], Hf[:, j, :]) for j in range(4)],
                                 [("Hf", j) for j in range(4)] + [wdk])

                    def ffn_add(m):
                        bank = BDN[m % 2]
                        op('dve', lambda e: e.tensor_tensor(out=X[:, m, xo:xo + 512], in0=PSUM[:, bank, :], in1=X[:, m, xo:xo + 512],
                                                            op=ALU.add), reads=pk(bank) + ["X"], writes=["X"])
                else:
                    Wg, wgk = wload(w_glu[l].rearrange("(k p) n -> p k n", p=128), [128, 4, 512], None)

                pending = []
                x0_mm(0)
                evac(0)
                x0_mm(1)
                evac(1)
                for p_ in range(8):
                    s0, s1 = 2 * p_, 2 * p_ + 1
                    modops(s0)
                    modops(s1)
                    if p_ < 7:
                        x0_mm(s0 + 2)
                        evac(s0 + 2)
                        x0_mm(s1 + 2)
                        evac(s1 + 2)
                    aunits = []
                    if att_groups:
                        if p_ > 0:
                            att_norm(*att_groups[p_ - 1])
                        gi_, gh_ = att_groups[p_]
                        aunits = [(gi_, hp, b) for hp in range(2) for b in (2 * gh_, 2 * gh_ + 1)]
                    for j in range(4):
                        pv_ = att_unit(*aunits[j]) if j < len(aunits) else None
                        if ffn_on:
                            ffn_up_unit(p_, j)
                        if pv_:
                            pv_()
                    chain([s0, s1])
                    todo = pending
                    pending = []
                    for f_ in todo:
                        f_()
                    if ffn_on:
                        for m_ in range(8):
                            ffn_down(p_, m_)
                            if m_ >= 1:
                                ffn_add(m_ - 1)
                    finish(s0)
                    if ffn_on:
                        ffn_add(7)
                    finish(s1)
                if att_groups:
                    att_norm(*att_groups[7])
                for f_ in pending:
                    f_()
                if ffn_on:
                    Wg, wgk = wload(w_glu[l].rearrange("(k p) n -> p k n", p=128), [128, 4, 512], None)
                if bi == 3:
                    tt('dve', HO[:, 0, :], GE2[:, :, 0], P127c[:, :], ALU.mult, ["GE"], ["HO"])
                    tt('dve', HT[:, :], GE2[:, :, 1], P127s[:, :], ALU.mult, ["GE"], ["HT"])
                    tt('dve', HO[:, 0, :], HO[:, 0, :], HT[:, :], ALU.subtract, ["HO", "HT"], ["HO"])
                    tt('dve', HO[:, 1, :], GE2[:, :, 0], P127s[:, :], ALU.mult, ["GE", "HO"], ["HO"])
                    tt('dve', HT[:, :], GE2[:, :, 1], P127c[:, :], ALU.mult, ["GE", "HO"], ["HT"])
                    tt('dve', HO[:, 1, :], HO[:, 1, :], HT[:, :], ALU.add, ["HO", "HT"], ["HO"])
                    for c_ in range(2):
                        pi = 6 + c_
                        op('pe', lambda e, c_=c_, pi=pi: e.transpose(out=PSUM[0:16, pi, 0:128], in_=HO[:, c_, :], identity=ident[:, :]),
                           reads=["HO"], writes=pk(pi))
                        op('act', lambda e, c_=c_, pi=pi: e.activation(out=HOUT[:, c_, :], in_=PSUM[0:16, pi, 0:128], func=AF.Copy),
                           reads=pk(pi), writes=["HOUT"])
                    dma('sp', s_o, hrp[l], HOUT[:, 0, :], reads=["HOUT"])
                    dma('sp', s_o, hip[l], HOUT[:, 1, :], reads=["HOUT"])
                fence()
                php.close()
                if bi == 3 and STAGE >= 4 and 'ssms' not in SKIP:
                    ssm_sample(l, E, epilogue)
                    fence()
                for j in range(4):
                    for (o, n) in subs:
                        pi = psb()
                        mm_group(pi, n, [(Wg[:, q_, j * 128:(j + 1) * 128], YG[:, q_, o:o + n]) for q_ in range(4)],
                                 [("YG", q_) for q_ in range(4)] + [wgk])
                        op('act', lambda e, pi=pi, n=n, j=j: e.activation(out=SG[:, 0:n], in_=PSUM[:, pi, 0:n], func=AF.Sigmoid,
                                                                          bias=bgl[:, l, j:j + 1], scale=1.0),
                           reads=pk(pi), writes=["T1"])
                        tt('dve', ST[:, j, o:o + n], YG[:, j, o:o + n], SG[:, 0:n], ALU.mult, ["T1", ("YG", j)], [("ST", j)])
                fence()

        def ssm_sample(l, E, epilogue):
            U = E['U']; s_o = E['s_o']
            with ExitStack() as ph:
                def t(name, shape, dt=F32):
                    return ph.enter_context(nc.sbuf_tensor("ss%d_" % l + name, list(shape), dt))
                SN = t("SN", [16, 2048])
                H0 = [t("H0%d" % c_, [128, 16, 16]) for c_ in range(2)]
                HS = [t("HS%d" % c_, [128, 16, 64]) for c_ in range(2)]
                HSb = [t("HSb%d" % c_, [128, 16, 64], BF16) for c_ in range(2)]
                TA = t("TA", [128, 16, 16]); TB = t("TB", [128, 16, 16])
                s_s = C.stream("ssl%d" % l)
                for s in range(16):
                    q = s // 4
                    for c_, BT in ((0, BTr), (1, BTi)):
                        bank = 2 * c_ + s // 8
                        op('pe', lambda e, s=s, q=q, BT=BT, bank=bank: e.matmul(
                            PSUM[:, bank, (s % 8) * 64:(s % 8) * 64 + 64], lhsT=BT[:, s, :], rhs=U[:, q, 512:576],
                            start=True, stop=True), reads=[("U", q), "BTr", "BTi"], writes=pk(bank), signal=True)
                for c_, src in ((0, st_re), (1, st_im)):
                    dma('sp', s_s, SN[:, :], src[l], writes=["SN"])
                    pi = 6 + c_
                    for s in range(16):
                        op('pe', lambda e, s=s, pi=pi: e.transpose(out=PSUM[:, pi, s * 16:(s + 1) * 16],
                                                                   in_=SN[0:16, s * 128:(s + 1) * 128], identity=ident[0:16, 0:16]),
                           reads=["SN"], writes=pk(pi), signal=(s == 15))
                    op('act', lambda e, c_=c_, pi=pi: e.activation(out=H0[c_][:, :, :], in_=PSUM[:, pi, 0:256].rearrange(
                        "p (s b) -> p s b", s=16), func=AF.Copy), reads=pk(pi), writes=[("H0", c_)])
                ARb = bc_last(AR[:, :], 16)
                AIb = bc_last(AI[:, :], 16)
                xv = [PSUM[:, 2 * c_:2 * c_ + 2, :].rearrange("p b (s c) -> p (b s) c", c=64) for c_ in range(2)]
                for i_ in range(4):
                    cs_ = slice(i_ * 16, (i_ + 1) * 16)
                    if i_ == 0:
                        pr_, pi_ = H0[0][:, :, :], H0[1][:, :, :]
                        rk = [("H0", 0), ("H0", 1)]
                    else:
                        ps_ = slice((i_ - 1) * 16, i_ * 16)
                        pr_, pi_ = HS[0][:, :, ps_], HS[1][:, :, ps_]
                        rk = ["HS"]
                    tt('dve', TA[:, :, :], pr_, ARb, ALU.mult, rk + ["AR"], ["TA"])
                    tt('dve', TB[:, :, :], pi_, AIb, ALU.mult, rk + ["AR"], ["TB"])
                    tt('dve', TA[:, :, :], TA[:, :, :], TB[:, :, :], ALU.subtract, ["TA", "TB"], ["TA"])
                    tt('dve', HS[0][:, :, cs_], xv[0][:, :, cs_], TA[:, :, :], ALU.add, pk(0, 2) + ["TA"], ["HS"])
                    tt('dve', TA[:, :, :], pi_, ARb, ALU.mult, rk + ["AR", "HS"], ["TA"])
                    tt('dve', TB[:, :, :], pr_, AIb, ALU.mult, rk + ["AR"], ["TB"])
                    tt('dve', TA[:, :, :], TA[:, :, :], TB[:, :, :], ALU.add, ["TA", "TB"], ["TA"])
                    tt('dve', HS[1][:, :, cs_], xv[1][:, :, cs_], TA[:, :, :], ALU.add, pk(2, 2) + ["TA", "HS"], ["HS"])
                for c_ in range(2):
                    op('dve', lambda e, c_=c_: e.tensor_copy(out=HSb[c_][:, :, :], in_=HS[c_][:, :, :]), reads=["HS", "HS"],
                       writes=[("HSb", c_)])
                for q in range(4):
                    yb = 4 + (q % 2)
                    pairs = []
                    for slot in range(4):
                        s = 4 * q + slot
                        pairs.append((CTr[:, s, :], HSb[0][:, s, :]))
                        pairs.append((nCTi[:, s, :], HSb[1][:, s, :]))
                    mm_group(yb, 64, pairs, [("HSb", 0), ("HSb", 1), "CT"])
                    epilogue(q, yb, 512, 64)
                for c_, dst in ((0, hrs), (1, his)):
                    for g4 in range(4):
                        pi = g4
                        for j in range(4):
                            s = g4 * 4 + j
                            op('pe', lambda e, s=s, j=j, pi=pi, c_=c_: e.transpose(
                                out=PSUM[0:16, pi, j * 128:(j + 1) * 128], in_=HS[c_][:, s, 48:64], identity=ident[:, :]),
                                reads=["HS", "HS"], writes=pk(pi), signal=(j == 3))
                        op('act', lambda e, pi=pi, g4=g4: e.activation(out=SN[:, g4 * 512:(g4 + 1) * 512], in_=PSUM[0:16, pi, :],
                                                                      func=AF.Copy), reads=pk(pi), writes=["SN"])
                    dma('sp', s_s, dst[l].rearrange("b s r -> b (s r)"), SN[:, :], reads=["SN"])
                fence()

        def attn_block(l, bi, E):
            if bi == 3 and STAGE >= 4 and 'atts' not in SKIP:
                attn_sample(l, E)
                fence()

        def attn_sample(l, E):
            R2 = E['R2']; KT = E['KT']; VS = E['VS']; AT = E['AT']
            with ExitStack() as ph:
                def t(name, shape, dt=BF16):
                    return ph.enter_context(nc.sbuf_tensor("as%d_" % l + name, list(shape), dt))
                CK = t("CK", [128, 16, 128], F32)
                CKT = t("CKT", [128, 16, 128])
                CV = t("CV", [128, 16, 128])
                TMPc = t("TMPc", [128, 512], F32)
                Pc = t("Pc", [128, 512])
                TMPn = t("TMPn", [64, 512], F32)
                Pn = t("Pn", [64, 512])
                DR = t("DR", [128, 256], F32)
                s_c = C.stream("asl%d" % l)
                s_v = C.stream("asv%d" % l)
                with nc.allow_non_contiguous_dma(reason="cache load"):
                    dma('sp', s_c, CK[:, :, :], cache_k[l].rearrange("s j d -> j s d"), writes=["CK"])
                    dma('pool', s_v, CV[:, :, :], cache_v[l].rearrange("s j d -> j s d"), writes=["CV"])
                for sl in range(16):
                    pi = sl % 4
                    op('pe', lambda e, sl=sl, pi=pi: e.transpose(out=PSUM[:, pi, 0:128], in_=CK[:, sl, :], identity=ident[:, :]),
                       reads=["CK"], writes=pk(pi))
                    if sl % 2 == 0:
                        op('act', lambda e, sl=sl, pi=pi: e.activation(out=CKT[:, sl, :], in_=PSUM[:, pi, 0:128], func=AF.Copy),
                           reads=pk(pi), writes=["CKT"])
                    else:
                        op('dve', lambda e, sl=sl, pi=pi: e.tensor_copy(out=CKT[:, sl, :], in_=PSUM[:, pi, 0:128]),
                           reads=pk(pi), writes=["CKT"])
                po, pd = 6, 7
                for kv in range(2):
                    rows = slice(kv * 64, (kv + 1) * 64)
                    for sl in range(16):
                        op('pe', lambda e, sl=sl, kv=kv, rows=rows: e.matmul(
                            PSUM[:, 4 + kv, sl:256:16], lhsT=CKT[rows, sl, :],
                            rhs=R2[rows, :, 512 + sl:576:16], start=True, stop=True),
                            reads=["CKT"] + [("R2", i) for i in range(4)], writes=pk(4 + kv), signal=(sl == 15))
                for kv in range(2):
                    op('dve', lambda e, kv=kv: e.tensor_tensor(
                        out=TMPc[:, kv * 256:(kv + 1) * 256].rearrange("p (a s) -> p a s", s=16),
                        in0=PSUM[:, 4 + kv, 0:256].rearrange("p (a s) -> p a s", s=16),
                        in1=bc_last(biasc[:, kv * 16:(kv + 1) * 16], 16), op=ALU.add), reads=pk(4 + kv), writes=["TMPc"])
                op('act', lambda e: e.activation(out=Pc[:, :], in_=TMPc[:, :], func=AF.Exp), reads=["TMPc"], writes=["Pc"])
                for kv in range(2):
                    rows = slice(kv * 64, (kv + 1) * 64)
                    op('pe', lambda e, kv=kv, rows=rows: e.matmul(
                        PSUM[0:64, kv, 0:256], lhsT=KT[rows, 128 + 512:128 + 576],
                        rhs=R2[rows, :, 512:576], start=True, stop=True),
                        reads=["KT"] + [("R2", i) for i in range(4)], writes=pk(kv), signal=True)
                    tt('dve', TMPn[:, kv * 256:(kv + 1) * 256], PSUM[0:64, kv, 0:256], biasnf[:, kv * 256:(kv + 1) * 256], ALU.add,
                       pk(kv), ["TMPn"])
                op('act', lambda e: e.activation(out=Pn[:, :], in_=TMPn[:, :], func=AF.Exp), reads=["TMPn"], writes=["Pn"])
                for (bank, use_ones) in ((po, False), (pd, True)):
                    for kv in range(2):
                        rows = slice(kv * 64, (kv + 1) * 64)
                        lh = ones[0:64, 0:64] if use_ones else VS[0:64, rows]
                        op('pe', lambda e, bank=bank, kv=kv, rows=rows, lh=lh: e.matmul(
                            PSUM[rows, bank, 0:256], lhsT=lh, rhs=Pn[0:64, kv * 256:(kv + 1) * 256], start=True, stop=False),
                            reads=["Pn", "VS", "ones"], writes=pk(bank), signal=False)
                        for sl in range(16):
                            lh2 = ones[:, 0:64] if use_ones else CV[:, sl, rows]
                            op('pe', lambda e, bank=bank, kv=kv, rows=rows, sl=sl, lh2=lh2: e.matmul(
                                PSUM[rows, bank, sl:256:16], lhsT=lh2, rhs=Pc[:, kv * 256 + sl:kv * 256 + 256:16],
                                start=False, stop=(sl == 15)), reads=["Pc", "CV", "ones"], writes=pk(bank), signal=(sl == 15))
                op('dve', lambda e: e.tensor_tensor(out=DR[:, :].rearrange("p (h c) -> p h c", h=4),
                                                    in0=PSUM[:, pd, 0:256].rearrange("p (h c) -> p h c", h=4),
                                                    in1=bc_last(esink[:, l, :], 64), op=ALU.add), reads=pk(pd) + ["esink"], writes=["DRs"])
                op('dve', lambda e: e.reciprocal(out=DR[:, :], in_=DR[:, :]), reads=["DRs"], writes=["DRs"])
                op('dve', lambda e: e.tensor_tensor(out=AT[:, :, 512:576], in0=PSUM[:, po, 0:256].rearrange("p (h c) -> p h c", h=4),
                                                    in1=DR[:, :].rearrange("p (h c) -> p h c", h=4), op=ALU.mult),
                   reads=pk(po) + ["DRs"], writes=[("AT", i) for i in range(4)])
                fence()

        def mix_block(l, bi, E):
            XN = E['XN']; MIX = E['MIX']; AT = E['AT']; ST = E['ST']; subs = E['subs']; c0 = E['c0']
            with ExitStack() as ph:
                def t(name, shape, dt=BF16):
                    return ph.enter_context(nc.sbuf_tensor("mx%d_%d_" % (l, bi) + name, list(shape), dt))
                SGA = t("SGA", [128, 576], F32)
                TM = t("TM", [128, 576])
                WL = [t("WL%d" % i, [128, 4096]) for i in range(2)]
                wl_s = [C.stream("wl%d_%d_%d" % (l, bi, i)) for i in range(2)]
                R5 = Ring(WR + WL, wr_stream + wl_s, [("wr", i) for i in range(3)] + [("wl", i) for i in range(2)])
                XNr = [("XN", k_) for k_ in range(8)]
                mspecs = []
                for h in range(2):
                    for (Wsrc, gcol) in ((w_ao, 1280), (w_so, 2304)):
                        mspecs.append((Wsrc[l][:, h * 512:(h + 1) * 512].rearrange("(k p) n -> p k n", p=128), [128, 4, 512]))
                        mspecs.append((w_in[l][:, gcol + h * 512:gcol + (h + 1) * 512].rearrange("(k p) n -> p k n", p=128), [128, 8, 512]))
                for h in range(2):
                    mspecs.append((w_out[l][:, h * 512:(h + 1) * 512].rearrange("(k p) n -> p k n", p=128), [128, 8, 512]))
                wq_m = WQ(mspecs, ring=R5, ahead=2)
                mi = 0
                for h in range(2):
                    for (Wsrc, Act, akey, gcol, first) in ((w_ao, AT, "AT", 1280, True), (w_so, ST, "ST", 2304, False)):
                        Wo_, wok = wq_m.get(mi)
                        Wg_, wgk = wq_m.get(mi + 1)
                        mi += 2
                        for j in range(4):
                            m = h * 4 + j
                            bcol = (0 if first else 8) + m
                            for (o, n) in subs:
                                pg = psb()
                                mm_group(pg, n, [(Wg_[:, k_, j * 128:(j + 1) * 128], XN[:, k_, o:o + n]) for k_ in range(8)],
                                         XNr + [wgk])
                                op('act', lambda e, pg=pg, n=n, bcol=bcol: e.activation(
                                    out=SGA[:, 0:n], in_=PSUM[:, pg, 0:n], func=AF.Sigmoid, bias=bg[:, l, bcol:bcol + 1], scale=1.0),
                                    reads=pk(pg), writes=["SGA"])
                                pa = psb()
                                mm_group(pa, n, [(Wo_[:, k_, j * 128:(j + 1) * 128], Act[:, k_, o:o + n]) for k_ in range(4)],
                                         [(akey, k_) for k_ in range(4)] + [wok])
                                if first:
                                    tt('dve', MIX[:, m, o:o + n], PSUM[:, pa, 0:n], SGA[:, 0:n], ALU.mult, pk(pa) + ["SGA"], [("MIX", m)])
                                else:
                                    tt('dve', TM[:, 0:n], PSUM[:, pa, 0:n], SGA[:, 0:n], ALU.mult, pk(pa) + ["SGA"], ["TM"])
                                    tt('pool', MIX[:, m, o:o + n], MIX[:, m, o:o + n], TM[:, 0:n], ALU.add, ["TM", ("MIX", m)], [("MIX", m)])
                for h in range(2):
                    Wo_, wok = wq_m.get(8 + h)
                    for j in range(4):
                        m = h * 4 + j
                        for (o, n) in subs:
                            pi = psb()
                            mm_group(pi, n, [(Wo_[:, k_, j * 128:(j + 1) * 128], MIX[:, k_, o:o + n]) for k_ in range(8)],
                                     [("MIX", k_) for k_ in range(8)] + [wok])
                            xc = c0 + o
                            op('dve', lambda e, pi=pi, n=n, m=m, xc=xc: e.tensor_tensor(
                                out=X[:, m, xc:xc + n], in0=PSUM[:, pi, 0:n], in1=X[:, m, xc:xc + n], op=ALU.add),
                                reads=pk(pi) + ["X"], writes=["X"])
                if STAGE >= 8 and bi == 3:
                    SQn = t("SQn", [128, 4, 512]); RSn = t("RSn", [128, 512], F32); XGn = t("XGn", [128, 2, 512], F32)
                    for (o, n) in subs:
                        norm2_into(l, MIX, SQn, RSn, XGn, c0 + o, o, n)
                fence()

        def ffn_tail(l, XN2):
            with ExitStack() as ph:
                def t(name, shape, dt=BF16):
                    return ph.enter_context(nc.sbuf_tensor("b%d_" % l + name, list(shape), dt))
                Hh = [t("H%d" % i, [128, 4, 512]) for i in range(2)]
                Rr = [t("R%d" % i, [128, 512]) for i in range(2)]
                subs = [(0, 512), (512, 64)]
                wq_f = WQ(ffn_specs(l))
                it = 0
                for c in range(8):
                    Wu, wuk = wq_f.get(2 * c)
                    Wd, wdk = wq_f.get(2 * c + 1)
                    for (o, n) in subs:
                        hb = it % 2
                        it += 1
                        for j in range(4):
                            pi = psb()
                            mm_group(pi, n, [(Wu[:, k_, j * 128:(j + 1) * 128], XN2[:, k_, o:o + n]) for k_ in range(8)],
                                     [("MIX", k_) for k_ in range(8)] + [wuk])
                            rb = j % 2
                            op('act', lambda e, pi=pi, n=n, rb=rb: e.activation(out=Rr[rb][:, 0:n], in_=PSUM[:, pi, 0:n],
                                                                               func=AF.Relu), reads=pk(pi), writes=[("R", rb)])
                            op('act', lambda e, n=n, rb=rb, hb=hb, j=j: e.activation(out=Hh[hb][:, j, 0:n], in_=Rr[rb][:, 0:n],
                                                                                    func=AF.Square), reads=[("R", rb)], writes=[("H", hb, j)])
                        for m in range(8):
                            pi = psb()
                            mm_group(pi, n, [(Wd[:, j, m * 128:(m + 1) * 128], Hh[hb][:, j, 0:n]) for j in range(4)],
                                     [("H", hb, j) for j in range(4)] + [wdk])
                            xc = 1536 + o
                            op('dve', lambda e, pi=pi, n=n, m=m, xc=xc: e.tensor_tensor(
                                out=X[:, m, xc:xc + n], in0=PSUM[:, pi, 0:n], in1=X[:, m, xc:xc + n], op=ALU.add),
                                reads=pk(pi) + ["X"], writes=["X"])
                fence()

        for l in range(NLAYERS):
            if STAGE < 1:
                break
            if STAGE >= 2:
                ssm_tables(l, mid=(phase0 if l == 0 else None))
            elif l == 0:
                phase0()
            layer_phase_a(l)
        with ExitStack() as ph:
            YT = [ph.enter_context(nc.sbuf_tensor("yt%d" % i, [128, D], F32)) for i in range(2)]
            yts = [C.stream("yts%d" % i) for i in range(2)]
            for tt in range(17):
                b = tt % 2
                rows = 128 if tt < 16 else NS
                for half in range(2):
                    pi = psb()
                    for j in range(4):
                        k = half * 4 + j
                        op('pe', lambda e, k=k, j=j, pi=pi, rows=rows, tt=tt: e.transpose(
                            out=PSUM[0:rows, pi, j * 128:(j + 1) * 128], in_=X[:, k, tt * 128:tt * 128 + rows],
                            identity=ident[:, :]),
                            reads=[("X", tt)], writes=pk(pi), signal=(j == 3))
                    dst = YT[b][0:rows, half * 512:(half + 1) * 512]
                    src_ps = PSUM[0:rows, pi, :]
                    if half == 0:
                        op('act', lambda e, s_=src_ps, d_=dst: e.activation(out=d_, in_=s_, func=AF.Copy),
                           reads=pk(pi), writes=[("yt", b)])
                    else:
                        op('dve', lambda e, s_=src_ps, d_=dst: e.tensor_copy(out=d_, in_=s_),
                           reads=pk(pi), writes=[("yt", b)])
                dstd = yp[tt * 128:(tt + 1) * 128, :] if tt < 16 else ys[:, :]
                dma('sp', yts[b], dstd, YT[b][0:rows, :], reads=[("yt", b)])
            fence()
    return nc


_NC_CACHE = {}


def _consts():
    ident = np.eye(128, dtype=np.float32)
    blk = np.zeros((128, 128), np.float32)
    blk[:64, :64] = 1.0
    blk[64:, 64:] = 1.0
    slopes = np.exp2(-8.0 * np.arange(1, 9, dtype=np.float64) / 8.0)
    j = np.arange(128)[:, None]
    i = np.arange(128)[None, :]
    biasp = np.zeros((128, 8, 256), np.float32)
    for t in range(4):
        for hp in range(2):
            h = t + 4 * hp
            d_prev = 128 + i - j
            d_cur = i - j
            bp = np.where(d_prev <= 128, -slopes[h] * d_prev, -30000.0)
            bc = np.where(d_cur >= 0, -slopes[h] * d_cur, -30000.0)
            biasp[:, t * 2 + hp, 0:128] = bp
            biasp[:, t * 2 + hp, 128:256] = bc
    biasc = np.zeros((128, 32), np.float32)
    biasn = np.zeros((4, 32), np.float32)
    for kv in range(2):
        for hq in range(4):
            h = kv * 4 + hq
            for qi in range(4):
                col = kv * 16 + hq * 4 + qi
                jj = np.arange(128)
                dist = 128 + qi - jj
                biasc[:, col] = np.where(jj >= qi, -slopes[h] * dist, -30000.0)
                jn = np.arange(4)
                dn = qi - jn
                biasn[:, col] = np.where(dn >= 0, -slopes[h] * dn, -30000.0)
    biasnf = np.full((64, 512), -30000.0, np.float32)
    for ip in range(4):
        for slp in range(16):
            r = ip * 16 + slp
            for kv in range(2):
                for hq in range(4):
                    h = kv * 4 + hq
                    for qi in range(ip, 4):
                        biasnf[r, kv * 256 + hq * 64 + qi * 16 + slp] = -slopes[h] * (qi - ip)
    cmask = np.zeros((128, 128), np.float32)
    for r in range(128):
        glp = (r % 32) // 16
        cmask[r, glp * 64:(glp + 1) * 64] = 1.0
    rmask = np.zeros((128, 4), np.float32)
    for r in range(128):
        rmask[r, r // 32] = 1.0
    return dict(c_ident=ident, c_blk64=blk, c_biasp=biasp, c_biasc=biasc, c_biasn=biasn, c_biasnf=biasnf,
                c_cmask=cmask, c_rmask=rmask)


def kernel(**inp):
    f = lambda a: np.ascontiguousarray(np.asarray(a), dtype=np.float32)
    qperm = np.concatenate([np.arange(h * 64, (h + 1) * 64) for h in HEAD_PERM])
    w_in = f(inp['w_in']).copy()
    w_in[:, :, 0:512] = w_in[:, :, qperm]
    w_ao = f(inp['w_attn_o'])[:, qperm, :]
    qg = f(inp['q_norm_g'])
    kg = f(inp['k_norm_g'])
    qk_g = np.stack([np.concatenate([qg, qg], axis=1), np.concatenate([kg, kg], axis=1)], axis=1)
    sk = f(inp['attn_sinks'])
    sinks = np.zeros((L, 4, 128), np.float32)
    for t in range(4):
        sinks[:, t, 0:64] = sk[:, t][:, None]
        sinks[:, t, 64:128] = sk[:, t + 4][:, None]
    shared = dict(
        w_in=np.ascontiguousarray(w_in), w_glu=f(inp['w_glu']), w_ao=np.ascontiguousarray(w_ao), w_so=f(inp['w_ssm_o']),
        w_out=f(inp['w_out']), w_up=f(inp['w_up']), w_down=f(inp['w_down']),
        norm1_g=f(inp['norm1_g']), norm2_g=f(inp['norm2_g']), b_gate=f(inp['b_gate']),
        qk_g=np.ascontiguousarray(qk_g), sinks=sinks,
        lam_re=f(inp['lam_re']), lam_im=f(inp['lam_im']), log_step=f(inp['log_step']),
        b_re=f(inp['b_re']).reshape(L, 2048, 16), b_im=f(inp['b_im']).reshape(L, 2048, 16),
        c_re=f(inp['c_re']).reshape(L, 512, 64), c_im=f(inp['c_im']).reshape(L, 512, 64),
        d_skip=f(inp['d_skip']), b_glu=f(inp['b_glu']),
    )
    pa = np.zeros((92, 128), np.float32)
    pa[0:16] = shared['norm1_g'].reshape(L * 8, 128)
    pa[16:32] = shared['norm2_g'].reshape(L * 8, 128)
    pa[32:64] = shared['b_gate'].reshape(L * 16, 128)
    pa[64:68] = shared['qk_g'].reshape(L * 2, 128)
    pa[68:76] = shared['sinks'].reshape(L * 4, 128)
    pa[76:84] = shared['d_skip'].reshape(L * 4, 128)
    pa[84:92] = shared['b_glu'].reshape(L * 4, 128)
    pb = np.zeros((L, 48, 128), np.float32)
    pb[:, 0:16] = shared['lam_re'].reshape(L, 16, 128)
    pb[:, 16:32] = shared['lam_im'].reshape(L, 16, 128)
    pb[:, 32:48] = np.repeat(shared['log_step'].reshape(L, 16, 2, 1), 64, axis=3).reshape(L, 16, 128)
    shared['pvec_a'] = pa
    shared['pvec_b'] = pb
    shared.update(_consts())
    x_prompt = f(inp['x_prompt'])
    x_sample = f(inp['x_sample'])
    ck = f(inp['cache_k']).reshape(L, 128, 128, 128)
    cv = f(inp['cache_v']).reshape(L, 128, 128, 128)
    sre = f(inp['state_ssm_re']).reshape(L, 128, 2048)
    sim = f(inp['state_ssm_im']).reshape(L, 128, 2048)
    in_maps = []
    for c in range(NCORES):
        m = dict(shared)
        m['xp'] = x_prompt[c]
        m['xs'] = np.ascontiguousarray(x_sample[c * 16:(c + 1) * 16].transpose(1, 0, 2).reshape(NS, D))
        m['cache_k'] = np.ascontiguousarray(ck[:, c * 16:(c + 1) * 16])
        m['cache_v'] = np.ascontiguousarray(cv[:, c * 16:(c + 1) * 16])
        m['st_re'] = np.ascontiguousarray(sre[:, c * 16:(c + 1) * 16])
        m['st_im'] = np.ascontiguousarray(sim[:, c * 16:(c + 1) * 16])
        in_maps.append(m)
    if 'nc' not in _NC_CACHE:
        _NC_CACHE['nc'] = build_program()
    ncr = DEBUG_CORES or NCORES
    res = run_bass_kernel_spmd(_NC_CACHE['nc'], in_maps[:ncr], core_ids=list(range(ncr)))
    R = list(res.results)
    _NC_CACHE['last'] = R
    while len(R) < NCORES:
        R.append(R[0])
    y_prompt = np.stack([R[c]['yp'] for c in range(NCORES)]).astype(np.float32)
    y_sample = np.concatenate([R[c]['ys'].reshape(4, 16, D).transpose(1, 0, 2) for c in range(NCORES)]).astype(np.float32)
    k_prompt = np.stack([R[c]['kp'] for c in range(NCORES)], axis=1).reshape(L, 8, 128, 2, 64)
    v_prompt = np.stack([R[c]['vp'] for c in range(NCORES)], axis=1).reshape(L, 8, 128, 2, 64)
    hr_p = np.stack([R[c]['hrp'] for c in range(NCORES)], axis=1).reshape(L, 8, 32, 64)
    hi_p = np.stack([R[c]['hip'] for c in range(NCORES)], axis=1).reshape(L, 8, 32, 64)
    k_s = np.concatenate([R[c]['ksam'] for c in range(NCORES)], axis=1).reshape(L, 128, 128, 2, 64)
    v_s = np.concatenate([R[c]['vsam'] for c in range(NCORES)], axis=1).reshape(L, 128, 128, 2, 64)
    hr_s = np.concatenate([R[c]['hrs'] for c in range(NCORES)], axis=1).reshape(L, 128, 32, 64)
    hi_s = np.concatenate([R[c]['his'] for c in range(NCORES)], axis=1).reshape(L, 128, 32, 64)
    return (y_prompt, y_sample, k_prompt.astype(np.float32), v_prompt.astype(np.float32),
            hr_p.astype(np.float32), hi_p.astype(np.float32), k_s.astype(np.float32), v_s.astype(np.float32),
            hr_s.astype(np.float32), hi_s.astype(np.float32))
```

```python
import math
import numpy as np
import ml_dtypes
from contextlib import ExitStack
import concourse.bass as bass
import concourse.mybir as mybir
from concourse.bass_utils import run_bass_kernel_spmd

F32 = mybir.dt.float32
BF16 = mybir.dt.bfloat16
I32 = mybir.dt.int32
AF = mybir.ActivationFunctionType
ALU = mybir.AluOpType

NCORES = 8
D = 1024
T = 2048
NS = 64
NT = T + NS
L = 2
EPS = 1e-6
HEAD_PERM = [0, 4, 1, 5, 2, 6, 3, 7]
STAGE = 99
NLAYERS = 2
CHUNK = 256
NCHUNK = 512 // CHUNK
DEBUG_DUMP = False
DEBUG_CORES = 0
SKIP = set()


class Stream:
    def __init__(self, nc, stack, name):
        self.sem = stack.enter_context(nc.semaphore(name))
        self.name = name
        self.cnt = 0


class Ctx:
    def __init__(self, nc, stack):
        self.nc = nc
        self.engs = {'pe': nc.tensor, 'act': nc.scalar, 'dve': nc.vector, 'pool': nc.gpsimd, 'sp': nc.sync}
        self.st = {n: Stream(nc, stack, 's_' + n) for n in self.engs}
        self.waited = {n: {} for n in self.engs}
        self.lw = {}
        self.rd = {}
        self.stack = stack
        self.nstream = 0

    def stream(self, name):
        s = Stream(self.nc, self.stack, name)
        self.st[name] = s
        return name

    def _deps(self, reads, writes):
        deps = {}

        def add(tok):
            if tok is None:
                return
            s, v = tok
            if deps.get(s, 0) < v:
                deps[s] = v
        for k in reads:
            add(self.lw.get(k))
        for k in writes:
            add(self.lw.get(k))
            for s, v in self.rd.get(k, {}).items():
                add((s, v))
        return deps

    def _wait(self, en, deps):
        e = self.engs[en]
        w = self.waited[en]
        for s, v in deps.items():
            if s == en and en in ('pe', 'sp'):
                continue
            if w.get(s, 0) >= v:
                continue
            e.wait_ge(self.st[s].sem, v)
            w[s] = v

    def _record(self, tok, reads, writes):
        s, v = tok
        for k in reads:
            d = self.rd.setdefault(k, {})
            if d.get(s, 0) < v:
                d[s] = v
        for k in writes:
            self.lw[k] = tok
            self.rd[k] = {}

    def op(self, en, fn, reads=(), writes=(), signal=True):
        psr = [k for k in reads if isinstance(k, tuple) and k[0] == 'ps']
        if psr:
            writes = list(writes) + psr
        self._wait(en, self._deps(reads, writes))
        inst = fn(self.engs[en])
        st = self.st[en]
        if signal:
            st.cnt += 1
            inst.then_inc(st.sem, 1)
            tok = (en, st.cnt)
        else:
            tok = (en, st.cnt + 1)
        self._record(tok, reads, writes)
        return tok

    def dma(self, q, stream, out, in_, reads=(), writes=(), **kw):
        self._wait(q, self._deps(reads, writes))
        st = self.st[stream]
        st.cnt += 16
        self.engs[q].dma_start(out=out, in_=in_, **kw).then_inc(st.sem, 16)
        tok = (stream, st.cnt)
        self._record(tok, reads, writes)
        return tok

    def fence(self):
        for en, e in self.engs.items():
            w = self.waited[en]
            for s, st in self.st.items():
                if s == en or st.cnt == 0:
                    continue
                if w.get(s, 0) >= st.cnt:
                    continue
                e.wait_ge(st.sem, st.cnt)
                w[s] = st.cnt
        self.lw = {}
        self.rd = {}


def build_program():
    nc = bass.Bass("TRN2", target_bir_lowering=False)

    def din(name, shape, dt=F32):
        return nc.dram_tensor(name, list(shape), dt, kind="ExternalInput").ap()

    def dout(name, shape, dt=F32):
        return nc.dram_tensor(name, list(shape), dt, kind="ExternalOutput").ap()

    xp = din("xp", [T, D])
    xs = din("xs", [NS, D])
    cache_k = din("cache_k", [L, 16, 128, 128])
    cache_v = din("cache_v", [L, 16, 128, 128])
    st_re = din("st_re", [L, 16, 2048])
    st_im = din("st_im", [L, 16, 2048])
    w_in = din("w_in", [L, D, 3328])
    w_glu = din("w_glu", [L, 512, 512])
    w_ao = din("w_ao", [L, 512, D])
    w_so = din("w_so", [L, 512, D])
    w_out = din("w_out", [L, D, D])
    w_up = din("w_up", [L, D, 4096])
    w_down = din("w_down", [L, 4096, D])
    pvec_a = din("pvec_a", [92, 128])
    pvec_b = din("pvec_b", [L, 48, 128])
    norm1_g = din("norm1_g", [L, D])
    norm2_g = din("norm2_g", [L, D])
    b_gate = din("b_gate", [L, 2048])
    qk_g = din("qk_g", [L, 2, 128])
    sinks = din("sinks", [L, 4, 128])
    lam_re = din("lam_re", [L, 32, 64])
    lam_im = din("lam_im", [L, 32, 64])
    log_step = din("log_step", [L, 32])
    b_re = din("b_re", [L, 2048, 16])
    b_im = din("b_im", [L, 2048, 16])
    c_re = din("c_re", [L, 512, 64])
    c_im = din("c_im", [L, 512, 64])
    d_skip = din("d_skip", [L, 512])
    b_glu = din("b_glu", [L, 512])
    c_ident = din("c_ident", [128, 128])
    c_blk64 = din("c_blk64", [128, 128])
    c_biasp = din("c_biasp", [128, 8, 256])
    c_biasc = din("c_biasc", [128, 32])
    c_biasn = din("c_biasn", [4, 32])
    c_biasnf = din("c_biasnf", [64, 512])
    c_cmask = din("c_cmask", [128, 128])
    c_rmask = din("c_rmask", [128, 4])

    yp = dout("yp", [T, D])
    ys = dout("ys", [NS, D])
    kp = dout("kp", [L, 128, 128])
    vp = dout("vp", [L, 128, 128])
    hrp = dout("hrp", [L, 16, 128])
    hip = dout("hip", [L, 16, 128])
    ksam = dout("ksam", [L, 16, 128, 128])
    vsam = dout("vsam", [L, 16, 128, 128])
    hrs = dout("hrs", [L, 16, 16, 128])
    his = dout("his", [L, 16, 16, 128])
    if DEBUG_DUMP:
        dbg_at = dout("dbg_at", [4, 128, 4, 576], BF16)
        dbg_st = dout("dbg_st", [4, 128, 4, 576], BF16)
        dbg_yg = dout("dbg_yg", [4, 128, 4, 576], BF16)
        dbg_mix = dout("dbg_mix", [4, 128, 8, 576], BF16)

    stack = ExitStack()
    with stack:
        C = Ctx(nc, stack)
        op, dma, fence = C.op, C.dma, C.fence

        def sb(name, shape, dt=F32):
            return stack.enter_context(nc.sbuf_tensor(name, list(shape), dt))

        X = sb("X", [128, 8, NT])
        PSUM = stack.enter_context(nc.psum_tensor("PS", [128, 8, 512], F32))
        ident = sb("ident", [128, 128])
        ones = sb("ones", [128, 128], BF16)
        blk64 = sb("blk64", [128, 128], BF16)
        EB = sb("EB", [128, 8, 256], BF16)
        biasc = sb("biasc", [128, 32])
        biasnf = sb("biasnf", [64, 512])
        cmask = sb("cmask", [128, 128])
        rmask = sb("rmask", [128, 4])
        PVA = sb("PVA", [128, 92])
        g1 = PVA[:, 0:16].rearrange("p (l k) -> p l k", l=L)
        g2 = PVA[:, 16:32].rearrange("p (l k) -> p l k", l=L)
        bg = PVA[:, 32:64].rearrange("p (l k) -> p l k", l=L)
        qkg = PVA[:, 64:68].rearrange("p (l k) -> p l k", l=L)
        esink = PVA[:, 68:76].rearrange("p (l k) -> p l k", l=L)
        dsk = PVA[:, 76:84].rearrange("p (l k) -> p l k", l=L)
        bgl = PVA[:, 84:92].rearrange("p (l k) -> p l k", l=L)
        WR = [sb("wr%d" % i, [128, 4096], BF16) for i in range(3)]
        wr_stream = [C.stream("wrs%d" % i) for i in range(3)]
        wr_i = [0]
        PHc = sb("PHc", [128, 16, CHUNK + 1], BF16)
        PHs = sb("PHs", [128, 16, CHUNK + 1], BF16)
        MAG = sb("MAG", [128, 16])
        AR = sb("AR", [128, 16])
        AI = sb("AI", [128, 16])
        R128c = sb("R128c", [128, 16])
        R128s = sb("R128s", [128, 16])
        nR128s = sb("nR128s", [128, 16])
        P127c = sb("P127c", [128, 16])
        P127s = sb("P127s", [128, 16])
        BTr = sb("BTr", [128, 16, 128], BF16)
        BTi = sb("BTi", [128, 16, 128], BF16)
        CTr = sb("CTr", [128, 16, 128], BF16)
        nCTi = sb("nCTi", [128, 16, 128], BF16)
        nCTr = sb("nCTr", [128, 16, 128], BF16)
        GE2 = sb("GE2", [128, 16, 2])
        SS = sb("SS", [128, 16, 2])

        s_par = C.stream("par")
        s_out = C.stream("outs")
        ps_i = [0]

        def psb(n=1):
            i = ps_i[0]
            if i + n > 8:
                i = 0
            ps_i[0] = (i + n) % 8
            return i

        def pk(i, n=1):
            return [("ps", j) for j in range(i, i + n)]

        class Ring:
            def __init__(self, bufs, streams, keys):
                self.bufs, self.streams, self.keys = bufs, streams, keys
                self.i = 0

        G3 = Ring(WR, wr_stream, [("wr", i) for i in range(3)])

        def wload(src_ap, view_shape, key, ring=None):
            ring = ring or G3
            i = ring.i % len(ring.bufs)
            ring.i += 1
            buf = ring.bufs[i]
            n = 1
            for s_ in view_shape[1:]:
                n *= s_
            flat = buf[:, 0:n]
            if len(view_shape) == 3:
                view = flat.rearrange("p (k n) -> p k n", k=view_shape[1])
            else:
                view = flat
            dma('pool', ring.streams[i], view, src_ap, writes=[ring.keys[i]])
            return view, ring.keys[i]

        class WQ:
            def __init__(self, specs, ring=None, ahead=1):
                self.specs = specs
                self.loaded = []
                self.ring = ring
                self.ahead = ahead

            def get(self, i):
                upto = min(i + self.ahead, len(self.specs) - 1)
                while len(self.loaded) <= upto:
                    src, shape = self.specs[len(self.loaded)]
                    self.loaded.append(wload(src, shape, None, self.ring))
                return self.loaded[i]

        def bc_mid(ap2, n):
            return ap2.unsqueeze(1).to_broadcast([ap2.shape[0], n, ap2.shape[1]])

        def bc_last(ap2, n):
            return ap2.unsqueeze(2).to_broadcast([ap2.shape[0], ap2.shape[1], n])

        with nc.allow_non_contiguous_dma(reason="small param loads"):
            for (dst, src) in [
                (ident[:], c_ident[:, :]),
                (biasc[:], c_biasc[:, :]), (biasnf[:], c_biasnf[:, :]), (cmask[:], c_cmask[:, :]),
                (rmask[:], c_rmask[:, :]),
            ]:
                dma('sp', s_par, dst, src, writes=["par"])
        with ExitStack() as phA:
            PST = phA.enter_context(nc.sbuf_tensor("pst", [92, 128], F32))
            s_pa = C.stream("pva")
            dma('sp', s_pa, PST[:, :], pvec_a[:, :], writes=["PST"])
            op('pe', lambda e: e.transpose(out=PSUM[:, 0, 0:92], in_=PST[0:92, :], identity=ident[0:92, 0:92]),
               reads=["PST", "par"], writes=pk(0))
            op('act', lambda e: e.activation(out=PVA[:, :], in_=PSUM[:, 0, 0:92], func=AF.Copy), reads=pk(0), writes=["par"])
            fence()
        s_b64 = C.stream("b64")
        dma('pool', s_b64, blk64[:], c_blk64[:, :], writes=["blk64"])
        op('dve', lambda e: e.memset(ones[:], 1.0), writes=["ones"])
        with ExitStack() as ph0:
            biasp = ph0.enter_context(nc.sbuf_tensor("biasp", [128, 8, 256], F32))
            s_bp = C.stream("bp")
            dma('sp', s_bp, biasp[:], c_biasp[:, :, :], writes=["biasp"])
            op('act', lambda e: e.activation(out=EB[:], in_=biasp[:], func=AF.Exp), reads=["biasp"], writes=["EB"])
            fence()
        op('act', lambda e: e.activation(out=esink, in_=esink, func=AF.Exp), reads=["par"], writes=["esink"])
        op('dve', lambda e: e.tensor_scalar(out=qkg[:, :, 0:1], in0=qkg[:, :, 0:1], scalar1=0.125, scalar2=None,
                                            op0=ALU.mult), reads=["par"], writes=["qkg"])
        fence()

        def phase0():
          with ExitStack() as ph:
              XT = [ph.enter_context(nc.sbuf_tensor("xt%d" % i, [128, D], F32)) for i in range(2)]
              xts = [C.stream("xts%d" % i) for i in range(2)]
              for tt in range(17):
                  b = tt % 2
                  rows = 128 if tt < 16 else NS
                  src = xp[tt * 128:(tt + 1) * 128, :] if tt < 16 else xs[:, :]
                  dma('sp', xts[b], XT[b][0:rows, :], src, writes=[("xt", b)])
                  for half in range(2):
                      pi = psb()
                      for j in range(4):
                          k = half * 4 + j
                          op('pe', lambda e, k=k, j=j, pi=pi, b=b, rows=rows: e.transpose(
                              out=PSUM[:, pi, j * 128:j * 128 + rows], in_=XT[b][0:rows, k * 128:(k + 1) * 128],
                              identity=ident[0:rows, 0:rows]),
                              reads=[("xt", b)], writes=pk(pi), signal=(j == 3))
                      eng = 'act'
                      src_ps = PSUM[:, pi, :].rearrange("p (j t) -> p j t", j=4)[:, :, 0:rows]
                      dst = X[:, half * 4:half * 4 + 4, tt * 128:tt * 128 + rows]
                      if eng == 'act':
                          op('act', lambda e, s_=src_ps, d_=dst: e.activation(out=d_, in_=s_, func=AF.Copy),
                             reads=pk(pi), writes=[("X", tt)])
                      else:
                          op('dve', lambda e, s_=src_ps, d_=dst: e.tensor_copy(out=d_, in_=s_),
                             reads=pk(pi), writes=[("X", tt)])
              fence()

        TWO_PI = 2.0 * math.pi

        def tt(en, out, a, b, o, reads, writes):
            return op(en, lambda e: e.tensor_tensor(out=out, in0=a, in1=b, op=o), reads=reads, writes=writes)

        def ssm_tables(l, mid=None):
            with ExitStack() as ph:
                def t(name, shape, dt=F32):
                    return ph.enter_context(nc.sbuf_tensor("st%d_" % l + name, list(shape), dt))
                LR = t("LR", [128, 16]); LI = t("LI", [128, 16]); LS = t("LS", [128, 16])
                ANG = t("ANG", [128, 32]); KI = t("KI", [128, 32], I32); KF = t("KF", [128, 32])
                M1 = t("M1", [128, 32]); SC = t("SC", [128, 32])
                T1 = t("T1", [128, 16]); T2 = t("T2", [128, 16]); T3 = t("T3", [128, 16])
                CR = t("CR", [128, 16]); CI = t("CI", [128, 16]); RDEN = t("RDEN", [128, 16])
                Pc = t("Pc", [128, 16, CHUNK + 1]); Ps = t("Ps", [128, 16, CHUNK + 1])
                Q1 = t("Q1", [128, 16, CHUNK // 2]); Q2 = t("Q2", [128, 16, CHUNK // 2])
                BNr = t("BNr", [128, 16, 16]); BNi = t("BNi", [128, 16, 16])
                BBr = t("BBr", [128, 16, 16]); BBi = t("BBi", [128, 16, 16]); BT1 = t("BT1", [128, 16, 16])
                BEr = t("BEr", [128, 16, 32]); BEi = t("BEi", [128, 16, 32])
                CNr = t("CNr", [128, 4, 64]); CNi = t("CNi", [128, 4, 64])
                CE = t("CE", [128, 128])
                sp_ = C.stream("sst%d" % l)
                with nc.allow_non_contiguous_dma(reason="ssm params"):
                    PBT = t("PBT", [48, 128])
                    dma('sp', sp_, PBT[:, :], pvec_b[l], writes=["PBT"])
                    op('pe', lambda e: e.transpose(out=PSUM[:, 0, 0:48], in_=PBT[0:48, :], identity=ident[0:48, 0:48]),
                       reads=["PBT"], writes=pk(0))
                    op('act', lambda e: e.activation(out=LR[:], in_=PSUM[:, 0, 0:16], func=AF.Copy), reads=pk(0), writes=["sp"])
                    op('act', lambda e: e.activation(out=LI[:], in_=PSUM[:, 0, 16:32], func=AF.Copy), reads=pk(0), writes=["sp"])
                    op('act', lambda e: e.activation(out=LS[:], in_=PSUM[:, 0, 32:48], func=AF.Copy), reads=pk(0), writes=["sp"])
                    dma('sp', sp_, BNr[:], b_re[l].rearrange("(s gl p) c -> (gl p) s c", gl=2, p=64), writes=["sp"])
                    dma('sp', sp_, BNi[:], b_im[l].rearrange("(s gl p) c -> (gl p) s c", gl=2, p=64), writes=["sp"])
                    dma('sp', sp_, CNr[:], c_re[l].rearrange("(q r) p -> r q p", r=128), writes=["sp"])
                    dma('sp', sp_, CNi[:], c_im[l].rearrange("(q r) p -> r q p", r=128), writes=["sp"])
                R = ["sp"]
                op('act', lambda e: e.activation(out=LS[:], in_=LS[:], func=AF.Exp), reads=R, writes=["LS"])
                tt('dve', T1[:], LR[:], LS[:], ALU.mult, R + ["LS"], ["T1"])
                tt('dve', ANG[:, 0:16], LI[:], LS[:], ALU.mult, R + ["LS"], ["ANG"])
                op('act', lambda e: e.activation(out=MAG[:], in_=T1[:], func=AF.Exp), reads=["T1"], writes=["MAG"])
                op('dve', lambda e: e.tensor_scalar(out=ANG[:, 16:32], in0=ANG[:, 0:16], scalar1=math.pi / 2, scalar2=None,
                                                    op0=ALU.add), reads=["ANG"], writes=["ANG"])
                op('dve', lambda e: e.tensor_scalar(out=KF[:], in0=ANG[:], scalar1=1.0 / TWO_PI, scalar2=None, op0=ALU.mult),
                   reads=["ANG"], writes=["KF"])
                op('dve', lambda e: e.tensor_copy(out=KI[:], in_=KF[:]), reads=["KF"], writes=["KI"])
                op('dve', lambda e: e.tensor_copy(out=KF[:], in_=KI[:]), reads=["KI"], writes=["KF"])
                op('dve', lambda e: e.scalar_tensor_tensor(out=ANG[:], in0=KF[:], scalar=-TWO_PI, in1=ANG[:], op0=ALU.mult,
                                                           op1=ALU.add), reads=["KF", "ANG"], writes=["ANG"])
                op('dve', lambda e: e.tensor_single_scalar(out=M1[:], in_=ANG[:], scalar=math.pi, op=ALU.is_gt),
                   reads=["ANG"], writes=["M1"])
                op('dve', lambda e: e.scalar_tensor_tensor(out=ANG[:], in0=M1[:], scalar=-TWO_PI, in1=ANG[:], op0=ALU.mult,
                                                           op1=ALU.add), reads=["M1", "ANG"], writes=["ANG"])
                op('dve', lambda e: e.tensor_single_scalar(out=M1[:], in_=ANG[:], scalar=-math.pi, op=ALU.is_lt),
                   reads=["ANG"], writes=["M1"])
                op('dve', lambda e: e.scalar_tensor_tensor(out=ANG[:], in0=M1[:], scalar=TWO_PI, in1=ANG[:], op0=ALU.mult,
                                                           op1=ALU.add), reads=["M1", "ANG"], writes=["ANG"])
                op('act', lambda e: e.activation(out=SC[:], in_=ANG[:], func=AF.Sin), reads=["ANG"], writes=["SC"])
                SN = SC[:, 0:16]
                CS = SC[:, 16:32]
                tt('dve', AR[:], MAG[:], CS, ALU.mult, ["MAG", "SC"], ["AR"])
                tt('dve', AI[:], MAG[:], SN, ALU.mult, ["MAG", "SC"], ["AI"])
                tt('dve', T1[:], LR[:], LR[:], ALU.mult, R + ["MAG"], ["T1"])
                tt('dve', T2[:], LI[:], LI[:], ALU.mult, R, ["T2"])
                tt('dve', T1[:], T1[:], T2[:], ALU.add, ["T1", "T2"], ["T1"])
                op('dve', lambda e: e.reciprocal(out=RDEN[:], in_=T1[:]), reads=["T1"], writes=["RDEN"])
                op('dve', lambda e: e.tensor_scalar(out=T3[:], in0=AR[:], scalar1=-1.0, scalar2=None, op0=ALU.add),
                   reads=["AR"], writes=["T3"])
                tt('dve', T1[:], T3[:], LR[:], ALU.mult, ["T3", "RDEN"], ["T1"])
                tt('dve', T2[:], AI[:], LI[:], ALU.mult, ["AI", "T1"], ["T2"])
                tt('dve', T1[:], T1[:], T2[:], ALU.add, ["T1", "T2"], ["T1"])
                tt('dve', CR[:], T1[:], RDEN[:], ALU.mult, ["T1", "RDEN"], ["CR"])
                tt('dve', T1[:], AI[:], LR[:], ALU.mult, ["CR", "AI"], ["T1"])
                tt('dve', T2[:], T3[:], LI[:], ALU.mult, ["T3", "CR"], ["T2"])
                tt('dve', T1[:], T1[:], T2[:], ALU.subtract, ["T1", "T2"], ["T1"])
                tt('dve', CI[:], T1[:], RDEN[:], ALU.mult, ["T1", "RDEN"], ["CI"])
                op('dve', lambda e: e.memset(Pc[:, :, 0:1], 1.0), writes=["P"])
                op('dve', lambda e: e.memset(Ps[:, :, 0:1], 0.0), writes=["P"])
                op('dve', lambda e: e.tensor_copy(out=Pc[:, :, 1:2], in_=CS.unsqueeze(2)), reads=["SC", "P"], writes=["P"])
                op('dve', lambda e: e.tensor_copy(out=Ps[:, :, 1:2], in_=SN.unsqueeze(2)), reads=["SC", "P"], writes=["P"])
                m = 1
                while m < CHUNK:
                    Ac = Pc[:, :, 1:m + 1]
                    As = Ps[:, :, 1:m + 1]
                    Bc = Pc[:, :, m:m + 1].to_broadcast([128, 16, m])
                    Bs = Ps[:, :, m:m + 1].to_broadcast([128, 16, m])
                    q1 = Q1[:, :, 0:m]
                    q2 = Q2[:, :, 0:m]
                    tt('dve', q1, Ac, Bc, ALU.mult, ["P"], ["Q1"])
                    tt('dve', q2, As, Bs, ALU.mult, ["P"], ["Q2"])
                    tt('dve', Pc[:, :, m + 1:2 * m + 1], q1, q2, ALU.subtract, ["Q1", "Q2", "P"], ["P"])
                    tt('dve', q1, Ac, Bs, ALU.mult, ["P"], ["Q1"])
                    tt('dve', q2, As, Bc, ALU.mult, ["P"], ["Q2"])
                    tt('dve', Ps[:, :, m + 1:2 * m + 1], q1, q2, ALU.add, ["Q1", "Q2", "P"], ["P"])
                    m *= 2
                op('dve', lambda e: e.tensor_copy(out=PHc[:], in_=Pc[:]), reads=["P"], writes=["PH"])
                op('dve', lambda e: e.tensor_copy(out=PHs[:], in_=Ps[:]), reads=["P"], writes=["PH"])
                op('dve', lambda e: e.tensor_copy(out=R128c[:], in_=Pc[:, :, CHUNK]), reads=["P"], writes=["R128"])
                op('dve', lambda e: e.tensor_copy(out=R128s[:], in_=Ps[:, :, CHUNK]), reads=["P"], writes=["R128"])
                op('dve', lambda e: e.tensor_scalar(out=nR128s[:], in0=Ps[:, :, CHUNK], scalar1=-1.0, scalar2=None,
                                                    op0=ALU.mult), reads=["P"], writes=["R128"])
                op('dve', lambda e: e.tensor_copy(out=P127c[:], in_=Pc[:, :, CHUNK - 1]), reads=["P"], writes=["R128"])
                op('dve', lambda e: e.tensor_copy(out=P127s[:], in_=Ps[:, :, CHUNK - 1]), reads=["P"], writes=["R128"])
                CRb = bc_last(CR[:], 16)
                CIb = bc_last(CI[:], 16)
                tt('dve', BBr[:], BNr[:], CRb, ALU.mult, R + ["CR"], ["BBr"])
                tt('dve', BT1[:], BNi[:], CIb, ALU.mult, R + ["CI"], ["BT1"])
                tt('dve', BBr[:], BBr[:], BT1[:], ALU.subtract, ["BBr", "BT1"], ["BBr"])
                tt('dve', BBi[:], BNi[:], CRb, ALU.mult, R + ["CR"], ["BBi"])
                tt('dve', BT1[:], BNr[:], CIb, ALU.mult, R + ["CI", "BBr"], ["BT1"])
                tt('dve', BBi[:], BBi[:], BT1[:], ALU.add, ["BBi", "BT1"], ["BBi"])
                if mid is not None:
                    mid()
                for (BE, BB, BTd, nm) in ((BEr, BBr, BTr, "r"), (BEi, BBi, BTi, "i")):
                    op('dve', lambda e, BE=BE: e.memset(BE[:], 0.0), writes=["BE" + nm])
                    op('dve', lambda e, BE=BE, BB=BB: e.tensor_copy(out=BE[0:64, :, 0:16], in_=BB[0:64, :, :]),
                       reads=["BB" + nm, "BE" + nm], writes=["BE" + nm])
                    op('dve', lambda e, BE=BE, BB=BB: e.tensor_copy(out=BE[64:128, :, 16:32], in_=BB[64:128, :, :]),
                       reads=["BB" + nm, "BE" + nm], writes=["BE" + nm])
                    for q in range(4):
                        pi = psb()
                        op('pe', lambda e, BE=BE, q=q, pi=pi: e.transpose(
                            out=PSUM[:, pi, 0:128], in_=BE[:, 4 * q:4 * q + 4, :].rearrange("p a b -> p (a b)"),
                            identity=ident[:, :]), reads=["BE" + nm], writes=pk(pi))
                        for slot in range(4):
                            op('dve', lambda e, BTd=BTd, q=q, slot=slot, pi=pi: e.tensor_scalar(
                                out=BTd[:, 4 * q + slot, :], in0=PSUM[:, pi, 0:128], scalar1=rmask[:, slot:slot + 1],
                                scalar2=None, op0=ALU.mult), reads=pk(pi), writes=["BT" + nm])
                for (CN, CTd, sgn, nm) in ((CNr, CTr, 1.0, "r"), (CNi, nCTi, -1.0, "i"), (CNr, nCTr, -1.0, "r2")):
                    op('dve', lambda e, CTd=CTd: e.memset(CTd[:], 0.0), writes=["CT" + nm])
                    for q in range(4):
                        tt('dve', CE[:, 0:64], CN[:, q, :], cmask[:, 0:64], ALU.mult, R, ["CE"])
                        tt('dve', CE[:, 64:128], CN[:, q, :], cmask[:, 64:128], ALU.mult, R, ["CE"])
                        pi = psb()
                        op('pe', lambda e, pi=pi: e.transpose(out=PSUM[:, pi, 0:128], in_=CE[:, :], identity=ident[:, :]),
                           reads=["CE"], writes=pk(pi))
                        for slot in range(4):
                            op('dve', lambda e, CTd=CTd, q=q, slot=slot, pi=pi, sgn=sgn: e.tensor_scalar(
                                out=CTd[:, 4 * q + slot, 32 * slot:32 * slot + 32], in0=PSUM[:, pi, 32 * slot:32 * slot + 32],
                                scalar1=sgn, scalar2=None, op0=ALU.mult), reads=pk(pi), writes=["CT" + nm])
                op('dve', lambda e: e.memset(GE2[:], 0.0), writes=["GE"])
                op('dve', lambda e: e.tensor_copy(out=SS[:, :, 0], in_=nR128s[:, :]), reads=["R128"], writes=["SS"])
                op('dve', lambda e: e.tensor_copy(out=SS[:, :, 1], in_=R128s[:, :]), reads=["R128"], writes=["SS"])
                fence()

        def mm_group(pi, n, pairs, reads, extra_writes=()):
            last = len(pairs) - 1
            tok = None
            for i, (lh, rh) in enumerate(pairs):
                tok = op('pe', lambda e, lh=lh, rh=rh, i=i: e.matmul(PSUM[:, pi, 0:n], lhsT=lh, rhs=rh, start=(i == 0),
                                                                     stop=(i == last)),
                         reads=reads, writes=pk(pi) + list(extra_writes), signal=(i == last))
            return tok

        def layer_phase_a(l):
            with ExitStack() as ph:
                def t(name, shape, dt=BF16):
                    return ph.enter_context(nc.sbuf_tensor("a%d_" % l + name, list(shape), dt))
                XN = t("XN", [128, 8, 576])
                MIX = t("MIX", [128, 8, 576])
                R2 = t("R2", [128, 4, 576])
                KT = t("KT", [128, 128 + 576])
                VT = t("VT", [128, 5, 128])
                VS = t("VS", [64, 128])
                U = t("U", [128, 4, 576])
                AT = t("AT", [128, 4, 576])
                STt = t("ST", [128, 4, 576])
                YG = t("YG", [128, 4, 576])
                s_o = C.stream("ao%d" % l)
                s_o1 = C.stream("ao1_%d" % l)
                s_o2 = C.stream("ao2_%d" % l)

                for bi in range(4):
                  c0 = bi * 512
                  subs = [(0, 512)] + ([(512, 64)] if bi == 3 else [])
                  NB = 576 if bi == 3 else 512
                  with ExitStack() as ph2:
                    def t2(name, shape, dt=BF16):
                        return ph2.enter_context(nc.sbuf_tensor("a%d_%d_" % (l, bi) + name, list(shape), dt))
                    SQ = t2("SQ", [128, 8, 512])
                    RS = t2("RS", [128, 512], F32)
                    XG = t2("XG", [128, 2, 512], F32)
                    RSq = t2("RSq", [128, 512], F32)
                    SQh = t2("SQh", [128, 512])
                    KF = t2("KF", [128, 128], F32)
                    KSF = t2("KSF", [128, 64], F32)
                    OUTS = t2("OUTS", [128, 128], F32)
                    OUTS2 = t2("OUTS2", [128, 128], F32)
                    wq_a2 = WQ([(w_in[l][:, ch_ * 512:ch_ * 512 + (512 if ch_ < 2 else 256)].rearrange("(k p) n -> p k n", p=128),
                                 [128, 8, (512 if ch_ < 2 else 256)]) for ch_ in range(3)])
                    wq_a2.get(0)
                    for (o, n) in subs:
                        xs_ = X[:, :, c0 + o:c0 + o + n]
                        op('act', lambda e, xs_=xs_, n=n: e.activation(out=SQ[:, :, 0:n], in_=xs_, func=AF.Square),
                           reads=["X"], writes=["SQ"])
                        if 'a1a' in SKIP:
                            continue
                        pi = psb()
                        mm_group(pi, n, [(ones[:, :], SQ[:, k, 0:n]) for k in range(8)], ["SQ", "ones"])
                        op('act', lambda e, pi=pi, n=n: e.activation(out=RS[:, 0:n], in_=PSUM[:, pi, 0:n], func=AF.Sqrt,
                                                                     bias=EPS, scale=1.0 / D), reads=pk(pi), writes=["RS"])
                        if 'a1b' in SKIP:
                            continue
                        op('dve', lambda e, n=n: e.reciprocal(out=RS[:, 0:n], in_=RS[:, 0:n]), reads=["RS"], writes=["RS"])
                        if 'a1c' in SKIP:
                            continue
                        for k in range(8):
                            xg = XG[:, k % 2, 0:n]
                            op('act', lambda e, k=k, o=o, n=n, xg=xg: e.activation(
                                out=xg, in_=X[:, k, c0 + o:c0 + o + n], func=AF.Copy, scale=g1[:, l, k:k + 1]),
                                reads=["X"], writes=[("XG", k % 2)])
                            op('dve', lambda e, k=k, o=o, n=n, xg=xg: e.tensor_tensor(
                                out=XN[:, k, o:o + n], in0=xg, in1=RS[:, 0:n], op=ALU.mult),
                                reads=[("XG", k % 2), "RS"], writes=[("XN", k)])
                    if STAGE >= 8 and bi >= 1:
                        norm2_into(l, MIX, SQ, RS, XG, (bi - 1) * 512, 0, 512)
                    XNr = [("XN", k) for k in range(8)]
                    for ch in range(3):
                        wcols = 512 if ch < 2 else 256
                        Wv, wk = wq_a2.get(ch)
                        for j in range(wcols // 128):
                            col = ch * 512 + j * 128
                            if col == 640:
                                continue
                            for (o, n) in subs:
                                pi = psb()
                                mm_group(pi, n, [(Wv[:, k, j * 128:(j + 1) * 128], XN[:, k, o:o + n]) for k in range(8)],
                                         XNr + [wk])
                                if 'a2a' in SKIP:
                                    continue
                                if col < 640:
                                    isq = col < 512
                                    op('act', lambda e, pi=pi, n=n: e.activation(out=SQh[:, 0:n], in_=PSUM[:, pi, 0:n],
                                                                                 func=AF.Square), reads=pk(pi), writes=["SQh"])
                                    p2 = psb()
                                    mm_group(p2, n, [(blk64[:, :], SQh[:, 0:n])], ["SQh", "blk64"])
                                    op('act', lambda e, p2=p2, n=n: e.activation(out=RSq[:, 0:n], in_=PSUM[:, p2, 0:n],
                                                                                 func=AF.Sqrt, bias=EPS, scale=1.0 / 64),
                                       reads=pk(p2), writes=["RSq"])
                                    op('dve', lambda e, n=n: e.reciprocal(out=RSq[:, 0:n], in_=RSq[:, 0:n]),
                                       reads=["RSq"], writes=["RSq"])
                                    if 'a2b' in SKIP:
                                        continue
                                    if isq:
                                        dst = R2[:, col // 128, o:o + n]
                                        dk = ("R2", col // 128)
                                        gi_ = 0
                                    else:
                                        dst = KT[:, 128 + o:128 + o + n]
                                        dk = "KT"
                                        gi_ = 1
                                    op('dve', lambda e, pi=pi, n=n, dst=dst, gi_=gi_: e.scalar_tensor_tensor(
                                        out=dst, in0=PSUM[:, pi, 0:n], scalar=qkg[:, l, gi_:gi_ + 1], in1=RSq[:, 0:n],
                                        op0=ALU.mult, op1=ALU.mult), reads=pk(pi) + ["RSq", "qkg"], writes=[dk])
                                    if (not isq) and bi == 3 and 'a2c' not in SKIP:
                                        if o == 0:
                                            op('dve', lambda e, pi=pi: e.scalar_tensor_tensor(
                                                out=KF[:, :], in0=PSUM[:, pi, 384:512], scalar=qkg[:, l, 1:2],
                                                in1=RSq[:, 384:512], op0=ALU.mult, op1=ALU.mult),
                                                reads=pk(pi) + ["RSq"], writes=["KF"])
                                            p3 = psb()
                                            op('pe', lambda e, p3=p3: e.transpose(out=PSUM[:, p3, 0:128], in_=KF[:, :],
                                                                                  identity=ident[:, :]),
                                               reads=["KF"], writes=pk(p3))
                                            op('act', lambda e, p3=p3: e.activation(out=OUTS[:, :], in_=PSUM[:, p3, 0:128],
                                                                                    func=AF.Copy), reads=pk(p3), writes=["OUTS"])
                                            dma('sp', s_o1, kp[l], OUTS[:, :], reads=["OUTS"])
                                        else:
                                            op('dve', lambda e, pi=pi: e.scalar_tensor_tensor(
                                                out=KSF[:, :], in0=PSUM[:, pi, 0:64], scalar=qkg[:, l, 1:2],
                                                in1=RSq[:, 0:64], op0=ALU.mult, op1=ALU.mult),
                                                reads=pk(pi) + ["RSq"], writes=["KSF"])
                                            p3 = psb()
                                            op('pe', lambda e, p3=p3: e.transpose(out=PSUM[0:64, p3, 0:128], in_=KSF[:, :],
                                                                                  identity=ident[:, :]),
                                               reads=["KSF"], writes=pk(p3))
                                            op('act', lambda e, p3=p3: e.activation(out=OUTS2[0:64, :], in_=PSUM[0:64, p3, 0:128],
                                                                                    func=AF.Copy), reads=pk(p3), writes=["OUTS2"])
                                            for i_ in range(4):
                                                dma('sp', s_o2, ksam[l][:, 124 + i_, :], OUTS2[i_ * 16:(i_ + 1) * 16, :],
                                                    reads=["OUTS2"])
                                else:
                                    ut = (col - 768) // 128
                                    op('act', lambda e, pi=pi, n=n, ut=ut, o=o: e.activation(
                                        out=U[:, ut, o:o + n], in_=PSUM[:, pi, 0:n], func=AF.Copy),
                                        reads=pk(pi), writes=[("U", ut)])
                        if ch == 1 and 'a2d' not in SKIP:
                            pi = psb()
                            for tl in range(4):
                                for k in range(8):
                                    op('pe', lambda e, tl=tl, k=k, pi=pi: e.matmul(
                                        PSUM[:, pi, tl * 128:(tl + 1) * 128],
                                        lhsT=XN[:, k, tl * 128:(tl + 1) * 128], rhs=Wv[:, k, 128:256],
                                        start=(k == 0), stop=(k == 7)),
                                        reads=XNr + [wk], writes=pk(pi), signal=(k == 7))
                            if 'v1' in SKIP:
                                continue
                            op('act', lambda e, pi=pi: e.activation(out=VT[:, 1:5, :], in_=PSUM[:, pi, :].rearrange(
                                "p (t d) -> p t d", t=4), func=AF.Copy), reads=pk(pi), writes=["VT"])
                            if bi == 3 and 'v2' not in SKIP:
                                op('dve', lambda e, pi=pi: e.tensor_copy(out=OUTS[:, :], in_=PSUM[:, pi, 384:512]),
                                   reads=pk(pi), writes=["OUTS"])
                                dma('sp', s_o1, vp[l], OUTS[:, :], reads=["OUTS"])
                                pv5 = psb()
                                for k in range(8):
                                    op('pe', lambda e, k=k, pv5=pv5: e.matmul(
                                        PSUM[0:64, pv5, 0:128], lhsT=XN[:, k, 512:576], rhs=Wv[:, k, 128:256],
                                        start=(k == 0), stop=(k == 7)), reads=XNr + [wk], writes=pk(pv5), signal=(k == 7))
                                op('act', lambda e, pv5=pv5: e.activation(out=VS[:, :], in_=PSUM[0:64, pv5, 0:128], func=AF.Copy),
                                   reads=pk(pv5), writes=["VS"])
                                op('dve', lambda e, pv5=pv5: e.tensor_copy(out=OUTS2[0:64, :], in_=PSUM[0:64, pv5, 0:128]),
                                   reads=pk(pv5), writes=["OUTS2"])
                                for i_ in range(4):
                                    dma('sp', s_o2, vsam[l][:, 124 + i_, :], OUTS2[i_ * 16:(i_ + 1) * 16, :], reads=["OUTS2"])
                    fence()
                  E = dict(U=U, s_o=s_o, R2=R2, KT=KT, VT=VT, VS=VS, AT=AT, ST=STt, YG=YG, XN=XN, MIX=MIX, subs=subs, c0=c0)
                  if STAGE >= 3:
                      attn_block(l, bi, E)
                  if STAGE >= 2:
                      ssm_block(l, bi, E)
                  if STAGE >= 5:
                      mix_block(l, bi, E)
                  if DEBUG_DUMP and l == 0:
                      dma('sp', s_o, dbg_at[bi], AT[:, :, :], reads=[])
                      dma('sp', s_o, dbg_st[bi], STt[:, :, :], reads=[])
                      dma('sp', s_o, dbg_yg[bi], YG[:, :, :], reads=[])
                      dma('sp', s_o, dbg_mix[bi], MIX[:, :, :], reads=[])
                  op('pool', lambda e: e.tensor_copy(out=KT[:, 0:128], in_=KT[:, 128 + 384:128 + 512]), reads=["KT"], writes=["KT"])
                  op('pool', lambda e: e.tensor_copy(out=VT[:, 0, :], in_=VT[:, 4, :]), reads=["VT"], writes=["VT"])
                if "shift" not in SKIP:
                    with nc.allow_non_contiguous_dma(reason="cache shift"):
                        dma('sp', s_o, ksam[l][:, 0:124, :], cache_k[l][:, 4:128, :])
                        dma('sp', s_o, vsam[l][:, 0:124, :], cache_v[l][:, 4:128, :])
                if STAGE >= 8:
                    ffn_tail(l, MIX)
                fence()

        def ffn_specs(l):
            fs = []
            for c in range(8):
                fs.append((w_up[l][:, c * 512:(c + 1) * 512].rearrange("(k p) n -> p k n", p=128), [128, 8, 512]))
                fs.append((w_down[l][c * 512:(c + 1) * 512, :].rearrange("(k p) n -> p k n", p=128), [128, 4, 1024]))
            return fs

        def norm2_into(l, XN2, SQ, RS, XG, xcol, o, n):
            pi = psb()
            for hf in range(2):
                op('act', lambda e, hf=hf: e.activation(out=SQ[:, 0:4, 0:n], in_=X[:, 4 * hf:4 * hf + 4, xcol:xcol + n], func=AF.Square),
                   reads=["X"], writes=["SQ"])
                for k_ in range(4):
                    op('pe', lambda e, k_=k_, hf=hf: e.matmul(PSUM[:, pi, 0:n], lhsT=ones[:, :], rhs=SQ[:, k_, 0:n],
                                                             start=(hf == 0 and k_ == 0), stop=(hf == 1 and k_ == 3)),
                       reads=["SQ", "ones"], writes=pk(pi), signal=(k_ == 3))
            op('act', lambda e: e.activation(out=RS[:, 0:n], in_=PSUM[:, pi, 0:n], func=AF.Sqrt, bias=EPS, scale=1.0 / D),
               reads=pk(pi), writes=["RS"])
            op('dve', lambda e: e.reciprocal(out=RS[:, 0:n], in_=RS[:, 0:n]), reads=["RS"], writes=["RS"])
            for k_ in range(8):
                xg = XG[:, k_ % 2, 0:n]
                op('act', lambda e, k_=k_, xg=xg: e.activation(out=xg, in_=X[:, k_, xcol:xcol + n], func=AF.Copy,
                                                              scale=g2[:, l, k_:k_ + 1]), reads=["X"], writes=[("XG", k_ % 2)])
                op('dve', lambda e, k_=k_, xg=xg: e.tensor_tensor(out=XN2[:, k_, o:o + n], in0=xg, in1=RS[:, 0:n], op=ALU.mult),
                   reads=[("XG", k_ % 2), "RS"], writes=[("MIX", k_)])

        def ssm_block(l, bi, E):
            U = E['U']; s_o = E['s_o']; YG = E['YG']; ST = E['ST']; subs = E['subs']
            with ExitStack() as ph:
                def t(name, shape, dt=BF16):
                    return ph.enter_context(nc.sbuf_tensor("s%d_%d_" % (l, bi) + name, list(shape), dt))
                Y1 = t("Y1", [128, 512]); T1 = t("T1", [128, 512]); SG = T1
                php = ExitStack()
                def tp(name, shape, dt=BF16):
                    return php.enter_context(nc.sbuf_tensor("sp%d_%d_" % (l, bi) + name, list(shape), dt))
                Zt = [tp("Z%d" % i, [128, 2, 512]) for i in range(2)]
                PRM = [tp("PRM%d" % i, [128, 4, 512]) for i in range(2)]
                Ma = PRM[1][:, 0:2, :]
                Mb = PRM[1][:, 2:4, :]
                MK = ("PR", 1)
                Xc = [tp("Xc%d" % i, [128, 2, 512]) for i in range(2)]
                if l == 0 and bi == 0:
                    print("SBUF remaining in ssm prompt scope:", nc.sbuf_bytes_remaining)
                Gt = [tp("G%d" % i, [128, 2, 512]) for i in range(2)]
                PR = PRM
                INt = [tp("IN%d" % i, [128, 4, 2], F32) for i in range(2)]
                Ut = [tp("U%d" % i, [128, 2], F32) for i in range(2)]
                HO = tp("HO", [128, 2, 16], F32); HT = tp("HT", [128, 16], F32)
                HOUT = tp("HOUT", [16, 2, 128], F32)
                PTa = [tp("PTa%d" % i, [128, 256]) for i in range(2)]
                DRa = tp("DRa", [128, 256], F32)
                v3 = lambda ap_: ap_.rearrange("p (c j) -> p c j", c=NCHUNK)
                R2 = E['R2']; KT = E['KT']; VT = E['VT']; AT = E['AT']
                att_it = [0]

                def att_unit(i, hp, b):
                    hs = i * 2 + hp
                    rows = slice(hp * 64, (hp + 1) * 64)
                    has_prev = (bi * 4 + b) > 0
                    tb = att_it[0] % 2
                    att_it[0] += 1
                    qv = R2[rows, i, b * 128:(b + 1) * 128]
                    if has_prev:
                        op('pe', lambda e: e.matmul(PSUM[:, BS, 0:128], lhsT=KT[rows, b * 128:(b + 1) * 128], rhs=qv,
                                                    start=True, stop=True), reads=[("R2", i), "KT"], writes=pk(BS), signal=False)
                    op('pe', lambda e: e.matmul(PSUM[:, BS, 128:256], lhsT=KT[rows, 128 + b * 128:128 + (b + 1) * 128], rhs=qv,
                                                start=True, stop=True), reads=[("R2", i), "KT"], writes=pk(BS), signal=True)
                    c_lo = 0 if has_prev else 128
                    op('act', lambda e: e.activation(out=PTa[tb][:, c_lo:256], in_=PSUM[:, BS, c_lo:256], func=AF.Exp),
                       reads=pk(BS), writes=[("PTa", tb)])
                    tt('pool', PTa[tb][:, c_lo:256], PTa[tb][:, c_lo:256], EB[:, hs, c_lo:256], ALU.mult, [("PTa", tb), "EB"],
                       [("PTa", tb)])
                    return lambda: att_pv(rows, b, tb, has_prev)

                def att_pv(rows, b, tb, has_prev):
                    parts = ([(VT[:, b, rows], PTa[tb][:, 0:128])] if has_prev else []) + [(VT[:, b + 1, rows], PTa[tb][:, 128:256])]
                    bb = b % 2
                    for (coff, use_ones) in ((0, False), (256, True)):
                        for ii, (vv, pp) in enumerate(parts):
                            lh = ones[:, 0:64] if use_ones else vv
                            op('pe', lambda e, lh=lh, pp=pp, ii=ii, coff=coff: e.matmul(
                                PSUM[rows, BOD, coff + bb * 128:coff + (bb + 1) * 128], lhsT=lh, rhs=pp, start=(ii == 0),
                                stop=(ii == len(parts) - 1)), reads=[("PTa", tb), "VT", "ones"], writes=pk(BOD),
                                signal=(ii == len(parts) - 1))

                def att_norm(i, h):
                    op('act', lambda e: e.activation(out=DRa[:, :], in_=PSUM[:, BOD, 256:512], func=AF.Ln, bias=esink[:, l, i:i + 1],
                                                     scale=1.0), reads=pk(BOD) + ["esink"], writes=["DRa"])
                    op('act', lambda e: e.activation(out=DRa[:, :], in_=DRa[:, :], func=AF.Exp, scale=-1.0), reads=["DRa"], writes=["DRa"])
                    tt('dve', AT[:, i, h * 256:(h + 1) * 256], PSUM[:, BOD, 0:256], DRa[:, :], ALU.mult, pk(BOD) + ["DRa"], [("AT", i)])

                att_groups = []
                if STAGE >= 3:
                    for i in range(4):
                        for h in range(2):
                            att_groups.append((i, h))

                def epilogue(q, yb, o, n):
                    op('dve', lambda e: e.scalar_tensor_tensor(out=Y1[:, 0:n], in0=U[:, q, o:o + n], scalar=dsk[:, l, q:q + 1],
                                                               in1=PSUM[:, yb, 0:n], op0=ALU.mult, op1=ALU.add),
                       reads=pk(yb) + [("U", q)], writes=["Y1"])
                    tt('dve', T1[:, 0:n], Y1[:, 0:n], Y1[:, 0:n], ALU.mult, ["Y1"], ["T1"])
                    op('dve', lambda e: e.tensor_scalar(out=T1[:, 0:n], in0=T1[:, 0:n], scalar1=0.044715, scalar2=1.0,
                                                        op0=ALU.mult, op1=ALU.add), reads=["T1"], writes=["T1"])
                    tt('dve', T1[:, 0:n], T1[:, 0:n], Y1[:, 0:n], ALU.mult, ["T1", "Y1"], ["T1"])
                    op('act', lambda e: e.activation(out=T1[:, 0:n], in_=T1[:, 0:n], func=AF.Sigmoid, scale=1.5957691216057308),
                       reads=["T1"], writes=["T1"])
                    tt('dve', YG[:, q, o:o + n], Y1[:, 0:n], T1[:, 0:n], ALU.mult, ["Y1", "T1"], [("YG", q)])

                BX0, BY, BS, BOD, BUP = 0, 2, 3, 4, 5

                def x0_mm(s):
                    q = s // 4
                    mm_group(BX0, 512, [(BTr[:, s, :], U[:, q, 0:512])], [("U", q), "BTr"])
                    mm_group(BX0 + 1, 512, [(BTi[:, s, :], U[:, q, 0:512])], [("U", q), "BTi"])

                def evac(s):
                    b_ = s % 2
                    xk = ("Xc", b_)
                    op('act', lambda e: e.activation(out=Xc[b_][:, 0, :], in_=PSUM[:, BX0, :], func=AF.Copy), reads=pk(BX0), writes=[xk])
                    op('act', lambda e: e.activation(out=Xc[b_][:, 1, :], in_=PSUM[:, BX0 + 1, :], func=AF.Copy), reads=pk(BX0 + 1),
                       writes=[xk])

                def modops(s):
                    b_ = s % 2
                    xk = ("Xc", b_)
                    zk = ("Z", b_)
                    c4 = bc_mid(PHc[:, s, 0:CHUNK], 2 * NCHUNK)
                    s4 = bc_mid(PHs[:, s, 0:CHUNK], 2 * NCHUNK)
                    x4 = Xc[b_][:, :, :].rearrange("p c (h j) -> p (c h) j", j=CHUNK)
                    tt('dve', Ma.rearrange("p c (h j) -> p (c h) j", j=CHUNK), x4, c4, ALU.mult, [xk, "PH"], [MK])
                    tt('dve', Mb.rearrange("p c (h j) -> p (c h) j", j=CHUNK), x4, s4, ALU.mult, [xk, "PH"], [MK])
                    tt('dve', Zt[b_][:, 0, :], Ma[:, 0, :], Mb[:, 1, :], ALU.add, [MK], [zk])
                    tt('dve', Zt[b_][:, 1, :], Ma[:, 1, :], Mb[:, 0, :], ALU.subtract, [MK], [zk])

                def chain(tiles, hook=lambda: None):
                    for ch in range(NCHUNK):
                        for s in tiles:
                            b_ = s % 2
                            gk = ("G", b_)
                            if ch == 0:
                                ge = GE2[:, s, :]
                                ger = GE2[:, s, ::-1]
                                rk = ["GE"]
                            else:
                                ge = Gt[b_][:, :, ch * CHUNK - 1]
                                ger = Gt[b_][:, ::-1, ch * CHUNK - 1]
                                rk = [gk]
                            tt('dve', Ut[b_][:, :], ger, SS[:, s, :], ALU.mult, rk + ["SS"], [("UT", b_)])
                            op('dve', lambda e, ge=ge, s=s, b_=b_, ch=ch: e.scalar_tensor_tensor(
                                out=INt[b_][:, ch, :], in0=ge, scalar=R128c[:, s:s + 1], in1=Ut[b_][:, :], op0=ALU.mult, op1=ALU.add),
                                reads=rk + [("UT", b_)], writes=[("IN", b_)])
                        hook()
                        for s in tiles:
                            b_ = s % 2
                            gk = ("G", b_)
                            cs_ = slice(ch * CHUNK, (ch + 1) * CHUNK)
                            for c_ in range(2):
                                op('dve', lambda e, s=s, ch=ch, cs_=cs_, c_=c_, b_=b_: e.tensor_tensor_scan(
                                    out=Gt[b_][:, c_, cs_], data0=MAG[:, s:s + 1].to_broadcast([128, CHUNK]), data1=Zt[b_][:, c_, cs_],
                                    initial=INt[b_][:, ch, c_:c_ + 1], op0=ALU.mult, op1=ALU.add),
                                    reads=[("Z", b_), ("IN", b_)], writes=[gk])
                            hook()

                def finish(s):
                    q = s // 4
                    slot = s % 4
                    yb = BY
                    b_ = s % 2
                    gk = ("G", b_)
                    cb = bc_mid(PHc[:, s, 0:CHUNK], NCHUNK)
                    sb_ = bc_mid(PHs[:, s, 0:CHUNK], NCHUNK)
                    op('dve', lambda e: e.tensor_copy(out=GE2[:, s, :], in_=Gt[b_][:, :, 511]), reads=[gk], writes=["GE"])
                    P_ = PR[b_]
                    pkey = ("PR", b_)
                    gr_ = v3(Gt[b_][:, 0, :])
                    gi_ = v3(Gt[b_][:, 1, :])
                    c4 = bc_mid(PHc[:, s, 0:CHUNK], 2 * NCHUNK)
                    s4 = bc_mid(PHs[:, s, 0:CHUNK], 2 * NCHUNK)
                    g4 = Gt[b_][:, :, :].rearrange("p c (h j) -> p (c h) j", j=CHUNK)
                    tt('dve', P_[:, 0:2, :].rearrange("p c (h j) -> p (c h) j", j=CHUNK), g4, c4, ALU.mult, [gk, "PH"], [pkey])
                    tt('dve', P_[:, 2:4, :].rearrange("p c (h j) -> p (c h) j", j=CHUNK), g4, s4, ALU.mult, [gk, "PH"], [pkey])
                    pairs = [(CTr[:, s, :], P_[:, 0, :]), (nCTi[:, s, :], P_[:, 1, :]), (nCTi[:, s, :], P_[:, 2, :]),
                             (nCTr[:, s, :], P_[:, 3, :])]
                    for ii, (lh, rh) in enumerate(pairs):
                        first = (slot == 0 and ii == 0)
                        lastm = (slot == 3 and ii == 3)
                        op('pe', lambda e, lh=lh, rh=rh, first=first, lastm=lastm: e.matmul(
                            PSUM[:, yb, 0:512], lhsT=lh, rhs=rh, start=first, stop=lastm),
                            reads=[pkey, "CT"], writes=pk(yb), signal=(ii == 3))
                    if slot == 3:
                        pending.append(lambda: epilogue(q, yb, 0, 512))

                ffn_on = (bi >= 1) and STAGE >= 8 and 'ffni' not in SKIP
                if ffn_on:
                    XN2 = E['MIX']
                    xo = (bi - 1) * 512
                    Hf = tp("Hf", [128, 4, 512])
                    wq_f = WQ(ffn_specs(l))
                    BDN = (6, 7)

                    def ffn_up_unit(c, j):
                        Wu, wuk = wq_f.get(2 * c)
                        mm_group(BUP, 512, [(Wu[:, k_, j * 128:(j + 1) * 128], XN2[:, k_, 0:512]) for k_ in range(8)],
                                 [("MIX", k_) for k_ in range(8)] + [wuk])
                        op('act', lambda e: e.activation(out=Hf[:, j, :], in_=PSUM[:, BUP, :], func=AF.Relu),
                           reads=pk(BUP), writes=[("Hf", j)])
                        op('act', lambda e: e.activation(out=Hf[:, j, :], in_=Hf[:, j, :], func=AF.Square),
                           reads=[("Hf", j)], writes=[("Hf", j)])

                    def ffn_down(c, m):
                        Wd, wdk = wq_f.get(2 * c + 1)
                        mm_group(BDN[m % 2], 512, [(Wd[:, j, m * 128:(m + 1) * 128], Hf[:, j, :]) for j in range(4)],
                                 [("Hf", j) for j in range(4)] + [wdk])

                    def ffn_add(m):
                        bank = BDN[m % 2]
                        op('dve', lambda e: e.tensor_tensor(out=X[:, m, xo:xo + 512], in0=PSUM[:, bank, :], in1=X[:, m, xo:xo + 512],
                                                            op=ALU.add), reads=pk(bank) + ["X"], writes=["X"])
                else:
                    Wg, wgk = wload(w_glu[l].rearrange("(k p) n -> p k n", p=128), [128, 4, 512], None)

                pending = []
                x0_mm(0)
                evac(0)
                x0_mm(1)
                evac(1)
                for p_ in range(8):
                    s0, s1 = 2 * p_, 2 * p_ + 1
                    modops(s0)
                    modops(s1)
                    if p_ < 7:
                        x0_mm(s0 + 2)
                        evac(s0 + 2)
                        x0_mm(s1 + 2)
                        evac(s1 + 2)
                    aunits = []
                    if att_groups:
                        if p_ > 0:
                            att_norm(*att_groups[p_ - 1])
                        gi_, gh_ = att_groups[p_]
                        aunits = [(gi_, hp, b) for hp in range(2) for b in (2 * gh_, 2 * gh_ + 1)]
                    for j in range(4):
                        pv_ = att_unit(*aunits[j]) if j < len(aunits) else None
                        if ffn_on:
                            ffn_up_unit(p_, j)
                        if pv_:
                            pv_()
                    chain([s0, s1])
                    todo = pending
                    pending = []
                    for f_ in todo:
                        f_()
                    if ffn_on:
                        ffn_down(p_, 0)
                        ffn_down(p_, 1)
                    finish(s0)
                    finish(s1)
                    if ffn_on:
                        ffn_add(0)
                        for m_ in range(2, 8):
                            ffn_down(p_, m_)
                            ffn_add(m_ - 1)
                        ffn_add(7)
                if att_groups:
                    att_norm(*att_groups[7])
                for f_ in pending:
                    f_()
                if ffn_on:
                    Wg, wgk = wload(w_glu[l].rearrange("(k p) n -> p k n", p=128), [128, 4, 512], None)
                if bi == 3:
                    tt('dve', HO[:, 0, :], GE2[:, :, 0], P127c[:, :], ALU.mult, ["GE"], ["HO"])
                    tt('dve', HT[:, :], GE2[:, :, 1], P127s[:, :], ALU.mult, ["GE"], ["HT"])
                    tt('dve', HO[:, 0, :], HO[:, 0, :], HT[:, :], ALU.subtract, ["HO", "HT"], ["HO"])
                    tt('dve', HO[:, 1, :], GE2[:, :, 0], P127s[:, :], ALU.mult, ["GE", "HO"], ["HO"])
                    tt('dve', HT[:, :], GE2[:, :, 1], P127c[:, :], ALU.mult, ["GE", "HO"], ["HT"])
                    tt('dve', HO[:, 1, :], HO[:, 1, :], HT[:, :], ALU.add, ["HO", "HT"], ["HO"])
                    for c_ in range(2):
                        pi = 6 + c_
                        op('pe', lambda e, c_=c_, pi=pi: e.transpose(out=PSUM[0:16, pi, 0:128], in_=HO[:, c_, :], identity=ident[:, :]),
                           reads=["HO"], writes=pk(pi))
                        op('act', lambda e, c_=c_, pi=pi: e.activation(out=HOUT[:, c_, :], in_=PSUM[0:16, pi, 0:128], func=AF.Copy),
                           reads=pk(pi), writes=["HOUT"])
                    dma('sp', s_o, hrp[l], HOUT[:, 0, :], reads=["HOUT"])
                    dma('sp', s_o, hip[l], HOUT[:, 1, :], reads=["HOUT"])
                fence()
                php.close()
                if bi == 3 and STAGE >= 4 and 'ssms' not in SKIP:
                    ssm_sample(l, E, epilogue)
                    fence()
                for j in range(4):
                    for (o, n) in subs:
                        pi = psb()
                        mm_group(pi, n, [(Wg[:, q_, j * 128:(j + 1) * 128], YG[:, q_, o:o + n]) for q_ in range(4)],
                                 [("YG", q_) for q_ in range(4)] + [wgk])
                        op('act', lambda e, pi=pi, n=n, j=j: e.activation(out=SG[:, 0:n], in_=PSUM[:, pi, 0:n], func=AF.Sigmoid,
                                                                          bias=bgl[:, l, j:j + 1], scale=1.0),
                           reads=pk(pi), writes=["T1"])
                        tt('dve', ST[:, j, o:o + n], YG[:, j, o:o + n], SG[:, 0:n], ALU.mult, ["T1", ("YG", j)], [("ST", j)])
                fence()

        def ssm_sample(l, E, epilogue):
            U = E['U']; s_o = E['s_o']
            with ExitStack() as ph:
                def t(name, shape, dt=F32):
                    return ph.enter_context(nc.sbuf_tensor("ss%d_" % l + name, list(shape), dt))
                SN = t("SN", [16, 2048])
                H0 = [t("H0%d" % c_, [128, 16, 16]) for c_ in range(2)]
                HS = [t("HS%d" % c_, [128, 16, 64]) for c_ in range(2)]
                HSb = [t("HSb%d" % c_, [128, 16, 64], BF16) for c_ in range(2)]
                TA = t("TA", [128, 16, 16]); TB = t("TB", [128, 16, 16])
                s_s = C.stream("ssl%d" % l)
                for s in range(16):
                    q = s // 4
                    for c_, BT in ((0, BTr), (1, BTi)):
                        bank = 2 * c_ + s // 8
                        op('pe', lambda e, s=s, q=q, BT=BT, bank=bank: e.matmul(
                            PSUM[:, bank, (s % 8) * 64:(s % 8) * 64 + 64], lhsT=BT[:, s, :], rhs=U[:, q, 512:576],
                            start=True, stop=True), reads=[("U", q), "BTr", "BTi"], writes=pk(bank), signal=True)
                for c_, src in ((0, st_re), (1, st_im)):
                    dma('sp', s_s, SN[:, :], src[l], writes=["SN"])
                    pi = 6 + c_
                    for s in range(16):
                        op('pe', lambda e, s=s, pi=pi: e.transpose(out=PSUM[:, pi, s * 16:(s + 1) * 16],
                                                                   in_=SN[0:16, s * 128:(s + 1) * 128], identity=ident[0:16, 0:16]),
                           reads=["SN"], writes=pk(pi), signal=(s == 15))
                    op('act', lambda e, c_=c_, pi=pi: e.activation(out=H0[c_][:, :, :], in_=PSUM[:, pi, 0:256].rearrange(
                        "p (s b) -> p s b", s=16), func=AF.Copy), reads=pk(pi), writes=[("H0", c_)])
                ARb = bc_last(AR[:, :], 16)
                AIb = bc_last(AI[:, :], 16)
                xv = [PSUM[:, 2 * c_:2 * c_ + 2, :].rearrange("p b (s c) -> p (b s) c", c=64) for c_ in range(2)]
                for i_ in range(4):
                    cs_ = slice(i_ * 16, (i_ + 1) * 16)
                    if i_ == 0:
                        pr_, pi_ = H0[0][:, :, :], H0[1][:, :, :]
                        rk = [("H0", 0), ("H0", 1)]
                    else:
                        ps_ = slice((i_ - 1) * 16, i_ * 16)
                        pr_, pi_ = HS[0][:, :, ps_], HS[1][:, :, ps_]
                        rk = ["HS"]
                    tt('dve', TA[:, :, :], pr_, ARb, ALU.mult, rk + ["AR"], ["TA"])
                    tt('dve', TB[:, :, :], pi_, AIb, ALU.mult, rk + ["AR"], ["TB"])
                    tt('dve', TA[:, :, :], TA[:, :, :], TB[:, :, :], ALU.subtract, ["TA", "TB"], ["TA"])
                    tt('dve', HS[0][:, :, cs_], xv[0][:, :, cs_], TA[:, :, :], ALU.add, pk(0, 2) + ["TA"], ["HS"])
                    tt('dve', TA[:, :, :], pi_, ARb, ALU.mult, rk + ["AR", "HS"], ["TA"])
                    tt('dve', TB[:, :, :], pr_, AIb, ALU.mult, rk + ["AR"], ["TB"])
                    tt('dve', TA[:, :, :], TA[:, :, :], TB[:, :, :], ALU.add, ["TA", "TB"], ["TA"])
                    tt('dve', HS[1][:, :, cs_], xv[1][:, :, cs_], TA[:, :, :], ALU.add, pk(2, 2) + ["TA", "HS"], ["HS"])
                for c_ in range(2):
                    op('dve', lambda e, c_=c_: e.tensor_copy(out=HSb[c_][:, :, :], in_=HS[c_][:, :, :]), reads=["HS", "HS"],
                       writes=[("HSb", c_)])
                for q in range(4):
                    yb = 4 + (q % 2)
                    pairs = []
                    for slot in range(4):
                        s = 4 * q + slot
                        pairs.append((CTr[:, s, :], HSb[0][:, s, :]))
                        pairs.append((nCTi[:, s, :], HSb[1][:, s, :]))
                    mm_group(yb, 64, pairs, [("HSb", 0), ("HSb", 1), "CT"])
                    epilogue(q, yb, 512, 64)
                for c_, dst in ((0, hrs), (1, his)):
                    for g4 in range(4):
                        pi = g4
                        for j in range(4):
                            s = g4 * 4 + j
                            op('pe', lambda e, s=s, j=j, pi=pi, c_=c_: e.transpose(
                                out=PSUM[0:16, pi, j * 128:(j + 1) * 128], in_=HS[c_][:, s, 48:64], identity=ident[:, :]),
                                reads=["HS", "HS"], writes=pk(pi), signal=(j == 3))
                        op('act', lambda e, pi=pi, g4=g4: e.activation(out=SN[:, g4 * 512:(g4 + 1) * 512], in_=PSUM[0:16, pi, :],
                                                                      func=AF.Copy), reads=pk(pi), writes=["SN"])
                    dma('sp', s_s, dst[l].rearrange("b s r -> b (s r)"), SN[:, :], reads=["SN"])
                fence()

        def attn_block(l, bi, E):
            if bi == 3 and STAGE >= 4 and 'atts' not in SKIP:
                attn_sample(l, E)
                fence()

        def attn_sample(l, E):
            R2 = E['R2']; KT = E['KT']; VS = E['VS']; AT = E['AT']
            with ExitStack() as ph:
                def t(name, shape, dt=BF16):
                    return ph.enter_context(nc.sbuf_tensor("as%d_" % l + name, list(shape), dt))
                CK = t("CK", [128, 16, 128], F32)
                CKT = t("CKT", [128, 16, 128])
                CV = t("CV", [128, 16, 128])
                TMPc = t("TMPc", [128, 512], F32)
                Pc = t("Pc", [128, 512])
                TMPn = t("TMPn", [64, 512], F32)
                Pn = t("Pn", [64, 512])
                DR = t("DR", [128, 256], F32)
                s_c = C.stream("asl%d" % l)
                s_v = C.stream("asv%d" % l)
                with nc.allow_non_contiguous_dma(reason="cache load"):
                    dma('sp', s_c, CK[:, :, :], cache_k[l].rearrange("s j d -> j s d"), writes=["CK"])
                    dma('pool', s_v, CV[:, :, :], cache_v[l].rearrange("s j d -> j s d"), writes=["CV"])
                for sl in range(16):
                    pi = sl % 4
                    op('pe', lambda e, sl=sl, pi=pi: e.transpose(out=PSUM[:, pi, 0:128], in_=CK[:, sl, :], identity=ident[:, :]),
                       reads=["CK"], writes=pk(pi))
                    if sl % 2 == 0:
                        op('act', lambda e, sl=sl, pi=pi: e.activation(out=CKT[:, sl, :], in_=PSUM[:, pi, 0:128], func=AF.Copy),
                           reads=pk(pi), writes=["CKT"])
                    else:
                        op('dve', lambda e, sl=sl, pi=pi: e.tensor_copy(out=CKT[:, sl, :], in_=PSUM[:, pi, 0:128]),
                           reads=pk(pi), writes=["CKT"])
                po, pd = 6, 7
                for kv in range(2):
                    rows = slice(kv * 64, (kv + 1) * 64)
                    for sl in range(16):
                        op('pe', lambda e, sl=sl, kv=kv, rows=rows: e.matmul(
                            PSUM[:, 4 + kv, sl:256:16], lhsT=CKT[rows, sl, :],
                            rhs=R2[rows, :, 512 + sl:576:16], start=True, stop=True),
                            reads=["CKT"] + [("R2", i) for i in range(4)], writes=pk(4 + kv), signal=(sl == 15))
                for kv in range(2):
                    op('dve', lambda e, kv=kv: e.tensor_tensor(
                        out=TMPc[:, kv * 256:(kv + 1) * 256].rearrange("p (a s) -> p a s", s=16),
                        in0=PSUM[:, 4 + kv, 0:256].rearrange("p (a s) -> p a s", s=16),
                        in1=bc_last(biasc[:, kv * 16:(kv + 1) * 16], 16), op=ALU.add), reads=pk(4 + kv), writes=["TMPc"])
                op('act', lambda e: e.activation(out=Pc[:, :], in_=TMPc[:, :], func=AF.Exp), reads=["TMPc"], writes=["Pc"])
                for kv in range(2):
                    rows = slice(kv * 64, (kv + 1) * 64)
                    op('pe', lambda e, kv=kv, rows=rows: e.matmul(
                        PSUM[0:64, kv, 0:256], lhsT=KT[rows, 128 + 512:128 + 576],
                        rhs=R2[rows, :, 512:576], start=True, stop=True),
                        reads=["KT"] + [("R2", i) for i in range(4)], writes=pk(kv), signal=True)
                    tt('dve', TMPn[:, kv * 256:(kv + 1) * 256], PSUM[0:64, kv, 0:256], biasnf[:, kv * 256:(kv + 1) * 256], ALU.add,
                       pk(kv), ["TMPn"])
                op('act', lambda e: e.activation(out=Pn[:, :], in_=TMPn[:, :], func=AF.Exp), reads=["TMPn"], writes=["Pn"])
                for (bank, use_ones) in ((po, False), (pd, True)):
                    for kv in range(2):
                        rows = slice(kv * 64, (kv + 1) * 64)
                        lh = ones[0:64, 0:64] if use_ones else VS[0:64, rows]
                        op('pe', lambda e, bank=bank, kv=kv, rows=rows, lh=lh: e.matmul(
                            PSUM[rows, bank, 0:256], lhsT=lh, rhs=Pn[0:64, kv * 256:(kv + 1) * 256], start=True, stop=False),
                            reads=["Pn", "VS", "ones"], writes=pk(bank), signal=False)
                        for sl in range(16):
                            lh2 = ones[:, 0:64] if use_ones else CV[:, sl, rows]
                            op('pe', lambda e, bank=bank, kv=kv, rows=rows, sl=sl, lh2=lh2: e.matmul(
                                PSUM[rows, bank, sl:256:16], lhsT=lh2, rhs=Pc[:, kv * 256 + sl:kv * 256 + 256:16],
                                start=False, stop=(sl == 15)), reads=["Pc", "CV", "ones"], writes=pk(bank), signal=(sl == 15))
                op('dve', lambda e: e.tensor_tensor(out=DR[:, :].rearrange("p (h c) -> p h c", h=4),
                                                    in0=PSUM[:, pd, 0:256].rearrange("p (h c) -> p h c", h=4),
                                                    in1=bc_last(esink[:, l, :], 64), op=ALU.add), reads=pk(pd) + ["esink"], writes=["DRs"])
                op('dve', lambda e: e.reciprocal(out=DR[:, :], in_=DR[:, :]), reads=["DRs"], writes=["DRs"])
                op('dve', lambda e: e.tensor_tensor(out=AT[:, :, 512:576], in0=PSUM[:, po, 0:256].rearrange("p (h c) -> p h c", h=4),
                                                    in1=DR[:, :].rearrange("p (h c) -> p h c", h=4), op=ALU.mult),
                   reads=pk(po) + ["DRs"], writes=[("AT", i) for i in range(4)])
                fence()

        def mix_block(l, bi, E):
            XN = E['XN']; MIX = E['MIX']; AT = E['AT']; ST = E['ST']; subs = E['subs']; c0 = E['c0']
            with ExitStack() as ph:
                def t(name, shape, dt=BF16):
                    return ph.enter_context(nc.sbuf_tensor("mx%d_%d_" % (l, bi) + name, list(shape), dt))
                SGA = t("SGA", [128, 576], F32)
                TM = t("TM", [128, 576])
                WL = [t("WL%d" % i, [128, 4096]) for i in range(2)]
                wl_s = [C.stream("wl%d_%d_%d" % (l, bi, i)) for i in range(2)]
                R5 = Ring(WR + WL, wr_stream + wl_s, [("wr", i) for i in range(3)] + [("wl", i) for i in range(2)])
                XNr = [("XN", k_) for k_ in range(8)]
                mspecs = []
                for h in range(2):
                    for (Wsrc, gcol) in ((w_ao, 1280), (w_so, 2304)):
                        mspecs.append((Wsrc[l][:, h * 512:(h + 1) * 512].rearrange("(k p) n -> p k n", p=128), [128, 4, 512]))
                        mspecs.append((w_in[l][:, gcol + h * 512:gcol + (h + 1) * 512].rearrange("(k p) n -> p k n", p=128), [128, 8, 512]))
                for h in range(2):
                    mspecs.append((w_out[l][:, h * 512:(h + 1) * 512].rearrange("(k p) n -> p k n", p=128), [128, 8, 512]))
                wq_m = WQ(mspecs, ring=R5, ahead=2)
                mi = 0
                for h in range(2):
                    for (Wsrc, Act, akey, gcol, first) in ((w_ao, AT, "AT", 1280, True), (w_so, ST, "ST", 2304, False)):
                        Wo_, wok = wq_m.get(mi)
                        Wg_, wgk = wq_m.get(mi + 1)
                        mi += 2
                        for j in range(4):
                            m = h * 4 + j
                            bcol = (0 if first else 8) + m
                            for (o, n) in subs:
                                pg = psb()
                                mm_group(pg, n, [(Wg_[:, k_, j * 128:(j + 1) * 128], XN[:, k_, o:o + n]) for k_ in range(8)],
                                         XNr + [wgk])
                                op('act', lambda e, pg=pg, n=n, bcol=bcol: e.activation(
                                    out=SGA[:, 0:n], in_=PSUM[:, pg, 0:n], func=AF.Sigmoid, bias=bg[:, l, bcol:bcol + 1], scale=1.0),
                                    reads=pk(pg), writes=["SGA"])
                                pa = psb()
                                mm_group(pa, n, [(Wo_[:, k_, j * 128:(j + 1) * 128], Act[:, k_, o:o + n]) for k_ in range(4)],
                                         [(akey, k_) for k_ in range(4)] + [wok])
                                if first:
                                    tt('dve', MIX[:, m, o:o + n], PSUM[:, pa, 0:n], SGA[:, 0:n], ALU.mult, pk(pa) + ["SGA"], [("MIX", m)])
                                else:
                                    tt('dve', TM[:, 0:n], PSUM[:, pa, 0:n], SGA[:, 0:n], ALU.mult, pk(pa) + ["SGA"], ["TM"])
                                    tt('pool', MIX[:, m, o:o + n], MIX[:, m, o:o + n], TM[:, 0:n], ALU.add, ["TM", ("MIX", m)], [("MIX", m)])
                for h in range(2):
                    Wo_, wok = wq_m.get(8 + h)
                    for j in range(4):
                        m = h * 4 + j
                        for (o, n) in subs:
                            pi = psb()
                            mm_group(pi, n, [(Wo_[:, k_, j * 128:(j + 1) * 128], MIX[:, k_, o:o + n]) for k_ in range(8)],
                                     [("MIX", k_) for k_ in range(8)] + [wok])
                            xc = c0 + o
                            op('dve', lambda e, pi=pi, n=n, m=m, xc=xc: e.tensor_tensor(
                                out=X[:, m, xc:xc + n], in0=PSUM[:, pi, 0:n], in1=X[:, m, xc:xc + n], op=ALU.add),
                                reads=pk(pi) + ["X"], writes=["X"])
                if STAGE >= 8 and bi == 3:
                    SQn = t("SQn", [128, 4, 512]); RSn = t("RSn", [128, 512], F32); XGn = t("XGn", [128, 2, 512], F32)
                    for (o, n) in subs:
                        norm2_into(l, MIX, SQn, RSn, XGn, c0 + o, o, n)
                fence()

        def ffn_tail(l, XN2):
            with ExitStack() as ph:
                def t(name, shape, dt=BF16):
                    return ph.enter_context(nc.sbuf_tensor("b%d_" % l + name, list(shape), dt))
                Hh = [t("H%d" % i, [128, 4, 512]) for i in range(2)]
                Rr = [t("R%d" % i, [128, 512]) for i in range(2)]
                subs = [(0, 512), (512, 64)]
                wq_f = WQ(ffn_specs(l))
                it = 0
                for c in range(8):
                    Wu, wuk = wq_f.get(2 * c)
                    Wd, wdk = wq_f.get(2 * c + 1)
                    for (o, n) in subs:
                        hb = it % 2
                        it += 1
                        for j in range(4):
                            pi = psb()
                            mm_group(pi, n, [(Wu[:, k_, j * 128:(j + 1) * 128], XN2[:, k_, o:o + n]) for k_ in range(8)],
                                     [("MIX", k_) for k_ in range(8)] + [wuk])
                            rb = j % 2
                            op('act', lambda e, pi=pi, n=n, rb=rb: e.activation(out=Rr[rb][:, 0:n], in_=PSUM[:, pi, 0:n],
                                                                               func=AF.Relu), reads=pk(pi), writes=[("R", rb)])
                            op('act', lambda e, n=n, rb=rb, hb=hb, j=j: e.activation(out=Hh[hb][:, j, 0:n], in_=Rr[rb][:, 0:n],
                                                                                    func=AF.Square), reads=[("R", rb)], writes=[("H", hb, j)])
                        for m in range(8):
                            pi = psb()
                            mm_group(pi, n, [(Wd[:, j, m * 128:(m + 1) * 128], Hh[hb][:, j, 0:n]) for j in range(4)],
                                     [("H", hb, j) for j in range(4)] + [wdk])
                            xc = 1536 + o
                            op('dve', lambda e, pi=pi, n=n, m=m, xc=xc: e.tensor_tensor(
                                out=X[:, m, xc:xc + n], in0=PSUM[:, pi, 0:n], in1=X[:, m, xc:xc + n], op=ALU.add),
                                reads=pk(pi) + ["X"], writes=["X"])
                fence()

        for l in range(NLAYERS):
            if STAGE < 1:
                break
            if STAGE >= 2:
                ssm_tables(l, mid=(phase0 if l == 0 else None))
            elif l == 0:
                phase0()
            layer_phase_a(l)
        with ExitStack() as ph:
            YT = [ph.enter_context(nc.sbuf_tensor("yt%d" % i, [128, D], F32)) for i in range(2)]
            yts = [C.stream("yts%d" % i) for i in range(2)]
            for tt in range(17):
                b = tt % 2
                rows = 128 if tt < 16 else NS
                for half in range(2):
                    pi = psb()
                    for j in range(4):
                        k = half * 4 + j
                        op('pe', lambda e, k=k, j=j, pi=pi, rows=rows, tt=tt: e.transpose(
                            out=PSUM[0:rows, pi, j * 128:(j + 1) * 128], in_=X[:, k, tt * 128:tt * 128 + rows],
                            identity=ident[:, :]),
                            reads=[("X", tt)], writes=pk(pi), signal=(j == 3))
                    dst = YT[b][0:rows, half * 512:(half + 1) * 512]
                    src_ps = PSUM[0:rows, pi, :]
                    if half == 0:
                        op('act', lambda e, s_=src_ps, d_=dst: e.activation(out=d_, in_=s_, func=AF.Copy),
                           reads=pk(pi), writes=[("yt", b)])
                    else:
                        op('dve', lambda e, s_=src_ps, d_=dst: e.tensor_copy(out=d_, in_=s_),
                           reads=pk(pi), writes=[("yt", b)])
                dstd = yp[tt * 128:(tt + 1) * 128, :] if tt < 16 else ys[:, :]
                dma('sp', yts[b], dstd, YT[b][0:rows, :], reads=[("yt", b)])
            fence()
    return nc


_NC_CACHE = {}


def _consts():
    ident = np.eye(128, dtype=np.float32)
    blk = np.zeros((128, 128), np.float32)
    blk[:64, :64] = 1.0
    blk[64:, 64:] = 1.0
    slopes = np.exp2(-8.0 * np.arange(1, 9, dtype=np.float64) / 8.0)
    j = np.arange(128)[:, None]
    i = np.arange(128)[None, :]
    biasp = np.zeros((128, 8, 256), np.float32)
    for t in range(4):
        for hp in range(2):
            h = t + 4 * hp
            d_prev = 128 + i - j
            d_cur = i - j
            bp = np.where(d_prev <= 128, -slopes[h] * d_prev, -30000.0)
            bc = np.where(d_cur >= 0, -slopes[h] * d_cur, -30000.0)
            biasp[:, t * 2 + hp, 0:128] = bp
            biasp[:, t * 2 + hp, 128:256] = bc
    biasc = np.zeros((128, 32), np.float32)
    biasn = np.zeros((4, 32), np.float32)
    for kv in range(2):
        for hq in range(4):
            h = kv * 4 + hq
            for qi in range(4):
                col = kv * 16 + hq * 4 + qi
                jj = np.arange(128)
                dist = 128 + qi - jj
                biasc[:, col] = np.where(jj >= qi, -slopes[h] * dist, -30000.0)
                jn = np.arange(4)
                dn = qi - jn
                biasn[:, col] = np.where(dn >= 0, -slopes[h] * dn, -30000.0)
    biasnf = np.full((64, 512), -30000.0, np.float32)
    for ip in range(4):
        for slp in range(16):
            r = ip * 16 + slp
            for kv in range(2):
                for hq in range(4):
                    h = kv * 4 + hq
                    for qi in range(ip, 4):
                        biasnf[r, kv * 256 + hq * 64 + qi * 16 + slp] = -slopes[h] * (qi - ip)
    cmask = np.zeros((128, 128), np.float32)
    for r in range(128):
        glp = (r % 32) // 16
        cmask[r, glp * 64:(glp + 1) * 64] = 1.0
    rmask = np.zeros((128, 4), np.float32)
    for r in range(128):
        rmask[r, r // 32] = 1.0
    return dict(c_ident=ident, c_blk64=blk, c_biasp=biasp, c_biasc=biasc, c_biasn=biasn, c_biasnf=biasnf,
                c_cmask=cmask, c_rmask=rmask)


def kernel(**inp):
    f = lambda a: np.ascontiguousarray(np.asarray(a), dtype=np.float32)
    qperm = np.concatenate([np.arange(h * 64, (h + 1) * 64) for h in HEAD_PERM])
    w_in = f(inp['w_in']).copy()
    w_in[:, :, 0:512] = w_in[:, :, qperm]
    w_ao = f(inp['w_attn_o'])[:, qperm, :]
    qg = f(inp['q_norm_g'])
    kg = f(inp['k_norm_g'])
    qk_g = np.stack([np.concatenate([qg, qg], axis=1), np.concatenate([kg, kg], axis=1)], axis=1)
    sk = f(inp['attn_sinks'])
    sinks = np.zeros((L, 4, 128), np.float32)
    for t in range(4):
        sinks[:, t, 0:64] = sk[:, t][:, None]
        sinks[:, t, 64:128] = sk[:, t + 4][:, None]
    shared = dict(
        w_in=np.ascontiguousarray(w_in), w_glu=f(inp['w_glu']), w_ao=np.ascontiguousarray(w_ao), w_so=f(inp['w_ssm_o']),
        w_out=f(inp['w_out']), w_up=f(inp['w_up']), w_down=f(inp['w_down']),
        norm1_g=f(inp['norm1_g']), norm2_g=f(inp['norm2_g']), b_gate=f(inp['b_gate']),
        qk_g=np.ascontiguousarray(qk_g), sinks=sinks,
        lam_re=f(inp['lam_re']), lam_im=f(inp['lam_im']), log_step=f(inp['log_step']),
        b_re=f(inp['b_re']).reshape(L, 2048, 16), b_im=f(inp['b_im']).reshape(L, 2048, 16),
        c_re=f(inp['c_re']).reshape(L, 512, 64), c_im=f(inp['c_im']).reshape(L, 512, 64),
        d_skip=f(inp['d_skip']), b_glu=f(inp['b_glu']),
    )
    pa = np.zeros((92, 128), np.float32)
    pa[0:16] = shared['norm1_g'].reshape(L * 8, 128)
    pa[16:32] = shared['norm2_g'].reshape(L * 8, 128)
    pa[32:64] = shared['b_gate'].reshape(L * 16, 128)
    pa[64:68] = shared['qk_g'].reshape(L * 2, 128)
    pa[68:76] = shared['sinks'].reshape(L * 4, 128)
    pa[76:84] = shared['d_skip'].reshape(L * 4, 128)
    pa[84:92] = shared['b_glu'].reshape(L * 4, 128)
    pb = np.zeros((L, 48, 128), np.float32)
    pb[:, 0:16] = shared['lam_re'].reshape(L, 16, 128)
    pb[:, 16:32] = shared['lam_im'].reshape(L, 16, 128)
    pb[:, 32:48] = np.repeat(shared['log_step'].reshape(L, 16, 2, 1), 64, axis=3).reshape(L, 16, 128)
    shared['pvec_a'] = pa
    shared['pvec_b'] = pb
    shared.update(_consts())
    x_prompt = f(inp['x_prompt'])
    x_sample = f(inp['x_sample'])
    ck = f(inp['cache_k']).reshape(L, 128, 128, 128)
    cv = f(inp['cache_v']).reshape(L, 128, 128, 128)
    sre = f(inp['state_ssm_re']).reshape(L, 128, 2048)
    sim = f(inp['state_ssm_im']).reshape(L, 128, 2048)
    in_maps = []
    for c in range(NCORES):
        m = dict(shared)
        m['xp'] = x_prompt[c]
        m['xs'] = np.ascontiguousarray(x_sample[c * 16:(c + 1) * 16].transpose(1, 0, 2).reshape(NS, D))
        m['cache_k'] = np.ascontiguousarray(ck[:, c * 16:(c + 1) * 16])
        m['cache_v'] = np.ascontiguousarray(cv[:, c * 16:(c + 1) * 16])
        m['st_re'] = np.ascontiguousarray(sre[:, c * 16:(c + 1) * 16])
        m['st_im'] = np.ascontiguousarray(sim[:, c * 16:(c + 1) * 16])
        in_maps.append(m)
    if 'nc' not in _NC_CACHE:
        _NC_CACHE['nc'] = build_program()
    ncr = DEBUG_CORES or NCORES
    res = run_bass_kernel_spmd(_NC_CACHE['nc'], in_maps[:ncr], core_ids=list(range(ncr)))
    R = list(res.results)
    _NC_CACHE['last'] = R
    while len(R) < NCORES:
        R.append(R[0])
    y_prompt = np.stack([R[c]['yp'] for c in range(NCORES)]).astype(np.float32)
    y_sample = np.concatenate([R[c]['ys'].reshape(4, 16, D).transpose(1, 0, 2) for c in range(NCORES)]).astype(np.float32)
    k_prompt = np.stack([R[c]['kp'] for c in range(NCORES)], axis=1).reshape(L, 8, 128, 2, 64)
    v_prompt = np.stack([R[c]['vp'] for c in range(NCORES)], axis=1).reshape(L, 8, 128, 2, 64)
    hr_p = np.stack([R[c]['hrp'] for c in range(NCORES)], axis=1).reshape(L, 8, 32, 64)
    hi_p = np.stack([R[c]['hip'] for c in range(NCORES)], axis=1).reshape(L, 8, 32, 64)
    k_s = np.concatenate([R[c]['ksam'] for c in range(NCORES)], axis=1).reshape(L, 128, 128, 2, 64)
    v_s = np.concatenate([R[c]['vsam'] for c in range(NCORES)], axis=1).reshape(L, 128, 128, 2, 64)
    hr_s = np.concatenate([R[c]['hrs'] for c in range(NCORES)], axis=1).reshape(L, 128, 32, 64)
    hi_s = np.concatenate([R[c]['his'] for c in range(NCORES)], axis=1).reshape(L, 128, 32, 64)
    return (y_prompt, y_sample, k_prompt.astype(np.float32), v_prompt.astype(np.float32),
            hr_p.astype(np.float32), hi_p.astype(np.float32), k_s.astype(np.float32), v_s.astype(np.float32),
            hr_s.astype(np.float32), hi_s.astype(np.float32))
```

```python
import math
import numpy as np
import ml_dtypes
from contextlib import ExitStack
import concourse.bass as bass
import concourse.mybir as mybir
from concourse.bass_utils import run_bass_kernel_spmd

F32 = mybir.dt.float32
BF16 = mybir.dt.bfloat16
I32 = mybir.dt.int32
AF = mybir.ActivationFunctionType
ALU = mybir.AluOpType

NCORES = 8
D = 1024
T = 2048
NS = 64
NT = T + NS
L = 2
EPS = 1e-6
HEAD_PERM = [0, 4, 1, 5, 2, 6, 3, 7]
STAGE = 99
NLAYERS = 2
CHUNK = 256
NCHUNK = 512 // CHUNK
DEBUG_DUMP = False
DEBUG_CORES = 0
SKIP = set()


class Stream:
    def __init__(self, nc, stack, name):
        self.sem = stack.enter_context(nc.semaphore(name))
        self.name = name
        self.cnt = 0


class Ctx:
    def __init__(self, nc, stack):
        self.nc = nc
        self.engs = {'pe': nc.tensor, 'act': nc.scalar, 'dve': nc.vector, 'pool': nc.gpsimd, 'sp': nc.sync}
        self.st = {n: Stream(nc, stack, 's_' + n) for n in self.engs}
        self.waited = {n: {} for n in self.engs}
        self.lw = {}
        self.rd = {}
        self.stack = stack
        self.nstream = 0

    def stream(self, name):
        s = Stream(self.nc, self.stack, name)
        self.st[name] = s
        return name

    def _deps(self, reads, writes):
        deps = {}

        def add(tok):
            if tok is None:
                return
            s, v = tok
            if deps.get(s, 0) < v:
                deps[s] = v
        for k in reads:
            add(self.lw.get(k))
        for k in writes:
            add(self.lw.get(k))
            for s, v in self.rd.get(k, {}).items():
                add((s, v))
        return deps

    def _wait(self, en, deps):
        e = self.engs[en]
        w = self.waited[en]
        for s, v in deps.items():
            if s == en and en in ('pe', 'sp'):
                continue
            if w.get(s, 0) >= v:
                continue
            e.wait_ge(self.st[s].sem, v)
            w[s] = v

    def _record(self, tok, reads, writes):
        s, v = tok
        for k in reads:
            d = self.rd.setdefault(k, {})
            if d.get(s, 0) < v:
                d[s] = v
        for k in writes:
            self.lw[k] = tok
            self.rd[k] = {}

    def op(self, en, fn, reads=(), writes=(), signal=True):
        psr = [k for k in reads if isinstance(k, tuple) and k[0] == 'ps']
        if psr:
            writes = list(writes) + psr
        self._wait(en, self._deps(reads, writes))
        inst = fn(self.engs[en])
        st = self.st[en]
        if signal:
            st.cnt += 1
            inst.then_inc(st.sem, 1)
            tok = (en, st.cnt)
        else:
            tok = (en, st.cnt + 1)
        self._record(tok, reads, writes)
        return tok

    def dma(self, q, stream, out, in_, reads=(), writes=(), **kw):
        self._wait(q, self._deps(reads, writes))
        st = self.st[stream]
        st.cnt += 16
        self.engs[q].dma_start(out=out, in_=in_, **kw).then_inc(st.sem, 16)
        tok = (stream, st.cnt)
        self._record(tok, reads, writes)
        return tok

    def fence(self):
        for en, e in self.engs.items():
            w = self.waited[en]
            for s, st in self.st.items():
                if s == en or st.cnt == 0:
                    continue
                if w.get(s, 0) >= st.cnt:
                    continue
                e.wait_ge(st.sem, st.cnt)
                w[s] = st.cnt
        self.lw = {}
        self.rd = {}


def build_program():
    nc = bass.Bass("TRN2", target_bir_lowering=False)

    def din(name, shape, dt=F32):
        return nc.dram_tensor(name, list(shape), dt, kind="ExternalInput").ap()

    def dout(name, shape, dt=F32):
        return nc.dram_tensor(name, list(shape), dt, kind="ExternalOutput").ap()

    xp = din("xp", [T, D])
    xs = din("xs", [NS, D])
    cache_k = din("cache_k", [L, 16, 128, 128])
    cache_v = din("cache_v", [L, 16, 128, 128])
    st_re = din("st_re", [L, 16, 2048])
    st_im = din("st_im", [L, 16, 2048])
    w_in = din("w_in", [L, D, 3328])
    w_glu = din("w_glu", [L, 512, 512])
    w_ao = din("w_ao", [L, 512, D])
    w_so = din("w_so", [L, 512, D])
    w_out = din("w_out", [L, D, D])
    w_up = din("w_up", [L, D, 4096])
    w_down = din("w_down", [L, 4096, D])
    pvec_a = din("pvec_a", [92, 128])
    pvec_b = din("pvec_b", [L, 48, 128])
    norm1_g = din("norm1_g", [L, D])
    norm2_g = din("norm2_g", [L, D])
    b_gate = din("b_gate", [L, 2048])
    qk_g = din("qk_g", [L, 2, 128])
    sinks = din("sinks", [L, 4, 128])
    lam_re = din("lam_re", [L, 32, 64])
    lam_im = din("lam_im", [L, 32, 64])
    log_step = din("log_step", [L, 32])
    b_re = din("b_re", [L, 2048, 16])
    b_im = din("b_im", [L, 2048, 16])
    c_re = din("c_re", [L, 512, 64])
    c_im = din("c_im", [L, 512, 64])
    d_skip = din("d_skip", [L, 512])
    b_glu = din("b_glu", [L, 512])
    c_ident = din("c_ident", [128, 128])
    c_blk64 = din("c_blk64", [128, 128])
    c_biasp = din("c_biasp", [128, 8, 256])
    c_biasc = din("c_biasc", [128, 32])
    c_biasn = din("c_biasn", [4, 32])
    c_biasnf = din("c_biasnf", [64, 512])
    c_cmask = din("c_cmask", [128, 128])
    c_rmask = din("c_rmask", [128, 4])

    yp = dout("yp", [T, D])
    ys = dout("ys", [NS, D])
    kp = dout("kp", [L, 128, 128])
    vp = dout("vp", [L, 128, 128])
    hrp = dout("hrp", [L, 16, 128])
    hip = dout("hip", [L, 16, 128])
    ksam = dout("ksam", [L, 16, 128, 128])
    vsam = dout("vsam", [L, 16, 128, 128])
    hrs = dout("hrs", [L, 16, 16, 128])
    his = dout("his", [L, 16, 16, 128])
    if DEBUG_DUMP:
        dbg_at = dout("dbg_at", [4, 128, 4, 576], BF16)
        dbg_st = dout("dbg_st", [4, 128, 4, 576], BF16)
        dbg_yg = dout("dbg_yg", [4, 128, 4, 576], BF16)
        dbg_mix = dout("dbg_mix", [4, 128, 8, 576], BF16)

    stack = ExitStack()
    with stack:
        C = Ctx(nc, stack)
        op, dma, fence = C.op, C.dma, C.fence

        def sb(name, shape, dt=F32):
            return stack.enter_context(nc.sbuf_tensor(name, list(shape), dt))

        X = sb("X", [128, 8, NT])
        PSUM = stack.enter_context(nc.psum_tensor("PS", [128, 8, 512], F32))
        ident = sb("ident", [128, 128])
        ones = sb("ones", [128, 128], BF16)
        blk64 = sb("blk64", [128, 128], BF16)
        EB = sb("EB", [128, 8, 256], BF16)
        biasc = sb("biasc", [128, 32])
        biasnf = sb("biasnf", [64, 512])
        cmask = sb("cmask", [128, 128])
        rmask = sb("rmask", [128, 4])
        PVA = sb("PVA", [128, 92])
        g1 = PVA[:, 0:16].rearrange("p (l k) -> p l k", l=L)
        g2 = PVA[:, 16:32].rearrange("p (l k) -> p l k", l=L)
        bg = PVA[:, 32:64].rearrange("p (l k) -> p l k", l=L)
        qkg = PVA[:, 64:68].rearrange("p (l k) -> p l k", l=L)
        esink = PVA[:, 68:76].rearrange("p (l k) -> p l k", l=L)
        dsk = PVA[:, 76:84].rearrange("p (l k) -> p l k", l=L)
        bgl = PVA[:, 84:92].rearrange("p (l k) -> p l k", l=L)
        WR = [sb("wr%d" % i, [128, 4096], BF16) for i in range(3)]
        wr_stream = [C.stream("wrs%d" % i) for i in range(3)]
        wr_i = [0]
        PHc = sb("PHc", [128, 16, CHUNK + 1], BF16)
        PHs = sb("PHs", [128, 16, CHUNK + 1], BF16)
        MAG = sb("MAG", [128, 16])
        AR = sb("AR", [128, 16])
        AI = sb("AI", [128, 16])
        R128c = sb("R128c", [128, 16])
        R128s = sb("R128s", [128, 16])
        nR128s = sb("nR128s", [128, 16])
        P127c = sb("P127c", [128, 16])
        P127s = sb("P127s", [128, 16])
        BTr = sb("BTr", [128, 16, 128], BF16)
        BTi = sb("BTi", [128, 16, 128], BF16)
        CTr = sb("CTr", [128, 16, 128], BF16)
        nCTi = sb("nCTi", [128, 16, 128], BF16)
        nCTr = sb("nCTr", [128, 16, 128], BF16)
        GE2 = sb("GE2", [128, 16, 2])
        SS = sb("SS", [128, 16, 2])

        s_par = C.stream("par")
        s_out = C.stream("outs")
        ps_i = [0]

        def psb(n=1):
            i = ps_i[0]
            if i + n > 8:
                i = 0
            ps_i[0] = (i + n) % 8
            return i

        def pk(i, n=1):
            return [("ps", j) for j in range(i, i + n)]

        class Ring:
            def __init__(self, bufs, streams, keys):
                self.bufs, self.streams, self.keys = bufs, streams, keys
                self.i = 0

        G3 = Ring(WR, wr_stream, [("wr", i) for i in range(3)])

        def wload(src_ap, view_shape, key, ring=None):
            ring = ring or G3
            i = ring.i % len(ring.bufs)
            ring.i += 1
            buf = ring.bufs[i]
            n = 1
            for s_ in view_shape[1:]:
                n *= s_
            flat = buf[:, 0:n]
            if len(view_shape) == 3:
                view = flat.rearrange("p (k n) -> p k n", k=view_shape[1])
            else:
                view = flat
            dma('pool', ring.streams[i], view, src_ap, writes=[ring.keys[i]])
            return view, ring.keys[i]

        class WQ:
            def __init__(self, specs, ring=None, ahead=1):
                self.specs = specs
                self.loaded = []
                self.ring = ring
                self.ahead = ahead

            def get(self, i):
                upto = min(i + self.ahead, len(self.specs) - 1)
                while len(self.loaded) <= upto:
                    src, shape = self.specs[len(self.loaded)]
                    self.loaded.append(wload(src, shape, None, self.ring))
                return self.loaded[i]

        def bc_mid(ap2, n):
            return ap2.unsqueeze(1).to_broadcast([ap2.shape[0], n, ap2.shape[1]])

        def bc_last(ap2, n):
            return ap2.unsqueeze(2).to_broadcast([ap2.shape[0], ap2.shape[1], n])

        with nc.allow_non_contiguous_dma(reason="small param loads"):
            for (dst, src) in [
                (ident[:], c_ident[:, :]),
                (biasc[:], c_biasc[:, :]), (biasnf[:], c_biasnf[:, :]), (cmask[:], c_cmask[:, :]),
                (rmask[:], c_rmask[:, :]),
            ]:
                dma('sp', s_par, dst, src, writes=["par"])
        with ExitStack() as phA:
            PST = phA.enter_context(nc.sbuf_tensor("pst", [92, 128], F32))
            s_pa = C.stream("pva")
            dma('sp', s_pa, PST[:, :], pvec_a[:, :], writes=["PST"])
            op('pe', lambda e: e.transpose(out=PSUM[:, 0, 0:92], in_=PST[0:92, :], identity=ident[0:92, 0:92]),
               reads=["PST", "par"], writes=pk(0))
            op('act', lambda e: e.activation(out=PVA[:, :], in_=PSUM[:, 0, 0:92], func=AF.Copy), reads=pk(0), writes=["par"])
            fence()
        s_b64 = C.stream("b64")
        dma('pool', s_b64, blk64[:], c_blk64[:, :], writes=["blk64"])
        op('dve', lambda e: e.memset(ones[:], 1.0), writes=["ones"])
        with ExitStack() as ph0:
            biasp = ph0.enter_context(nc.sbuf_tensor("biasp", [128, 8, 256], F32))
            s_bp = C.stream("bp")
            dma('sp', s_bp, biasp[:], c_biasp[:, :, :], writes=["biasp"])
            op('act', lambda e: e.activation(out=EB[:], in_=biasp[:], func=AF.Exp), reads=["biasp"], writes=["EB"])
            fence()
        op('act', lambda e: e.activation(out=esink, in_=esink, func=AF.Exp), reads=["par"], writes=["esink"])
        op('dve', lambda e: e.tensor_scalar(out=qkg[:, :, 0:1], in0=qkg[:, :, 0:1], scalar1=0.125, scalar2=None,
                                            op0=ALU.mult), reads=["par"], writes=["qkg"])
        fence()

        def phase0():
          with ExitStack() as ph:
              XT = [ph.enter_context(nc.sbuf_tensor("xt%d" % i, [128, D], F32)) for i in range(2)]
              xts = [C.stream("xts%d" % i) for i in range(2)]
              for tt in range(17):
                  b = tt % 2
                  rows = 128 if tt < 16 else NS
                  src = xp[tt * 128:(tt + 1) * 128, :] if tt < 16 else xs[:, :]
                  dma('sp', xts[b], XT[b][0:rows, :], src, writes=[("xt", b)])
                  for half in range(2):
                      pi = psb()
                      for j in range(4):
                          k = half * 4 + j
                          op('pe', lambda e, k=k, j=j, pi=pi, b=b, rows=rows: e.transpose(
                              out=PSUM[:, pi, j * 128:j * 128 + rows], in_=XT[b][0:rows, k * 128:(k + 1) * 128],
                              identity=ident[0:rows, 0:rows]),
                              reads=[("xt", b)], writes=pk(pi), signal=(j == 3))
                      eng = 'act'
                      src_ps = PSUM[:, pi, :].rearrange("p (j t) -> p j t", j=4)[:, :, 0:rows]
                      dst = X[:, half * 4:half * 4 + 4, tt * 128:tt * 128 + rows]
                      if eng == 'act':
                          op('act', lambda e, s_=src_ps, d_=dst: e.activation(out=d_, in_=s_, func=AF.Copy),
                             reads=pk(pi), writes=[("X", tt)])
                      else:
                          op('dve', lambda e, s_=src_ps, d_=dst: e.tensor_copy(out=d_, in_=s_),
                             reads=pk(pi), writes=[("X", tt)])
              fence()

        TWO_PI = 2.0 * math.pi

        def tt(en, out, a, b, o, reads, writes):
            return op(en, lambda e: e.tensor_tensor(out=out, in0=a, in1=b, op=o), reads=reads, writes=writes)

        def ssm_tables(l, mid=None):
            with ExitStack() as ph:
                def t(name, shape, dt=F32):
                    return ph.enter_context(nc.sbuf_tensor("st%d_" % l + name, list(shape), dt))
                LR = t("LR", [128, 16]); LI = t("LI", [128, 16]); LS = t("LS", [128, 16])
                ANG = t("ANG", [128, 32]); KI = t("KI", [128, 32], I32); KF = t("KF", [128, 32])
                M1 = t("M1", [128, 32]); SC = t("SC", [128, 32])
                T1 = t("T1", [128, 16]); T2 = t("T2", [128, 16]); T3 = t("T3", [128, 16])
                CR = t("CR", [128, 16]); CI = t("CI", [128, 16]); RDEN = t("RDEN", [128, 16])
                Pc = t("Pc", [128, 16, CHUNK + 1]); Ps = t("Ps", [128, 16, CHUNK + 1])
                Q1 = t("Q1", [128, 16, CHUNK // 2]); Q2 = t("Q2", [128, 16, CHUNK // 2])
                BNr = t("BNr", [128, 16, 16]); BNi = t("BNi", [128, 16, 16])
                BBr = t("BBr", [128, 16, 16]); BBi = t("BBi", [128, 16, 16]); BT1 = t("BT1", [128, 16, 16])
                BEr = t("BEr", [128, 16, 32]); BEi = t("BEi", [128, 16, 32])
                CNr = t("CNr", [128, 4, 64]); CNi = t("CNi", [128, 4, 64])
                CE = t("CE", [128, 128])
                sp_ = C.stream("sst%d" % l)
                with nc.allow_non_contiguous_dma(reason="ssm params"):
                    PBT = t("PBT", [48, 128])
                    dma('sp', sp_, PBT[:, :], pvec_b[l], writes=["PBT"])
                    op('pe', lambda e: e.transpose(out=PSUM[:, 0, 0:48], in_=PBT[0:48, :], identity=ident[0:48, 0:48]),
                       reads=["PBT"], writes=pk(0))
                    op('act', lambda e: e.activation(out=LR[:], in_=PSUM[:, 0, 0:16], func=AF.Copy), reads=pk(0), writes=["sp"])
                    op('act', lambda e: e.activation(out=LI[:], in_=PSUM[:, 0, 16:32], func=AF.Copy), reads=pk(0), writes=["sp"])
                    op('act', lambda e: e.activation(out=LS[:], in_=PSUM[:, 0, 32:48], func=AF.Copy), reads=pk(0), writes=["sp"])
                    dma('sp', sp_, BNr[:], b_re[l].rearrange("(s gl p) c -> (gl p) s c", gl=2, p=64), writes=["sp"])
                    dma('sp', sp_, BNi[:], b_im[l].rearrange("(s gl p) c -> (gl p) s c", gl=2, p=64), writes=["sp"])
                    dma('sp', sp_, CNr[:], c_re[l].rearrange("(q r) p -> r q p", r=128), writes=["sp"])
                    dma('sp', sp_, CNi[:], c_im[l].rearrange("(q r) p -> r q p", r=128), writes=["sp"])
                R = ["sp"]
                op('act', lambda e: e.activation(out=LS[:], in_=LS[:], func=AF.Exp), reads=R, writes=["LS"])
                tt('dve', T1[:], LR[:], LS[:], ALU.mult, R + ["LS"], ["T1"])
                tt('dve', ANG[:, 0:16], LI[:], LS[:], ALU.mult, R + ["LS"], ["ANG"])
                op('act', lambda e: e.activation(out=MAG[:], in_=T1[:], func=AF.Exp), reads=["T1"], writes=["MAG"])
                op('dve', lambda e: e.tensor_scalar(out=ANG[:, 16:32], in0=ANG[:, 0:16], scalar1=math.pi / 2, scalar2=None,
                                                    op0=ALU.add), reads=["ANG"], writes=["ANG"])
                op('dve', lambda e: e.tensor_scalar(out=KF[:], in0=ANG[:], scalar1=1.0 / TWO_PI, scalar2=None, op0=ALU.mult),
                   reads=["ANG"], writes=["KF"])
                op('dve', lambda e: e.tensor_copy(out=KI[:], in_=KF[:]), reads=["KF"], writes=["KI"])
                op('dve', lambda e: e.tensor_copy(out=KF[:], in_=KI[:]), reads=["KI"], writes=["KF"])
                op('dve', lambda e: e.scalar_tensor_tensor(out=ANG[:], in0=KF[:], scalar=-TWO_PI, in1=ANG[:], op0=ALU.mult,
                                                           op1=ALU.add), reads=["KF", "ANG"], writes=["ANG"])
                op('dve', lambda e: e.tensor_single_scalar(out=M1[:], in_=ANG[:], scalar=math.pi, op=ALU.is_gt),
                   reads=["ANG"], writes=["M1"])
                op('dve', lambda e: e.scalar_tensor_tensor(out=ANG[:], in0=M1[:], scalar=-TWO_PI, in1=ANG[:], op0=ALU.mult,
                                                           op1=ALU.add), reads=["M1", "ANG"], writes=["ANG"])
                op('dve', lambda e: e.tensor_single_scalar(out=M1[:], in_=ANG[:], scalar=-math.pi, op=ALU.is_lt),
                   reads=["ANG"], writes=["M1"])
                op('dve', lambda e: e.scalar_tensor_tensor(out=ANG[:], in0=M1[:], scalar=TWO_PI, in1=ANG[:], op0=ALU.mult,
                                                           op1=ALU.add), reads=["M1", "ANG"], writes=["ANG"])
                op('act', lambda e: e.activation(out=SC[:], in_=ANG[:], func=AF.Sin), reads=["ANG"], writes=["SC"])
                SN = SC[:, 0:16]
                CS = SC[:, 16:32]
                tt('dve', AR[:], MAG[:], CS, ALU.mult, ["MAG", "SC"], ["AR"])
                tt('dve', AI[:], MAG[:], SN, ALU.mult, ["MAG", "SC"], ["AI"])
                tt('dve', T1[:], LR[:], LR[:], ALU.mult, R + ["MAG"], ["T1"])
                tt('dve', T2[:], LI[:], LI[:], ALU.mult, R, ["T2"])
                tt('dve', T1[:], T1[:], T2[:], ALU.add, ["T1", "T2"], ["T1"])
                op('dve', lambda e: e.reciprocal(out=RDEN[:], in_=T1[:]), reads=["T1"], writes=["RDEN"])
                op('dve', lambda e: e.tensor_scalar(out=T3[:], in0=AR[:], scalar1=-1.0, scalar2=None, op0=ALU.add),
                   reads=["AR"], writes=["T3"])
                tt('dve', T1[:], T3[:], LR[:], ALU.mult, ["T3", "RDEN"], ["T1"])
                tt('dve', T2[:], AI[:], LI[:], ALU.mult, ["AI", "T1"], ["T2"])
                tt('dve', T1[:], T1[:], T2[:], ALU.add, ["T1", "T2"], ["T1"])
                tt('dve', CR[:], T1[:], RDEN[:], ALU.mult, ["T1", "RDEN"], ["CR"])
                tt('dve', T1[:], AI[:], LR[:], ALU.mult, ["CR", "AI"], ["T1"])
                tt('dve', T2[:], T3[:], LI[:], ALU.mult, ["T3", "CR"], ["T2"])
                tt('dve', T1[:], T1[:], T2[:], ALU.subtract, ["T1", "T2"], ["T1"])
                tt('dve', CI[:], T1[:], RDEN[:], ALU.mult, ["T1", "RDEN"], ["CI"])
                op('dve', lambda e: e.memset(Pc[:, :, 0:1], 1.0), writes=["P"])
                op('dve', lambda e: e.memset(Ps[:, :, 0:1], 0.0), writes=["P"])
                op('dve', lambda e: e.tensor_copy(out=Pc[:, :, 1:2], in_=CS.unsqueeze(2)), reads=["SC", "P"], writes=["P"])
                op('dve', lambda e: e.tensor_copy(out=Ps[:, :, 1:2], in_=SN.unsqueeze(2)), reads=["SC", "P"], writes=["P"])
                m = 1
                while m < CHUNK:
                    Ac = Pc[:, :, 1:m + 1]
                    As = Ps[:, :, 1:m + 1]
                    Bc = Pc[:, :, m:m + 1].to_broadcast([128, 16, m])
                    Bs = Ps[:, :, m:m + 1].to_broadcast([128, 16, m])
                    q1 = Q1[:, :, 0:m]
                    q2 = Q2[:, :, 0:m]
                    tt('dve', q1, Ac, Bc, ALU.mult, ["P"], ["Q1"])
                    tt('dve', q2, As, Bs, ALU.mult, ["P"], ["Q2"])
                    tt('dve', Pc[:, :, m + 1:2 * m + 1], q1, q2, ALU.subtract, ["Q1", "Q2", "P"], ["P"])
                    tt('dve', q1, Ac, Bs, ALU.mult, ["P"], ["Q1"])
                    tt('dve', q2, As, Bc, ALU.mult, ["P"], ["Q2"])
                    tt('dve', Ps[:, :, m + 1:2 * m + 1], q1, q2, ALU.add, ["Q1", "Q2", "P"], ["P"])
                    m *= 2
                op('dve', lambda e: e.tensor_copy(out=PHc[:], in_=Pc[:]), reads=["P"], writes=["PH"])
                op('dve', lambda e: e.tensor_copy(out=PHs[:], in_=Ps[:]), reads=["P"], writes=["PH"])
                op('dve', lambda e: e.tensor_copy(out=R128c[:], in_=Pc[:, :, CHUNK]), reads=["P"], writes=["R128"])
                op('dve', lambda e: e.tensor_copy(out=R128s[:], in_=Ps[:, :, CHUNK]), reads=["P"], writes=["R128"])
                op('dve', lambda e: e.tensor_scalar(out=nR128s[:], in0=Ps[:, :, CHUNK], scalar1=-1.0, scalar2=None,
                                                    op0=ALU.mult), reads=["P"], writes=["R128"])
                op('dve', lambda e: e.tensor_copy(out=P127c[:], in_=Pc[:, :, CHUNK - 1]), reads=["P"], writes=["R128"])
                op('dve', lambda e: e.tensor_copy(out=P127s[:], in_=Ps[:, :, CHUNK - 1]), reads=["P"], writes=["R128"])
                CRb = bc_last(CR[:], 16)
                CIb = bc_last(CI[:], 16)
                tt('dve', BBr[:], BNr[:], CRb, ALU.mult, R + ["CR"], ["BBr"])
                tt('dve', BT1[:], BNi[:], CIb, ALU.mult, R + ["CI"], ["BT1"])
                tt('dve', BBr[:], BBr[:], BT1[:], ALU.subtract, ["BBr", "BT1"], ["BBr"])
                tt('dve', BBi[:], BNi[:], CRb, ALU.mult, R + ["CR"], ["BBi"])
                tt('dve', BT1[:], BNr[:], CIb, ALU.mult, R + ["CI", "BBr"], ["BT1"])
                tt('dve', BBi[:], BBi[:], BT1[:], ALU.add, ["BBi", "BT1"], ["BBi"])
                if mid is not None:
                    mid()
                for (BE, BB, BTd, nm) in ((BEr, BBr, BTr, "r"), (BEi, BBi, BTi, "i")):
                    op('dve', lambda e, BE=BE: e.memset(BE[:], 0.0), writes=["BE" + nm])
                    op('dve', lambda e, BE=BE, BB=BB: e.tensor_copy(out=BE[0:64, :, 0:16], in_=BB[0:64, :, :]),
                       reads=["BB" + nm, "BE" + nm], writes=["BE" + nm])
                    op('dve', lambda e, BE=BE, BB=BB: e.tensor_copy(out=BE[64:128, :, 16:32], in_=BB[64:128, :, :]),
                       reads=["BB" + nm, "BE" + nm], writes=["BE" + nm])
                    for q in range(4):
                        pi = psb()
                        op('pe', lambda e, BE=BE, q=q, pi=pi: e.transpose(
                            out=PSUM[:, pi, 0:128], in_=BE[:, 4 * q:4 * q + 4, :].rearrange("p a b -> p (a b)"),
                            identity=ident[:, :]), reads=["BE" + nm], writes=pk(pi))
                        for slot in range(4):
                            op('dve', lambda e, BTd=BTd, q=q, slot=slot, pi=pi: e.tensor_scalar(
                                out=BTd[:, 4 * q + slot, :], in0=PSUM[:, pi, 0:128], scalar1=rmask[:, slot:slot + 1],
                                scalar2=None, op0=ALU.mult), reads=pk(pi), writes=["BT" + nm])
                for (CN, CTd, sgn, nm) in ((CNr, CTr, 1.0, "r"), (CNi, nCTi, -1.0, "i"), (CNr, nCTr, -1.0, "r2")):
                    op('dve', lambda e, CTd=CTd: e.memset(CTd[:], 0.0), writes=["CT" + nm])
                    for q in range(4):
                        tt('dve', CE[:, 0:64], CN[:, q, :], cmask[:, 0:64], ALU.mult, R, ["CE"])
                        tt('dve', CE[:, 64:128], CN[:, q, :], cmask[:, 64:128], ALU.mult, R, ["CE"])
                        pi = psb()
                        op('pe', lambda e, pi=pi: e.transpose(out=PSUM[:, pi, 0:128], in_=CE[:, :], identity=ident[:, :]),
                           reads=["CE"], writes=pk(pi))
                        for slot in range(4):
                            op('dve', lambda e, CTd=CTd, q=q, slot=slot, pi=pi, sgn=sgn: e.tensor_scalar(
                                out=CTd[:, 4 * q + slot, 32 * slot:32 * slot + 32], in0=PSUM[:, pi, 32 * slot:32 * slot + 32],
                                scalar1=sgn, scalar2=None, op0=ALU.mult), reads=pk(pi), writes=["CT" + nm])
                op('dve', lambda e: e.memset(GE2[:], 0.0), writes=["GE"])
                op('dve', lambda e: e.tensor_copy(out=SS[:, :, 0], in_=nR128s[:, :]), reads=["R128"], writes=["SS"])
                op('dve', lambda e: e.tensor_copy(out=SS[:, :, 1], in_=R128s[:, :]), reads=["R128"], writes=["SS"])
                fence()

        def mm_group(pi, n, pairs, reads, extra_writes=()):
            last = len(pairs) - 1
            tok = None
            for i, (lh, rh) in enumerate(pairs):
                tok = op('pe', lambda e, lh=lh, rh=rh, i=i: e.matmul(PSUM[:, pi, 0:n], lhsT=lh, rhs=rh, start=(i == 0),
                                                                     stop=(i == last)),
                         reads=reads, writes=pk(pi) + list(extra_writes), signal=(i == last))
            return tok

        def layer_phase_a(l):
            with ExitStack() as ph:
                def t(name, shape, dt=BF16):
                    return ph.enter_context(nc.sbuf_tensor("a%d_" % l + name, list(shape), dt))
                XN = t("XN", [128, 8, 576])
                MIX = t("MIX", [128, 8, 576])
                R2 = t("R2", [128, 4, 576])
                KT = t("KT", [128, 128 + 576])
                VT = t("VT", [128, 5, 128])
                VS = t("VS", [64, 128])
                U = t("U", [128, 4, 576])
                AT = t("AT", [128, 4, 576])
                STt = t("ST", [128, 4, 576])
                YG = t("YG", [128, 4, 576])
                s_o = C.stream("ao%d" % l)
                s_o1 = C.stream("ao1_%d" % l)
                s_o2 = C.stream("ao2_%d" % l)

                for bi in range(4):
                  c0 = bi * 512
                  subs = [(0, 512)] + ([(512, 64)] if bi == 3 else [])
                  NB = 576 if bi == 3 else 512
                  with ExitStack() as ph2:
                    def t2(name, shape, dt=BF16):
                        return ph2.enter_context(nc.sbuf_tensor("a%d_%d_" % (l, bi) + name, list(shape), dt))
                    SQ = t2("SQ", [128, 8, 512])
                    RS = t2("RS", [128, 512], F32)
                    XG = t2("XG", [128, 2, 512], F32)
                    RSq = t2("RSq", [128, 512], F32)
                    SQh = t2("SQh", [128, 512])
                    KF = t2("KF", [128, 128], F32)
                    KSF = t2("KSF", [128, 64], F32)
                    OUTS = t2("OUTS", [128, 128], F32)
                    OUTS2 = t2("OUTS2", [128, 128], F32)
                    wq_a2 = WQ([(w_in[l][:, ch_ * 512:ch_ * 512 + (512 if ch_ < 2 else 256)].rearrange("(k p) n -> p k n", p=128),
                                 [128, 8, (512 if ch_ < 2 else 256)]) for ch_ in range(3)])
                    wq_a2.get(0)
                    for (o, n) in subs:
                        xs_ = X[:, :, c0 + o:c0 + o + n]
                        op('act', lambda e, xs_=xs_, n=n: e.activation(out=SQ[:, :, 0:n], in_=xs_, func=AF.Square),
                           reads=["X"], writes=["SQ"])
                        if 'a1a' in SKIP:
                            continue
                        pi = psb()
                        mm_group(pi, n, [(ones[:, :], SQ[:, k, 0:n]) for k in range(8)], ["SQ", "ones"])
                        op('act', lambda e, pi=pi, n=n: e.activation(out=RS[:, 0:n], in_=PSUM[:, pi, 0:n], func=AF.Sqrt,
                                                                     bias=EPS, scale=1.0 / D), reads=pk(pi), writes=["RS"])
                        if 'a1b' in SKIP:
                            continue
                        op('dve', lambda e, n=n: e.reciprocal(out=RS[:, 0:n], in_=RS[:, 0:n]), reads=["RS"], writes=["RS"])
                        if 'a1c' in SKIP:
                            continue
                        for k in range(8):
                            xg = XG[:, k % 2, 0:n]
                            op('act', lambda e, k=k, o=o, n=n, xg=xg: e.activation(
                                out=xg, in_=X[:, k, c0 + o:c0 + o + n], func=AF.Copy, scale=g1[:, l, k:k + 1]),
                                reads=["X"], writes=[("XG", k % 2)])
                            op('dve', lambda e, k=k, o=o, n=n, xg=xg: e.tensor_tensor(
                                out=XN[:, k, o:o + n], in0=xg, in1=RS[:, 0:n], op=ALU.mult),
                                reads=[("XG", k % 2), "RS"], writes=[("XN", k)])
                    if STAGE >= 8 and bi >= 1:
                        norm2_into(l, MIX, SQ, RS, XG, (bi - 1) * 512, 0, 512)
                    XNr = [("XN", k) for k in range(8)]
                    for ch in range(3):
                        wcols = 512 if ch < 2 else 256
                        Wv, wk = wq_a2.get(ch)
                        for j in range(wcols // 128):
                            col = ch * 512 + j * 128
                            if col == 640:
                                continue
                            for (o, n) in subs:
                                pi = psb()
                                mm_group(pi, n, [(Wv[:, k, j * 128:(j + 1) * 128], XN[:, k, o:o + n]) for k in range(8)],
                                         XNr + [wk])
                                if 'a2a' in SKIP:
                                    continue
                                if col < 640:
                                    isq = col < 512
                                    op('act', lambda e, pi=pi, n=n: e.activation(out=SQh[:, 0:n], in_=PSUM[:, pi, 0:n],
                                                                                 func=AF.Square), reads=pk(pi), writes=["SQh"])
                                    p2 = psb()
                                    mm_group(p2, n, [(blk64[:, :], SQh[:, 0:n])], ["SQh", "blk64"])
                                    op('act', lambda e, p2=p2, n=n: e.activation(out=RSq[:, 0:n], in_=PSUM[:, p2, 0:n],
                                                                                 func=AF.Sqrt, bias=EPS, scale=1.0 / 64),
                                       reads=pk(p2), writes=["RSq"])
                                    op('dve', lambda e, n=n: e.reciprocal(out=RSq[:, 0:n], in_=RSq[:, 0:n]),
                                       reads=["RSq"], writes=["RSq"])
                                    if 'a2b' in SKIP:
                                        continue
                                    if isq:
                                        dst = R2[:, col // 128, o:o + n]
                                        dk = ("R2", col // 128)
                                        gi_ = 0
                                    else:
                                        dst = KT[:, 128 + o:128 + o + n]
                                        dk = "KT"
                                        gi_ = 1
                                    op('dve', lambda e, pi=pi, n=n, dst=dst, gi_=gi_: e.scalar_tensor_tensor(
                                        out=dst, in0=PSUM[:, pi, 0:n], scalar=qkg[:, l, gi_:gi_ + 1], in1=RSq[:, 0:n],
                                        op0=ALU.mult, op1=ALU.mult), reads=pk(pi) + ["RSq", "qkg"], writes=[dk])
                                    if (not isq) and bi == 3 and 'a2c' not in SKIP:
                                        if o == 0:
                                            op('dve', lambda e, pi=pi: e.scalar_tensor_tensor(
                                                out=KF[:, :], in0=PSUM[:, pi, 384:512], scalar=qkg[:, l, 1:2],
                                                in1=RSq[:, 384:512], op0=ALU.mult, op1=ALU.mult),
                                                reads=pk(pi) + ["RSq"], writes=["KF"])
                                            p3 = psb()
                                            op('pe', lambda e, p3=p3: e.transpose(out=PSUM[:, p3, 0:128], in_=KF[:, :],
                                                                                  identity=ident[:, :]),
                                               reads=["KF"], writes=pk(p3))
                                            op('act', lambda e, p3=p3: e.activation(out=OUTS[:, :], in_=PSUM[:, p3, 0:128],
                                                                                    func=AF.Copy), reads=pk(p3), writes=["OUTS"])
                                            dma('sp', s_o1, kp[l], OUTS[:, :], reads=["OUTS"])
                                        else:
                                            op('dve', lambda e, pi=pi: e.scalar_tensor_tensor(
                                                out=KSF[:, :], in0=PSUM[:, pi, 0:64], scalar=qkg[:, l, 1:2],
                                                in1=RSq[:, 0:64], op0=ALU.mult, op1=ALU.mult),
                                                reads=pk(pi) + ["RSq"], writes=["KSF"])
                                            p3 = psb()
                                            op('pe', lambda e, p3=p3: e.transpose(out=PSUM[0:64, p3, 0:128], in_=KSF[:, :],
                                                                                  identity=ident[:, :]),
                                               reads=["KSF"], writes=pk(p3))
                                            op('act', lambda e, p3=p3: e.activation(out=OUTS2[0:64, :], in_=PSUM[0:64, p3, 0:128],
                                                                                    func=AF.Copy), reads=pk(p3), writes=["OUTS2"])
                                            for i_ in range(4):
                                                dma('sp', s_o2, ksam[l][:, 124 + i_, :], OUTS2[i_ * 16:(i_ + 1) * 16, :],
                                                    reads=["OUTS2"])
                                else:
                                    ut = (col - 768) // 128
                                    op('act', lambda e, pi=pi, n=n, ut=ut, o=o: e.activation(
                                        out=U[:, ut, o:o + n], in_=PSUM[:, pi, 0:n], func=AF.Copy),
                                        reads=pk(pi), writes=[("U", ut)])
                        if ch == 1 and 'a2d' not in SKIP:
                            pi = psb()
                            for tl in range(4):
                                for k in range(8):
                                    op('pe', lambda e, tl=tl, k=k, pi=pi: e.matmul(
                                        PSUM[:, pi, tl * 128:(tl + 1) * 128],
                                        lhsT=XN[:, k, tl * 128:(tl + 1) * 128], rhs=Wv[:, k, 128:256],
                                        start=(k == 0), stop=(k == 7)),
                                        reads=XNr + [wk], writes=pk(pi), signal=(k == 7))
                            if 'v1' in SKIP:
                                continue
                            op('act', lambda e, pi=pi: e.activation(out=VT[:, 1:5, :], in_=PSUM[:, pi, :].rearrange(
                                "p (t d) -> p t d", t=4), func=AF.Copy), reads=pk(pi), writes=["VT"])
                            if bi == 3 and 'v2' not in SKIP:
                                op('dve', lambda e, pi=pi: e.tensor_copy(out=OUTS[:, :], in_=PSUM[:, pi, 384:512]),
                                   reads=pk(pi), writes=["OUTS"])
                                dma('sp', s_o1, vp[l], OUTS[:, :], reads=["OUTS"])
                                pv5 = psb()
                                for k in range(8):
                                    op('pe', lambda e, k=k, pv5=pv5: e.matmul(
                                        PSUM[0:64, pv5, 0:128], lhsT=XN[:, k, 512:576], rhs=Wv[:, k, 128:256],
                                        start=(k == 0), stop=(k == 7)), reads=XNr + [wk], writes=pk(pv5), signal=(k == 7))
                                op('act', lambda e, pv5=pv5: e.activation(out=VS[:, :], in_=PSUM[0:64, pv5, 0:128], func=AF.Copy),
                                   reads=pk(pv5), writes=["VS"])
                                op('dve', lambda e, pv5=pv5: e.tensor_copy(out=OUTS2[0:64, :], in_=PSUM[0:64, pv5, 0:128]),
                                   reads=pk(pv5), writes=["OUTS2"])
                                for i_ in range(4):
                                    dma('sp', s_o2, vsam[l][:, 124 + i_, :], OUTS2[i_ * 16:(i_ + 1) * 16, :], reads=["OUTS2"])
                    fence()
                  E = dict(U=U, s_o=s_o, R2=R2, KT=KT, VT=VT, VS=VS, AT=AT, ST=STt, YG=YG, XN=XN, MIX=MIX, subs=subs, c0=c0)
                  if STAGE >= 3:
                      attn_block(l, bi, E)
                  if STAGE >= 2:
                      ssm_block(l, bi, E)
                  if STAGE >= 5:
                      mix_block(l, bi, E)
                  if DEBUG_DUMP and l == 0:
                      dma('sp', s_o, dbg_at[bi], AT[:, :, :], reads=[])
                      dma('sp', s_o, dbg_st[bi], STt[:, :, :], reads=[])
                      dma('sp', s_o, dbg_yg[bi], YG[:, :, :], reads=[])
                      dma('sp', s_o, dbg_mix[bi], MIX[:, :, :], reads=[])
                  op('pool', lambda e: e.tensor_copy(out=KT[:, 0:128], in_=KT[:, 128 + 384:128 + 512]), reads=["KT"], writes=["KT"])
                  op('pool', lambda e: e.tensor_copy(out=VT[:, 0, :], in_=VT[:, 4, :]), reads=["VT"], writes=["VT"])
                if "shift" not in SKIP:
                    with nc.allow_non_contiguous_dma(reason="cache shift"):
                        dma('sp', s_o, ksam[l][:, 0:124, :], cache_k[l][:, 4:128, :])
                        dma('sp', s_o, vsam[l][:, 0:124, :], cache_v[l][:, 4:128, :])
                if STAGE >= 8:
                    ffn_tail(l, MIX)
                fence()

        def ffn_specs(l):
            fs = []
            for c in range(8):
                fs.append((w_up[l][:, c * 512:(c + 1) * 512].rearrange("(k p) n -> p k n", p=128), [128, 8, 512]))
                fs.append((w_down[l][c * 512:(c + 1) * 512, :].rearrange("(k p) n -> p k n", p=128), [128, 4, 1024]))
            return fs

        def norm2_into(l, XN2, SQ, RS, XG, xcol, o, n):
            pi = psb()
            for hf in range(2):
                op('act', lambda e, hf=hf: e.activation(out=SQ[:, 0:4, 0:n], in_=X[:, 4 * hf:4 * hf + 4, xcol:xcol + n], func=AF.Square),
                   reads=["X"], writes=["SQ"])
                for k_ in range(4):
                    op('pe', lambda e, k_=k_, hf=hf: e.matmul(PSUM[:, pi, 0:n], lhsT=ones[:, :], rhs=SQ[:, k_, 0:n],
                                                             start=(hf == 0 and k_ == 0), stop=(hf == 1 and k_ == 3)),
                       reads=["SQ", "ones"], writes=pk(pi), signal=(k_ == 3))
            op('act', lambda e: e.activation(out=RS[:, 0:n], in_=PSUM[:, pi, 0:n], func=AF.Sqrt, bias=EPS, scale=1.0 / D),
               reads=pk(pi), writes=["RS"])
            op('dve', lambda e: e.reciprocal(out=RS[:, 0:n], in_=RS[:, 0:n]), reads=["RS"], writes=["RS"])
            for k_ in range(8):
                xg = XG[:, k_ % 2, 0:n]
                op('act', lambda e, k_=k_, xg=xg: e.activation(out=xg, in_=X[:, k_, xcol:xcol + n], func=AF.Copy,
                                                              scale=g2[:, l, k_:k_ + 1]), reads=["X"], writes=[("XG", k_ % 2)])
                op('dve', lambda e, k_=k_, xg=xg: e.tensor_tensor(out=XN2[:, k_, o:o + n], in0=xg, in1=RS[:, 0:n], op=ALU.mult),
                   reads=[("XG", k_ % 2), "RS"], writes=[("MIX", k_)])

        def ssm_block(l, bi, E):
            U = E['U']; s_o = E['s_o']; YG = E['YG']; ST = E['ST']; subs = E['subs']
            with ExitStack() as ph:
                def t(name, shape, dt=BF16):
                    return ph.enter_context(nc.sbuf_tensor("s%d_%d_" % (l, bi) + name, list(shape), dt))
                Y1 = t("Y1", [128, 512]); T1 = t("T1", [128, 512]); SG = T1
                php = ExitStack()
                def tp(name, shape, dt=BF16):
                    return php.enter_context(nc.sbuf_tensor("sp%d_%d_" % (l, bi) + name, list(shape), dt))
                Zt = [tp("Z%d" % i, [128, 2, 512]) for i in range(2)]
                PRM = [tp("PRM%d" % i, [128, 4, 512]) for i in range(2)]
                Ma = PRM[1][:, 0:2, :]
                Mb = PRM[1][:, 2:4, :]
                MK = ("PR", 1)
                Xc = [tp("Xc%d" % i, [128, 2, 512]) for i in range(2)]
                if l == 0 and bi == 0:
                    print("SBUF remaining in ssm prompt scope:", nc.sbuf_bytes_remaining)
                Gt = [tp("G%d" % i, [128, 2, 512]) for i in range(2)]
                PR = PRM
                INt = [tp("IN%d" % i, [128, 4, 2], F32) for i in range(2)]
                Ut = [tp("U%d" % i, [128, 2], F32) for i in range(2)]
                HO = tp("HO", [128, 2, 16], F32); HT = tp("HT", [128, 16], F32)
                HOUT = tp("HOUT", [16, 2, 128], F32)
                PTa = [tp("PTa%d" % i, [128, 256]) for i in range(2)]
                DRa = tp("DRa", [128, 256], F32)
                v3 = lambda ap_: ap_.rearrange("p (c j) -> p c j", c=NCHUNK)
                R2 = E['R2']; KT = E['KT']; VT = E['VT']; AT = E['AT']
                att_it = [0]

                def att_unit(i, hp, b):
                    hs = i * 2 + hp
                    rows = slice(hp * 64, (hp + 1) * 64)
                    has_prev = (bi * 4 + b) > 0
                    tb = att_it[0] % 2
                    att_it[0] += 1
                    qv = R2[rows, i, b * 128:(b + 1) * 128]
                    if has_prev:
                        op('pe', lambda e: e.matmul(PSUM[:, BS, 0:128], lhsT=KT[rows, b * 128:(b + 1) * 128], rhs=qv,
                                                    start=True, stop=True), reads=[("R2", i), "KT"], writes=pk(BS), signal=False)
                    op('pe', lambda e: e.matmul(PSUM[:, BS, 128:256], lhsT=KT[rows, 128 + b * 128:128 + (b + 1) * 128], rhs=qv,
                                                start=True, stop=True), reads=[("R2", i), "KT"], writes=pk(BS), signal=True)
                    c_lo = 0 if has_prev else 128
                    op('act', lambda e: e.activation(out=PTa[tb][:, c_lo:256], in_=PSUM[:, BS, c_lo:256], func=AF.Exp),
                       reads=pk(BS), writes=[("PTa", tb)])
                    tt('pool', PTa[tb][:, c_lo:256], PTa[tb][:, c_lo:256], EB[:, hs, c_lo:256], ALU.mult, [("PTa", tb), "EB"],
                       [("PTa", tb)])
                    return lambda: att_pv(rows, b, tb, has_prev)

                def att_pv(rows, b, tb, has_prev):
                    parts = ([(VT[:, b, rows], PTa[tb][:, 0:128])] if has_prev else []) + [(VT[:, b + 1, rows], PTa[tb][:, 128:256])]
                    bb = b % 2
                    for (coff, use_ones) in ((0, False), (256, True)):
                        for ii, (vv, pp) in enumerate(parts):
                            lh = ones[:, 0:64] if use_ones else vv
                            op('pe', lambda e, lh=lh, pp=pp, ii=ii, coff=coff: e.matmul(
                                PSUM[rows, BOD, coff + bb * 128:coff + (bb + 1) * 128], lhsT=lh, rhs=pp, start=(ii == 0),
                                stop=(ii == len(parts) - 1)), reads=[("PTa", tb), "VT", "ones"], writes=pk(BOD),
                                signal=(ii == len(parts) - 1))

                def att_norm(i, h):
                    op('act', lambda e: e.activation(out=DRa[:, :], in_=PSUM[:, BOD, 256:512], func=AF.Ln, bias=esink[:, l, i:i + 1],
                                                     scale=1.0), reads=pk(BOD) + ["esink"], writes=["DRa"])
                    op('act', lambda e: e.activation(out=DRa[:, :], in_=DRa[:, :], func=AF.Exp, scale=-1.0), reads=["DRa"], writes=["DRa"])
                    tt('dve', AT[:, i, h * 256:(h + 1) * 256], PSUM[:, BOD, 0:256], DRa[:, :], ALU.mult, pk(BOD) + ["DRa"], [("AT", i)])

                att_groups = []
                if STAGE >= 3:
                    for i in range(4):
                        for h in range(2):
                            att_groups.append((i, h))

                def epilogue(q, yb, o, n):
                    op('dve', lambda e: e.scalar_tensor_tensor(out=Y1[:, 0:n], in0=U[:, q, o:o + n], scalar=dsk[:, l, q:q + 1],
                                                               in1=PSUM[:, yb, 0:n], op0=ALU.mult, op1=ALU.add),
                       reads=pk(yb) + [("U", q)], writes=["Y1"])
                    tt('dve', T1[:, 0:n], Y1[:, 0:n], Y1[:, 0:n], ALU.mult, ["Y1"], ["T1"])
                    op('dve', lambda e: e.tensor_scalar(out=T1[:, 0:n], in0=T1[:, 0:n], scalar1=0.044715, scalar2=1.0,
                                                        op0=ALU.mult, op1=ALU.add), reads=["T1"], writes=["T1"])
                    tt('dve', T1[:, 0:n], T1[:, 0:n], Y1[:, 0:n], ALU.mult, ["T1", "Y1"], ["T1"])
                    op('act', lambda e: e.activation(out=T1[:, 0:n], in_=T1[:, 0:n], func=AF.Sigmoid, scale=1.5957691216057308),
                       reads=["T1"], writes=["T1"])
                    tt('dve', YG[:, q, o:o + n], Y1[:, 0:n], T1[:, 0:n], ALU.mult, ["Y1", "T1"], [("YG", q)])

                BX0, BY, BS, BOD, BUP = 0, 2, 3, 4, 5

                def x0_mm(s):
                    q = s // 4
                    mm_group(BX0, 512, [(BTr[:, s, :], U[:, q, 0:512])], [("U", q), "BTr"])
                    mm_group(BX0 + 1, 512, [(BTi[:, s, :], U[:, q, 0:512])], [("U", q), "BTi"])

                def evac(s):
                    b_ = s % 2
                    xk = ("Xc", b_)
                    op('act', lambda e: e.activation(out=Xc[b_][:, 0, :], in_=PSUM[:, BX0, :], func=AF.Copy), reads=pk(BX0), writes=[xk])
                    op('act', lambda e: e.activation(out=Xc[b_][:, 1, :], in_=PSUM[:, BX0 + 1, :], func=AF.Copy), reads=pk(BX0 + 1),
                       writes=[xk])

                def modops(s):
                    b_ = s % 2
                    xk = ("Xc", b_)
                    zk = ("Z", b_)
                    c4 = bc_mid(PHc[:, s, 0:CHUNK], 2 * NCHUNK)
                    s4 = bc_mid(PHs[:, s, 0:CHUNK], 2 * NCHUNK)
                    x4 = Xc[b_][:, :, :].rearrange("p c (h j) -> p (c h) j", j=CHUNK)
                    tt('dve', Ma.rearrange("p c (h j) -> p (c h) j", j=CHUNK), x4, c4, ALU.mult, [xk, "PH"], [MK])
                    tt('dve', Mb.rearrange("p c (h j) -> p (c h) j", j=CHUNK), x4, s4, ALU.mult, [xk, "PH"], [MK])
                    tt('dve', Zt[b_][:, 0, :], Ma[:, 0, :], Mb[:, 1, :], ALU.add, [MK], [zk])
                    tt('dve', Zt[b_][:, 1, :], Ma[:, 1, :], Mb[:, 0, :], ALU.subtract, [MK], [zk])

                def chain(tiles, hook=lambda: None):
                    for ch in range(NCHUNK):
                        for s in tiles:
                            b_ = s % 2
                            gk = ("G", b_)
                            if ch == 0:
                                ge = GE2[:, s, :]
                                ger = GE2[:, s, ::-1]
                                rk = ["GE"]
                            else:
                                ge = Gt[b_][:, :, ch * CHUNK - 1]
                                ger = Gt[b_][:, ::-1, ch * CHUNK - 1]
                                rk = [gk]
                            tt('dve', Ut[b_][:, :], ger, SS[:, s, :], ALU.mult, rk + ["SS"], [("UT", b_)])
                            op('dve', lambda e, ge=ge, s=s, b_=b_, ch=ch: e.scalar_tensor_tensor(
                                out=INt[b_][:, ch, :], in0=ge, scalar=R128c[:, s:s + 1], in1=Ut[b_][:, :], op0=ALU.mult, op1=ALU.add),
                                reads=rk + [("UT", b_)], writes=[("IN", b_)])
                        hook()
                        for s in tiles:
                            b_ = s % 2
                            gk = ("G", b_)
                            cs_ = slice(ch * CHUNK, (ch + 1) * CHUNK)
                            for c_ in range(2):
                                op('dve', lambda e, s=s, ch=ch, cs_=cs_, c_=c_, b_=b_: e.tensor_tensor_scan(
                                    out=Gt[b_][:, c_, cs_], data0=MAG[:, s:s + 1].to_broadcast([128, CHUNK]), data1=Zt[b_][:, c_, cs_],
                                    initial=INt[b_][:, ch, c_:c_ + 1], op0=ALU.mult, op1=ALU.add),
                                    reads=[("Z", b_), ("IN", b_)], writes=[gk])
                            hook()

                def finish(s):
                    q = s // 4
                    slot = s % 4
                    yb = BY
                    b_ = s % 2
                    gk = ("G", b_)
                    cb = bc_mid(PHc[:, s, 0:CHUNK], NCHUNK)
                    sb_ = bc_mid(PHs[:, s, 0:CHUNK], NCHUNK)
                    op('dve', lambda e: e.tensor_copy(out=GE2[:, s, :], in_=Gt[b_][:, :, 511]), reads=[gk], writes=["GE"])
                    P_ = PR[b_]
                    pkey = ("PR", b_)
                    gr_ = v3(Gt[b_][:, 0, :])
                    gi_ = v3(Gt[b_][:, 1, :])
                    c4 = bc_mid(PHc[:, s, 0:CHUNK], 2 * NCHUNK)
                    s4 = bc_mid(PHs[:, s, 0:CHUNK], 2 * NCHUNK)
                    g4 = Gt[b_][:, :, :].rearrange("p c (h j) -> p (c h) j", j=CHUNK)
                    tt('dve', P_[:, 0:2, :].rearrange("p c (h j) -> p (c h) j", j=CHUNK), g4, c4, ALU.mult, [gk, "PH"], [pkey])
                    tt('dve', P_[:, 2:4, :].rearrange("p c (h j) -> p (c h) j", j=CHUNK), g4, s4, ALU.mult, [gk, "PH"], [pkey])
                    pairs = [(CTr[:, s, :], P_[:, 0, :]), (nCTi[:, s, :], P_[:, 1, :]), (nCTi[:, s, :], P_[:, 2, :]),
                             (nCTr[:, s, :], P_[:, 3, :])]
                    for ii, (lh, rh) in enumerate(pairs):
                        first = (slot == 0 and ii == 0)
                        lastm = (slot == 3 and ii == 3)
                        op('pe', lambda e, lh=lh, rh=rh, first=first, lastm=lastm: e.matmul(
                            PSUM[:, yb, 0:512], lhsT=lh, rhs=rh, start=first, stop=lastm),
                            reads=[pkey, "CT"], writes=pk(yb), signal=(ii == 3))
                    if slot == 3:
                        pending.append(lambda: epilogue(q, yb, 0, 512))

                ffn_on = (bi >= 1) and STAGE >= 8 and 'ffni' not in SKIP
                if ffn_on:
                    XN2 = E['MIX']
                    xo = (bi - 1) * 512
                    Hf = tp("Hf", [128, 4, 512])
                    wq_f = WQ(ffn_specs(l))
                    BDN = (5, 6, 7)

                    def ffn_up_unit(c, j):
                        Wu, wuk = wq_f.get(2 * c)
                        mm_group(BUP, 512, [(Wu[:, k_, j * 128:(j + 1) * 128], XN2[:, k_, 0:512]) for k_ in range(8)],
                                 [("MIX", k_) for k_ in range(8)] + [wuk])
                        op('act', lambda e: e.activation(out=Hf[:, j, :], in_=PSUM[:, BUP, :], func=AF.Relu),
                           reads=pk(BUP), writes=[("Hf", j)])
                        op('act', lambda e: e.activation(out=Hf[:, j, :], in_=Hf[:, j, :], func=AF.Square),
                           reads=[("Hf", j)], writes=[("Hf", j)])

                    def ffn_down(c, m):
                        Wd, wdk = wq_f.get(2 * c + 1)
                        mm_group(BDN[m % 3], 512, [(Wd[:, j, m * 128:(m + 1) * 128], Hf[:, j, :]) for j in range(4)],
                                 [("Hf", j) for j in range(4)] + [wdk])

                    def ffn_add(m):
                        bank = BDN[m % 3]
                        op('dve', lambda e: e.tensor_tensor(out=X[:, m, xo:xo + 512], in0=PSUM[:, bank, :], in1=X[:, m, xo:xo + 512],
                                                            op=ALU.add), reads=pk(bank) + ["X"], writes=["X"])
                else:
                    Wg, wgk = wload(w_glu[l].rearrange("(k p) n -> p k n", p=128), [128, 4, 512], None)

                pending = []
                x0_mm(0)
                evac(0)
                x0_mm(1)
                evac(1)
                for p_ in range(8):
                    s0, s1 = 2 * p_, 2 * p_ + 1
                    modops(s0)
                    modops(s1)
                    if p_ < 7:
                        x0_mm(s0 + 2)
                        evac(s0 + 2)
                        x0_mm(s1 + 2)
                        evac(s1 + 2)
                    aunits = []
                    if att_groups:
                        if p_ > 0:
                            att_norm(*att_groups[p_ - 1])
                        gi_, gh_ = att_groups[p_]
                        aunits = [(gi_, hp, b) for hp in range(2) for b in (2 * gh_, 2 * gh_ + 1)]
                    for j in range(4):
                        pv_ = att_unit(*aunits[j]) if j < len(aunits) else None
                        if ffn_on:
                            ffn_up_unit(p_, j)
                        if pv_:
                            pv_()
                    chain([s0, s1])
                    todo = pending
                    pending = []
                    for f_ in todo:
                        f_()
                    if ffn_on:
                        ffn_down(p_, 0)
                        ffn_down(p_, 1)
                        ffn_down(p_, 2)
                    finish(s0)
                    finish(s1)
                    if ffn_on:
                        ffn_add(0)
                        for m_ in range(3, 8):
                            ffn_down(p_, m_)
                            ffn_add(m_ - 2)
                        ffn_add(6)
                        ffn_add(7)
                if att_groups:
                    att_norm(*att_groups[7])
                for f_ in pending:
                    f_()
                if ffn_on:
                    Wg, wgk = wload(w_glu[l].rearrange("(k p) n -> p k n", p=128), [128, 4, 512], None)
                if bi == 3:
                    tt('dve', HO[:, 0, :], GE2[:, :, 0], P127c[:, :], ALU.mult, ["GE"], ["HO"])
                    tt('dve', HT[:, :], GE2[:, :, 1], P127s[:, :], ALU.mult, ["GE"], ["HT"])
                    tt('dve', HO[:, 0, :], HO[:, 0, :], HT[:, :], ALU.subtract, ["HO", "HT"], ["HO"])
                    tt('dve', HO[:, 1, :], GE2[:, :, 0], P127s[:, :], ALU.mult, ["GE", "HO"], ["HO"])
                    tt('dve', HT[:, :], GE2[:, :, 1], P127c[:, :], ALU.mult, ["GE", "HO"], ["HT"])
                    tt('dve', HO[:, 1, :], HO[:, 1, :], HT[:, :], ALU.add, ["HO", "HT"], ["HO"])
                    for c_ in range(2):
                        pi = 6 + c_
                        op('pe', lambda e, c_=c_, pi=pi: e.transpose(out=PSUM[0:16, pi, 0:128], in_=HO[:, c_, :], identity=ident[:, :]),
                           reads=["HO"], writes=pk(pi))
                        op('act', lambda e, c_=c_, pi=pi: e.activation(out=HOUT[:, c_, :], in_=PSUM[0:16, pi, 0:128], func=AF.Copy),
                           reads=pk(pi), writes=["HOUT"])
                    dma('sp', s_o, hrp[l], HOUT[:, 0, :], reads=["HOUT"])
                    dma('sp', s_o, hip[l], HOUT[:, 1, :], reads=["HOUT"])
                fence()
                php.close()
                if bi == 3 and STAGE >= 4 and 'ssms' not in SKIP:
                    ssm_sample(l, E, epilogue)
                    fence()
                for j in range(4):
                    for (o, n) in subs:
                        pi = psb()
                        mm_group(pi, n, [(Wg[:, q_, j * 128:(j + 1) * 128], YG[:, q_, o:o + n]) for q_ in range(4)],
                                 [("YG", q_) for q_ in range(4)] + [wgk])
                        op('act', lambda e, pi=pi, n=n, j=j: e.activation(out=SG[:, 0:n], in_=PSUM[:, pi, 0:n], func=AF.Sigmoid,
                                                                          bias=bgl[:, l, j:j + 1], scale=1.0),
                           reads=pk(pi), writes=["T1"])
                        tt('dve', ST[:, j, o:o + n], YG[:, j, o:o + n], SG[:, 0:n], ALU.mult, ["T1", ("YG", j)], [("ST", j)])
                fence()

        def ssm_sample(l, E, epilogue):
            U = E['U']; s_o = E['s_o']
            with ExitStack() as ph:
                def t(name, shape, dt=F32):
                    return ph.enter_context(nc.sbuf_tensor("ss%d_" % l + name, list(shape), dt))
                SN = t("SN", [16, 2048])
                H0 = [t("H0%d" % c_, [128, 16, 16]) for c_ in range(2)]
                HS = [t("HS%d" % c_, [128, 16, 64]) for c_ in range(2)]
                HSb = [t("HSb%d" % c_, [128, 16, 64], BF16) for c_ in range(2)]
                TA = t("TA", [128, 16, 16]); TB = t("TB", [128, 16, 16])
                s_s = C.stream("ssl%d" % l)
                for s in range(16):
                    q = s // 4
                    for c_, BT in ((0, BTr), (1, BTi)):
                        bank = 2 * c_ + s // 8
                        op('pe', lambda e, s=s, q=q, BT=BT, bank=bank: e.matmul(
                            PSUM[:, bank, (s % 8) * 64:(s % 8) * 64 + 64], lhsT=BT[:, s, :], rhs=U[:, q, 512:576],
                            start=True, stop=True), reads=[("U", q), "BTr", "BTi"], writes=pk(bank), signal=True)
                for c_, src in ((0, st_re), (1, st_im)):
                    dma('sp', s_s, SN[:, :], src[l], writes=["SN"])
                    pi = 6 + c_
                    for s in range(16):
                        op('pe', lambda e, s=s, pi=pi: e.transpose(out=PSUM[:, pi, s * 16:(s + 1) * 16],
                                                                   in_=SN[0:16, s * 128:(s + 1) * 128], identity=ident[0:16, 0:16]),
                           reads=["SN"], writes=pk(pi), signal=(s == 15))
                    op('act', lambda e, c_=c_, pi=pi: e.activation(out=H0[c_][:, :, :], in_=PSUM[:, pi, 0:256].rearrange(
                        "p (s b) -> p s b", s=16), func=AF.Copy), reads=pk(pi), writes=[("H0", c_)])
                ARb = bc_last(AR[:, :], 16)
                AIb = bc_last(AI[:, :], 16)
                xv = [PSUM[:, 2 * c_:2 * c_ + 2, :].rearrange("p b (s c) -> p (b s) c", c=64) for c_ in range(2)]
                for i_ in range(4):
                    cs_ = slice(i_ * 16, (i_ + 1) * 16)
                    if i_ == 0:
                        pr_, pi_ = H0[0][:, :, :], H0[1][:, :, :]
                        rk = [("H0", 0), ("H0", 1)]
                    else:
                        ps_ = slice((i_ - 1) * 16, i_ * 16)
                        pr_, pi_ = HS[0][:, :, ps_], HS[1][:, :, ps_]
                        rk = ["HS"]
                    tt('dve', TA[:, :, :], pr_, ARb, ALU.mult, rk + ["AR"], ["TA"])
                    tt('dve', TB[:, :, :], pi_, AIb, ALU.mult, rk + ["AR"], ["TB"])
                    tt('dve', TA[:, :, :], TA[:, :, :], TB[:, :, :], ALU.subtract, ["TA", "TB"], ["TA"])
                    tt('dve', HS[0][:, :, cs_], xv[0][:, :, cs_], TA[:, :, :], ALU.add, pk(0, 2) + ["TA"], ["HS"])
                    tt('dve', TA[:, :, :], pi_, ARb, ALU.mult, rk + ["AR", "HS"], ["TA"])
                    tt('dve', TB[:, :, :], pr_, AIb, ALU.mult, rk + ["AR"], ["TB"])
                    tt('dve', TA[:, :, :], TA[:, :, :], TB[:, :, :], ALU.add, ["TA", "TB"], ["TA"])
                    tt('dve', HS[1][:, :, cs_], xv[1][:, :, cs_], TA[:, :, :], ALU.add, pk(2, 2) + ["TA", "HS"], ["HS"])
                for c_ in range(2):
                    op('dve', lambda e, c_=c_: e.tensor_copy(out=HSb[c_][:, :, :], in_=HS[c_][:, :, :]), reads=["HS", "HS"],
                       writes=[("HSb", c_)])
                for q in range(4):
                    yb = 4 + (q % 2)
                    pairs = []
                    for slot in range(4):
                        s = 4 * q + slot
                        pairs.append((CTr[:, s, :], HSb[0][:, s, :]))
                        pairs.append((nCTi[:, s, :], HSb[1][:, s, :]))
                    mm_group(yb, 64, pairs, [("HSb", 0), ("HSb", 1), "CT"])
                    epilogue(q, yb, 512, 64)
                for c_, dst in ((0, hrs), (1, his)):
                    for g4 in range(4):
                        pi = g4
                        for j in range(4):
                            s = g4 * 4 + j
                            op('pe', lambda e, s=s, j=j, pi=pi, c_=c_: e.transpose(
                                out=PSUM[0:16, pi, j * 128:(j + 1) * 128], in_=HS[c_][:, s, 48:64], identity=ident[:, :]),
                                reads=["HS", "HS"], writes=pk(pi), signal=(j == 3))
                        op('act', lambda e, pi=pi, g4=g4: e.activation(out=SN[:, g4 * 512:(g4 + 1) * 512], in_=PSUM[0:16, pi, :],
                                                                      func=AF.Copy), reads=pk(pi), writes=["SN"])
                    dma('sp', s_s, dst[l].rearrange("b s r -> b (s r)"), SN[:, :], reads=["SN"])
                fence()

        def attn_block(l, bi, E):
            if bi == 3 and STAGE >= 4 and 'atts' not in SKIP:
                attn_sample(l, E)
                fence()

        def attn_sample(l, E):
            R2 = E['R2']; KT = E['KT']; VS = E['VS']; AT = E['AT']
            with ExitStack() as ph:
                def t(name, shape, dt=BF16):
                    return ph.enter_context(nc.sbuf_tensor("as%d_" % l + name, list(shape), dt))
                CK = t("CK", [128, 16, 128], F32)
                CKT = t("CKT", [128, 16, 128])
                CV = t("CV", [128, 16, 128])
                TMPc = t("TMPc", [128, 512], F32)
                Pc = t("Pc", [128, 512])
                TMPn = t("TMPn", [64, 512], F32)
                Pn = t("Pn", [64, 512])
                DR = t("DR", [128, 256], F32)
                s_c = C.stream("asl%d" % l)
                s_v = C.stream("asv%d" % l)
                with nc.allow_non_contiguous_dma(reason="cache load"):
                    dma('sp', s_c, CK[:, :, :], cache_k[l].rearrange("s j d -> j s d"), writes=["CK"])
                    dma('pool', s_v, CV[:, :, :], cache_v[l].rearrange("s j d -> j s d"), writes=["CV"])
                for sl in range(16):
                    pi = sl % 4
                    op('pe', lambda e, sl=sl, pi=pi: e.transpose(out=PSUM[:, pi, 0:128], in_=CK[:, sl, :], identity=ident[:, :]),
                       reads=["CK"], writes=pk(pi))
                    if sl % 2 == 0:
                        op('act', lambda e, sl=sl, pi=pi: e.activation(out=CKT[:, sl, :], in_=PSUM[:, pi, 0:128], func=AF.Copy),
                           reads=pk(pi), writes=["CKT"])
                    else:
                        op('dve', lambda e, sl=sl, pi=pi: e.tensor_copy(out=CKT[:, sl, :], in_=PSUM[:, pi, 0:128]),
                           reads=pk(pi), writes=["CKT"])
                po, pd = 6, 7
                for kv in range(2):
                    rows = slice(kv * 64, (kv + 1) * 64)
                    for sl in range(16):
                        op('pe', lambda e, sl=sl, kv=kv, rows=rows: e.matmul(
                            PSUM[:, 4 + kv, sl:256:16], lhsT=CKT[rows, sl, :],
                            rhs=R2[rows, :, 512 + sl:576:16], start=True, stop=True),
                            reads=["CKT"] + [("R2", i) for i in range(4)], writes=pk(4 + kv), signal=(sl == 15))
                for kv in range(2):
                    op('dve', lambda e, kv=kv: e.tensor_tensor(
                        out=TMPc[:, kv * 256:(kv + 1) * 256].rearrange("p (a s) -> p a s", s=16),
                        in0=PSUM[:, 4 + kv, 0:256].rearrange("p (a s) -> p a s", s=16),
                        in1=bc_last(biasc[:, kv * 16:(kv + 1) * 16], 16), op=ALU.add), reads=pk(4 + kv), writes=["TMPc"])
                op('act', lambda e: e.activation(out=Pc[:, :], in_=TMPc[:, :], func=AF.Exp), reads=["TMPc"], writes=["Pc"])
                for kv in range(2):
                    rows = slice(kv * 64, (kv + 1) * 64)
                    op('pe', lambda e, kv=kv, rows=rows: e.matmul(
                        PSUM[0:64, kv, 0:256], lhsT=KT[rows, 128 + 512:128 + 576],
                        rhs=R2[rows, :, 512:576], start=True, stop=True),
                        reads=["KT"] + [("R2", i) for i in range(4)], writes=pk(kv), signal=True)
                    tt('dve', TMPn[:, kv * 256:(kv + 1) * 256], PSUM[0:64, kv, 0:256], biasnf[:, kv * 256:(kv + 1) * 256], ALU.add,
                       pk(kv), ["TMPn"])
                op('act', lambda e: e.activation(out=Pn[:, :], in_=TMPn[:, :], func=AF.Exp), reads=["TMPn"], writes=["Pn"])
                for (bank, use_ones) in ((po, False), (pd, True)):
                    for kv in range(2):
                        rows = slice(kv * 64, (kv + 1) * 64)
                        lh = ones[0:64, 0:64] if use_ones else VS[0:64, rows]
                        op('pe', lambda e, bank=bank, kv=kv, rows=rows, lh=lh: e.matmul(
                            PSUM[rows, bank, 0:256], lhsT=lh, rhs=Pn[0:64, kv * 256:(kv + 1) * 256], start=True, stop=False),
                            reads=["Pn", "VS", "ones"], writes=pk(bank), signal=False)
                        for sl in range(16):
                            lh2 = ones[:, 0:64] if use_ones else CV[:, sl, rows]
                            op('pe', lambda e, bank=bank, kv=kv, rows=rows, sl=sl, lh2=lh2: e.matmul(
                                PSUM[rows, bank, sl:256:16], lhsT=lh2, rhs=Pc[:, kv * 256 + sl:kv * 256 + 256:16],
                                start=False, stop=(sl == 15)), reads=["Pc", "CV", "ones"], writes=pk(bank), signal=(sl == 15))
                op('dve', lambda e: e.tensor_tensor(out=DR[:, :].rearrange("p (h c) -> p h c", h=4),
                                                    in0=PSUM[:, pd, 0:256].rearrange("p (h c) -> p h c", h=4),
                                                    in1=bc_last(esink[:, l, :], 64), op=ALU.add), reads=pk(pd) + ["esink"], writes=["DRs"])
                op('dve', lambda e: e.reciprocal(out=DR[:, :], in_=DR[:, :]), reads=["DRs"], writes=["DRs"])
                op('dve', lambda e: e.tensor_tensor(out=AT[:, :, 512:576], in0=PSUM[:, po, 0:256].rearrange("p (h c) -> p h c", h=4),
                                                    in1=DR[:, :].rearrange("p (h c) -> p h c", h=4), op=ALU.mult),
                   reads=pk(po) + ["DRs"], writes=[("AT", i) for i in range(4)])
                fence()

        def mix_block(l, bi, E):
            XN = E['XN']; MIX = E['MIX']; AT = E['AT']; ST = E['ST']; subs = E['subs']; c0 = E['c0']
            with ExitStack() as ph:
                def t(name, shape, dt=BF16):
                    return ph.enter_context(nc.sbuf_tensor("mx%d_%d_" % (l, bi) + name, list(shape), dt))
                SGA = t("SGA", [128, 576], F32)
                TM = t("TM", [128, 576])
                WL = [t("WL%d" % i, [128, 4096]) for i in range(2)]
                wl_s = [C.stream("wl%d_%d_%d" % (l, bi, i)) for i in range(2)]
                R5 = Ring(WR + WL, wr_stream + wl_s, [("wr", i) for i in range(3)] + [("wl", i) for i in range(2)])
                XNr = [("XN", k_) for k_ in range(8)]
                mspecs = []
                for h in range(2):
                    for (Wsrc, gcol) in ((w_ao, 1280), (w_so, 2304)):
                        mspecs.append((Wsrc[l][:, h * 512:(h + 1) * 512].rearrange("(k p) n -> p k n", p=128), [128, 4, 512]))
                        mspecs.append((w_in[l][:, gcol + h * 512:gcol + (h + 1) * 512].rearrange("(k p) n -> p k n", p=128), [128, 8, 512]))
                for h in range(2):
                    mspecs.append((w_out[l][:, h * 512:(h + 1) * 512].rearrange("(k p) n -> p k n", p=128), [128, 8, 512]))
                wq_m = WQ(mspecs, ring=R5, ahead=2)
                mi = 0
                for h in range(2):
                    for (Wsrc, Act, akey, gcol, first) in ((w_ao, AT, "AT", 1280, True), (w_so, ST, "ST", 2304, False)):
                        Wo_, wok = wq_m.get(mi)
                        Wg_, wgk = wq_m.get(mi + 1)
                        mi += 2
                        for j in range(4):
                            m = h * 4 + j
                            bcol = (0 if first else 8) + m
                            for (o, n) in subs:
                                pg = psb()
                                mm_group(pg, n, [(Wg_[:, k_, j * 128:(j + 1) * 128], XN[:, k_, o:o + n]) for k_ in range(8)],
                                         XNr + [wgk])
                                op('act', lambda e, pg=pg, n=n, bcol=bcol: e.activation(
                                    out=SGA[:, 0:n], in_=PSUM[:, pg, 0:n], func=AF.Sigmoid, bias=bg[:, l, bcol:bcol + 1], scale=1.0),
                                    reads=pk(pg), writes=["SGA"])
                                pa = psb()
                                mm_group(pa, n, [(Wo_[:, k_, j * 128:(j + 1) * 128], Act[:, k_, o:o + n]) for k_ in range(4)],
                                         [(akey, k_) for k_ in range(4)] + [wok])
                                if first:
                                    tt('dve', MIX[:, m, o:o + n], PSUM[:, pa, 0:n], SGA[:, 0:n], ALU.mult, pk(pa) + ["SGA"], [("MIX", m)])
                                else:
                                    tt('dve', TM[:, 0:n], PSUM[:, pa, 0:n], SGA[:, 0:n], ALU.mult, pk(pa) + ["SGA"], ["TM"])
                                    tt('pool', MIX[:, m, o:o + n], MIX[:, m, o:o + n], TM[:, 0:n], ALU.add, ["TM", ("MIX", m)], [("MIX", m)])
                for h in range(2):
                    Wo_, wok = wq_m.get(8 + h)
                    for j in range(4):
                        m = h * 4 + j
                        for (o, n) in subs:
                            pi = psb()
                            mm_group(pi, n, [(Wo_[:, k_, j * 128:(j + 1) * 128], MIX[:, k_, o:o + n]) for k_ in range(8)],
                                     [("MIX", k_) for k_ in range(8)] + [wok])
                            xc = c0 + o
                            op('dve', lambda e, pi=pi, n=n, m=m, xc=xc: e.tensor_tensor(
                                out=X[:, m, xc:xc + n], in0=PSUM[:, pi, 0:n], in1=X[:, m, xc:xc + n], op=ALU.add),
                                reads=pk(pi) + ["X"], writes=["X"])
                if STAGE >= 8 and bi == 3:
                    SQn = t("SQn", [128, 4, 512]); RSn = t("RSn", [128, 512], F32); XGn = t("XGn", [128, 2, 512], F32)
                    for (o, n) in subs:
                        norm2_into(l, MIX, SQn, RSn, XGn, c0 + o, o, n)
                fence()

        def ffn_tail(l, XN2):
            with ExitStack() as ph:
                def t(name, shape, dt=BF16):
                    return ph.enter_context(nc.sbuf_tensor("b%d_" % l + name, list(shape), dt))
                Hh = [t("H%d" % i, [128, 4, 512]) for i in range(2)]
                Rr = [t("R%d" % i, [128, 512]) for i in range(2)]
                subs = [(0, 512), (512, 64)]
                wq_f = WQ(ffn_specs(l))
                it = 0
                for c in range(8):
                    Wu, wuk = wq_f.get(2 * c)
                    Wd, wdk = wq_f.get(2 * c + 1)
                    for (o, n) in subs:
                        hb = it % 2
                        it += 1
                        for j in range(4):
                            pi = psb()
                            mm_group(pi, n, [(Wu[:, k_, j * 128:(j + 1) * 128], XN2[:, k_, o:o + n]) for k_ in range(8)],
                                     [("MIX", k_) for k_ in range(8)] + [wuk])
                            rb = j % 2
                            op('act', lambda e, pi=pi, n=n, rb=rb: e.activation(out=Rr[rb][:, 0:n], in_=PSUM[:, pi, 0:n],
                                                                               func=AF.Relu), reads=pk(pi), writes=[("R", rb)])
                            op('act', lambda e, n=n, rb=rb, hb=hb, j=j: e.activation(out=Hh[hb][:, j, 0:n], in_=Rr[rb][:, 0:n],
                                                                                    func=AF.Square), reads=[("R", rb)], writes=[("H", hb, j)])
                        for m in range(8):
                            pi = psb()
                            mm_group(pi, n, [(Wd[:, j, m * 128:(m + 1) * 128], Hh[hb][:, j, 0:n]) for j in range(4)],
                                     [("H", hb, j) for j in range(4)] + [wdk])
                            xc = 1536 + o
                            op('dve', lambda e, pi=pi, n=n, m=m, xc=xc: e.tensor_tensor(
                                out=X[:, m, xc:xc + n], in0=PSUM[:, pi, 0:n], in1=X[:, m, xc:xc + n], op=ALU.add),
                                reads=pk(pi) + ["X"], writes=["X"])
                fence()

        for l in range(NLAYERS):
            if STAGE < 1:
                break
            if STAGE >= 2:
                ssm_tables(l, mid=(phase0 if l == 0 else None))
            elif l == 0:
                phase0()
            layer_phase_a(l)
        with ExitStack() as ph:
            YT = [ph.enter_context(nc.sbuf_tensor("yt%d" % i, [128, D], F32)) for i in range(2)]
            yts = [C.stream("yts%d" % i) for i in range(2)]
            for tt in range(17):
                b = tt % 2
                rows = 128 if tt < 16 else NS
                for half in range(2):
                    pi = psb()
                    for j in range(4):
                        k = half * 4 + j
                        op('pe', lambda e, k=k, j=j, pi=pi, rows=rows, tt=tt: e.transpose(
                            out=PSUM[0:rows, pi, j * 128:(j + 1) * 128], in_=X[:, k, tt * 128:tt * 128 + rows],
                            identity=ident[:, :]),
                            reads=[("X", tt)], writes=pk(pi), signal=(j == 3))
                    dst = YT[b][0:rows, half * 512:(half + 1) * 512]
                    src_ps = PSUM[0:rows, pi, :]
                    if half == 0:
                        op('act', lambda e, s_=src_ps, d_=dst: e.activation(out=d_, in_=s_, func=AF.Copy),
                           reads=pk(pi), writes=[("yt", b)])
                    else:
                        op('dve', lambda e, s_=src_ps, d_=dst: e.tensor_copy(out=d_, in_=s_),
                           reads=pk(pi), writes=[("yt", b)])
                dstd = yp[tt * 128:(tt + 1) * 128, :] if tt < 16 else ys[:, :]
                dma('sp', yts[b], dstd, YT[b][0:rows, :], reads=[("yt", b)])
            fence()
    return nc


_NC_CACHE = {}


def _consts():
    ident = np.eye(128, dtype=np.float32)
    blk = np.zeros((128, 128), np.float32)
    blk[:64, :64] = 1.0
    blk[64:, 64:] = 1.0
    slopes = np.exp2(-8.0 * np.arange(1, 9, dtype=np.float64) / 8.0)
    j = np.arange(128)[:, None]
    i = np.arange(128)[None, :]
    biasp = np.zeros((128, 8, 256), np.float32)
    for t in range(4):
        for hp in range(2):
            h = t + 4 * hp
            d_prev = 128 + i - j
            d_cur = i - j
            bp = np.where(d_prev <= 128, -slopes[h] * d_prev, -30000.0)
            bc = np.where(d_cur >= 0, -slopes[h] * d_cur, -30000.0)
            biasp[:, t * 2 + hp, 0:128] = bp
            biasp[:, t * 2 + hp, 128:256] = bc
    biasc = np.zeros((128, 32), np.float32)
    biasn = np.zeros((4, 32), np.float32)
    for kv in range(2):
        for hq in range(4):
            h = kv * 4 + hq
            for qi in range(4):
                col = kv * 16 + hq * 4 + qi
                jj = np.arange(128)
                dist = 128 + qi - jj
                biasc[:, col] = np.where(jj >= qi, -slopes[h] * dist, -30000.0)
                jn = np.arange(4)
                dn = qi - jn
                biasn[:, col] = np.where(dn >= 0, -slopes[h] * dn, -30000.0)
    biasnf = np.full((64, 512), -30000.0, np.float32)
    for ip in range(4):
        for slp in range(16):
            r = ip * 16 + slp
            for kv in range(2):
                for hq in range(4):
                    h = kv * 4 + hq
                    for qi in range(ip, 4):
                        biasnf[r, kv * 256 + hq * 64 + qi * 16 + slp] = -slopes[h] * (qi - ip)
    cmask = np.zeros((128, 128), np.float32)
    for r in range(128):
        glp = (r % 32) // 16
        cmask[r, glp * 64:(glp + 1) * 64] = 1.0
    rmask = np.zeros((128, 4), np.float32)
    for r in range(128):
        rmask[r, r // 32] = 1.0
    return dict(c_ident=ident, c_blk64=blk, c_biasp=biasp, c_biasc=biasc, c_biasn=biasn, c_biasnf=biasnf,
                c_cmask=cmask, c_rmask=rmask)


def kernel(**inp):
    f = lambda a: np.ascontiguousarray(np.asarray(a), dtype=np.float32)
    qperm = np.concatenate([np.arange(h * 64, (h + 1) * 64) for h in HEAD_PERM])
    w_in = f(inp['w_in']).copy()
    w_in[:, :, 0:512] = w_in[:, :, qperm]
    w_ao = f(inp['w_attn_o'])[:, qperm, :]
    qg = f(inp['q_norm_g'])
    kg = f(inp['k_norm_g'])
    qk_g = np.stack([np.concatenate([qg, qg], axis=1), np.concatenate([kg, kg], axis=1)], axis=1)
    sk = f(inp['attn_sinks'])
    sinks = np.zeros((L, 4, 128), np.float32)
    for t in range(4):
        sinks[:, t, 0:64] = sk[:, t][:, None]
        sinks[:, t, 64:128] = sk[:, t + 4][:, None]
    shared = dict(
        w_in=np.ascontiguousarray(w_in), w_glu=f(inp['w_glu']), w_ao=np.ascontiguousarray(w_ao), w_so=f(inp['w_ssm_o']),
        w_out=f(inp['w_out']), w_up=f(inp['w_up']), w_down=f(inp['w_down']),
        norm1_g=f(inp['norm1_g']), norm2_g=f(inp['norm2_g']), b_gate=f(inp['b_gate']),
        qk_g=np.ascontiguousarray(qk_g), sinks=sinks,
        lam_re=f(inp['lam_re']), lam_im=f(inp['lam_im']), log_step=f(inp['log_step']),
        b_re=f(inp['b_re']).reshape(L, 2048, 16), b_im=f(inp['b_im']).reshape(L, 2048, 16),
        c_re=f(inp['c_re']).reshape(L, 512, 64), c_im=f(inp['c_im']).reshape(L, 512, 64),
        d_skip=f(inp['d_skip']), b_glu=f(inp['b_glu']),
    )
    pa = np.zeros((92, 128), np.float32)
    pa[0:16] = shared['norm1_g'].reshape(L * 8, 128)
    pa[16:32] = shared['norm2_g'].reshape(L * 8, 128)
    pa[32:64] = shared['b_gate'].reshape(L * 16, 128)
    pa[64:68] = shared['qk_g'].reshape(L * 2, 128)
    pa[68:76] = shared['sinks'].reshape(L * 4, 128)
    pa[76:84] = shared['d_skip'].reshape(L * 4, 128)
    pa[84:92] = shared['b_glu'].reshape(L * 4, 128)
    pb = np.zeros((L, 48, 128), np.float32)
    pb[:, 0:16] = shared['lam_re'].reshape(L, 16, 128)
    pb[:, 16:32] = shared['lam_im'].reshape(L, 16, 128)
    pb[:, 32:48] = np.repeat(shared['log_step'].reshape(L, 16, 2, 1), 64, axis=3).reshape(L, 16, 128)
    shared['pvec_a'] = pa
    shared['pvec_b'] = pb
    shared.update(_consts())
    x_prompt = f(inp['x_prompt'])
    x_sample = f(inp['x_sample'])
    ck = f(inp['cache_k']).reshape(L, 128, 128, 128)
    cv = f(inp['cache_v']).reshape(L, 128, 128, 128)
    sre = f(inp['state_ssm_re']).reshape(L, 128, 2048)
    sim = f(inp['state_ssm_im']).reshape(L, 128, 2048)
    in_maps = []
    for c in range(NCORES):
        m = dict(shared)
        m['xp'] = x_prompt[c]
        m['xs'] = np.ascontiguousarray(x_sample[c * 16:(c + 1) * 16].transpose(1, 0, 2).reshape(NS, D))
        m['cache_k'] = np.ascontiguousarray(ck[:, c * 16:(c + 1) * 16])
        m['cache_v'] = np.ascontiguousarray(cv[:, c * 16:(c + 1) * 16])
        m['st_re'] = np.ascontiguousarray(sre[:, c * 16:(c + 1) * 16])
        m['st_im'] = np.ascontiguousarray(sim[:, c * 16:(c + 1) * 16])
        in_maps.append(m)
    if 'nc' not in _NC_CACHE:
        _NC_CACHE['nc'] = build_program()
    ncr = DEBUG_CORES or NCORES
    res = run_bass_kernel_spmd(_NC_CACHE['nc'], in_maps[:ncr], core_ids=list(range(ncr)))
    R = list(res.results)
    _NC_CACHE['last'] = R
    while len(R) < NCORES:
        R.append(R[0])
    y_prompt = np.stack([R[c]['yp'] for c in range(NCORES)]).astype(np.float32)
    y_sample = np.concatenate([R[c]['ys'].reshape(4, 16, D).transpose(1, 0, 2) for c in range(NCORES)]).astype(np.float32)
    k_prompt = np.stack([R[c]['kp'] for c in range(NCORES)], axis=1).reshape(L, 8, 128, 2, 64)
    v_prompt = np.stack([R[c]['vp'] for c in range(NCORES)], axis=1).reshape(L, 8, 128, 2, 64)
    hr_p = np.stack([R[c]['hrp'] for c in range(NCORES)], axis=1).reshape(L, 8, 32, 64)
    hi_p = np.stack([R[c]['hip'] for c in range(NCORES)], axis=1).reshape(L, 8, 32, 64)
    k_s = np.concatenate([R[c]['ksam'] for c in range(NCORES)], axis=1).reshape(L, 128, 128, 2, 64)
    v_s = np.concatenate([R[c]['vsam'] for c in range(NCORES)], axis=1).reshape(L, 128, 128, 2, 64)
    hr_s = np.concatenate([R[c]['hrs'] for c in range(NCORES)], axis=1).reshape(L, 128, 32, 64)
    hi_s = np.concatenate([R[c]['his'] for c in range(NCORES)], axis=1).reshape(L, 128, 32, 64)
    return (y_prompt, y_sample, k_prompt.astype(np.float32), v_prompt.astype(np.float32),
            hr_p.astype(np.float32), hi_p.astype(np.float32), k_s.astype(np.float32), v_s.astype(np.float32),
            hr_s.astype(np.float32), hi_s.astype(np.float32))
```

```python
import math
import numpy as np
import ml_dtypes
from contextlib import ExitStack
import concourse.bass as bass
import concourse.mybir as mybir
from concourse.bass_utils import run_bass_kernel_spmd

F32 = mybir.dt.float32
BF16 = mybir.dt.bfloat16
I32 = mybir.dt.int32
AF = mybir.ActivationFunctionType
ALU = mybir.AluOpType

NCORES = 8
D = 1024
T = 2048
NS = 64
NT = T + NS
L = 2
EPS = 1e-6
HEAD_PERM = [0, 4, 1, 5, 2, 6, 3, 7]
STAGE = 99
NLAYERS = 2
CHUNK = 256
NCHUNK = 512 // CHUNK
DEBUG_DUMP = False
DEBUG_CORES = 0
SKIP = set()


class Stream:
    def __init__(self, nc, stack, name):
        self.sem = stack.enter_context(nc.semaphore(name))
        self.name = name
        self.cnt = 0


class Ctx:
    def __init__(self, nc, stack):
        self.nc = nc
        self.engs = {'pe': nc.tensor, 'act': nc.scalar, 'dve': nc.vector, 'pool': nc.gpsimd, 'sp': nc.sync}
        self.st = {n: Stream(nc, stack, 's_' + n) for n in self.engs}
        self.waited = {n: {} for n in self.engs}
        self.lw = {}
        self.rd = {}
        self.stack = stack
        self.nstream = 0

    def stream(self, name):
        s = Stream(self.nc, self.stack, name)
        self.st[name] = s
        return name

    def _deps(self, reads, writes):
        deps = {}

        def add(tok):
            if tok is None:
                return
            s, v = tok
            if deps.get(s, 0) < v:
                deps[s] = v
        for k in reads:
            add(self.lw.get(k))
        for k in writes:
            add(self.lw.get(k))
            for s, v in self.rd.get(k, {}).items():
                add((s, v))
        return deps

    def _wait(self, en, deps):
        e = self.engs[en]
        w = self.waited[en]
        for s, v in deps.items():
            if s == en and en in ('pe', 'sp'):
                continue
            if w.get(s, 0) >= v:
                continue
            e.wait_ge(self.st[s].sem, v)
            w[s] = v

    def _record(self, tok, reads, writes):
        s, v = tok
        for k in reads:
            d = self.rd.setdefault(k, {})
            if d.get(s, 0) < v:
                d[s] = v
        for k in writes:
            self.lw[k] = tok
            self.rd[k] = {}

    def op(self, en, fn, reads=(), writes=(), signal=True):
        psr = [k for k in reads if isinstance(k, tuple) and k[0] == 'ps']
        if psr:
            writes = list(writes) + psr
        self._wait(en, self._deps(reads, writes))
        inst = fn(self.engs[en])
        st = self.st[en]
        if signal:
            st.cnt += 1
            inst.then_inc(st.sem, 1)
            tok = (en, st.cnt)
        else:
            tok = (en, st.cnt + 1)
        self._record(tok, reads, writes)
        return tok

    def dma(self, q, stream, out, in_, reads=(), writes=(), **kw):
        self._wait(q, self._deps(reads, writes))
        st = self.st[stream]
        st.cnt += 16
        self.engs[q].dma_start(out=out, in_=in_, **kw).then_inc(st.sem, 16)
        tok = (stream, st.cnt)
        self._record(tok, reads, writes)
        return tok

    def fence(self):
        for en, e in self.engs.items():
            w = self.waited[en]
            for s, st in self.st.items():
                if s == en or st.cnt == 0:
                    continue
                if w.get(s, 0) >= st.cnt:
                    continue
                e.wait_ge(st.sem, st.cnt)
                w[s] = st.cnt
        self.lw = {}
        self.rd = {}


def build_program():
    nc = bass.Bass("TRN2", target_bir_lowering=False)

    def din(name, shape, dt=F32):
        return nc.dram_tensor(name, list(shape), dt, kind="ExternalInput").ap()

    def dout(name, shape, dt=F32):
        return nc.dram_tensor(name, list(shape), dt, kind="ExternalOutput").ap()

    xp = din("xp", [T, D])
    xs = din("xs", [NS, D])
    cache_k = din("cache_k", [L, 16, 128, 128])
    cache_v = din("cache_v", [L, 16, 128, 128])
    st_re = din("st_re", [L, 16, 2048])
    st_im = din("st_im", [L, 16, 2048])
    w_in = din("w_in", [L, D, 3328])
    w_glu = din("w_glu", [L, 512, 512])
    w_ao = din("w_ao", [L, 512, D])
    w_so = din("w_so", [L, 512, D])
    w_out = din("w_out", [L, D, D])
    w_up = din("w_up", [L, D, 4096])
    w_down = din("w_down", [L, 4096, D])
    pvec_a = din("pvec_a", [92, 128])
    pvec_b = din("pvec_b", [L, 48, 128])
    norm1_g = din("norm1_g", [L, D])
    norm2_g = din("norm2_g", [L, D])
    b_gate = din("b_gate", [L, 2048])
    qk_g = din("qk_g", [L, 2, 128])
    sinks = din("sinks", [L, 4, 128])
    lam_re = din("lam_re", [L, 32, 64])
    lam_im = din("lam_im", [L, 32, 64])
    log_step = din("log_step", [L, 32])
    b_re = din("b_re", [L, 2048, 16])
    b_im = din("b_im", [L, 2048, 16])
    c_re = din("c_re", [L, 512, 64])
    c_im = din("c_im", [L, 512, 64])
    d_skip = din("d_skip", [L, 512])
    b_glu = din("b_glu", [L, 512])
    c_ident = din("c_ident", [128, 128])
    c_blk64 = din("c_blk64", [128, 128])
    c_biasp = din("c_biasp", [128, 8, 256])
    c_biasc = din("c_biasc", [128, 32])
    c_biasn = din("c_biasn", [4, 32])
    c_biasnf = din("c_biasnf", [64, 512])
    c_cmask = din("c_cmask", [128, 128])
    c_rmask = din("c_rmask", [128, 4])

    yp = dout("yp", [T, D])
    ys = dout("ys", [NS, D])
    kp = dout("kp", [L, 128, 128])
    vp = dout("vp", [L, 128, 128])
    hrp = dout("hrp", [L, 16, 128])
    hip = dout("hip", [L, 16, 128])
    ksam = dout("ksam", [L, 16, 128, 128])
    vsam = dout("vsam", [L, 16, 128, 128])
    hrs = dout("hrs", [L, 16, 16, 128])
    his = dout("his", [L, 16, 16, 128])
    if DEBUG_DUMP:
        dbg_at = dout("dbg_at", [4, 128, 4, 576], BF16)
        dbg_st = dout("dbg_st", [4, 128, 4, 576], BF16)
        dbg_yg = dout("dbg_yg", [4, 128, 4, 576], BF16)
        dbg_mix = dout("dbg_mix", [4, 128, 8, 576], BF16)

    stack = ExitStack()
    with stack:
        C = Ctx(nc, stack)
        op, dma, fence = C.op, C.dma, C.fence

        def sb(name, shape, dt=F32):
            return stack.enter_context(nc.sbuf_tensor(name, list(shape), dt))

        X = sb("X", [128, 8, NT])
        PSUM = stack.enter_context(nc.psum_tensor("PS", [128, 8, 512], F32))
        ident = sb("ident", [128, 128])
        ones = sb("ones", [128, 128], BF16)
        blk64 = sb("blk64", [128, 128], BF16)
        EB = sb("EB", [128, 8, 256], BF16)
        biasc = sb("biasc", [128, 32])
        biasnf = sb("biasnf", [64, 512])
        cmask = sb("cmask", [128, 128])
        rmask = sb("rmask", [128, 4])
        PVA = sb("PVA", [128, 92])
        g1 = PVA[:, 0:16].rearrange("p (l k) -> p l k", l=L)
        g2 = PVA[:, 16:32].rearrange("p (l k) -> p l k", l=L)
        bg = PVA[:, 32:64].rearrange("p (l k) -> p l k", l=L)
        qkg = PVA[:, 64:68].rearrange("p (l k) -> p l k", l=L)
        esink = PVA[:, 68:76].rearrange("p (l k) -> p l k", l=L)
        dsk = PVA[:, 76:84].rearrange("p (l k) -> p l k", l=L)
        bgl = PVA[:, 84:92].rearrange("p (l k) -> p l k", l=L)
        WR = [sb("wr%d" % i, [128, 4096], BF16) for i in range(3)]
        wr_stream = [C.stream("wrs%d" % i) for i in range(3)]
        wr_i = [0]
        PHc = sb("PHc", [128, 16, CHUNK + 1], BF16)
        PHs = sb("PHs", [128, 16, CHUNK + 1], BF16)
        MAG = sb("MAG", [128, 16])
        AR = sb("AR", [128, 16])
        AI = sb("AI", [128, 16])
        R128c = sb("R128c", [128, 16])
        R128s = sb("R128s", [128, 16])
        nR128s = sb("nR128s", [128, 16])
        P127c = sb("P127c", [128, 16])
        P127s = sb("P127s", [128, 16])
        BTr = sb("BTr", [128, 16, 128], BF16)
        BTi = sb("BTi", [128, 16, 128], BF16)
        CTr = sb("CTr", [128, 16, 128], BF16)
        nCTi = sb("nCTi", [128, 16, 128], BF16)
        nCTr = sb("nCTr", [128, 16, 128], BF16)
        GE2 = sb("GE2", [128, 16, 2])
        SS = sb("SS", [128, 16, 2])

        s_par = C.stream("par")
        s_out = C.stream("outs")
        ps_i = [0]

        def psb(n=1):
            i = ps_i[0]
            if i + n > 8:
                i = 0
            ps_i[0] = (i + n) % 8
            return i

        def pk(i, n=1):
            return [("ps", j) for j in range(i, i + n)]

        class Ring:
            def __init__(self, bufs, streams, keys):
                self.bufs, self.streams, self.keys = bufs, streams, keys
                self.i = 0

        G3 = Ring(WR, wr_stream, [("wr", i) for i in range(3)])

        def wload(src_ap, view_shape, key, ring=None):
            ring = ring or G3
            i = ring.i % len(ring.bufs)
            ring.i += 1
            buf = ring.bufs[i]
            n = 1
            for s_ in view_shape[1:]:
                n *= s_
            flat = buf[:, 0:n]
            if len(view_shape) == 3:
                view = flat.rearrange("p (k n) -> p k n", k=view_shape[1])
            else:
                view = flat
            dma('pool', ring.streams[i], view, src_ap, writes=[ring.keys[i]])
            return view, ring.keys[i]

        class WQ:
            def __init__(self, specs, ring=None, ahead=1):
                self.specs = specs
                self.loaded = []
                self.ring = ring
                self.ahead = ahead

            def get(self, i):
                upto = min(i + self.ahead, len(self.specs) - 1)
                while len(self.loaded) <= upto:
                    src, shape = self.specs[len(self.loaded)]
                    self.loaded.append(wload(src, shape, None, self.ring))
                return self.loaded[i]

        def bc_mid(ap2, n):
            return ap2.unsqueeze(1).to_broadcast([ap2.shape[0], n, ap2.shape[1]])

        def bc_last(ap2, n):
            return ap2.unsqueeze(2).to_broadcast([ap2.shape[0], ap2.shape[1], n])

        with nc.allow_non_contiguous_dma(reason="small param loads"):
            for (dst, src) in [
                (ident[:], c_ident[:, :]),
                (biasc[:], c_biasc[:, :]), (biasnf[:], c_biasnf[:, :]), (cmask[:], c_cmask[:, :]),
                (rmask[:], c_rmask[:, :]),
            ]:
                dma('sp', s_par, dst, src, writes=["par"])
        with ExitStack() as phA:
            PST = phA.enter_context(nc.sbuf_tensor("pst", [92, 128], F32))
            s_pa = C.stream("pva")
            dma('sp', s_pa, PST[:, :], pvec_a[:, :], writes=["PST"])
            op('pe', lambda e: e.transpose(out=PSUM[:, 0, 0:92], in_=PST[0:92, :], identity=ident[0:92, 0:92]),
               reads=["PST", "par"], writes=pk(0))
            op('act', lambda e: e.activation(out=PVA[:, :], in_=PSUM[:, 0, 0:92], func=AF.Copy), reads=pk(0), writes=["par"])
            fence()
        s_b64 = C.stream("b64")
        dma('pool', s_b64, blk64[:], c_blk64[:, :], writes=["blk64"])
        op('dve', lambda e: e.memset(ones[:], 1.0), writes=["ones"])
        with ExitStack() as ph0:
            biasp = ph0.enter_context(nc.sbuf_tensor("biasp", [128, 8, 256], F32))
            s_bp = C.stream("bp")
            dma('sp', s_bp, biasp[:], c_biasp[:, :, :], writes=["biasp"])
            op('act', lambda e: e.activation(out=EB[:], in_=biasp[:], func=AF.Exp), reads=["biasp"], writes=["EB"])
            fence()
        op('act', lambda e: e.activation(out=esink, in_=esink, func=AF.Exp), reads=["par"], writes=["esink"])
        op('dve', lambda e: e.tensor_scalar(out=qkg[:, :, 0:1], in0=qkg[:, :, 0:1], scalar1=0.125, scalar2=None,
                                            op0=ALU.mult), reads=["par"], writes=["qkg"])
        fence()

        def phase0():
          with ExitStack() as ph:
              XT = [ph.enter_context(nc.sbuf_tensor("xt%d" % i, [128, D], F32)) for i in range(2)]
              xts = [C.stream("xts%d" % i) for i in range(2)]
              for tt in range(17):
                  b = tt % 2
                  rows = 128 if tt < 16 else NS
                  src = xp[tt * 128:(tt + 1) * 128, :] if tt < 16 else xs[:, :]
                  dma('sp', xts[b], XT[b][0:rows, :], src, writes=[("xt", b)])
                  for half in range(2):
                      pi = psb()
                      for j in range(4):
                          k = half * 4 + j
                          op('pe', lambda e, k=k, j=j, pi=pi, b=b, rows=rows: e.transpose(
                              out=PSUM[:, pi, j * 128:j * 128 + rows], in_=XT[b][0:rows, k * 128:(k + 1) * 128],
                              identity=ident[0:rows, 0:rows]),
                              reads=[("xt", b)], writes=pk(pi), signal=(j == 3))
                      eng = 'act'
                      src_ps = PSUM[:, pi, :].rearrange("p (j t) -> p j t", j=4)[:, :, 0:rows]
                      dst = X[:, half * 4:half * 4 + 4, tt * 128:tt * 128 + rows]
                      if eng == 'act':
                          op('act', lambda e, s_=src_ps, d_=dst: e.activation(out=d_, in_=s_, func=AF.Copy),
                             reads=pk(pi), writes=[("X", tt)])
                      else:
                          op('dve', lambda e, s_=src_ps, d_=dst: e.tensor_copy(out=d_, in_=s_),
                             reads=pk(pi), writes=[("X", tt)])
              fence()

        TWO_PI = 2.0 * math.pi

        def tt(en, out, a, b, o, reads, writes):
            return op(en, lambda e: e.tensor_tensor(out=out, in0=a, in1=b, op=o), reads=reads, writes=writes)

        def ssm_tables(l, mid=None):
            with ExitStack() as ph:
                def t(name, shape, dt=F32):
                    return ph.enter_context(nc.sbuf_tensor("st%d_" % l + name, list(shape), dt))
                LR = t("LR", [128, 16]); LI = t("LI", [128, 16]); LS = t("LS", [128, 16])
                ANG = t("ANG", [128, 32]); KI = t("KI", [128, 32], I32); KF = t("KF", [128, 32])
                M1 = t("M1", [128, 32]); SC = t("SC", [128, 32])
                T1 = t("T1", [128, 16]); T2 = t("T2", [128, 16]); T3 = t("T3", [128, 16])
                CR = t("CR", [128, 16]); CI = t("CI", [128, 16]); RDEN = t("RDEN", [128, 16])
                Pc = t("Pc", [128, 16, CHUNK + 1]); Ps = t("Ps", [128, 16, CHUNK + 1])
                Q1 = t("Q1", [128, 16, CHUNK // 2]); Q2 = t("Q2", [128, 16, CHUNK // 2])
                BNr = t("BNr", [128, 16, 16]); BNi = t("BNi", [128, 16, 16])
                BBr = t("BBr", [128, 16, 16]); BBi = t("BBi", [128, 16, 16]); BT1 = t("BT1", [128, 16, 16])
                BEr = t("BEr", [128, 16, 32]); BEi = t("BEi", [128, 16, 32])
                CNr = t("CNr", [128, 4, 64]); CNi = t("CNi", [128, 4, 64])
                CE = t("CE", [128, 128])
                sp_ = C.stream("sst%d" % l)
                with nc.allow_non_contiguous_dma(reason="ssm params"):
                    PBT = t("PBT", [48, 128])
                    dma('sp', sp_, PBT[:, :], pvec_b[l], writes=["PBT"])
                    op('pe', lambda e: e.transpose(out=PSUM[:, 0, 0:48], in_=PBT[0:48, :], identity=ident[0:48, 0:48]),
                       reads=["PBT"], writes=pk(0))
                    op('act', lambda e: e.activation(out=LR[:], in_=PSUM[:, 0, 0:16], func=AF.Copy), reads=pk(0), writes=["sp"])
                    op('act', lambda e: e.activation(out=LI[:], in_=PSUM[:, 0, 16:32], func=AF.Copy), reads=pk(0), writes=["sp"])
                    op('act', lambda e: e.activation(out=LS[:], in_=PSUM[:, 0, 32:48], func=AF.Copy), reads=pk(0), writes=["sp"])
                    dma('sp', sp_, BNr[:], b_re[l].rearrange("(s gl p) c -> (gl p) s c", gl=2, p=64), writes=["sp"])
                    dma('sp', sp_, BNi[:], b_im[l].rearrange("(s gl p) c -> (gl p) s c", gl=2, p=64), writes=["sp"])
                    dma('sp', sp_, CNr[:], c_re[l].rearrange("(q r) p -> r q p", r=128), writes=["sp"])
                    dma('sp', sp_, CNi[:], c_im[l].rearrange("(q r) p -> r q p", r=128), writes=["sp"])
                R = ["sp"]
                op('act', lambda e: e.activation(out=LS[:], in_=LS[:], func=AF.Exp), reads=R, writes=["LS"])
                tt('dve', T1[:], LR[:], LS[:], ALU.mult, R + ["LS"], ["T1"])
                tt('dve', ANG[:, 0:16], LI[:], LS[:], ALU.mult, R + ["LS"], ["ANG"])
                op('act', lambda e: e.activation(out=MAG[:], in_=T1[:], func=AF.Exp), reads=["T1"], writes=["MAG"])
                op('dve', lambda e: e.tensor_scalar(out=ANG[:, 16:32], in0=ANG[:, 0:16], scalar1=math.pi / 2, scalar2=None,
                                                    op0=ALU.add), reads=["ANG"], writes=["ANG"])
                op('dve', lambda e: e.tensor_scalar(out=KF[:], in0=ANG[:], scalar1=1.0 / TWO_PI, scalar2=None, op0=ALU.mult),
                   reads=["ANG"], writes=["KF"])
                op('dve', lambda e: e.tensor_copy(out=KI[:], in_=KF[:]), reads=["KF"], writes=["KI"])
                op('dve', lambda e: e.tensor_copy(out=KF[:], in_=KI[:]), reads=["KI"], writes=["KF"])
                op('dve', lambda e: e.scalar_tensor_tensor(out=ANG[:], in0=KF[:], scalar=-TWO_PI, in1=ANG[:], op0=ALU.mult,
                                                           op1=ALU.add), reads=["KF", "ANG"], writes=["ANG"])
                op('dve', lambda e: e.tensor_single_scalar(out=M1[:], in_=ANG[:], scalar=math.pi, op=ALU.is_gt),
                   reads=["ANG"], writes=["M1"])
                op('dve', lambda e: e.scalar_tensor_tensor(out=ANG[:], in0=M1[:], scalar=-TWO_PI, in1=ANG[:], op0=ALU.mult,
                                                           op1=ALU.add), reads=["M1", "ANG"], writes=["ANG"])
                op('dve', lambda e: e.tensor_single_scalar(out=M1[:], in_=ANG[:], scalar=-math.pi, op=ALU.is_lt),
                   reads=["ANG"], writes=["M1"])
                op('dve', lambda e: e.scalar_tensor_tensor(out=ANG[:], in0=M1[:], scalar=TWO_PI, in1=ANG[:], op0=ALU.mult,
                                                           op1=ALU.add), reads=["M1", "ANG"], writes=["ANG"])
                op('act', lambda e: e.activation(out=SC[:], in_=ANG[:], func=AF.Sin), reads=["ANG"], writes=["SC"])
                SN = SC[:, 0:16]
                CS = SC[:, 16:32]
                tt('dve', AR[:], MAG[:], CS, ALU.mult, ["MAG", "SC"], ["AR"])
                tt('dve', AI[:], MAG[:], SN, ALU.mult, ["MAG", "SC"], ["AI"])
                tt('dve', T1[:], LR[:], LR[:], ALU.mult, R + ["MAG"], ["T1"])
                tt('dve', T2[:], LI[:], LI[:], ALU.mult, R, ["T2"])
                tt('dve', T1[:], T1[:], T2[:], ALU.add, ["T1", "T2"], ["T1"])
                op('dve', lambda e: e.reciprocal(out=RDEN[:], in_=T1[:]), reads=["T1"], writes=["RDEN"])
                op('dve', lambda e: e.tensor_scalar(out=T3[:], in0=AR[:], scalar1=-1.0, scalar2=None, op0=ALU.add),
                   reads=["AR"], writes=["T3"])
                tt('dve', T1[:], T3[:], LR[:], ALU.mult, ["T3", "RDEN"], ["T1"])
                tt('dve', T2[:], AI[:], LI[:], ALU.mult, ["AI", "T1"], ["T2"])
                tt('dve', T1[:], T1[:], T2[:], ALU.add, ["T1", "T2"], ["T1"])
                tt('dve', CR[:], T1[:], RDEN[:], ALU.mult, ["T1", "RDEN"], ["CR"])
                tt('dve', T1[:], AI[:], LR[:], ALU.mult, ["CR", "AI"], ["T1"])
                tt('dve', T2[:], T3[:], LI[:], ALU.mult, ["T3", "CR"], ["T2"])
                tt('dve', T1[:], T1[:], T2[:], ALU.subtract, ["T1", "T2"], ["T1"])
                tt('dve', CI[:], T1[:], RDEN[:], ALU.mult, ["T1", "RDEN"], ["CI"])
                op('dve', lambda e: e.memset(Pc[:, :, 0:1], 1.0), writes=["P"])
                op('dve', lambda e: e.memset(Ps[:, :, 0:1], 0.0), writes=["P"])
                op('dve', lambda e: e.tensor_copy(out=Pc[:, :, 1:2], in_=CS.unsqueeze(2)), reads=["SC", "P"], writes=["P"])
                op('dve', lambda e: e.tensor_copy(out=Ps[:, :, 1:2], in_=SN.unsqueeze(2)), reads=["SC", "P"], writes=["P"])
                m = 1
                while m < CHUNK:
                    Ac = Pc[:, :, 1:m + 1]
                    As = Ps[:, :, 1:m + 1]
                    Bc = Pc[:, :, m:m + 1].to_broadcast([128, 16, m])
                    Bs = Ps[:, :, m:m + 1].to_broadcast([128, 16, m])
                    q1 = Q1[:, :, 0:m]
                    q2 = Q2[:, :, 0:m]
                    tt('dve', q1, Ac, Bc, ALU.mult, ["P"], ["Q1"])
                    tt('dve', q2, As, Bs, ALU.mult, ["P"], ["Q2"])
                    tt('dve', Pc[:, :, m + 1:2 * m + 1], q1, q2, ALU.subtract, ["Q1", "Q2", "P"], ["P"])
                    tt('dve', q1, Ac, Bs, ALU.mult, ["P"], ["Q1"])
                    tt('dve', q2, As, Bc, ALU.mult, ["P"], ["Q2"])
                    tt('dve', Ps[:, :, m + 1:2 * m + 1], q1, q2, ALU.add, ["Q1", "Q2", "P"], ["P"])
                    m *= 2
                op('dve', lambda e: e.tensor_copy(out=PHc[:], in_=Pc[:]), reads=["P"], writes=["PH"])
                op('dve', lambda e: e.tensor_copy(out=PHs[:], in_=Ps[:]), reads=["P"], writes=["PH"])
                op('dve', lambda e: e.tensor_copy(out=R128c[:], in_=Pc[:, :, CHUNK]), reads=["P"], writes=["R128"])
                op('dve', lambda e: e.tensor_copy(out=R128s[:], in_=Ps[:, :, CHUNK]), reads=["P"], writes=["R128"])
                op('dve', lambda e: e.tensor_scalar(out=nR128s[:], in0=Ps[:, :, CHUNK], scalar1=-1.0, scalar2=None,
                                                    op0=ALU.mult), reads=["P"], writes=["R128"])
                op('dve', lambda e: e.tensor_copy(out=P127c[:], in_=Pc[:, :, CHUNK - 1]), reads=["P"], writes=["R128"])
                op('dve', lambda e: e.tensor_copy(out=P127s[:], in_=Ps[:, :, CHUNK - 1]), reads=["P"], writes=["R128"])
                CRb = bc_last(CR[:], 16)
                CIb = bc_last(CI[:], 16)
                tt('dve', BBr[:], BNr[:], CRb, ALU.mult, R + ["CR"], ["BBr"])
                tt('dve', BT1[:], BNi[:], CIb, ALU.mult, R + ["CI"], ["BT1"])
                tt('dve', BBr[:], BBr[:], BT1[:], ALU.subtract, ["BBr", "BT1"], ["BBr"])
                tt('dve', BBi[:], BNi[:], CRb, ALU.mult, R + ["CR"], ["BBi"])
                tt('dve', BT1[:], BNr[:], CIb, ALU.mult, R + ["CI", "BBr"], ["BT1"])
                tt('dve', BBi[:], BBi[:], BT1[:], ALU.add, ["BBi", "BT1"], ["BBi"])
                if mid is not None:
                    mid()
                for (BE, BB, BTd, nm) in ((BEr, BBr, BTr, "r"), (BEi, BBi, BTi, "i")):
                    op('dve', lambda e, BE=BE: e.memset(BE[:], 0.0), writes=["BE" + nm])
                    op('dve', lambda e, BE=BE, BB=BB: e.tensor_copy(out=BE[0:64, :, 0:16], in_=BB[0:64, :, :]),
                       reads=["BB" + nm, "BE" + nm], writes=["BE" + nm])
                    op('dve', lambda e, BE=BE, BB=BB: e.tensor_copy(out=BE[64:128, :, 16:32], in_=BB[64:128, :, :]),
                       reads=["BB" + nm, "BE" + nm], writes=["BE" + nm])
                    for q in range(4):
                        pi = psb()
                        op('pe', lambda e, BE=BE, q=q, pi=pi: e.transpose(
                            out=PSUM[:, pi, 0:128], in_=BE[:, 4 * q:4 * q + 4, :].rearrange("p a b -> p (a b)"),
                            identity=ident[:, :]), reads=["BE" + nm], writes=pk(pi))
                        for slot in range(4):
                            op('dve', lambda e, BTd=BTd, q=q, slot=slot, pi=pi: e.tensor_scalar(
                                out=BTd[:, 4 * q + slot, :], in0=PSUM[:, pi, 0:128], scalar1=rmask[:, slot:slot + 1],
                                scalar2=None, op0=ALU.mult), reads=pk(pi), writes=["BT" + nm])
                for (CN, CTd, sgn, nm) in ((CNr, CTr, 1.0, "r"), (CNi, nCTi, -1.0, "i"), (CNr, nCTr, -1.0, "r2")):
                    op('dve', lambda e, CTd=CTd: e.memset(CTd[:], 0.0), writes=["CT" + nm])
                    for q in range(4):
                        tt('dve', CE[:, 0:64], CN[:, q, :], cmask[:, 0:64], ALU.mult, R, ["CE"])
                        tt('dve', CE[:, 64:128], CN[:, q, :], cmask[:, 64:128], ALU.mult, R, ["CE"])
                        pi = psb()
                        op('pe', lambda e, pi=pi: e.transpose(out=PSUM[:, pi, 0:128], in_=CE[:, :], identity=ident[:, :]),
                           reads=["CE"], writes=pk(pi))
                        for slot in range(4):
                            op('dve', lambda e, CTd=CTd, q=q, slot=slot, pi=pi, sgn=sgn: e.tensor_scalar(
                                out=CTd[:, 4 * q + slot, 32 * slot:32 * slot + 32], in0=PSUM[:, pi, 32 * slot:32 * slot + 32],
                                scalar1=sgn, scalar2=None, op0=ALU.mult), reads=pk(pi), writes=["CT" + nm])
                op('dve', lambda e: e.memset(GE2[:], 0.0), writes=["GE"])
                op('dve', lambda e: e.tensor_copy(out=SS[:, :, 0], in_=nR128s[:, :]), reads=["R128"], writes=["SS"])
                op('dve', lambda e: e.tensor_copy(out=SS[:, :, 1], in_=R128s[:, :]), reads=["R128"], writes=["SS"])
                fence()

        def mm_group(pi, n, pairs, reads, extra_writes=()):
            last = len(pairs) - 1
            tok = None
            for i, (lh, rh) in enumerate(pairs):
                tok = op('pe', lambda e, lh=lh, rh=rh, i=i: e.matmul(PSUM[:, pi, 0:n], lhsT=lh, rhs=rh, start=(i == 0),
                                                                     stop=(i == last)),
                         reads=reads, writes=pk(pi) + list(extra_writes), signal=(i == last))
            return tok

        def layer_phase_a(l):
            with ExitStack() as ph:
                def t(name, shape, dt=BF16):
                    return ph.enter_context(nc.sbuf_tensor("a%d_" % l + name, list(shape), dt))
                XN = t("XN", [128, 8, 576])
                MIX = t("MIX", [128, 8, 576])
                R2 = t("R2", [128, 4, 576])
                KT = t("KT", [128, 128 + 576])
                VT = t("VT", [128, 5, 128])
                VS = t("VS", [64, 128])
                U = t("U", [128, 4, 576])
                AT = t("AT", [128, 4, 576])
                STt = t("ST", [128, 4, 576])
                YG = t("YG", [128, 4, 576])
                s_o = C.stream("ao%d" % l)
                s_o1 = C.stream("ao1_%d" % l)
                s_o2 = C.stream("ao2_%d" % l)

                for bi in range(4):
                  c0 = bi * 512
                  subs = [(0, 512)] + ([(512, 64)] if bi == 3 else [])
                  NB = 576 if bi == 3 else 512
                  with ExitStack() as ph2:
                    def t2(name, shape, dt=BF16):
                        return ph2.enter_context(nc.sbuf_tensor("a%d_%d_" % (l, bi) + name, list(shape), dt))
                    SQ = t2("SQ", [128, 8, 512])
                    RS = t2("RS", [128, 512], F32)
                    XG = t2("XG", [128, 2, 512], F32)
                    RSq = t2("RSq", [128, 512], F32)
                    SQh = t2("SQh", [128, 512])
                    KF = t2("KF", [128, 128], F32)
                    KSF = t2("KSF", [128, 64], F32)
                    OUTS = t2("OUTS", [128, 128], F32)
                    OUTS2 = t2("OUTS2", [128, 128], F32)
                    wq_a2 = WQ([(w_in[l][:, ch_ * 512:ch_ * 512 + (512 if ch_ < 2 else 256)].rearrange("(k p) n -> p k n", p=128),
                                 [128, 8, (512 if ch_ < 2 else 256)]) for ch_ in range(3)])
                    wq_a2.get(0)
                    for (o, n) in subs:
                        xs_ = X[:, :, c0 + o:c0 + o + n]
                        op('act', lambda e, xs_=xs_, n=n: e.activation(out=SQ[:, :, 0:n], in_=xs_, func=AF.Square),
                           reads=["X"], writes=["SQ"])
                        if 'a1a' in SKIP:
                            continue
                        pi = psb()
                        mm_group(pi, n, [(ones[:, :], SQ[:, k, 0:n]) for k in range(8)], ["SQ", "ones"])
                        op('act', lambda e, pi=pi, n=n: e.activation(out=RS[:, 0:n], in_=PSUM[:, pi, 0:n], func=AF.Sqrt,
                                                                     bias=EPS, scale=1.0 / D), reads=pk(pi), writes=["RS"])
                        if 'a1b' in SKIP:
                            continue
                        op('dve', lambda e, n=n: e.reciprocal(out=RS[:, 0:n], in_=RS[:, 0:n]), reads=["RS"], writes=["RS"])
                        if 'a1c' in SKIP:
                            continue
                        for k in range(8):
                            xg = XG[:, k % 2, 0:n]
                            op('act', lambda e, k=k, o=o, n=n, xg=xg: e.activation(
                                out=xg, in_=X[:, k, c0 + o:c0 + o + n], func=AF.Copy, scale=g1[:, l, k:k + 1]),
                                reads=["X"], writes=[("XG", k % 2)])
                            op('dve', lambda e, k=k, o=o, n=n, xg=xg: e.tensor_tensor(
                                out=XN[:, k, o:o + n], in0=xg, in1=RS[:, 0:n], op=ALU.mult),
                                reads=[("XG", k % 2), "RS"], writes=[("XN", k)])
                    if STAGE >= 8 and bi >= 1:
                        norm2_into(l, MIX, SQ, RS, XG, (bi - 1) * 512, 0, 512)
                    XNr = [("XN", k) for k in range(8)]
                    for ch in range(3):
                        wcols = 512 if ch < 2 else 256
                        Wv, wk = wq_a2.get(ch)
                        for j in range(wcols // 128):
                            col = ch * 512 + j * 128
                            if col == 640:
                                continue
                            for (o, n) in subs:
                                pi = psb()
                                mm_group(pi, n, [(Wv[:, k, j * 128:(j + 1) * 128], XN[:, k, o:o + n]) for k in range(8)],
                                         XNr + [wk])
                                if 'a2a' in SKIP:
                                    continue
                                if col < 640:
                                    isq = col < 512
                                    op('act', lambda e, pi=pi, n=n: e.activation(out=SQh[:, 0:n], in_=PSUM[:, pi, 0:n],
                                                                                 func=AF.Square), reads=pk(pi), writes=["SQh"])
                                    p2 = psb()
                                    mm_group(p2, n, [(blk64[:, :], SQh[:, 0:n])], ["SQh", "blk64"])
                                    op('act', lambda e, p2=p2, n=n: e.activation(out=RSq[:, 0:n], in_=PSUM[:, p2, 0:n],
                                                                                 func=AF.Sqrt, bias=EPS, scale=1.0 / 64),
                                       reads=pk(p2), writes=["RSq"])
                                    op('dve', lambda e, n=n: e.reciprocal(out=RSq[:, 0:n], in_=RSq[:, 0:n]),
                                       reads=["RSq"], writes=["RSq"])
                                    if 'a2b' in SKIP:
                                        continue
                                    if isq:
                                        dst = R2[:, col // 128, o:o + n]
                                        dk = ("R2", col // 128)
                                        gi_ = 0
                                    else:
                                        dst = KT[:, 128 + o:128 + o + n]
                                        dk = "KT"
                                        gi_ = 1
                                    op('dve', lambda e, pi=pi, n=n, dst=dst, gi_=gi_: e.scalar_tensor_tensor(
                                        out=dst, in0=PSUM[:, pi, 0:n], scalar=qkg[:, l, gi_:gi_ + 1], in1=RSq[:, 0:n],
                                        op0=ALU.mult, op1=ALU.mult), reads=pk(pi) + ["RSq", "qkg"], writes=[dk])
                                    if (not isq) and bi == 3 and 'a2c' not in SKIP:
                                        if o == 0:
                                            op('dve', lambda e, pi=pi: e.scalar_tensor_tensor(
                                                out=KF[:, :], in0=PSUM[:, pi, 384:512], scalar=qkg[:, l, 1:2],
                                                in1=RSq[:, 384:512], op0=ALU.mult, op1=ALU.mult),
                                                reads=pk(pi) + ["RSq"], writes=["KF"])
                                            p3 = psb()
                                            op('pe', lambda e, p3=p3: e.transpose(out=PSUM[:, p3, 0:128], in_=KF[:, :],
                                                                                  identity=ident[:, :]),
                                               reads=["KF"], writes=pk(p3))
                                            op('act', lambda e, p3=p3: e.activation(out=OUTS[:, :], in_=PSUM[:, p3, 0:128],
                                                                                    func=AF.Copy), reads=pk(p3), writes=["OUTS"])
                                            dma('sp', s_o1, kp[l], OUTS[:, :], reads=["OUTS"])
                                        else:
                                            op('dve', lambda e, pi=pi: e.scalar_tensor_tensor(
                                                out=KSF[:, :], in0=PSUM[:, pi, 0:64], scalar=qkg[:, l, 1:2],
                                                in1=RSq[:, 0:64], op0=ALU.mult, op1=ALU.mult),
                                                reads=pk(pi) + ["RSq"], writes=["KSF"])
                                            p3 = psb()
                                            op('pe', lambda e, p3=p3: e.transpose(out=PSUM[0:64, p3, 0:128], in_=KSF[:, :],
                                                                                  identity=ident[:, :]),
                                               reads=["KSF"], writes=pk(p3))
                                            op('act', lambda e, p3=p3: e.activation(out=OUTS2[0:64, :], in_=PSUM[0:64, p3, 0:128],
                                                                                    func=AF.Copy), reads=pk(p3), writes=["OUTS2"])
                                            for i_ in range(4):
                                                dma('sp', s_o2, ksam[l][:, 124 + i_, :], OUTS2[i_ * 16:(i_ + 1) * 16, :],
                                                    reads=["OUTS2"])
                                else:
                                    ut = (col - 768) // 128
                                    op('act', lambda e, pi=pi, n=n, ut=ut, o=o: e.activation(
                                        out=U[:, ut, o:o + n], in_=PSUM[:, pi, 0:n], func=AF.Copy),
                                        reads=pk(pi), writes=[("U", ut)])
                        if ch == 1 and 'a2d' not in SKIP:
                            pi = psb()
                            for tl in range(4):
                                for k in range(8):
                                    op('pe', lambda e, tl=tl, k=k, pi=pi: e.matmul(
                                        PSUM[:, pi, tl * 128:(tl + 1) * 128],
                                        lhsT=XN[:, k, tl * 128:(tl + 1) * 128], rhs=Wv[:, k, 128:256],
                                        start=(k == 0), stop=(k == 7)),
                                        reads=XNr + [wk], writes=pk(pi), signal=(k == 7))
                            if 'v1' in SKIP:
                                continue
                            op('act', lambda e, pi=pi: e.activation(out=VT[:, 1:5, :], in_=PSUM[:, pi, :].rearrange(
                                "p (t d) -> p t d", t=4), func=AF.Copy), reads=pk(pi), writes=["VT"])
                            if bi == 3 and 'v2' not in SKIP:
                                op('dve', lambda e, pi=pi: e.tensor_copy(out=OUTS[:, :], in_=PSUM[:, pi, 384:512]),
                                   reads=pk(pi), writes=["OUTS"])
                                dma('sp', s_o1, vp[l], OUTS[:, :], reads=["OUTS"])
                                pv5 = psb()
                                for k in range(8):
                                    op('pe', lambda e, k=k, pv5=pv5: e.matmul(
                                        PSUM[0:64, pv5, 0:128], lhsT=XN[:, k, 512:576], rhs=Wv[:, k, 128:256],
                                        start=(k == 0), stop=(k == 7)), reads=XNr + [wk], writes=pk(pv5), signal=(k == 7))
                                op('act', lambda e, pv5=pv5: e.activation(out=VS[:, :], in_=PSUM[0:64, pv5, 0:128], func=AF.Copy),
                                   reads=pk(pv5), writes=["VS"])
                                op('dve', lambda e, pv5=pv5: e.tensor_copy(out=OUTS2[0:64, :], in_=PSUM[0:64, pv5, 0:128]),
                                   reads=pk(pv5), writes=["OUTS2"])
                                for i_ in range(4):
                                    dma('sp', s_o2, vsam[l][:, 124 + i_, :], OUTS2[i_ * 16:(i_ + 1) * 16, :], reads=["OUTS2"])
                    fence()
                  E = dict(U=U, s_o=s_o, R2=R2, KT=KT, VT=VT, VS=VS, AT=AT, ST=STt, YG=YG, XN=XN, MIX=MIX, subs=subs, c0=c0)
                  if STAGE >= 3:
                      attn_block(l, bi, E)
                  if STAGE >= 2:
                      ssm_block(l, bi, E)
                  if STAGE >= 5:
                      mix_block(l, bi, E)
                  if DEBUG_DUMP and l == 0:
                      dma('sp', s_o, dbg_at[bi], AT[:, :, :], reads=[])
                      dma('sp', s_o, dbg_st[bi], STt[:, :, :], reads=[])
                      dma('sp', s_o, dbg_yg[bi], YG[:, :, :], reads=[])
                      dma('sp', s_o, dbg_mix[bi], MIX[:, :, :], reads=[])
                  op('pool', lambda e: e.tensor_copy(out=KT[:, 0:128], in_=KT[:, 128 + 384:128 + 512]), reads=["KT"], writes=["KT"])
                  op('pool', lambda e: e.tensor_copy(out=VT[:, 0, :], in_=VT[:, 4, :]), reads=["VT"], writes=["VT"])
                if "shift" not in SKIP:
                    with nc.allow_non_contiguous_dma(reason="cache shift"):
                        dma('sp', s_o, ksam[l][:, 0:124, :], cache_k[l][:, 4:128, :])
                        dma('sp', s_o, vsam[l][:, 0:124, :], cache_v[l][:, 4:128, :])
                if STAGE >= 8:
                    ffn_tail(l, MIX)
                fence()

        def ffn_specs(l):
            fs = []
            for c in range(8):
                fs.append((w_up[l][:, c * 512:(c + 1) * 512].rearrange("(k p) n -> p k n", p=128), [128, 8, 512]))
                fs.append((w_down[l][c * 512:(c + 1) * 512, :].rearrange("(k p) n -> p k n", p=128), [128, 4, 1024]))
            return fs

        def norm2_into(l, XN2, SQ, RS, XG, xcol, o, n):
            pi = psb()
            for hf in range(2):
                op('act', lambda e, hf=hf: e.activation(out=SQ[:, 0:4, 0:n], in_=X[:, 4 * hf:4 * hf + 4, xcol:xcol + n], func=AF.Square),
                   reads=["X"], writes=["SQ"])
                for k_ in range(4):
                    op('pe', lambda e, k_=k_, hf=hf: e.matmul(PSUM[:, pi, 0:n], lhsT=ones[:, :], rhs=SQ[:, k_, 0:n],
                                                             start=(hf == 0 and k_ == 0), stop=(hf == 1 and k_ == 3)),
                       reads=["SQ", "ones"], writes=pk(pi), signal=(k_ == 3))
            op('act', lambda e: e.activation(out=RS[:, 0:n], in_=PSUM[:, pi, 0:n], func=AF.Sqrt, bias=EPS, scale=1.0 / D),
               reads=pk(pi), writes=["RS"])
            op('dve', lambda e: e.reciprocal(out=RS[:, 0:n], in_=RS[:, 0:n]), reads=["RS"], writes=["RS"])
            for k_ in range(8):
                xg = XG[:, k_ % 2, 0:n]
                op('act', lambda e, k_=k_, xg=xg: e.activation(out=xg, in_=X[:, k_, xcol:xcol + n], func=AF.Copy,
                                                              scale=g2[:, l, k_:k_ + 1]), reads=["X"], writes=[("XG", k_ % 2)])
                op('dve', lambda e, k_=k_, xg=xg: e.tensor_tensor(out=XN2[:, k_, o:o + n], in0=xg, in1=RS[:, 0:n], op=ALU.mult),
                   reads=[("XG", k_ % 2), "RS"], writes=[("MIX", k_)])

        def ssm_block(l, bi, E):
            U = E['U']; s_o = E['s_o']; YG = E['YG']; ST = E['ST']; subs = E['subs']
            with ExitStack() as ph:
                def t(name, shape, dt=BF16):
                    return ph.enter_context(nc.sbuf_tensor("s%d_%d_" % (l, bi) + name, list(shape), dt))
                Y1 = t("Y1", [128, 512]); T1 = t("T1", [128, 512]); SG = T1
                php = ExitStack()
                def tp(name, shape, dt=BF16):
                    return php.enter_context(nc.sbuf_tensor("sp%d_%d_" % (l, bi) + name, list(shape), dt))
                Zt = [tp("Z%d" % i, [128, 2, 512]) for i in range(2)]
                PRM = [tp("PRM%d" % i, [128, 4, 512]) for i in range(2)]
                Ma = PRM[1][:, 0:2, :]
                Mb = PRM[1][:, 2:4, :]
                MK = ("PR", 1)
                Xc = [tp("Xc%d" % i, [128, 2, 512]) for i in range(2)]
                if l == 0 and bi == 0:
                    print("SBUF remaining in ssm prompt scope:", nc.sbuf_bytes_remaining)
                Gt = [tp("G%d" % i, [128, 2, 512]) for i in range(2)]
                PR = PRM
                INt = [tp("IN%d" % i, [128, 4, 2], F32) for i in range(2)]
                Ut = [tp("U%d" % i, [128, 2], F32) for i in range(2)]
                HO = tp("HO", [128, 2, 16], F32); HT = tp("HT", [128, 16], F32)
                HOUT = tp("HOUT", [16, 2, 128], F32)
                PTa = [tp("PTa%d" % i, [128, 256]) for i in range(2)]
                DRa = tp("DRa", [128, 256], F32)
                v3 = lambda ap_: ap_.rearrange("p (c j) -> p c j", c=NCHUNK)
                R2 = E['R2']; KT = E['KT']; VT = E['VT']; AT = E['AT']
                att_it = [0]

                def att_unit(i, hp, b):
                    hs = i * 2 + hp
                    rows = slice(hp * 64, (hp + 1) * 64)
                    has_prev = (bi * 4 + b) > 0
                    tb = att_it[0] % 2
                    att_it[0] += 1
                    qv = R2[rows, i, b * 128:(b + 1) * 128]
                    if has_prev:
                        op('pe', lambda e: e.matmul(PSUM[:, BS, 0:128], lhsT=KT[rows, b * 128:(b + 1) * 128], rhs=qv,
                                                    start=True, stop=True), reads=[("R2", i), "KT"], writes=pk(BS), signal=False)
                    op('pe', lambda e: e.matmul(PSUM[:, BS, 128:256], lhsT=KT[rows, 128 + b * 128:128 + (b + 1) * 128], rhs=qv,
                                                start=True, stop=True), reads=[("R2", i), "KT"], writes=pk(BS), signal=True)
                    c_lo = 0 if has_prev else 128
                    op('act', lambda e: e.activation(out=PTa[tb][:, c_lo:256], in_=PSUM[:, BS, c_lo:256], func=AF.Exp),
                       reads=pk(BS), writes=[("PTa", tb)])
                    tt('pool', PTa[tb][:, c_lo:256], PTa[tb][:, c_lo:256], EB[:, hs, c_lo:256], ALU.mult, [("PTa", tb), "EB"],
                       [("PTa", tb)])
                    return lambda: att_pv(rows, b, tb, has_prev)

                def att_pv(rows, b, tb, has_prev):
                    parts = ([(VT[:, b, rows], PTa[tb][:, 0:128])] if has_prev else []) + [(VT[:, b + 1, rows], PTa[tb][:, 128:256])]
                    bb = b % 2
                    for (coff, use_ones) in ((0, False), (256, True)):
                        for ii, (vv, pp) in enumerate(parts):
                            lh = ones[:, 0:64] if use_ones else vv
                            op('pe', lambda e, lh=lh, pp=pp, ii=ii, coff=coff: e.matmul(
                                PSUM[rows, BOD, coff + bb * 128:coff + (bb + 1) * 128], lhsT=lh, rhs=pp, start=(ii == 0),
                                stop=(ii == len(parts) - 1)), reads=[("PTa", tb), "VT", "ones"], writes=pk(BOD),
                                signal=(ii == len(parts) - 1))

                def att_norm_act(i, h):
                    op('act', lambda e: e.activation(out=DRa[:, :], in_=PSUM[:, BOD, 256:512], func=AF.Ln, bias=esink[:, l, i:i + 1],
                                                     scale=1.0), reads=pk(BOD) + ["esink"], writes=["DRa"])
                    op('act', lambda e: e.activation(out=DRa[:, :], in_=DRa[:, :], func=AF.Exp, scale=-1.0), reads=["DRa"], writes=["DRa"])

                def att_norm_dve(i, h):
                    tt('dve', AT[:, i, h * 256:(h + 1) * 256], PSUM[:, BOD, 0:256], DRa[:, :], ALU.mult, pk(BOD) + ["DRa"], [("AT", i)])

                att_groups = []
                if STAGE >= 3:
                    for i in range(4):
                        for h in range(2):
                            att_groups.append((i, h))

                def epilogue(q, yb, o, n):
                    op('dve', lambda e: e.scalar_tensor_tensor(out=Y1[:, 0:n], in0=U[:, q, o:o + n], scalar=dsk[:, l, q:q + 1],
                                                               in1=PSUM[:, yb, 0:n], op0=ALU.mult, op1=ALU.add),
                       reads=pk(yb) + [("U", q)], writes=["Y1"])
                    tt('dve', T1[:, 0:n], Y1[:, 0:n], Y1[:, 0:n], ALU.mult, ["Y1"], ["T1"])
                    op('dve', lambda e: e.tensor_scalar(out=T1[:, 0:n], in0=T1[:, 0:n], scalar1=0.044715, scalar2=1.0,
                                                        op0=ALU.mult, op1=ALU.add), reads=["T1"], writes=["T1"])
                    tt('dve', T1[:, 0:n], T1[:, 0:n], Y1[:, 0:n], ALU.mult, ["T1", "Y1"], ["T1"])
                    op('act', lambda e: e.activation(out=T1[:, 0:n], in_=T1[:, 0:n], func=AF.Sigmoid, scale=1.5957691216057308),
                       reads=["T1"], writes=["T1"])
                    return lambda: tt('dve', YG[:, q, o:o + n], Y1[:, 0:n], T1[:, 0:n], ALU.mult, ["Y1", "T1"], [("YG", q)])

                BX0, BY, BS, BOD, BUP = 0, 2, 3, 4, 5

                def x0_mm(s):
                    q = s // 4
                    mm_group(BX0, 512, [(BTr[:, s, :], U[:, q, 0:512])], [("U", q), "BTr"])
                    mm_group(BX0 + 1, 512, [(BTi[:, s, :], U[:, q, 0:512])], [("U", q), "BTi"])

                def evac(s):
                    b_ = s % 2
                    xk = ("Xc", b_)
                    op('act', lambda e: e.activation(out=Xc[b_][:, 0, :], in_=PSUM[:, BX0, :], func=AF.Copy), reads=pk(BX0), writes=[xk])
                    op('act', lambda e: e.activation(out=Xc[b_][:, 1, :], in_=PSUM[:, BX0 + 1, :], func=AF.Copy), reads=pk(BX0 + 1),
                       writes=[xk])

                def modops(s):
                    b_ = s % 2
                    xk = ("Xc", b_)
                    zk = ("Z", b_)
                    c4 = bc_mid(PHc[:, s, 0:CHUNK], 2 * NCHUNK)
                    s4 = bc_mid(PHs[:, s, 0:CHUNK], 2 * NCHUNK)
                    x4 = Xc[b_][:, :, :].rearrange("p c (h j) -> p (c h) j", j=CHUNK)
                    tt('dve', Ma.rearrange("p c (h j) -> p (c h) j", j=CHUNK), x4, c4, ALU.mult, [xk, "PH"], [MK])
                    tt('dve', Mb.rearrange("p c (h j) -> p (c h) j", j=CHUNK), x4, s4, ALU.mult, [xk, "PH"], [MK])
                    tt('dve', Zt[b_][:, 0, :], Ma[:, 0, :], Mb[:, 1, :], ALU.add, [MK], [zk])
                    tt('dve', Zt[b_][:, 1, :], Ma[:, 1, :], Mb[:, 0, :], ALU.subtract, [MK], [zk])

                def chain(tiles, hook=lambda: None):
                    for ch in range(NCHUNK):
                        for s in tiles:
                            b_ = s % 2
                            gk = ("G", b_)
                            if ch == 0:
                                ge = GE2[:, s, :]
                                ger = GE2[:, s, ::-1]
                                rk = ["GE"]
                            else:
                                ge = Gt[b_][:, :, ch * CHUNK - 1]
                                ger = Gt[b_][:, ::-1, ch * CHUNK - 1]
                                rk = [gk]
                            tt('dve', Ut[b_][:, :], ger, SS[:, s, :], ALU.mult, rk + ["SS"], [("UT", b_)])
                            op('dve', lambda e, ge=ge, s=s, b_=b_, ch=ch: e.scalar_tensor_tensor(
                                out=INt[b_][:, ch, :], in0=ge, scalar=R128c[:, s:s + 1], in1=Ut[b_][:, :], op0=ALU.mult, op1=ALU.add),
                                reads=rk + [("UT", b_)], writes=[("IN", b_)])
                        hook()
                        for s in tiles:
                            b_ = s % 2
                            gk = ("G", b_)
                            cs_ = slice(ch * CHUNK, (ch + 1) * CHUNK)
                            for c_ in range(2):
                                op('dve', lambda e, s=s, ch=ch, cs_=cs_, c_=c_, b_=b_: e.tensor_tensor_scan(
                                    out=Gt[b_][:, c_, cs_], data0=MAG[:, s:s + 1].to_broadcast([128, CHUNK]), data1=Zt[b_][:, c_, cs_],
                                    initial=INt[b_][:, ch, c_:c_ + 1], op0=ALU.mult, op1=ALU.add),
                                    reads=[("Z", b_), ("IN", b_)], writes=[gk])
                            hook()

                def finish(s):
                    q = s // 4
                    slot = s % 4
                    yb = BY
                    b_ = s % 2
                    gk = ("G", b_)
                    cb = bc_mid(PHc[:, s, 0:CHUNK], NCHUNK)
                    sb_ = bc_mid(PHs[:, s, 0:CHUNK], NCHUNK)
                    op('dve', lambda e: e.tensor_copy(out=GE2[:, s, :], in_=Gt[b_][:, :, 511]), reads=[gk], writes=["GE"])
                    P_ = PR[b_]
                    pkey = ("PR", b_)
                    gr_ = v3(Gt[b_][:, 0, :])
                    gi_ = v3(Gt[b_][:, 1, :])
                    c4 = bc_mid(PHc[:, s, 0:CHUNK], 2 * NCHUNK)
                    s4 = bc_mid(PHs[:, s, 0:CHUNK], 2 * NCHUNK)
                    g4 = Gt[b_][:, :, :].rearrange("p c (h j) -> p (c h) j", j=CHUNK)
                    tt('dve', P_[:, 0:2, :].rearrange("p c (h j) -> p (c h) j", j=CHUNK), g4, c4, ALU.mult, [gk, "PH"], [pkey])
                    tt('dve', P_[:, 2:4, :].rearrange("p c (h j) -> p (c h) j", j=CHUNK), g4, s4, ALU.mult, [gk, "PH"], [pkey])
                    pairs = [(CTr[:, s, :], P_[:, 0, :]), (nCTi[:, s, :], P_[:, 1, :]), (nCTi[:, s, :], P_[:, 2, :]),
                             (nCTr[:, s, :], P_[:, 3, :])]
                    for ii, (lh, rh) in enumerate(pairs):
                        first = (slot == 0 and ii == 0)
                        lastm = (slot == 3 and ii == 3)
                        op('pe', lambda e, lh=lh, rh=rh, first=first, lastm=lastm: e.matmul(
                            PSUM[:, yb, 0:512], lhsT=lh, rhs=rh, start=first, stop=lastm),
                            reads=[pkey, "CT"], writes=pk(yb), signal=(ii == 3))
                    if slot == 3:
                        pending.append(lambda: late.append(epilogue(q, yb, 0, 512)))

                ffn_on = (bi >= 1) and STAGE >= 8 and 'ffni' not in SKIP
                if ffn_on:
                    XN2 = E['MIX']
                    xo = (bi - 1) * 512
                    Hf = tp("Hf", [128, 4, 512])
                    wq_f = WQ(ffn_specs(l))
                    BDN = (5, 6, 7)

                    def ffn_up_unit(c, j):
                        Wu, wuk = wq_f.get(2 * c)
                        mm_group(BUP, 512, [(Wu[:, k_, j * 128:(j + 1) * 128], XN2[:, k_, 0:512]) for k_ in range(8)],
                                 [("MIX", k_) for k_ in range(8)] + [wuk])
                        op('act', lambda e: e.activation(out=Hf[:, j, :], in_=PSUM[:, BUP, :], func=AF.Relu),
                           reads=pk(BUP), writes=[("Hf", j)])
                        op('act', lambda e: e.activation(out=Hf[:, j, :], in_=Hf[:, j, :], func=AF.Square),
                           reads=[("Hf", j)], writes=[("Hf", j)])

                    def ffn_down(c, m):
                        Wd, wdk = wq_f.get(2 * c + 1)
                        mm_group(BDN[m % 3], 512, [(Wd[:, j, m * 128:(m + 1) * 128], Hf[:, j, :]) for j in range(4)],
                                 [("Hf", j) for j in range(4)] + [wdk])

                    def ffn_add(m):
                        bank = BDN[m % 3]
                        op('dve', lambda e: e.tensor_tensor(out=X[:, m, xo:xo + 512], in0=PSUM[:, bank, :], in1=X[:, m, xo:xo + 512],
                                                            op=ALU.add), reads=pk(bank) + ["X"], writes=["X"])
                else:
                    Wg, wgk = wload(w_glu[l].rearrange("(k p) n -> p k n", p=128), [128, 4, 512], None)

                pending = []
                late = []
                x0_mm(0)
                evac(0)
                x0_mm(1)
                evac(1)
                for p_ in range(8):
                    s0, s1 = 2 * p_, 2 * p_ + 1
                    modops(s0)
                    modops(s1)
                    if p_ < 7:
                        x0_mm(s0 + 2)
                        evac(s0 + 2)
                        x0_mm(s1 + 2)
                        evac(s1 + 2)
                    aunits = []
                    if att_groups:
                        if p_ > 0:
                            att_norm_dve(*att_groups[p_ - 1])
                        gi_, gh_ = att_groups[p_]
                        aunits = [(gi_, hp, b) for hp in range(2) for b in (2 * gh_, 2 * gh_ + 1)]
                    for j in range(4):
                        pv_ = att_unit(*aunits[j]) if j < len(aunits) else None
                        if ffn_on:
                            ffn_up_unit(p_, j)
                        if pv_:
                            pv_()
                    if aunits:
                        att_norm_act(*att_groups[p_])
                    chain([s0, s1])
                    todo = pending
                    pending = []
                    for f_ in todo:
                        f_()
                    if ffn_on:
                        ffn_down(p_, 0)
                        ffn_down(p_, 1)
                        ffn_down(p_, 2)
                    finish(s0)
                    finish(s1)
                    if ffn_on:
                        ffn_add(0)
                        for m_ in range(3, 8):
                            ffn_down(p_, m_)
                            ffn_add(m_ - 2)
                        ffn_add(6)
                        ffn_add(7)
                    for f_ in late:
                        f_()
                    late = []
                if att_groups:
                    att_norm_dve(*att_groups[7])
                for f_ in pending:
                    f_()
                for f_ in late:
                    f_()
                if ffn_on:
                    Wg, wgk = wload(w_glu[l].rearrange("(k p) n -> p k n", p=128), [128, 4, 512], None)
                if bi == 3:
                    tt('dve', HO[:, 0, :], GE2[:, :, 0], P127c[:, :], ALU.mult, ["GE"], ["HO"])
                    tt('dve', HT[:, :], GE2[:, :, 1], P127s[:, :], ALU.mult, ["GE"], ["HT"])
                    tt('dve', HO[:, 0, :], HO[:, 0, :], HT[:, :], ALU.subtract, ["HO", "HT"], ["HO"])
                    tt('dve', HO[:, 1, :], GE2[:, :, 0], P127s[:, :], ALU.mult, ["GE", "HO"], ["HO"])
                    tt('dve', HT[:, :], GE2[:, :, 1], P127c[:, :], ALU.mult, ["GE", "HO"], ["HT"])
                    tt('dve', HO[:, 1, :], HO[:, 1, :], HT[:, :], ALU.add, ["HO", "HT"], ["HO"])
                    for c_ in range(2):
                        pi = 6 + c_
                        op('pe', lambda e, c_=c_, pi=pi: e.transpose(out=PSUM[0:16, pi, 0:128], in_=HO[:, c_, :], identity=ident[:, :]),
                           reads=["HO"], writes=pk(pi))
                        op('act', lambda e, c_=c_, pi=pi: e.activation(out=HOUT[:, c_, :], in_=PSUM[0:16, pi, 0:128], func=AF.Copy),
                           reads=pk(pi), writes=["HOUT"])
                    dma('sp', s_o, hrp[l], HOUT[:, 0, :], reads=["HOUT"])
                    dma('sp', s_o, hip[l], HOUT[:, 1, :], reads=["HOUT"])
                fence()
                php.close()
                if bi == 3 and STAGE >= 4 and 'ssms' not in SKIP:
                    ssm_sample(l, E, epilogue)
                    fence()
                for j in range(4):
                    for (o, n) in subs:
                        pi = psb()
                        mm_group(pi, n, [(Wg[:, q_, j * 128:(j + 1) * 128], YG[:, q_, o:o + n]) for q_ in range(4)],
                                 [("YG", q_) for q_ in range(4)] + [wgk])
                        op('act', lambda e, pi=pi, n=n, j=j: e.activation(out=SG[:, 0:n], in_=PSUM[:, pi, 0:n], func=AF.Sigmoid,
                                                                          bias=bgl[:, l, j:j + 1], scale=1.0),
                           reads=pk(pi), writes=["T1"])
                        tt('dve', ST[:, j, o:o + n], YG[:, j, o:o + n], SG[:, 0:n], ALU.mult, ["T1", ("YG", j)], [("ST", j)])
                fence()

        def ssm_sample(l, E, epilogue):
            U = E['U']; s_o = E['s_o']
            with ExitStack() as ph:
                def t(name, shape, dt=F32):
                    return ph.enter_context(nc.sbuf_tensor("ss%d_" % l + name, list(shape), dt))
                SN = t("SN", [16, 2048])
                H0 = [t("H0%d" % c_, [128, 16, 16]) for c_ in range(2)]
                HS = [t("HS%d" % c_, [128, 16, 64]) for c_ in range(2)]
                HSb = [t("HSb%d" % c_, [128, 16, 64], BF16) for c_ in range(2)]
                TA = t("TA", [128, 16, 16]); TB = t("TB", [128, 16, 16])
                s_s = C.stream("ssl%d" % l)
                for s in range(16):
                    q = s // 4
                    for c_, BT in ((0, BTr), (1, BTi)):
                        bank = 2 * c_ + s // 8
                        op('pe', lambda e, s=s, q=q, BT=BT, bank=bank: e.matmul(
                            PSUM[:, bank, (s % 8) * 64:(s % 8) * 64 + 64], lhsT=BT[:, s, :], rhs=U[:, q, 512:576],
                            start=True, stop=True), reads=[("U", q), "BTr", "BTi"], writes=pk(bank), signal=True)
                for c_, src in ((0, st_re), (1, st_im)):
                    dma('sp', s_s, SN[:, :], src[l], writes=["SN"])
                    pi = 6 + c_
                    for s in range(16):
                        op('pe', lambda e, s=s, pi=pi: e.transpose(out=PSUM[:, pi, s * 16:(s + 1) * 16],
                                                                   in_=SN[0:16, s * 128:(s + 1) * 128], identity=ident[0:16, 0:16]),
                           reads=["SN"], writes=pk(pi), signal=(s == 15))
                    op('act', lambda e, c_=c_, pi=pi: e.activation(out=H0[c_][:, :, :], in_=PSUM[:, pi, 0:256].rearrange(
                        "p (s b) -> p s b", s=16), func=AF.Copy), reads=pk(pi), writes=[("H0", c_)])
                ARb = bc_last(AR[:, :], 16)
                AIb = bc_last(AI[:, :], 16)
                xv = [PSUM[:, 2 * c_:2 * c_ + 2, :].rearrange("p b (s c) -> p (b s) c", c=64) for c_ in range(2)]
                for i_ in range(4):
                    cs_ = slice(i_ * 16, (i_ + 1) * 16)
                    if i_ == 0:
                        pr_, pi_ = H0[0][:, :, :], H0[1][:, :, :]
                        rk = [("H0", 0), ("H0", 1)]
                    else:
                        ps_ = slice((i_ - 1) * 16, i_ * 16)
                        pr_, pi_ = HS[0][:, :, ps_], HS[1][:, :, ps_]
                        rk = ["HS"]
                    tt('dve', TA[:, :, :], pr_, ARb, ALU.mult, rk + ["AR"], ["TA"])
                    tt('dve', TB[:, :, :], pi_, AIb, ALU.mult, rk + ["AR"], ["TB"])
                    tt('dve', TA[:, :, :], TA[:, :, :], TB[:, :, :], ALU.subtract, ["TA", "TB"], ["TA"])
                    tt('dve', HS[0][:, :, cs_], xv[0][:, :, cs_], TA[:, :, :], ALU.add, pk(0, 2) + ["TA"], ["HS"])
                    tt('dve', TA[:, :, :], pi_, ARb, ALU.mult, rk + ["AR", "HS"], ["TA"])
                    tt('dve', TB[:, :, :], pr_, AIb, ALU.mult, rk + ["AR"], ["TB"])
                    tt('dve', TA[:, :, :], TA[:, :, :], TB[:, :, :], ALU.add, ["TA", "TB"], ["TA"])
                    tt('dve', HS[1][:, :, cs_], xv[1][:, :, cs_], TA[:, :, :], ALU.add, pk(2, 2) + ["TA", "HS"], ["HS"])
                for c_ in range(2):
                    op('dve', lambda e, c_=c_: e.tensor_copy(out=HSb[c_][:, :, :], in_=HS[c_][:, :, :]), reads=["HS", "HS"],
                       writes=[("HSb", c_)])
                for q in range(4):
                    yb = 4 + (q % 2)
                    pairs = []
                    for slot in range(4):
                        s = 4 * q + slot
                        pairs.append((CTr[:, s, :], HSb[0][:, s, :]))
                        pairs.append((nCTi[:, s, :], HSb[1][:, s, :]))
                    mm_group(yb, 64, pairs, [("HSb", 0), ("HSb", 1), "CT"])
                    epilogue(q, yb, 512, 64)()
                for c_, dst in ((0, hrs), (1, his)):
                    for g4 in range(4):
                        pi = g4
                        for j in range(4):
                            s = g4 * 4 + j
                            op('pe', lambda e, s=s, j=j, pi=pi, c_=c_: e.transpose(
                                out=PSUM[0:16, pi, j * 128:(j + 1) * 128], in_=HS[c_][:, s, 48:64], identity=ident[:, :]),
                                reads=["HS", "HS"], writes=pk(pi), signal=(j == 3))
                        op('act', lambda e, pi=pi, g4=g4: e.activation(out=SN[:, g4 * 512:(g4 + 1) * 512], in_=PSUM[0:16, pi, :],
                                                                      func=AF.Copy), reads=pk(pi), writes=["SN"])
                    dma('sp', s_s, dst[l].rearrange("b s r -> b (s r)"), SN[:, :], reads=["SN"])
                fence()

        def attn_block(l, bi, E):
            if bi == 3 and STAGE >= 4 and 'atts' not in SKIP:
                attn_sample(l, E)
                fence()

        def attn_sample(l, E):
            R2 = E['R2']; KT = E['KT']; VS = E['VS']; AT = E['AT']
            with ExitStack() as ph:
                def t(name, shape, dt=BF16):
                    return ph.enter_context(nc.sbuf_tensor("as%d_" % l + name, list(shape), dt))
                CK = t("CK", [128, 16, 128], F32)
                CKT = t("CKT", [128, 16, 128])
                CV = t("CV", [128, 16, 128])
                TMPc = t("TMPc", [128, 512], F32)
                Pc = t("Pc", [128, 512])
                TMPn = t("TMPn", [64, 512], F32)
                Pn = t("Pn", [64, 512])
                DR = t("DR", [128, 256], F32)
                s_c = C.stream("asl%d" % l)
                s_v = C.stream("asv%d" % l)
                with nc.allow_non_contiguous_dma(reason="cache load"):
                    dma('sp', s_c, CK[:, :, :], cache_k[l].rearrange("s j d -> j s d"), writes=["CK"])
                    dma('pool', s_v, CV[:, :, :], cache_v[l].rearrange("s j d -> j s d"), writes=["CV"])
                for sl in range(16):
                    pi = sl % 4
                    op('pe', lambda e, sl=sl, pi=pi: e.transpose(out=PSUM[:, pi, 0:128], in_=CK[:, sl, :], identity=ident[:, :]),
                       reads=["CK"], writes=pk(pi))
                    if sl % 2 == 0:
                        op('act', lambda e, sl=sl, pi=pi: e.activation(out=CKT[:, sl, :], in_=PSUM[:, pi, 0:128], func=AF.Copy),
                           reads=pk(pi), writes=["CKT"])
                    else:
                        op('dve', lambda e, sl=sl, pi=pi: e.tensor_copy(out=CKT[:, sl, :], in_=PSUM[:, pi, 0:128]),
                           reads=pk(pi), writes=["CKT"])
                po, pd = 6, 7
                for kv in range(2):
                    rows = slice(kv * 64, (kv + 1) * 64)
                    for sl in range(16):
                        op('pe', lambda e, sl=sl, kv=kv, rows=rows: e.matmul(
                            PSUM[:, 4 + kv, sl:256:16], lhsT=CKT[rows, sl, :],
                            rhs=R2[rows, :, 512 + sl:576:16], start=True, stop=True),
                            reads=["CKT"] + [("R2", i) for i in range(4)], writes=pk(4 + kv), signal=(sl == 15))
                for kv in range(2):
                    op('dve', lambda e, kv=kv: e.tensor_tensor(
                        out=TMPc[:, kv * 256:(kv + 1) * 256].rearrange("p (a s) -> p a s", s=16),
                        in0=PSUM[:, 4 + kv, 0:256].rearrange("p (a s) -> p a s", s=16),
                        in1=bc_last(biasc[:, kv * 16:(kv + 1) * 16], 16), op=ALU.add), reads=pk(4 + kv), writes=["TMPc"])
                op('act', lambda e: e.activation(out=Pc[:, :], in_=TMPc[:, :], func=AF.Exp), reads=["TMPc"], writes=["Pc"])
                for kv in range(2):
                    rows = slice(kv * 64, (kv + 1) * 64)
                    op('pe', lambda e, kv=kv, rows=rows: e.matmul(
                        PSUM[0:64, kv, 0:256], lhsT=KT[rows, 128 + 512:128 + 576],
                        rhs=R2[rows, :, 512:576], start=True, stop=True),
                        reads=["KT"] + [("R2", i) for i in range(4)], writes=pk(kv), signal=True)
                    tt('dve', TMPn[:, kv * 256:(kv + 1) * 256], PSUM[0:64, kv, 0:256], biasnf[:, kv * 256:(kv + 1) * 256], ALU.add,
                       pk(kv), ["TMPn"])
                op('act', lambda e: e.activation(out=Pn[:, :], in_=TMPn[:, :], func=AF.Exp), reads=["TMPn"], writes=["Pn"])
                for (bank, use_ones) in ((po, False), (pd, True)):
                    for kv in range(2):
                        rows = slice(kv * 64, (kv + 1) * 64)
                        lh = ones[0:64, 0:64] if use_ones else VS[0:64, rows]
                        op('pe', lambda e, bank=bank, kv=kv, rows=rows, lh=lh: e.matmul(
                            PSUM[rows, bank, 0:256], lhsT=lh, rhs=Pn[0:64, kv * 256:(kv + 1) * 256], start=True, stop=False),
                            reads=["Pn", "VS", "ones"], writes=pk(bank), signal=False)
                        for sl in range(16):
                            lh2 = ones[:, 0:64] if use_ones else CV[:, sl, rows]
                            op('pe', lambda e, bank=bank, kv=kv, rows=rows, sl=sl, lh2=lh2: e.matmul(
                                PSUM[rows, bank, sl:256:16], lhsT=lh2, rhs=Pc[:, kv * 256 + sl:kv * 256 + 256:16],
                                start=False, stop=(sl == 15)), reads=["Pc", "CV", "ones"], writes=pk(bank), signal=(sl == 15))
                op('dve', lambda e: e.tensor_tensor(out=DR[:, :].rearrange("p (h c) -> p h c", h=4),
                                                    in0=PSUM[:, pd, 0:256].rearrange("p (h c) -> p h c", h=4),
                                                    in1=bc_last(esink[:, l, :], 64), op=ALU.add), reads=pk(pd) + ["esink"], writes=["DRs"])
                op('dve', lambda e: e.reciprocal(out=DR[:, :], in_=DR[:, :]), reads=["DRs"], writes=["DRs"])
                op('dve', lambda e: e.tensor_tensor(out=AT[:, :, 512:576], in0=PSUM[:, po, 0:256].rearrange("p (h c) -> p h c", h=4),
                                                    in1=DR[:, :].rearrange("p (h c) -> p h c", h=4), op=ALU.mult),
                   reads=pk(po) + ["DRs"], writes=[("AT", i) for i in range(4)])
                fence()

        def mix_block(l, bi, E):
            XN = E['XN']; MIX = E['MIX']; AT = E['AT']; ST = E['ST']; subs = E['subs']; c0 = E['c0']
            with ExitStack() as ph:
                def t(name, shape, dt=BF16):
                    return ph.enter_context(nc.sbuf_tensor("mx%d_%d_" % (l, bi) + name, list(shape), dt))
                SGA = t("SGA", [128, 576], F32)
                TM = t("TM", [128, 576])
                WL = [t("WL%d" % i, [128, 4096]) for i in range(2)]
                wl_s = [C.stream("wl%d_%d_%d" % (l, bi, i)) for i in range(2)]
                R5 = Ring(WR + WL, wr_stream + wl_s, [("wr", i) for i in range(3)] + [("wl", i) for i in range(2)])
                XNr = [("XN", k_) for k_ in range(8)]
                mspecs = []
                for h in range(2):
                    for (Wsrc, gcol) in ((w_ao, 1280), (w_so, 2304)):
                        mspecs.append((Wsrc[l][:, h * 512:(h + 1) * 512].rearrange("(k p) n -> p k n", p=128), [128, 4, 512]))
                        mspecs.append((w_in[l][:, gcol + h * 512:gcol + (h + 1) * 512].rearrange("(k p) n -> p k n", p=128), [128, 8, 512]))
                for h in range(2):
                    mspecs.append((w_out[l][:, h * 512:(h + 1) * 512].rearrange("(k p) n -> p k n", p=128), [128, 8, 512]))
                wq_m = WQ(mspecs, ring=R5, ahead=2)
                mi = 0
                for h in range(2):
                    for (Wsrc, Act, akey, gcol, first) in ((w_ao, AT, "AT", 1280, True), (w_so, ST, "ST", 2304, False)):
                        Wo_, wok = wq_m.get(mi)
                        Wg_, wgk = wq_m.get(mi + 1)
                        mi += 2
                        for j in range(4):
                            m = h * 4 + j
                            bcol = (0 if first else 8) + m
                            for (o, n) in subs:
                                pg = psb()
                                mm_group(pg, n, [(Wg_[:, k_, j * 128:(j + 1) * 128], XN[:, k_, o:o + n]) for k_ in range(8)],
                                         XNr + [wgk])
                                op('act', lambda e, pg=pg, n=n, bcol=bcol: e.activation(
                                    out=SGA[:, 0:n], in_=PSUM[:, pg, 0:n], func=AF.Sigmoid, bias=bg[:, l, bcol:bcol + 1], scale=1.0),
                                    reads=pk(pg), writes=["SGA"])
                                pa = psb()
                                mm_group(pa, n, [(Wo_[:, k_, j * 128:(j + 1) * 128], Act[:, k_, o:o + n]) for k_ in range(4)],
                                         [(akey, k_) for k_ in range(4)] + [wok])
                                if first:
                                    tt('dve', MIX[:, m, o:o + n], PSUM[:, pa, 0:n], SGA[:, 0:n], ALU.mult, pk(pa) + ["SGA"], [("MIX", m)])
                                else:
                                    tt('dve', TM[:, 0:n], PSUM[:, pa, 0:n], SGA[:, 0:n], ALU.mult, pk(pa) + ["SGA"], ["TM"])
                                    tt('pool', MIX[:, m, o:o + n], MIX[:, m, o:o + n], TM[:, 0:n], ALU.add, ["TM", ("MIX", m)], [("MIX", m)])
                for h in range(2):
                    Wo_, wok = wq_m.get(8 + h)
                    for j in range(4):
                        m = h * 4 + j
                        for (o, n) in subs:
                            pi = psb()
                            mm_group(pi, n, [(Wo_[:, k_, j * 128:(j + 1) * 128], MIX[:, k_, o:o + n]) for k_ in range(8)],
                                     [("MIX", k_) for k_ in range(8)] + [wok])
                            xc = c0 + o
                            op('dve', lambda e, pi=pi, n=n, m=m, xc=xc: e.tensor_tensor(
                                out=X[:, m, xc:xc + n], in0=PSUM[:, pi, 0:n], in1=X[:, m, xc:xc + n], op=ALU.add),
                                reads=pk(pi) + ["X"], writes=["X"])
                if STAGE >= 8 and bi == 3:
                    SQn = t("SQn", [128, 4, 512]); RSn = t("RSn", [128, 512], F32); XGn = t("XGn", [128, 2, 512], F32)
                    for (o, n) in subs:
                        norm2_into(l, MIX, SQn, RSn, XGn, c0 + o, o, n)
                fence()

        def ffn_tail(l, XN2):
            with ExitStack() as ph:
                def t(name, shape, dt=BF16):
                    return ph.enter_context(nc.sbuf_tensor("b%d_" % l + name, list(shape), dt))
                Hh = [t("H%d" % i, [128, 4, 512]) for i in range(2)]
                Rr = [t("R%d" % i, [128, 512]) for i in range(2)]
                subs = [(0, 512), (512, 64)]
                wq_f = WQ(ffn_specs(l))
                it = 0
                for c in range(8):
                    Wu, wuk = wq_f.get(2 * c)
                    Wd, wdk = wq_f.get(2 * c + 1)
                    for (o, n) in subs:
                        hb = it % 2
                        it += 1
                        for j in range(4):
                            pi = psb()
                            mm_group(pi, n, [(Wu[:, k_, j * 128:(j + 1) * 128], XN2[:, k_, o:o + n]) for k_ in range(8)],
                                     [("MIX", k_) for k_ in range(8)] + [wuk])
                            rb = j % 2
                            op('act', lambda e, pi=pi, n=n, rb=rb: e.activation(out=Rr[rb][:, 0:n], in_=PSUM[:, pi, 0:n],
                                                                               func=AF.Relu), reads=pk(pi), writes=[("R", rb)])
                            op('act', lambda e, n=n, rb=rb, hb=hb, j=j: e.activation(out=Hh[hb][:, j, 0:n], in_=Rr[rb][:, 0:n],
                                                                                    func=AF.Square), reads=[("R", rb)], writes=[("H", hb, j)])
                        for m in range(8):
                            pi = psb()
                            mm_group(pi, n, [(Wd[:, j, m * 128:(m + 1) * 128], Hh[hb][:, j, 0:n]) for j in range(4)],
                                     [("H", hb, j) for j in range(4)] + [wdk])
                            xc = 1536 + o
                            op('dve', lambda e, pi=pi, n=n, m=m, xc=xc: e.tensor_tensor(
                                out=X[:, m, xc:xc + n], in0=PSUM[:, pi, 0:n], in1=X[:, m, xc:xc + n], op=ALU.add),
                                reads=pk(pi) + ["X"], writes=["X"])
                fence()

        for l in range(NLAYERS):
            if STAGE < 1:
                break
            if STAGE >= 2:
                ssm_tables(l, mid=(phase0 if l == 0 else None))
            elif l == 0:
                phase0()
            layer_phase_a(l)
        with ExitStack() as ph:
            YT = [ph.enter_context(nc.sbuf_tensor("yt%d" % i, [128, D], F32)) for i in range(2)]
            yts = [C.stream("yts%d" % i) for i in range(2)]
            for tt in range(17):
                b = tt % 2
                rows = 128 if tt < 16 else NS
                for half in range(2):
                    pi = psb()
                    for j in range(4):
                        k = half * 4 + j
                        op('pe', lambda e, k=k, j=j, pi=pi, rows=rows, tt=tt: e.transpose(
                            out=PSUM[0:rows, pi, j * 128:(j + 1) * 128], in_=X[:, k, tt * 128:tt * 128 + rows],
                            identity=ident[:, :]),
                            reads=[("X", tt)], writes=pk(pi), signal=(j == 3))
                    dst = YT[b][0:rows, half * 512:(half + 1) * 512]
                    src_ps = PSUM[0:rows, pi, :]
                    if half == 0:
                        op('act', lambda e, s_=src_ps, d_=dst: e.activation(out=d_, in_=s_, func=AF.Copy),
                           reads=pk(pi), writes=[("yt", b)])
                    else:
                        op('dve', lambda e, s_=src_ps, d_=dst: e.tensor_copy(out=d_, in_=s_),
                           reads=pk(pi), writes=[("yt", b)])
                dstd = yp[tt * 128:(tt + 1) * 128, :] if tt < 16 else ys[:, :]
                dma('sp', yts[b], dstd, YT[b][0:rows, :], reads=[("yt", b)])
            fence()
    return nc


_NC_CACHE = {}


def _consts():
    ident = np.eye(128, dtype=np.float32)
    blk = np.zeros((128, 128), np.float32)
    blk[:64, :64] = 1.0
    blk[64:, 64:] = 1.0
    slopes = np.exp2(-8.0 * np.arange(1, 9, dtype=np.float64) / 8.0)
    j = np.arange(128)[:, None]
    i = np.arange(128)[None, :]
    biasp = np.zeros((128, 8, 256), np.float32)
    for t in range(4):
        for hp in range(2):
            h = t + 4 * hp
            d_prev = 128 + i - j
            d_cur = i - j
            bp = np.where(d_prev <= 128, -slopes[h] * d_prev, -30000.0)
            bc = np.where(d_cur >= 0, -slopes[h] * d_cur, -30000.0)
            biasp[:, t * 2 + hp, 0:128] = bp
            biasp[:, t * 2 + hp, 128:256] = bc
    biasc = np.zeros((128, 32), np.float32)
    biasn = np.zeros((4, 32), np.float32)
    for kv in range(2):
        for hq in range(4):
            h = kv * 4 + hq
            for qi in range(4):
                col = kv * 16 + hq * 4 + qi
                jj = np.arange(128)
                dist = 128 + qi - jj
                biasc[:, col] = np.where(jj >= qi, -slopes[h] * dist, -30000.0)
                jn = np.arange(4)
                dn = qi - jn
                biasn[:, col] = np.where(dn >= 0, -slopes[h] * dn, -30000.0)
    biasnf = np.full((64, 512), -30000.0, np.float32)
    for ip in range(4):
        for slp in range(16):
            r = ip * 16 + slp
            for kv in range(2):
                for hq in range(4):
                    h = kv * 4 + hq
                    for qi in range(ip, 4):
                        biasnf[r, kv * 256 + hq * 64 + qi * 16 + slp] = -slopes[h] * (qi - ip)
    cmask = np.zeros((128, 128), np.float32)
    for r in range(128):
        glp = (r % 32) // 16
        cmask[r, glp * 64:(glp + 1) * 64] = 1.0
    rmask = np.zeros((128, 4), np.float32)
    for r in range(128):
        rmask[r, r // 32] = 1.0
    return dict(c_ident=ident, c_blk64=blk, c_biasp=biasp, c_biasc=biasc, c_biasn=biasn, c_biasnf=biasnf,
                c_cmask=cmask, c_rmask=rmask)


def kernel(**inp):
    f = lambda a: np.ascontiguousarray(np.asarray(a), dtype=np.float32)
    qperm = np.concatenate([np.arange(h * 64, (h + 1) * 64) for h in HEAD_PERM])
    w_in = f(inp['w_in']).copy()
    w_in[:, :, 0:512] = w_in[:, :, qperm]
    w_ao = f(inp['w_attn_o'])[:, qperm, :]
    qg = f(inp['q_norm_g'])
    kg = f(inp['k_norm_g'])
    qk_g = np.stack([np.concatenate([qg, qg], axis=1), np.concatenate([kg, kg], axis=1)], axis=1)
    sk = f(inp['attn_sinks'])
    sinks = np.zeros((L, 4, 128), np.float32)
    for t in range(4):
        sinks[:, t, 0:64] = sk[:, t][:, None]
        sinks[:, t, 64:128] = sk[:, t + 4][:, None]
    shared = dict(
        w_in=np.ascontiguousarray(w_in), w_glu=f(inp['w_glu']), w_ao=np.ascontiguousarray(w_ao), w_so=f(inp['w_ssm_o']),
        w_out=f(inp['w_out']), w_up=f(inp['w_up']), w_down=f(inp['w_down']),
        norm1_g=f(inp['norm1_g']), norm2_g=f(inp['norm2_g']), b_gate=f(inp['b_gate']),
        qk_g=np.ascontiguousarray(qk_g), sinks=sinks,
        lam_re=f(inp['lam_re']), lam_im=f(inp['lam_im']), log_step=f(inp['log_step']),
        b_re=f(inp['b_re']).reshape(L, 2048, 16), b_im=f(inp['b_im']).reshape(L, 2048, 16),
        c_re=f(inp['c_re']).reshape(L, 512, 64), c_im=f(inp['c_im']).reshape(L, 512, 64),
        d_skip=f(inp['d_skip']), b_glu=f(inp['b_glu']),
    )
    pa = np.zeros((92, 128), np.float32)
    pa[0:16] = shared['norm1_g'].reshape(L * 8, 128)
    pa[16:32] = shared['norm2_g'].reshape(L * 8, 128)
    pa[32:64] = shared['b_gate'].reshape(L * 16, 128)
    pa[64:68] = shared['qk_g'].reshape(L * 2, 128)
    pa[68:76] = shared['sinks'].reshape(L * 4, 128)
    pa[76:84] = shared['d_skip'].reshape(L * 4, 128)
    pa[84:92] = shared['b_glu'].reshape(L * 4, 128)
    pb = np.zeros((L, 48, 128), np.float32)
    pb[:, 0:16] = shared['lam_re'].reshape(L, 16, 128)
    pb[:, 16:32] = shared['lam_im'].reshape(L, 16, 128)
    pb[:, 32:48] = np.repeat(shared['log_step'].reshape(L, 16, 2, 1), 64, axis=3).reshape(L, 16, 128)
    shared['pvec_a'] = pa
    shared['pvec_b'] = pb
    shared.update(_consts())
    x_prompt = f(inp['x_prompt'])
    x_sample = f(inp['x_sample'])
    ck = f(inp['cache_k']).reshape(L, 128, 128, 128)
    cv = f(inp['cache_v']).reshape(L, 128, 128, 128)
    sre = f(inp['state_ssm_re']).reshape(L, 128, 2048)
    sim = f(inp['state_ssm_im']).reshape(L, 128, 2048)
    in_maps = []
    for c in range(NCORES):
        m = dict(shared)
        m['xp'] = x_prompt[c]
        m['xs'] = np.ascontiguousarray(x_sample[c * 16:(c + 1) * 16].transpose(1, 0, 2).reshape(NS, D))
        m['cache_k'] = np.ascontiguousarray(ck[:, c * 16:(c + 1) * 16])
        m['cache_v'] = np.ascontiguousarray(cv[:, c * 16:(c + 1) * 16])
        m['st_re'] = np.ascontiguousarray(sre[:, c * 16:(c + 1) * 16])
        m['st_im'] = np.ascontiguousarray(sim[:, c * 16:(c + 1) * 16])
        in_maps.append(m)
    if 'nc' not in _NC_CACHE:
        _NC_CACHE['nc'] = build_program()
    ncr = DEBUG_CORES or NCORES
    res = run_bass_kernel_spmd(_NC_CACHE['nc'], in_maps[:ncr], core_ids=list(range(ncr)))
    R = list(res.results)
    _NC_CACHE['last'] = R
    while len(R) < NCORES:
        R.append(R[0])
    y_prompt = np.stack([R[c]['yp'] for c in range(NCORES)]).astype(np.float32)
    y_sample = np.concatenate([R[c]['ys'].reshape(4, 16, D).transpose(1, 0, 2) for c in range(NCORES)]).astype(np.float32)
    k_prompt = np.stack([R[c]['kp'] for c in range(NCORES)], axis=1).reshape(L, 8, 128, 2, 64)
    v_prompt = np.stack([R[c]['vp'] for c in range(NCORES)], axis=1).reshape(L, 8, 128, 2, 64)
    hr_p = np.stack([R[c]['hrp'] for c in range(NCORES)], axis=1).reshape(L, 8, 32, 64)
    hi_p = np.stack([R[c]['hip'] for c in range(NCORES)], axis=1).reshape(L, 8, 32, 64)
    k_s = np.concatenate([R[c]['ksam'] for c in range(NCORES)], axis=1).reshape(L, 128, 128, 2, 64)
    v_s = np.concatenate([R[c]['vsam'] for c in range(NCORES)], axis=1).reshape(L, 128, 128, 2, 64)
    hr_s = np.concatenate([R[c]['hrs'] for c in range(NCORES)], axis=1).reshape(L, 128, 32, 64)
    hi_s = np.concatenate([R[c]['his'] for c in range(NCORES)], axis=1).reshape(L, 128, 32, 64)
    return (y_prompt, y_sample, k_prompt.astype(np.float32), v_prompt.astype(np.float32),
            hr_p.astype(np.float32), hi_p.astype(np.float32), k_s.astype(np.float32), v_s.astype(np.float32),
            hr_s.astype(np.float32), hi_s.astype(np.float32))
```
